# Optimizing a Trainium2 kernel written in Bass

```python
import jax
import jax.numpy as jnp
from jax import lax
import numpy as np

D_MODEL = 1024
BATCH = 8
SEQ = 4096
DEPTH = 2

CTX_LEN = 256
GRID_W = 64
HEAD_DIM = 64
Q_BLOCK = 128
ROPE_BASE = 10000.0
A_HEADS = D_MODEL // (2 * HEAD_DIM)
A_KV_HEADS = 2
B_GROUPS = 4
B_GROUP_DIM = D_MODEL // (4 * B_GROUPS)
C_HEADS = D_MODEL // (4 * HEAD_DIM)
C_KV_HEADS = 2
WINDOW = 128
N_EXPERTS = 16
EC_CAPACITY_FACTOR = 2
EXPERT_FF = 2 * D_MODEL

A_Q = A_HEADS * HEAD_DIM
A_KV = A_KV_HEADS * HEAD_DIM
B_W = B_GROUPS * B_GROUP_DIM
C_Q = C_HEADS * HEAD_DIM
C_KV = C_KV_HEADS * HEAD_DIM
MIX_WIDTH = A_Q + B_W + C_Q
QU_WIDTH = A_Q + C_Q + B_W
IN_WIDTH = QU_WIDTH + 2 * A_KV + 2 * C_KV

LN_EPS = 1e-5
RMS_EPS = 1e-6
NEG_INF = -1e30
DEEPNORM_ALPHA = (2 * DEPTH) ** 0.25
DEEPNORM_BETA = (8 * DEPTH) ** -0.25

kernel_name = 'hybrid_gqa_fourier_swa_ecmoe_diffusion'


def _layer_norm(x):
    xf = x.astype(jnp.float32)
    mu = jnp.mean(xf, -1, keepdims=True)
    var = jnp.mean(jnp.square(xf - mu), -1, keepdims=True)
    return ((xf - mu) * lax.rsqrt(var + LN_EPS)).astype(x.dtype)


def _post_norm(res, g, b):
    return _layer_norm(res) * g + b


def _modulate(x, shift, scale):
    return _layer_norm(x) * (1 + scale) + shift


def _rms_norm(x, g):
    xf = x.astype(jnp.float32)
    y = xf * lax.rsqrt(jnp.mean(jnp.square(xf), -1, keepdims=True) + RMS_EPS)
    return y.astype(x.dtype) * g


def _rope_2d_tables(n_tokens, dtype):
    rows = n_tokens // GRID_W
    row = jnp.repeat(jnp.arange(rows, dtype=jnp.int32), GRID_W)
    col = jnp.tile(jnp.arange(GRID_W, dtype=jnp.int32), rows)
    axis_dim = HEAD_DIM // 2
    inv_freq = ROPE_BASE ** (-jnp.arange(0, axis_dim, 2, dtype=jnp.float32) / axis_dim)
    ang = jnp.stack([row[:, None] * inv_freq, col[:, None] * inv_freq], axis=1)
    return jnp.cos(ang).astype(dtype), jnp.sin(ang).astype(dtype)


def _apply_rope_2d(x, cos, sin):
    b, s, h, dh = x.shape
    xr = x.reshape(b, s, h, 2, 2, dh // 4)
    x1, x2 = xr[..., 0, :], xr[..., 1, :]
    cs, sn = cos[None, :, None], sin[None, :, None]
    out = jnp.stack([x1 * cs - x2 * sn, x2 * cs + x1 * sn], axis=-2)
    return out.reshape(b, s, h, dh)


def _heads(p, n):
    b, t, _ = p.shape
    return p.reshape(b, t, n, HEAD_DIM)


def _group_q(q, n_kv):
    b, t, h, dh = q.shape
    return q.reshape(b, t, n_kv, h // n_kv, dh)


def _split_qu(p):
    return p[..., :A_Q], p[..., A_Q:A_Q + C_Q], p[..., A_Q + C_Q:QU_WIDTH]


def _split_kv(p):
    ka = _heads(p[..., :A_KV], A_KV_HEADS)
    va = _heads(p[..., A_KV:2 * A_KV], A_KV_HEADS)
    kc = _heads(p[..., 2 * A_KV:2 * A_KV + C_KV], C_KV_HEADS)
    vc = _heads(p[..., 2 * A_KV + C_KV:], C_KV_HEADS)
    return ka, va, kc, vc


def _dense_block_attention(q, k, v):
    b, s, hkv, g, dh = q.shape
    nb = s // Q_BLOCK
    qb = jnp.moveaxis(q.reshape(b, nb, Q_BLOCK, hkv, g, dh), 1, 0)
    scale = dh ** -0.5

    def one(qblk):
        sc = jnp.einsum('bqhgd,bkhd->bhgqk', qblk, k).astype(jnp.float32) * scale
        p = jax.nn.softmax(sc, axis=-1).astype(v.dtype)
        return jnp.einsum('bhgqk,bkhd->bqhgd', p, v)

    out = lax.map(one, qb)
    return jnp.moveaxis(out, 0, 1).reshape(b, s, hkv * g * dh)


def _sink_softmax(sc, sink):
    b, hkv, g, qn, _ = sc.shape
    sk = jnp.broadcast_to(sink.astype(jnp.float32)[None, :, :, None, None], (b, hkv, g, qn, 1))
    return jax.nn.softmax(jnp.concatenate([sc, sk], axis=-1), axis=-1)[..., :-1]


def _ctx_sink_attention(q, k, v, sink):
    scale = q.shape[-1] ** -0.5
    sc = jnp.einsum('bqhgd,bkhd->bhgqk', q, k).astype(jnp.float32) * scale
    p = _sink_softmax(sc, sink).astype(v.dtype)
    out = jnp.einsum('bhgqk,bkhd->bqhgd', p, v)
    b, t = q.shape[:2]
    return out.reshape(b, t, -1)


def _window_sink_attention(q, k, v, k_ctx, v_ctx, sink):
    b, s, hkv, g, dh = q.shape
    nb = s // Q_BLOCK
    band = Q_BLOCK + 2 * WINDOW
    pad = ((0, 0), (WINDOW, WINDOW), (0, 0), (0, 0))
    kp, vp = jnp.pad(k, pad), jnp.pad(v, pad)
    qb = jnp.moveaxis(q.reshape(b, nb, Q_BLOCK, hkv, g, dh), 1, 0)
    scale = dh ** -0.5
    q_off = jnp.arange(Q_BLOCK)
    k_off = jnp.arange(band) - WINDOW
    in_window = jnp.abs(k_off[None, :] - q_off[:, None]) <= WINDOW
    ctx_valid = jnp.ones((Q_BLOCK, k_ctx.shape[1]), dtype=bool)

    def one(args):
        qblk, i = args
        start = i * Q_BLOCK
        kb = lax.dynamic_slice_in_dim(kp, start, band, axis=1)
        vb = lax.dynamic_slice_in_dim(vp, start, band, axis=1)
        k_pos = start + k_off
        valid = in_window & ((k_pos >= 0) & (k_pos < s))[None, :]
        mask = jnp.concatenate([ctx_valid, valid], axis=-1)
        kk = jnp.concatenate([k_ctx, kb], axis=1)
        vv = jnp.concatenate([v_ctx, vb], axis=1)
        sc = jnp.einsum('bqhgd,bkhd->bhgqk', qblk, kk).astype(jnp.float32) * scale
        sc = jnp.where(mask, sc, NEG_INF)
        p = _sink_softmax(sc, sink).astype(vv.dtype)
        return jnp.einsum('bhgqk,bkhd->bqhgd', p, vv)

    out = lax.map(one, (qb, jnp.arange(nb)))
    return jnp.moveaxis(out, 0, 1).reshape(b, s, hkv * g * dh)


def _fourier_mix(u, w, bias):
    b, t, _ = u.shape
    ug = u.reshape(b, t, B_GROUPS, B_GROUP_DIM).astype(jnp.float32)
    f = jnp.fft.fft2(ug, axes=(1, 3), norm='ortho').real.astype(u.dtype)
    y = jnp.einsum('btgc,gcd->btgd', f, w) + bias
    return y.reshape(b, t, B_W)


def _mixing(h_lat, h_ctx, w_in, q_norm, k_norm, w_four, b_four, sink, w_out, cos, sin, update_ctx):
    sink = sink.reshape(C_KV_HEADS, C_HEADS // C_KV_HEADS)
    ka_c, va_c, kc_c, vc_c = _split_kv(h_ctx @ w_in[:, QU_WIDTH:])
    ka_c = _rms_norm(ka_c, k_norm)
    p = h_lat @ w_in
    qa, qc, u = _split_qu(p)
    ka, va, kc, vc = _split_kv(p[..., QU_WIDTH:])
    qa = _apply_rope_2d(_rms_norm(_heads(qa, A_HEADS), q_norm), cos, sin)
    ka = _apply_rope_2d(_rms_norm(ka, k_norm), cos, sin)
    qc = _apply_rope_2d(_heads(qc, C_HEADS), cos, sin)
    kc = _apply_rope_2d(kc, cos, sin)
    out_a = _dense_block_attention(_group_q(qa, A_KV_HEADS),
                                   jnp.concatenate([ka_c, ka], axis=1),
                                   jnp.concatenate([va_c, va], axis=1))
    out_b = _fourier_mix(u, w_four, b_four)
    out_c = _window_sink_attention(_group_q(qc, C_KV_HEADS), kc, vc, kc_c, vc_c, sink)
    o_lat = jnp.concatenate([out_a, out_b, out_c], axis=-1) @ w_out
    if not update_ctx:
        return o_lat, None
    qa_c, qc_c, u_c = _split_qu(h_ctx @ w_in[:, :QU_WIDTH])
    qa_c = _rms_norm(_heads(qa_c, A_HEADS), q_norm)
    out_a_c = _dense_block_attention(_group_q(qa_c, A_KV_HEADS), ka_c, va_c)
    out_b_c = _fourier_mix(u_c, w_four, b_four)
    out_c_c = _ctx_sink_attention(_group_q(_heads(qc_c, C_HEADS), C_KV_HEADS), kc_c, vc_c, sink)
    o_ctx = jnp.concatenate([out_a_c, out_b_c, out_c_c], axis=-1) @ w_out
    return o_lat, o_ctx


def _expert_choice_moe(h, w_router, w_gate, w_up, w_down):
    b, t, d = h.shape
    capacity = EC_CAPACITY_FACTOR * t // N_EXPERTS
    logits = jnp.einsum('btd,de->bte', h, w_router).astype(jnp.float32)
    affinity = jax.nn.softmax(logits, axis=-1)
    gate, idx = lax.top_k(jnp.swapaxes(affinity, 1, 2), capacity)
    xg = jax.vmap(lambda hb, ib: hb[ib])(h, idx)
    a = jnp.einsum('becd,edf->becf', xg, w_gate)
    up = jnp.einsum('becd,edf->becf', xg, w_up)
    y = jnp.einsum('becf,efd->becd', jax.nn.silu(a) * up, w_down) * gate[..., None].astype(h.dtype)
    return jax.vmap(lambda ib, yb: jnp.zeros((t, d), yb.dtype).at[ib.reshape(-1)].add(yb.reshape(-1, d)))(idx, y)


def setup_inputs(seed: int = 0) -> dict:
    key = jax.random.key(seed)
    ks = jax.random.split(key, 24)
    D = D_MODEL

    def nrm(k, shape, scale):
        return jax.random.normal(k, shape, jnp.float32) * scale

    return {
        'x': nrm(ks[0], (BATCH, SEQ, D), 1.0),
        'c': nrm(ks[1], (BATCH, D), 1.0),
        'ctx': nrm(ks[2], (BATCH, CTX_LEN, D), 1.0),
        'c_ctx': nrm(ks[3], (D,), 1.0),
        'w_mod': nrm(ks[4], (DEPTH, D, 6 * D), 0.5 * D ** -0.5),
        'b_mod': nrm(ks[5], (DEPTH, 6 * D), 0.02),
        'w_in': nrm(ks[6], (DEPTH, D, IN_WIDTH), D ** -0.5),
        'q_norm_a': 1.0 + nrm(ks[7], (DEPTH, HEAD_DIM), 0.02),
        'k_norm_a': 1.0 + nrm(ks[8], (DEPTH, HEAD_DIM), 0.02),
        'w_fourier': nrm(ks[9], (DEPTH, B_GROUPS, B_GROUP_DIM, B_GROUP_DIM), B_GROUP_DIM ** -0.5),
        'b_fourier': nrm(ks[10], (DEPTH, B_GROUPS, B_GROUP_DIM), 0.02),
        'sink_c': nrm(ks[11], (DEPTH, C_HEADS), 0.5),
        'w_out': nrm(ks[12], (DEPTH, MIX_WIDTH, D), DEEPNORM_BETA * MIX_WIDTH ** -0.5),
        'ln1_g': 1.0 + nrm(ks[13], (DEPTH, D), 0.02),
        'ln1_b': nrm(ks[14], (DEPTH, D), 0.02),
        'w_router': nrm(ks[15], (DEPTH, D, N_EXPERTS), D ** -0.5),
        'w_gate': nrm(ks[16], (DEPTH, N_EXPERTS, D, EXPERT_FF), D ** -0.5),
        'w_up': nrm(ks[17], (DEPTH, N_EXPERTS, D, EXPERT_FF), D ** -0.5),
        'w_down': nrm(ks[18], (DEPTH, N_EXPERTS, EXPERT_FF, D), DEEPNORM_BETA * EXPERT_FF ** -0.5),
        'ln2_g': 1.0 + nrm(ks[19], (DEPTH, D), 0.02),
        'ln2_b': nrm(ks[20], (DEPTH, D), 0.02),
    }


def reference(x, c, ctx, c_ctx, w_mod, b_mod, w_in, q_norm_a, k_norm_a, w_fourier, b_fourier, sink_c,
              w_out, ln1_g, ln1_b, w_router, w_gate, w_up, w_down, ln2_g, ln2_b):
    cos, sin = _rope_2d_tables(x.shape[1], x.dtype)
    silu_c = jax.nn.silu(c)
    silu_cc = jax.nn.silu(c_ctx)
    x_lat, x_ctx = x, ctx
    for layer in range(DEPTH):
        update_ctx = layer < DEPTH - 1
        mod_lat = silu_c @ w_mod[layer] + b_mod[layer]
        mod_ctx = silu_cc @ w_mod[layer] + b_mod[layer]
        sh1, sc1, g1, sh2, sc2, g2 = jnp.split(mod_lat[:, None, :], 6, axis=-1)
        csh1, csc1, cg1, csh2, csc2, cg2 = jnp.split(mod_ctx, 6, axis=-1)
        h_lat = _modulate(x_lat, sh1, sc1)
        h_ctx = _modulate(x_ctx, csh1, csc1)
        o_lat, o_ctx = _mixing(h_lat, h_ctx, w_in[layer], q_norm_a[layer], k_norm_a[layer],
                               w_fourier[layer], b_fourier[layer], sink_c[layer], w_out[layer],
                               cos, sin, update_ctx)
        x_lat = _post_norm(DEEPNORM_ALPHA * x_lat + g1 * o_lat, ln1_g[layer], ln1_b[layer])
        y_lat = _expert_choice_moe(_modulate(x_lat, sh2, sc2), w_router[layer], w_gate[layer],
                                   w_up[layer], w_down[layer])
        x_lat = _post_norm(DEEPNORM_ALPHA * x_lat + g2 * y_lat, ln2_g[layer], ln2_b[layer])
        if update_ctx:
            x_ctx = _post_norm(DEEPNORM_ALPHA * x_ctx + cg1 * o_ctx, ln1_g[layer], ln1_b[layer])
            y_ctx = _expert_choice_moe(_modulate(x_ctx, csh2, csc2), w_router[layer], w_gate[layer],
                                       w_up[layer], w_down[layer])
            x_ctx = _post_norm(DEEPNORM_ALPHA * x_ctx + cg2 * y_ctx, ln2_g[layer], ln2_b[layer])
    return x_lat
```

```python
import numpy as np
import ml_dtypes
from contextlib import ExitStack
import concourse.bass as bass
import concourse.mybir as mybir
from concourse.bass_utils import run_bass_kernel_spmd

F32 = mybir.dt.float32
BF16 = mybir.dt.bfloat16
I32 = mybir.dt.int32
ALU = mybir.AluOpType
AF = mybir.ActivationFunctionType
AX = mybir.AxisListType

D = 1024
NT = 34
NE = 16
FF = 2048
ALPHA = 4.0 ** 0.25
LN_EPS = 1e-5
RMS_EPS = 1e-6
C_QA, C_KA, C_QC, C_KC, C_UC, C_US, C_VA, C_VC = 0, 512, 640, 896, 1024, 1280, 1536, 1664
NCOL = 1792


class _Stop(Exception):
    pass


class Tok:
    __slots__ = ("w", "r")

    def __init__(self):
        self.w = None
        self.r = {}


def toks(n):
    return [Tok() for _ in range(n)]


class Sched:
    def __init__(self, nc, es, n_dma=44):
        self.nc = nc
        self.E = {"pe": nc.tensor, "act": nc.scalar, "dve": nc.vector, "pool": nc.gpsimd, "sp": nc.sync}
        self.sem = {k: es.enter_context(nc.semaphore("c_" + k)) for k in ("pe", "act", "dve", "pool")}
        self.cnt = {k: 0 for k in self.sem}
        self.dsem = [es.enter_context(nc.semaphore("d%d" % i)) for i in range(n_dma)]
        self.dcnt = [0] * n_dma
        self.dnext = 0
        self.dnext_sw = 0
        self.known = {k: {} for k in self.E}
        self.nwait = 0
        self.nins = 0
        self.enabled = True

    def _semobj(self, key):
        return self.sem[key] if isinstance(key, str) else self.dsem[key]

    def _deps(self, e, reads, writes):
        need = {}
        for t in reads:
            if t.w is not None:
                k, v = t.w
                if not (k == e and e == "pe"):
                    if need.get(k, 0) < v:
                        need[k] = v
        for t in writes:
            if t.w is not None:
                k, v = t.w
                if k != e and need.get(k, 0) < v:
                    need[k] = v
            for k, v in t.r.items():
                if k != e and need.get(k, 0) < v:
                    need[k] = v
        return need

    def _wait(self, e, need):
        kn = self.known[e]
        for k, v in need.items():
            if kn.get(k, 0) >= v:
                continue
            self.E[e].wait_ge(self._semobj(k), v)
            kn[k] = v
            self.nwait += 1

    def _mark(self, me, reads, writes):
        k, v = me
        for t in reads:
            if t.r.get(k, 0) < v:
                t.r[k] = v
        for t in writes:
            t.w = me
            t.r = {}

    def op(self, e, fn, reads=(), writes=()):
        if not self.enabled:
            return
        self._wait(e, self._deps(e, reads, writes))
        ins = fn(self.E[e])
        self.cnt[e] += 1
        ins.then_inc(self.sem[e], 1)
        self.nins += 1
        self._mark((e, self.cnt[e]), reads, writes)

    def dma(self, e, fn, reads=(), writes=()):
        if not self.enabled:
            return
        half = len(self.dsem) // 2
        if e == "pool":
            k = half + self.dnext_sw
            self.dnext_sw = (self.dnext_sw + 1) % (len(self.dsem) - half)
        else:
            k = self.dnext
            self.dnext = (self.dnext + 1) % half
        need = self._deps(e, reads, writes)
        if self.dcnt[k] > 0:
            need[k] = max(need.get(k, 0), 16 * self.dcnt[k])
        self._wait(e, need)
        ins = fn(self.E[e])
        self.dcnt[k] += 1
        ins.then_inc(self.dsem[k], 16)
        self.nins += 1
        self._mark((k, 16 * self.dcnt[k]), reads, writes)

    def barrier(self):
        if not self.enabled:
            return
        need = {k: v for k, v in self.cnt.items() if v > 0}
        for k in range(len(self.dsem)):
            if self.dcnt[k] > 0:
                need[k] = 16 * self.dcnt[k]
        for e in self.E:
            self._wait(e, {k: v for k, v in need.items() if k != e})


def build(nc, n_layers=2, dbg=None, stop_after=None, ml_tiles=None, stop_at=None, bc_bf16=False):
    dbg = dbg or {}
    es_top = ExitStack()
    with es_top as es:
        S = Sched(nc, es)
        E = S.E

        def chk(tag):
            if stop_at == tag:
                S.enabled = False

        def dram(name, shape, dt, kind="ExternalInput"):
            return nc.dram_tensor(name, list(shape), dt, kind=kind).ap()

        x_d = dram("x", [4096, D], F32)
        ctx_d = dram("ctx", [256, D], F32)
        cT_d = dram("cT", [128, 16], F32)
        w_mod_d = dram("w_mod", [2, D, 6 * D], F32)
        b_mod_d = dram("b_mod", [2, 6 * D], F32)
        w_in_d = dram("w_in", [2, D, 1536], F32)
        w_in_uT_d = dram("w_in_uT", [2, 256, D], F32)
        qn_d = dram("q_norm_a", [2, 64], F32)
        kn_d = dram("k_norm_a", [2, 64], F32)
        wf_d = dram("w_fourier", [2, 4, 64, 64], F32)
        bf_d = dram("b_fourier", [2, 256], F32)
        sink_d = dram("sink_c", [2, 4], F32)
        w_out_d = dram("w_out", [2, D, D], F32)
        ln1g_d = dram("ln1_g", [2, D], F32)
        ln1b_d = dram("ln1_b", [2, D], F32)
        wr_d = dram("w_router", [2, D, NE], F32)
        wg_d = dram("w_gate", [2, NE, D, FF], F32)
        wu_d = dram("w_up", [2, NE, D, FF], F32)
        wd_d = dram("w_down", [2, NE, FF, D], F32)
        ln2g_d = dram("ln2_g", [2, D], F32)
        ln2b_d = dram("ln2_b", [2, D], F32)
        ident_d = dram("k_ident", [128, 128], BF16)
        onesf_d = dram("k_ones", [128, 128], F32)
        tri_d = dram("k_tri", [128, 128], F32)
        maskL_d = dram("k_maskL", [128, 128], BF16)
        maskU_d = dram("k_maskU", [128, 128], BF16)
        iotaB_d = dram("k_iotaB", [128, 16 * 128], F32)
        iotaA_d = dram("k_iotaA", [128, 64], F32)
        pcol_d = dram("k_pcol", [128, 1], F32)
        cos_d = dram("k_cos", [NT * 128, 32], F32)
        sin_d = dram("k_sin", [NT * 128, 32], F32)
        ct_d = dram("k_ct", [4096, 4096], BF16)
        sn_d = dram("k_sn", [4096, 4096], BF16)
        ctc_d = dram("k_ctc", [256, 512], BF16)
        cc_d = dram("k_cc", [64, 128], F32)
        tgt_d = dram("k_tgt", [128, 32], F32)
        out_d = dram("out", [4096, D], F32, kind="ExternalOutput")
        dbg_d = {k: dram("dbg_" + k, shp, F32, kind="ExternalOutput") for k, shp in dbg.items()}
        mod_d = dram("s_mod", [2, 2, 6 * D], F32, kind="Internal")
        xcur_d = dram("s_xcur", [NT * 128, D], F32, kind="Internal")
        h2_d = dram("s_h2", [NT * 128, D], BF16, kind="Internal")
        acc_d = dram("s_acc", [NT * 128, D], F32, kind="Internal")
        mod_tok = toks(2)
        xcur_tok = toks(NT)
        h2_tok = toks(NT)
        acc_tok = toks(NT)
        accs_tok = Tok()
        out_tok = toks(NT)
        dbg_tok = Tok()

        uid = [0]

        def sb(stack, name, shape, dt):
            uid[0] += 1
            return stack.enter_context(nc.sbuf_tensor("sb%d_%s" % (uid[0], name), list(shape), dt))

        ps = es.enter_context(nc.psum_tensor("ps", [128, 4096], F32))
        pst = toks(8)

        def bank(b, p0=0, p1=128, c0=0, c1=512):
            return ps[p0:p1, b * 512 + c0:b * 512 + c1]

        ident = sb(es, "ident", [128, 128], BF16)
        onesf = sb(es, "onesf", [128, 128], F32)
        cT = sb(es, "cT", [128, 16], F32)
        cTs = sb(es, "cTs", [128, 16], F32)
        k_tok = Tok()
        S.dma("sp", lambda e: e.dma_start(out=ident[:], in_=ident_d), [], [k_tok])
        S.dma("sp", lambda e: e.dma_start(out=onesf[:], in_=onesf_d), [], [k_tok])
        S.dma("sp", lambda e: e.dma_start(out=cT[:], in_=cT_d), [], [k_tok])
        S.op("act", lambda e: e.activation(out=cTs[:], in_=cT[:], func=AF.Silu), [k_tok], [k_tok])

        AFF = [sb(es, "AFF", [128, NT, NE], F32)] * 2
        IDX = [sb(es, "IDX", [128, 16, 4], I32)] * 2
        GATE = [sb(es, "GATE", [128, 16, 4], F32)] * 2
        IDXC = [sb(es, "IDXC", [128, 16], I32)] * 2
        GATEC = [sb(es, "GATEC", [128, 16], F32)] * 2
        idx_tok = toks(2)
        aff_t = Tok()

        def dump(name, src_ap, dst_ap, rtoks):
            if name in dbg_d:
                S.dma("sp", lambda e: e.dma_start(out=dst_ap, in_=src_ap), list(rtoks), [dbg_tok])

        try:
          for L in range(n_layers):
            last = (L == n_layers - 1)
            with ExitStack() as ph:
                wm = [sb(ph, "wm%d" % i, [128, 8, 512], F32) for i in range(2)]
                wm_t = toks(2)
                bm = sb(ph, "bm", [2, 6 * D], F32)
                md = sb(ph, "md", [2, 6 * D], F32)
                bm_t, md_t = Tok(), Tok()
                S.dma("sp", lambda e: e.dma_start(out=bm[:], in_=b_mod_d[L].partition_broadcast(2)), [], [bm_t])
                wsrc = w_mod_d[L].rearrange("(k p) n -> p k n", p=128)
                for n in range(12):
                    b = n % 2
                    S.dma("sp", lambda e: e.dma_start(out=wm[b][:], in_=wsrc[:, :, n * 512:(n + 1) * 512]),
                          [], [wm_t[b]])
                    for k in range(8):
                        S.op("pe", lambda e: e.matmul(bank(b, 0, 2), cTs[:, 2 * k:2 * k + 2], wm[b][:, k, :],
                                                      start=(k == 0), stop=(k == 7)),
                             [k_tok, wm_t[b]], [pst[b]])
                    S.op("dve", lambda e: e.tensor_tensor(out=md[:, n * 512:(n + 1) * 512], in0=bank(b, 0, 2),
                                                          in1=bm[:, n * 512:(n + 1) * 512], op=ALU.add),
                         [pst[b], bm_t], [md_t])
                for c in (1, 4):
                    S.op("dve", lambda e: e.tensor_scalar(out=md[:, c * D:(c + 1) * D], in0=md[:, c * D:(c + 1) * D],
                                                          scalar1=1.0, scalar2=None, op0=ALU.add), [md_t], [md_t])
                S.dma("sp", lambda e: e.dma_start(out=mod_d[L], in_=md[:]), [md_t], [mod_tok[L]])
                dump("mod%d" % L, md[:], dbg_d.get("mod%d" % L), [md_t])
                S.barrier()
            if stop_after == ("mod", L):
                break

            def modrow(dst, which, chunk, tok):
                src = mod_d[L, which, chunk * D:(chunk + 1) * D].partition_broadcast(128)
                S.dma("sp", lambda e: e.dma_start(out=dst[:], in_=src), [mod_tok[L]], [tok])

            def rowb(dst, src_row, tok):
                S.dma("sp", lambda e: e.dma_start(out=dst[:], in_=src_row.partition_broadcast(128)), [], [tok])

            def src_rows(j):
                if L == 0:
                    return (ctx_d[j * 128:(j + 1) * 128, :] if j < 2 else x_d[(j - 2) * 128:(j - 1) * 128, :]), []
                return xcur_d[j * 128:(j + 1) * 128, :], [xcur_tok[j]]

            with ExitStack() as lay:
                QA = sb(lay, "QA", [128, NT, 4, 128], BF16)
                KA = sb(lay, "KA", [128, NT * 128], BF16)
                VA = sb(lay, "VA", [128, NT, 2, 65], BF16)
                QC = sb(lay, "QC", [128, NT, 2, 128], BF16)
                KC = sb(lay, "KC", [128, NT * 128], BF16)
                VC = sb(lay, "VC", [128, NT, 2, 65], BF16)
                OBs = sb(lay, "OBs", [128, NT, 256], BF16)
                qa_t, ka_t, va_t, qc_t, kc_t, vc_t, ob_t = (toks(NT) for _ in range(7))
                vinit = Tok()
                S.op("pool", lambda e: e.memset(VA[:], 1.0), [], [vinit])
                S.op("pool", lambda e: e.memset(VC[:], 1.0), [], [vinit])

                with ExitStack() as ph_outer:
                  U2 = sb(ph_outer, "U2", [128, NT, 512], BF16)
                  u2_t = toks(NT)
                  with ExitStack() as ph:
                    WINB = sb(ph, "WINB", [128, 8, NCOL], BF16)
                    winb_t = Tok()
                    wsrc = w_in_d[L].rearrange("(k p) n -> p k n", p=128)
                    for (dc, sc, wd) in ((C_QA, 0, 512), (C_KA, 1024, 128), (C_QC, 512, 256), (C_KC, 1280, 128),
                                         (C_VA, 1152, 128), (C_VC, 1408, 128)):
                        S.dma("pool", lambda e: e.dma_start(out=WINB[:, :, dc:dc + wd], in_=wsrc[:, :, sc:sc + wd]),
                              [], [winb_t])
                    with ExitStack() as ff:
                        CC = sb(ff, "CC", [64, 128], F32)
                        WF = sb(ff, "WF", [64, 4, 64], F32)
                        MCS = sb(ff, "MCS", [64, 2, 4, 64], F32)
                        WUT = sb(ff, "WUT", [64, 4, D], F32)
                        f_t = Tok()
                        S.dma("sp", lambda e: e.dma_start(out=CC[:], in_=cc_d), [], [f_t])
                        S.dma("sp", lambda e: e.dma_start(out=WF[:], in_=wf_d[L].rearrange("g c d -> c g d")), [], [f_t])
                        S.dma("sp", lambda e: e.dma_start(out=WUT[:], in_=w_in_uT_d[L].rearrange("(g c) d -> c g d", c=64)),
                              [], [f_t])
                        for cs in range(2):
                            S.op("pe", lambda e: e.matmul(bank(cs, 0, 64, 0, 256), CC[:, cs * 64:(cs + 1) * 64],
                                                          WF[:].rearrange("c g d -> c (g d)"), start=True, stop=True),
                                 [f_t], [pst[cs]])
                            S.op("dve", lambda e: e.tensor_copy(out=MCS[:, cs].rearrange("c g d -> c (g d)"),
                                                                in_=bank(cs, 0, 64, 0, 256)), [pst[cs]], [f_t])
                        for k in range(8):
                            b = 2 + (k % 2)
                            for cs in range(2):
                                for g in range(4):
                                    c0 = cs * 256 + g * 64
                                    S.op("pe", lambda e: e.matmul(bank(b, 0, 128, c0, c0 + 64),
                                                                  WUT[:, g, k * 128:(k + 1) * 128], MCS[:, cs, g, :],
                                                                  start=True, stop=True), [f_t], [pst[b]])
                            S.op("dve", lambda e: e.tensor_copy(out=WINB[:, k, C_UC:C_UC + 512], in_=bank(b)),
                                 [pst[b]], [winb_t])
                        S.barrier()
                    SH1 = sb(ph, "SH1", [128, D], F32)
                    SC1 = sb(ph, "SC1", [128, D], F32)
                    GQK = sb(ph, "GQK", [128, 10, 64], F32)
                    mr_t = Tok()
                    g_t = Tok()
                    for h in range(8):
                        rowb(GQK[:, h, :], qn_d[L], g_t)
                    for h in range(8, 10):
                        rowb(GQK[:, h, :], kn_d[L], g_t)
                    XT = [sb(ph, "XT%d" % i, [128, D], F32) for i in range(2)]
                    xt_t = toks(2)
                    XN = sb(ph, "XN", [128, D], F32)
                    Hb = sb(ph, "Hb", [128, D], BF16)
                    HT = sb(ph, "HT", [128, 8, 128], BF16)
                    ST = sb(ph, "ST", [128, 2, 6], F32)
                    MV = sb(ph, "MV", [128, 8], F32)
                    SQ = sb(ph, "SQ", [128, 640], F32)
                    MS = sb(ph, "MS", [128, 16], F32)
                    NRM = SQ[:].rearrange("p (h d) -> p h d", d=64)
                    R = sb(ph, "R", [128, 16, 64], F32)
                    RO = sb(ph, "RO", [128, 16, 64], BF16)
                    T1 = sb(ph, "T1", [128, 16, 2, 16], F32)
                    T2 = sb(ph, "T2", [128, 16, 2, 16], F32)
                    CS = [sb(ph, "CS%d" % i, [128, 2, 32], F32) for i in range(2)]
                    cs_t = toks(2)
                    xn_t, hb_t, ht_t, st_t, mv_t, sq_t, ms_t, nrm_t, r_t, ro_t, tt_t = (Tok() for _ in range(11))
                    for j in range(NT):
                        if j == 0 or j == 2:
                            w = 1 if j == 0 else 0
                            modrow(SH1, w, 0, mr_t)
                            modrow(SC1, w, 1, mr_t)
                        b = j % 2
                        rows, rt = src_rows(j)
                        S.dma("sp", lambda e: e.dma_start(out=XT[b][:], in_=rows), rt, [xt_t[b]])
                        S.dma("sp", lambda e: e.dma_start(out=CS[b][:, 0, :], in_=cos_d[j * 128:(j + 1) * 128, :]), [], [cs_t[b]])
                        S.dma("sp", lambda e: e.dma_start(out=CS[b][:, 1, :], in_=sin_d[j * 128:(j + 1) * 128, :]), [], [cs_t[b]])
                        xt = XT[b]
                        for hh in range(2):
                            S.op("dve", lambda e: e.bn_stats(out=ST[:, hh, :], in_=xt[:, hh * 512:(hh + 1) * 512]),
                                 [xt_t[b]], [st_t])
                        S.op("dve", lambda e: e.bn_aggr(out=MV[:, 0:2], in_=ST[:].rearrange("p a b -> p (a b)")),
                             [st_t], [mv_t])
                        S.op("act", lambda e: e.activation(out=MV[:, 2:3], in_=MV[:, 1:2], func=AF.Sqrt, bias=LN_EPS,
                                                           scale=1.0), [mv_t], [mv_t])
                        S.op("dve", lambda e: e.reciprocal(out=MV[:, 3:4], in_=MV[:, 2:3]), [mv_t], [mv_t])
                        S.op("dve", lambda e: e.tensor_scalar(out=MV[:, 4:5], in0=MV[:, 0:1], scalar1=MV[:, 3:4],
                                                              scalar2=-1.0, op0=ALU.mult, op1=ALU.mult), [mv_t], [mv_t])
                        S.op("act", lambda e: e.activation(out=XN[:], in_=xt[:], func=AF.Identity, bias=MV[:, 4:5],
                                                           scale=MV[:, 3:4]), [mv_t, xt_t[b]], [xn_t])
                        S.op("dve", lambda e: e.tensor_tensor(out=XN[:], in0=XN[:], in1=SC1[:], op=ALU.mult),
                             [xn_t, mr_t], [xn_t])
                        S.op("dve", lambda e: e.tensor_tensor(out=Hb[:], in0=XN[:], in1=SH1[:], op=ALU.add),
                             [xn_t, mr_t], [hb_t])
                        for k in range(8):
                            S.op("pe", lambda e: e.matmul(bank(k // 4, 0, 128, (k % 4) * 128, (k % 4) * 128 + 128),
                                                          Hb[:, k * 128:(k + 1) * 128], ident[:], start=True, stop=True),
                                 [hb_t, k_tok], [pst[k // 4]])
                        S.op("act", lambda e: e.copy(out=HT[:, 0:4, :].rearrange("p a b -> p (a b)"), in_=bank(0)),
                             [pst[0]], [ht_t])
                        S.op("dve", lambda e: e.tensor_copy(out=HT[:, 4:8, :].rearrange("p a b -> p (a b)"), in_=bank(1)),
                             [pst[1]], [ht_t])
                        for cg in range(4):
                            c0, c1 = cg * 512, min(NCOL, cg * 512 + 512)
                            for k in range(8):
                                S.op("pe", lambda e: e.matmul(bank(2 + cg, 0, 128, 0, c1 - c0), HT[:, k, :],
                                                              WINB[:, k, c0:c1], start=(k == 0), stop=(k == 7)),
                                     [ht_t, winb_t], [pst[2 + cg]])
                        P = ps[:, 2 * 512:2 * 512 + NCOL]
                        if ("p%d" % L) in dbg_d and j == 2:
                            for (a0, a1) in ((0, 1024), (1024, NCOL)):
                                S.op("dve", lambda e: e.tensor_copy(out=XN[:, 0:a1 - a0], in_=P[:, a0:a1]),
                                     [pst[2], pst[3], pst[4], pst[5]], [xn_t])
                                S.dma("sp", lambda e: e.dma_start(out=dbg_d["p%d" % L][:, a0:a1], in_=XN[:, 0:a1 - a0]),
                                      [xn_t], [dbg_tok])
                        S.op("act", lambda e: e.activation(out=SQ[:], in_=P[:, 0:640], func=AF.Square),
                             [pst[2], pst[3]], [sq_t])
                        S.op("dve", lambda e: e.tensor_reduce(out=MS[:, 0:10], in_=SQ[:].rearrange("p (h d) -> p h d", d=64),
                                                              axis=AX.X, op=ALU.add), [sq_t], [ms_t])
                        S.op("act", lambda e: e.activation(out=MS[:, 0:10], in_=MS[:, 0:10], func=AF.Sqrt, bias=RMS_EPS,
                                                           scale=1.0 / 64.0), [ms_t], [ms_t])
                        S.op("dve", lambda e: e.reciprocal(out=MS[:, 0:10], in_=MS[:, 0:10]), [ms_t], [ms_t])
                        S.op("dve", lambda e: e.tensor_tensor(out=NRM, in0=P[:, 0:640].rearrange("p (h d) -> p h d", d=64),
                                                              in1=MS[:, 0:10].unsqueeze(2).to_broadcast([128, 10, 64]),
                                                              op=ALU.mult), [ms_t, pst[2], pst[3]], [nrm_t])
                        S.op("dve", lambda e: e.tensor_tensor(
                            out=R[:, 0:8, :].rearrange("p (g kv) d -> p kv g d", kv=2),
                            in0=NRM[:, 0:8, :].rearrange("p (kv g) d -> p kv g d", kv=2),
                            in1=GQK[:, 0:8, :].rearrange("p (kv g) d -> p kv g d", kv=2), op=ALU.mult),
                            [nrm_t, g_t], [r_t])
                        S.op("dve", lambda e: e.tensor_tensor(out=R[:, 8:10, :], in0=NRM[:, 8:10, :], in1=GQK[:, 8:10, :],
                                                              op=ALU.mult), [nrm_t, g_t], [r_t])
                        S.op("act", lambda e: e.copy(
                            out=R[:, 10:14, :].rearrange("p (g kv) d -> p kv g d", kv=2),
                            in_=P[:, C_QC:C_QC + 256].rearrange("p (kv g d) -> p kv g d", kv=2, g=2)), [pst[3]], [r_t])
                        S.op("act", lambda e: e.copy(out=R[:, 14:16, :].rearrange("p h d -> p (h d)"),
                                                     in_=P[:, C_KC:C_KC + 128]), [pst[3]], [r_t])
                        Rv = R[:].rearrange("p h (a b f) -> p h a b f", a=2, b=2)
                        ROv = RO[:].rearrange("p h (a b f) -> p h a b f", a=2, b=2)
                        x1, x2 = Rv[:, :, :, 0, :], Rv[:, :, :, 1, :]
                        cosb = CS[b][:, 0, :].rearrange("p (a f) -> p a f", a=2).unsqueeze(1).to_broadcast([128, 16, 2, 16])
                        sinb = CS[b][:, 1, :].rearrange("p (a f) -> p a f", a=2).unsqueeze(1).to_broadcast([128, 16, 2, 16])
                        S.op("dve", lambda e: e.tensor_tensor(out=T1[:], in0=x1, in1=cosb, op=ALU.mult), [r_t, cs_t[b]], [tt_t])
                        S.op("dve", lambda e: e.tensor_tensor(out=T2[:], in0=x2, in1=sinb, op=ALU.mult), [r_t, cs_t[b]], [tt_t])
                        S.op("dve", lambda e: e.tensor_tensor(out=ROv[:, :, :, 0, :], in0=T1[:], in1=T2[:], op=ALU.subtract),
                             [tt_t], [ro_t])
                        S.op("dve", lambda e: e.tensor_tensor(out=T1[:], in0=x2, in1=cosb, op=ALU.mult), [r_t, cs_t[b]], [tt_t])
                        S.op("dve", lambda e: e.tensor_tensor(out=T2[:], in0=x1, in1=sinb, op=ALU.mult), [r_t, cs_t[b]], [tt_t])
                        S.op("dve", lambda e: e.tensor_tensor(out=ROv[:, :, :, 1, :], in0=T1[:], in1=T2[:], op=ALU.add),
                             [tt_t], [ro_t])
                        for blk in range(8):
                            S.op("pe", lambda e: e.matmul(bank(blk // 4, 0, 128, (blk % 4) * 128, (blk % 4) * 128 + 128),
                                                          RO[:, 2 * blk:2 * blk + 2, :].rearrange("p h d -> p (h d)"),
                                                          ident[:], start=True, stop=True), [ro_t, k_tok], [pst[blk // 4]])
                        S.op("act", lambda e: e.copy(out=QA[:, j].rearrange("p g t -> p (g t)"), in_=bank(0)),
                             [pst[0]], [qa_t[j]])
                        S.op("dve", lambda e: e.tensor_copy(out=KA[:, j * 128:(j + 1) * 128], in_=bank(1, 0, 128, 0, 128)),
                             [pst[1]], [ka_t[j]])
                        S.op("dve", lambda e: e.tensor_copy(out=QC[:, j].rearrange("p g t -> p (g t)"),
                                                            in_=bank(1, 0, 128, 128, 384)), [pst[1]], [qc_t[j]])
                        S.op("dve", lambda e: e.tensor_copy(out=KC[:, j * 128:(j + 1) * 128], in_=bank(1, 0, 128, 384, 512)),
                             [pst[1]], [kc_t[j]])
                        S.op("act", lambda e: e.copy(out=VA[:, j, :, 1:65],
                                                     in_=P[:, C_VA:C_VA + 128].rearrange("p (h d) -> p h d", d=64)),
                             [pst[5], vinit], [va_t[j]])
                        S.op("act", lambda e: e.copy(out=VC[:, j, :, 1:65],
                                                     in_=P[:, C_VC:C_VC + 128].rearrange("p (h d) -> p h d", d=64)),
                             [pst[5], vinit], [vc_t[j]])
                        S.op("dve", lambda e: e.tensor_copy(out=U2[:, j, :], in_=P[:, C_UC:C_UC + 512]), [pst[4]], [u2_t[j]])
                    S.barrier()
                  if True:
                    with ExitStack() as pd:
                        BFR = sb(pd, "BFR", [128, 256], F32)
                        bfr_t = Tok()
                        rowb(BFR, bf_d[L], bfr_t)
                        TB = [sb(pd, "TB%d" % i, [128, 2, 2048], BF16) for i in range(3)]
                        tb_t = toks(3)
                        it = 0
                        for half in range(2):
                            for tc in range(32):
                                b = it % 3
                                it += 1
                                S.dma("sp", lambda e: e.dma_start(out=TB[b][:, 0, :],
                                                                  in_=ct_d[tc * 128:(tc + 1) * 128, half * 2048:(half + 1) * 2048]),
                                      [], [tb_t[b]])
                                S.dma("sp", lambda e: e.dma_start(out=TB[b][:, 1, :],
                                                                  in_=sn_d[tc * 128:(tc + 1) * 128, half * 2048:(half + 1) * 2048]),
                                      [], [tb_t[b]])
                                for kc in range(16):
                                    for cs in range(2):
                                        S.op("pe", lambda e: e.matmul(
                                            bank(kc // 2, 0, 128, (kc % 2) * 256, (kc % 2) * 256 + 256),
                                            TB[b][:, cs, kc * 128:(kc + 1) * 128], U2[:, 2 + tc, cs * 256:(cs + 1) * 256],
                                            start=(tc == 0 and cs == 0 and kc % 2 == 0), stop=(tc == 31 and cs == 1),
                                            skip_group_check=True),
                                            [tb_t[b], u2_t[2 + tc]], [pst[kc // 2]])
                            for kc in range(16):
                                jj = 2 + half * 16 + kc
                                S.op("dve", lambda e: e.tensor_tensor(
                                    out=OBs[:, jj, :], in0=bank(kc // 2, 0, 128, (kc % 2) * 256, (kc % 2) * 256 + 256),
                                    in1=BFR[:], op=ALU.add), [pst[kc // 2], bfr_t], [ob_t[jj]])
                        if not last:
                            TBC = sb(pd, "TBC", [128, 2, 512], BF16)
                            tbc_t = Tok()
                            for tc in range(2):
                                S.dma("sp", lambda e: e.dma_start(out=TBC[:, tc, :], in_=ctc_d[tc * 128:(tc + 1) * 128, :]),
                                      [], [tbc_t])
                            for kc in range(2):
                                for tc in range(2):
                                    for cs in range(2):
                                        S.op("pe", lambda e: e.matmul(
                                            bank(0, 0, 128, kc * 256, kc * 256 + 256),
                                            TBC[:, tc, cs * 256 + kc * 128:cs * 256 + kc * 128 + 128],
                                            U2[:, tc, cs * 256:(cs + 1) * 256],
                                            start=(tc == 0 and cs == 0 and kc == 0), stop=(tc == 1 and cs == 1),
                                            skip_group_check=True),
                                            [tbc_t, u2_t[tc]], [pst[0]])
                                S.op("dve", lambda e: e.tensor_tensor(out=OBs[:, kc, :], in0=bank(0, 0, 128, kc * 256, kc * 256 + 256),
                                                                      in1=BFR[:], op=ALU.add), [pst[0], bfr_t], [ob_t[kc]])
                        S.barrier()
                if ("QA%d" % L) in dbg_d:
                    with ExitStack() as dd:
                        TMPD = sb(dd, "TMPD", [128, 4352], F32)
                        td = Tok()
                        for nm, src in (("KA", KA[:]), ("KC", KC[:])):
                            S.op("dve", lambda e: e.tensor_copy(out=TMPD[:], in_=src), ka_t + kc_t, [td])
                            dump(nm + "%d" % L, TMPD[:], dbg_d.get(nm + "%d" % L), [td])
                        for nm, src in (("QA", QA[:, 2].rearrange("p g t -> p (g t)")), ("OB", OBs[:, 2, :]),
                                        ("VA", VA[:, 2].rearrange("p h d -> p (h d)"))):
                            n = src.shape[1]
                            S.op("dve", lambda e: e.tensor_copy(out=TMPD[:, 0:n], in_=src), qa_t + ob_t + va_t, [td])
                            dump(nm + "%d" % L, TMPD[:, 0:n], dbg_d.get(nm + "%d" % L), [td])
                        S.barrier()
                if stop_after == ("A", L):
                    break
                with ExitStack() as ml:
                    WOA = sb(ml, "WOA", [128, 8, D], BF16)
                    WOB = sb(ml, "WOB", [128, 2, D], BF16)
                    WOC = sb(ml, "WOC", [128, 4, D], BF16)
                    WR = sb(ml, "WR", [128, 8, NE], BF16)
                    w_t = Tok()
                    S.op("pool", lambda e: e.memset(WOA[:], 0.0), [], [w_t])
                    S.op("pool", lambda e: e.memset(WOC[:], 0.0), [], [w_t])
                    S.dma("pool", lambda e: e.dma_start(out=WOA[1:65], in_=w_out_d[L, 0:512, :].rearrange("(h p) n -> p h n", p=64)), [w_t], [w_t])
                    S.dma("pool", lambda e: e.dma_start(out=WOB[:], in_=w_out_d[L, 512:768, :].rearrange("(h p) n -> p h n", p=128)), [], [w_t])
                    S.dma("pool", lambda e: e.dma_start(out=WOC[1:65], in_=w_out_d[L, 768:1024, :].rearrange("(h p) n -> p h n", p=64)), [w_t], [w_t])
                    S.dma("pool", lambda e: e.dma_start(out=WR[:], in_=wr_d[L].rearrange("(k p) n -> p k n", p=128)), [], [w_t])
                    G1 = sb(ml, "G1", [128, D], F32)
                    SC2 = sb(ml, "SC2", [128, D], F32)
                    SH2 = sb(ml, "SH2", [128, D], F32)
                    LN1G = sb(ml, "LN1G", [128, D], F32)
                    LN1B = sb(ml, "LN1B", [128, D], F32)
                    mr_t, ln_t = Tok(), Tok()
                    rowb(LN1G, ln1g_d[L], ln_t)
                    rowb(LN1B, ln1b_d[L], ln_t)
                    MKL = sb(ml, "MKL", [128, 128], BF16)
                    MKU = sb(ml, "MKU", [128, 128], BF16)
                    SINKE = sb(ml, "SINKE", [128, 4, 128], F32)
                    SK4 = sb(ml, "SK4", [128, 4], F32)
                    mk_t, sk_t = Tok(), Tok()
                    S.dma("sp", lambda e: e.dma_start(out=MKL[:], in_=maskL_d), [], [mk_t])
                    S.dma("sp", lambda e: e.dma_start(out=MKU[:], in_=maskU_d), [], [mk_t])
                    S.dma("sp", lambda e: e.dma_start(out=SK4[0:1, :], in_=sink_d[L:L + 1, :]), [], [sk_t])
                    S.op("act", lambda e: e.activation(out=SK4[0:1, :], in_=SK4[0:1, :], func=AF.Exp), [sk_t], [sk_t])
                    S.op("dve", lambda e: e.tensor_copy(out=SINKE[0:1, :, :], in_=SK4[0:1, :].unsqueeze(2).to_broadcast([1, 4, 128])),
                         [sk_t], [sk_t])
                    PT = [sb(ml, "PT%d" % i, [128, 512], BF16) for i in range(3)]
                    pt_t = toks(3)
                    PTC = [sb(ml, "PTC%d" % i, [128, 512], BF16) for i in range(2)]
                    ptc_t = toks(2)
                    REC = sb(ml, "REC", [128, 1024], F32)
                    BCS = sb(ml, "BCS", [128, 1024], F32)
                    RECC = REC[:, 0:512]
                    BCC = BCS[:, 0:512]
                    CATA = sb(ml, "CATA", [128, 8, 128], BF16)
                    CATC = sb(ml, "CATC", [128, 4, 128], BF16)
                    CATB = sb(ml, "CATB", [128, 2, 128], BF16)
                    rec_t, bcs_t, cata_t, catc_t, catb_t = (Tok() for _ in range(5))
                    recc_t, bcc_t = rec_t, bcs_t
                    S.op("pool", lambda e: e.memset(CATA[:], 0.0), [], [cata_t])
                    S.op("pool", lambda e: e.memset(CATC[:], 0.0), [], [catc_t])
                    QZ = [sb(ml, "QZ%d" % i, [128, 2, 512], BF16) for i in range(2)]
                    QCZ = [sb(ml, "QCZ%d" % i, [128, 2, 256], BF16) for i in range(2)]
                    qz_t, qcz_t = toks(2), toks(2)
                    for i in range(2):
                        S.op("pool", lambda e: e.memset(QZ[i][:], 0.0), [], [qz_t[i]])
                        S.op("pool", lambda e: e.memset(QCZ[i][:], 0.0), [], [qcz_t[i]])
                    XT2 = [sb(ml, "XT20", [128, D], F32)] * 2
                    xt2_t = [Tok()] * 2
                    TMP = sb(ml, "TMP", [128, D], F32)
                    RR = sb(ml, "RR", [128, D], F32)
                    XN2 = sb(ml, "XN2", [128, D], F32)
                    ACC = TMP
                    H2 = sb(ml, "H2", [128, D], BF16)
                    HT2 = sb(ml, "HT2", [128, 8, 128], BF16)
                    ST2 = sb(ml, "ST2", [128, 2, 6], F32)
                    MV2 = sb(ml, "MV2", [128, 8], F32)
                    LG = sb(ml, "LG", [128, NE], F32)
                    SM = sb(ml, "SM", [128, 4], F32)
                    tmp_t, rr_t, xn2_t, h2b_t, ht2_t, st2_t, mv2_t, lg_t, sm_t = (Tok() for _ in range(9))
                    accb_t = tmp_t

                    NEGH = sb(ml, "NEGH", [128, 1], F32)
                    ngh_t = Tok()
                    S.op("pool", lambda e: e.memset(NEGH[:], -0.5), [], [ngh_t])

                    def ln_norm(dst, src, src_toks, dst_tok):
                        for hh in range(2):
                            S.op("dve", lambda e: e.bn_stats(out=ST2[:, hh, :], in_=src[:, hh * 512:(hh + 1) * 512]),
                                 src_toks, [st2_t])
                        S.op("dve", lambda e: e.bn_aggr(out=MV2[:, 0:2], in_=ST2[:].rearrange("p a b -> p (a b)")),
                             [st2_t], [mv2_t])
                        S.op("pool", lambda e: e.tensor_scalar(out=MV2[:, 2:3], in0=MV2[:, 1:2], scalar1=LN_EPS, scalar2=None,
                                                               op0=ALU.add), [mv2_t], [mv2_t])
                        S.op("pool", lambda e: e.tensor_tensor(out=MV2[:, 3:4], in0=MV2[:, 2:3], in1=NEGH[:], op=ALU.pow),
                             [mv2_t, ngh_t], [mv2_t])
                        S.op("dve", lambda e: e.tensor_scalar(out=dst[:], in0=src[:], scalar1=MV2[:, 0:1], scalar2=MV2[:, 3:4],
                                                              op0=ALU.subtract, op1=ALU.mult), [mv2_t] + list(src_toks), [dst_tok])

                    chk("ML_SETUP")
                    tiles = list(range(2, NT)) if last else list(range(NT))
                    if ml_tiles is not None:
                        tiles = list(ml_tiles)
                    CATA2 = [CATA, sb(ml, "CATA1", [128, 8, 128], BF16)]
                    CATC2 = [CATC, sb(ml, "CATC1", [128, 4, 128], BF16)]
                    CATB2 = [CATB, sb(ml, "CATB1", [128, 2, 128], BF16)]
                    H22 = [H2, sb(ml, "H21", [128, D], BF16)]
                    cata2_t, catc2_t, catb2_t, h22_t = [cata_t, Tok()], [catc_t, Tok()], [catb_t, Tok()], [h2b_t, Tok()]
                    S.op("pool", lambda e: e.memset(CATA2[1][:], 0.0), [], [cata2_t[1]])
                    S.op("pool", lambda e: e.memset(CATC2[1][:], 0.0), [], [catc2_t[1]])
                    state = {"pi": 0, "pc": 0}

                    def emit_BC(j, b):
                        CA, CC = CATA2[b], CATC2[b]
                        ca_t, cc_t = cata2_t[b], catc2_t[b]
                        for h in range(2):
                            S.op("pool", lambda e: e.tensor_copy(out=QZ[b][64 * h:64 * h + 64, h, :],
                                                                 in_=QA[64 * h:64 * h + 64, j].rearrange("p g t -> p (g t)")),
                                 [qa_t[j]], [qz_t[b]])
                            S.op("pool", lambda e: e.tensor_copy(out=QCZ[b][64 * h:64 * h + 64, h, :],
                                                                 in_=QC[64 * h:64 * h + 64, j].rearrange("p g t -> p (g t)")),
                                 [qc_t[j]], [qcz_t[b]])
                        chunks = [0, 1] if j < 2 else list(range(NT))
                        steps = [(c, h) for c in chunks for h in range(2)]
                        ns = len(steps)

                        def emit_S(n):
                            c, h = steps[n]
                            S.op("pe", lambda e: e.matmul(bank(n % 2), KA[:, c * 128:(c + 1) * 128], QZ[b][:, h, :],
                                                          start=True, stop=True), [ka_t[c], qz_t[b]], [pst[n % 2]])
                        emit_S(0)
                        emit_S(1)
                        for n in range(ns):
                            c, h = steps[n]
                            p_ = state["pi"]
                            state["pi"] = (p_ + 1) % 3
                            S.op("act", lambda e: e.activation(out=PT[p_][:], in_=bank(n % 2), func=AF.Exp, scale=0.125),
                                 [pst[n % 2]], [pt_t[p_]])
                            S.op("pe", lambda e: e.matmul(bank(2 + h, 0, 65), VA[:, c, h, :], PT[p_][:],
                                                          start=(n < 2), stop=(n >= ns - 2)),
                                 [va_t[c], pt_t[p_]], [pst[2 + h]])
                            if n + 2 < ns:
                                emit_S(n + 2)
                        chk("B_LOOP")
                        for h in range(2):
                            S.op("dve", lambda e: e.reciprocal(out=REC[0:1, h * 512:(h + 1) * 512], in_=bank(2 + h, 0, 1)),
                                 [pst[2 + h]], [rec_t])
                            S.op("pe", lambda e: e.matmul(bank(h, 0, 65), onesf[0:1, 0:65], REC[0:1, h * 512:(h + 1) * 512],
                                                          start=True, stop=True), [rec_t, k_tok], [pst[h]])
                            S.op("dve", lambda e: e.tensor_copy(out=BCS[0:65, h * 512:(h + 1) * 512], in_=bank(h, 0, 65)), [pst[h]], [bcs_t])
                            S.op("dve", lambda e: e.tensor_tensor(out=CA[0:65, 4 * h:4 * h + 4, :].rearrange("p g t -> p (g t)"),
                                                                  in0=bank(2 + h, 0, 65), in1=BCS[0:65, h * 512:(h + 1) * 512],
                                                                  op=ALU.mult), [pst[2 + h], bcs_t], [ca_t])
                        chk("B_NORM")
                        cks = [(0, None), (1, None)]
                        if j >= 2:
                            if j - 1 >= 2:
                                cks.append((j - 1, MKL))
                            cks.append((j, None))
                            if j + 1 < NT:
                                cks.append((j + 1, MKU))
                        for ci, (c, mk) in enumerate(cks):
                            for h in range(2):
                                S.op("pe", lambda e: e.matmul(bank(h, 0, 128, 0, 256), KC[:, c * 128:(c + 1) * 128], QCZ[b][:, h, :],
                                                              start=True, stop=True), [kc_t[c], qcz_t[b]], [pst[h]])
                            q_ = state["pc"]
                            state["pc"] = (q_ + 1) % 2
                            S.op("act", lambda e: e.activation(out=PTC[q_][:].rearrange("p (b c) -> p b c", b=2),
                                                               in_=ps[:, 0:1024].rearrange("p (b c) -> p b c", b=2)[:, :, 0:256],
                                                               func=AF.Exp, scale=0.125),
                                 [pst[0], pst[1]], [ptc_t[q_]])
                            if mk is not None:
                                S.op("pool", lambda e: e.tensor_tensor(
                                    out=PTC[q_][:].rearrange("p (a t) -> p a t", a=4),
                                    in0=PTC[q_][:].rearrange("p (a t) -> p a t", a=4),
                                    in1=mk[:].unsqueeze(1).to_broadcast([128, 4, 128]), op=ALU.mult),
                                    [ptc_t[q_], mk_t], [ptc_t[q_]])
                            for h in range(2):
                                S.op("pe", lambda e: e.matmul(bank(5, 0, 65, h * 256, (h + 1) * 256), VC[:, c, h, :],
                                                              PTC[q_][:, h * 256:(h + 1) * 256],
                                                              start=(ci == 0 and h == 0), stop=(ci == len(cks) - 1),
                                                              skip_group_check=True), [vc_t[c], ptc_t[q_]], [pst[5]])
                        S.op("dve", lambda e: e.tensor_tensor(out=RECC[0:1, :], in0=bank(5, 0, 1),
                                                              in1=SINKE[0:1].rearrange("p a t -> p (a t)"), op=ALU.add),
                             [pst[5], sk_t], [recc_t])
                        S.op("dve", lambda e: e.reciprocal(out=RECC[0:1, :], in_=RECC[0:1, :]), [recc_t], [recc_t])
                        S.op("pe", lambda e: e.matmul(bank(4, 0, 65), onesf[0:1, 0:65], RECC[0:1, :], start=True, stop=True),
                             [recc_t, k_tok], [pst[4]])
                        S.op("dve", lambda e: e.tensor_copy(out=BCC[0:65, :], in_=bank(4, 0, 65)), [pst[4]], [bcc_t])
                        S.op("dve", lambda e: e.tensor_tensor(out=CC[0:65].rearrange("p a t -> p (a t)"), in0=bank(5, 0, 65),
                                                              in1=BCC[0:65, :], op=ALU.mult), [pst[5], bcc_t], [cc_t])
                        chk("C")

                    def emit_E1(j, b):
                        CA, CC, CB, HH = CATA2[b], CATC2[b], CATB2[b], H22[b]
                        ca_t, cc_t, cb_t, hh_t = cata2_t[b], catc2_t[b], catb2_t[b], h22_t[b]
                        if j == tiles[0] or j == 2:
                            w = 1 if j < 2 else 0
                            modrow(G1, w, 2, mr_t)
                            modrow(SH2, w, 3, mr_t)
                            modrow(SC2, w, 4, mr_t)
                        rows, rt = src_rows(j)
                        S.dma("sp", lambda e: e.dma_start(out=XT2[b][:], in_=rows), rt, [xt2_t[b]])
                        for m in range(2):
                            S.op("pe", lambda e: e.matmul(bank(6, 0, 128, m * 128, (m + 1) * 128), OBs[:, j, m * 128:(m + 1) * 128],
                                                          ident[:], start=True, stop=True), [ob_t[j], k_tok], [pst[6]])
                        S.op("dve", lambda e: e.tensor_copy(out=CB[:].rearrange("p m t -> p (m t)"), in_=bank(6, 0, 128, 0, 256)),
                             [pst[6]], [cb_t])
                        for n in range(2):
                            mms = []
                            for hd in range(8):
                                mms.append((CA[:, hd, :], WOA[:, hd, n * 512:(n + 1) * 512], ca_t))
                            for m in range(2):
                                mms.append((CB[:, m, :], WOB[:, m, n * 512:(n + 1) * 512], cb_t))
                            for hd in range(4):
                                mms.append((CC[:, hd, :], WOC[:, hd, n * 512:(n + 1) * 512], cc_t))
                            for i, (l_, r_, t_) in enumerate(mms):
                                S.op("pe", lambda e: e.matmul(bank(6 + n), l_, r_, start=(i == 0), stop=(i == len(mms) - 1)),
                                     [t_, w_t], [pst[6 + n]])
                        chk("E_PROJ")
                        O = ps[:, 6 * 512:8 * 512]
                        S.op("dve", lambda e: e.tensor_tensor(out=TMP[:], in0=O, in1=G1[:], op=ALU.mult),
                             [pst[6], pst[7], mr_t], [tmp_t])
                        S.op("dve", lambda e: e.scalar_tensor_tensor(out=RR[:], in0=XT2[b][:], scalar=ALPHA, in1=TMP[:],
                                                                     op0=ALU.mult, op1=ALU.add), [xt2_t[b], tmp_t], [rr_t])
                        ln_norm(XN2, RR, [rr_t], xn2_t)
                        S.op("dve", lambda e: e.tensor_tensor(out=XN2[:], in0=XN2[:], in1=LN1G[:], op=ALU.mult), [xn2_t, ln_t], [xn2_t])
                        S.op("pool", lambda e: e.tensor_tensor(out=RR[:], in0=XN2[:], in1=LN1B[:], op=ALU.add), [xn2_t, ln_t], [rr_t])
                        if ("x1_%d" % L) in dbg_d and j in (0, 2):
                            S.dma("sp", lambda e: e.dma_start(out=dbg_d["x1_%d" % L][(0 if j == 0 else 128):(128 if j == 0 else 256), :],
                                                              in_=RR[:]), [rr_t], [dbg_tok])
                        S.op("pool", lambda e: e.tensor_scalar(out=ACC[:], in0=RR[:], scalar1=ALPHA, scalar2=None, op0=ALU.mult),
                             [rr_t], [accb_t])
                        S.dma("sp", lambda e: e.dma_start(out=acc_d[j * 128:(j + 1) * 128, :], in_=ACC[:]), [accb_t], [acc_tok[j]])
                        ln_norm(XN2, RR, [rr_t], xn2_t)
                        S.op("dve", lambda e: e.tensor_tensor(out=XN2[:], in0=XN2[:], in1=SC2[:], op=ALU.mult), [xn2_t, mr_t], [xn2_t])
                        S.op("pool", lambda e: e.tensor_tensor(out=HH[:], in0=XN2[:], in1=SH2[:], op=ALU.add), [xn2_t, mr_t], [hh_t])
                        S.dma("sp", lambda e: e.dma_start(out=h2_d[j * 128:(j + 1) * 128, :], in_=HH[:]), [hh_t], [h2_tok[j]])
                        chk("E_LN")

                    def emit_E2(j, b):
                        HH, hh_t = H22[b], h22_t[b]
                        for k in range(8):
                            S.op("pe", lambda e: e.matmul(bank(6 + k // 4, 0, 128, (k % 4) * 128, (k % 4) * 128 + 128),
                                                          HH[:, k * 128:(k + 1) * 128], ident[:], start=True, stop=True),
                                 [hh_t, k_tok], [pst[6 + k // 4]])
                        S.op("dve", lambda e: e.tensor_copy(out=HT2[:, 0:4, :].rearrange("p a b -> p (a b)"), in_=bank(6)), [pst[6]], [ht2_t])
                        S.op("dve", lambda e: e.tensor_copy(out=HT2[:, 4:8, :].rearrange("p a b -> p (a b)"), in_=bank(7)), [pst[7]], [ht2_t])
                        for k in range(8):
                            S.op("pe", lambda e: e.matmul(bank(6, 0, 128, 0, NE), HT2[:, k, :], WR[:, k, :], start=(k == 0), stop=(k == 7)),
                                 [ht2_t, w_t], [pst[6]])
                        S.op("dve", lambda e: e.reduce_max(out=SM[:, 0:1], in_=bank(6, 0, 128, 0, NE), axis=AX.X), [pst[6]], [sm_t])
                        S.op("dve", lambda e: e.tensor_scalar(out=SM[:, 1:2], in0=SM[:, 0:1], scalar1=-1.0, scalar2=None, op0=ALU.mult),
                             [sm_t], [sm_t])
                        S.op("act", lambda e: e.activation(out=LG[:], in_=bank(6, 0, 128, 0, NE), func=AF.Exp, bias=SM[:, 1:2], scale=1.0,
                                                           accum_out=SM[:, 2:3]), [pst[6], sm_t], [lg_t, sm_t])
                        S.op("dve", lambda e: e.reciprocal(out=SM[:, 3:4], in_=SM[:, 2:3]), [sm_t], [sm_t])
                        S.op("dve", lambda e: e.tensor_scalar(out=AFF[L][:, j, :], in0=LG[:], scalar1=SM[:, 3:4], scalar2=None, op0=ALU.mult),
                             [lg_t, sm_t], [aff_t])

                    nt_ = len(tiles)
                    for idx in range(nt_ + 2):
                        if idx < nt_:
                            emit_BC(tiles[idx], idx % 2)
                        if 1 <= idx <= nt_:
                            emit_E1(tiles[idx - 1], (idx - 1) % 2)
                        if idx >= 2:
                            emit_E2(tiles[idx - 2], (idx - 2) % 2)
                    S.barrier()
                    for nm, src in (("cata", CATA[0:65].rearrange("p a t -> p (a t)")), ("catc", CATC[0:65].rearrange("p a t -> p (a t)"))):
                        if (nm + "%d" % L) in dbg_d:
                            n_ = src.shape[1]
                            S.op("dve", lambda e: e.tensor_copy(out=TMP[0:65, 0:n_], in_=src), [cata_t, catc_t], [tmp_t])
                            S.dma("sp", lambda e: e.dma_start(out=dbg_d[nm + "%d" % L], in_=TMP[0:65, 0:n_]), [tmp_t], [dbg_tok])
                            S.barrier()
            if stop_after == ("ML", L):
                break
            with ExitStack() as pf:
                TRI = sb(pf, "TRI", [128, 128], F32)
                IOB = sb(pf, "IOB", [128, 16, 128], F32)
                IOA = sb(pf, "IOA", [128, 16, 4], F32)
                PCOL = sb(pf, "PCOL", [128, 1], F32)
                TGT = sb(pf, "TGT", [128, 32], F32)
                kf_t = Tok()
                S.dma("sp", lambda e: e.dma_start(out=TRI[:], in_=tri_d), [], [kf_t])
                S.dma("sp", lambda e: e.dma_start(out=IOB[:].rearrange("p a b -> p (a b)"), in_=iotaB_d), [], [kf_t])
                S.dma("sp", lambda e: e.dma_start(out=IOA[:].rearrange("p a b -> p (a b)"), in_=iotaA_d), [], [kf_t])
                S.dma("sp", lambda e: e.dma_start(out=PCOL[:], in_=pcol_d), [], [kf_t])
                S.dma("sp", lambda e: e.dma_start(out=TGT[:], in_=tgt_d), [], [kf_t])
                THR = sb(pf, "THR", [128, 32], F32)
                LO = sb(pf, "LO", [128, 32], F32)
                CNTP = sb(pf, "CNTP", [128, 32], F32)
                IND = sb(pf, "IND", [128, 32], F32)
                MSK = sb(pf, "MSK", [128, NT, NE], F32)
                thr_t, lo_t, cntp_t, ind_t, msk_t = (Tok() for _ in range(5))
                S.op("dve", lambda e: e.memset(LO[:], 0.0), [], [lo_t])
                S.op("dve", lambda e: e.memset(CNTP[:], 0.0), [], [cntp_t])
                S.op("dve", lambda e: e.memset(MSK[:], 0.0), [], [msk_t])
                A_ = AFF[L]
                do_ctx = not last

                def make_mask(thr):
                    S.op("dve", lambda e: e.tensor_tensor(out=MSK[:, 2:NT, :], in0=A_[:, 2:NT, :],
                                                          in1=thr[:, 0:16].unsqueeze(1).to_broadcast([128, 32, 16]), op=ALU.is_ge),
                         [aff_t, thr_t, lo_t], [msk_t])
                    if do_ctx:
                        S.op("dve", lambda e: e.tensor_tensor(out=MSK[:, 0:2, :], in0=A_[:, 0:2, :],
                                                              in1=thr[:, 16:32].unsqueeze(1).to_broadcast([128, 2, 16]), op=ALU.is_ge),
                             [aff_t, thr_t, lo_t], [msk_t])

                for it in range(28):
                    wv = 2.0 ** -(it + 1)
                    S.op("dve", lambda e: e.tensor_scalar(out=THR[:], in0=LO[:], scalar1=wv, scalar2=None, op0=ALU.add), [lo_t], [thr_t])
                    make_mask(THR)
                    S.op("dve", lambda e: e.tensor_reduce(out=CNTP[:, 0:16], in_=MSK[:, 2:NT, :].rearrange("p j e -> p e j"),
                                                          axis=AX.X, op=ALU.add), [msk_t], [cntp_t])
                    if do_ctx:
                        S.op("dve", lambda e: e.tensor_reduce(out=CNTP[:, 16:32], in_=MSK[:, 0:2, :].rearrange("p j e -> p e j"),
                                                              axis=AX.X, op=ALU.add), [msk_t], [cntp_t])
                    S.op("pe", lambda e: e.matmul(bank(0, 0, 128, 0, 32), onesf[:], CNTP[:], start=True, stop=True),
                         [cntp_t, k_tok], [pst[0]])
                    S.op("dve", lambda e: e.tensor_tensor(out=IND[:], in0=bank(0, 0, 128, 0, 32), in1=TGT[:], op=ALU.is_ge),
                         [pst[0], kf_t], [ind_t])
                    S.op("dve", lambda e: e.scalar_tensor_tensor(out=LO[:], in0=IND[:], scalar=wv, in1=LO[:], op0=ALU.mult, op1=ALU.add),
                         [ind_t, lo_t], [lo_t])
                make_mask(LO)
                POS = sb(pf, "POS", [128, NT, NE], F32)
                OFF = sb(pf, "OFF", [128, NT, NE], F32)
                POSI = sb(pf, "POSI", [128, NT, NE], I32)
                BI = sb(pf, "BI", [128, NT, NE], I32)
                AI = sb(pf, "AI", [128, NT, NE], I32)
                BFl = sb(pf, "BFl", [128, NT, NE], F32)
                AFl = sb(pf, "AFl", [128, NT, NE], F32)
                pos_t, off_t = Tok(), Tok()
                groups = [(2, NT, 1, 2)] + ([(0, 2, 3, 3)] if do_ctx else [])
                for (j0, j1, bpw, btot) in groups:
                    n_ = (j1 - j0) * NE
                    c0 = 0 if j0 == 2 else 0
                    c1 = 0 if j0 == 2 else 64
                    mview = MSK[:, j0:j1, :].rearrange("p j e -> p (j e)")
                    S.op("pe", lambda e: e.matmul(bank(bpw, 0, 128, c0, c0 + n_), TRI[:], mview, start=True, stop=True),
                         [msk_t, kf_t], [pst[bpw]])
                    S.op("pe", lambda e: e.matmul(bank(btot, 0, 128, c1, c1 + n_), onesf[:], mview, start=True, stop=True),
                         [msk_t, k_tok], [pst[btot]])
                    S.op("dve", lambda e: e.memset(OFF[:, j0, :], 0.0), [], [off_t])
                    for jj in range(j0 + 1, j1):
                        S.op("dve", lambda e: e.tensor_tensor(out=OFF[:, jj, :], in0=OFF[:, jj - 1, :],
                                                              in1=bank(btot, 0, 128, c1 + (jj - 1 - j0) * NE, c1 + (jj - j0) * NE),
                                                              op=ALU.add), [off_t, pst[btot]], [off_t])
                    pv = POS[:, j0:j1, :].rearrange("p j e -> p (j e)")
                    S.op("dve", lambda e: e.tensor_tensor(out=pv, in0=bank(bpw, 0, 128, c0, c0 + n_),
                                                          in1=OFF[:, j0:j1, :].rearrange("p j e -> p (j e)"), op=ALU.add),
                         [pst[bpw], off_t], [pos_t])
                    S.op("dve", lambda e: e.scalar_tensor_tensor(out=pv, in0=pv, scalar=1.0, in1=mview, op0=ALU.add, op1=ALU.mult),
                         [pos_t, msk_t], [pos_t])
                    S.op("dve", lambda e: e.tensor_scalar(out=pv, in0=pv, scalar1=-1.0, scalar2=None, op0=ALU.add), [pos_t], [pos_t])
                    for (o_, i_, fn) in ((POSI, POS, None), (BI, POSI, ("and", 127)), (AI, POSI, ("shr", 7)), (BFl, BI, None), (AFl, AI, None)):
                        ov = o_[:, j0:j1, :].rearrange("p j e -> p (j e)")
                        iv = i_[:, j0:j1, :].rearrange("p j e -> p (j e)")
                        if fn is None:
                            S.op("dve", lambda e: e.tensor_copy(out=ov, in_=iv), [pos_t], [pos_t])
                        else:
                            opx = ALU.bitwise_and if fn[0] == "and" else ALU.arith_shift_right
                            S.op("dve", lambda e: e.tensor_scalar(out=ov, in0=iv, scalar1=fn[1], scalar2=None, op0=opx), [pos_t], [pos_t])
                if ("pos%d" % L) in dbg_d:
                    S.dma("sp", lambda e: e.dma_start(out=dbg_d["pos%d" % L], in_=POS[:].rearrange("p j e -> p (j e)")), [pos_t], [dbg_tok])
                    S.dma("sp", lambda e: e.dma_start(out=dbg_d["aff%d" % L], in_=A_[:].rearrange("p j e -> p (j e)")), [aff_t], [dbg_tok])
                OHB = [sb(pf, "OHB%d" % i, [128, 16, 128], F32) for i in range(2)]
                ohb_t = toks(2)
                RA = sb(pf, "RA", [128, 16, 4], F32)
                R3 = [sb(pf, "R3%d" % i, [128, 16, 3, 4], F32) for i in range(2)]
                ra_t = Tok()
                r3_t = toks(2)
                for jj in range(32):
                    b = jj % 2
                    j = 2 + jj
                    S.op("dve", lambda e: e.tensor_tensor(out=OHB[b][:], in0=IOB[:], in1=BFl[:, j, :].unsqueeze(2).to_broadcast([128, 16, 128]),
                                                          op=ALU.is_equal), [kf_t, pos_t], [ohb_t[b]])
                    S.op("dve", lambda e: e.tensor_tensor(out=RA[:], in0=IOA[:], in1=AFl[:, j, :].unsqueeze(2).to_broadcast([128, 16, 4]),
                                                          op=ALU.is_equal), [kf_t, pos_t], [ra_t])
                    S.op("dve", lambda e: e.tensor_scalar(out=R3[b][:, :, 0, :], in0=RA[:], scalar1=PCOL[:, 0:1], scalar2=None, op0=ALU.mult),
                         [ra_t, kf_t], [r3_t[b]])
                    S.op("dve", lambda e: e.tensor_scalar(out=R3[b][:, :, 1, :], in0=RA[:], scalar1=float(j), scalar2=None, op0=ALU.mult),
                         [ra_t], [r3_t[b]])
                    S.op("dve", lambda e: e.tensor_tensor(out=R3[b][:, :, 2, :], in0=RA[:],
                                                          in1=A_[:, j, :].unsqueeze(2).to_broadcast([128, 16, 4]), op=ALU.mult),
                         [ra_t, aff_t], [r3_t[b]])
                    for ex in range(NE):
                        S.op("pe", lambda e: e.matmul(bank(4, 0, 128, ex * 12, ex * 12 + 12), OHB[b][:, ex, :],
                                                      R3[b][:, ex].rearrange("p c a -> p (c a)"),
                                                      start=(jj == 0 and ex == 0), stop=(jj == 31), skip_group_check=True),
                             [ohb_t[b], r3_t[b]], [pst[4]])
                CMPS = sb(pf, "CMPS", [128, 192], F32)
                cmps_t = Tok()
                S.op("dve", lambda e: e.tensor_copy(out=CMPS[:], in_=bank(4, 0, 128, 0, 192)), [pst[4]], [cmps_t])
                CMP = CMPS[:].rearrange("p (e c a) -> p e c a", e=16, c=3)
                IDXF = sb(pf, "IDXF", [128, 16, 4], F32)
                idx_t = idx_tok[L]
                S.op("dve", lambda e: e.scalar_tensor_tensor(out=IDXF[:], in0=CMP[:, :, 1, :], scalar=128.0, in1=CMP[:, :, 0, :],
                                                             op0=ALU.mult, op1=ALU.add), [cmps_t], [idx_t])
                S.op("dve", lambda e: e.tensor_copy(out=IDX[L][:], in_=IDXF[:]), [idx_t], [idx_t])
                S.op("dve", lambda e: e.tensor_copy(out=GATE[L][:], in_=CMP[:, :, 2, :]), [cmps_t], [idx_t])
                if do_ctx:
                    OHC = sb(pf, "OHC", [128, 16, 32], F32)
                    R3C = sb(pf, "R3C", [128, 16, 3], F32)
                    ohc_t, r3c_t = Tok(), Tok()
                    for j in range(2):
                        S.op("dve", lambda e: e.tensor_tensor(out=OHC[:], in0=IOB[:, :, 0:32],
                                                              in1=BFl[:, j, :].unsqueeze(2).to_broadcast([128, 16, 32]), op=ALU.is_equal),
                             [kf_t, pos_t], [ohc_t])
                        S.op("dve", lambda e: e.tensor_copy(out=R3C[:, :, 0], in_=PCOL[:, 0:1].to_broadcast([128, 16])), [kf_t], [r3c_t])
                        S.op("dve", lambda e: e.memset(R3C[:, :, 1], float(j)), [], [r3c_t])
                        S.op("dve", lambda e: e.tensor_copy(out=R3C[:, :, 2], in_=A_[:, j, :]), [aff_t], [r3c_t])
                        for ex in range(NE):
                            S.op("pe", lambda e: e.matmul(bank(5, 0, 32, ex * 3, ex * 3 + 3), OHC[:, ex, :], R3C[:, ex, :],
                                                          start=(j == 0 and ex == 0), stop=(j == 1), skip_group_check=True),
                                 [ohc_t, r3c_t], [pst[5]])
                    S.op("dve", lambda e: e.tensor_copy(out=CMPS[0:32, 0:48], in_=bank(5, 0, 32, 0, 48)), [pst[5], cmps_t, idx_t], [cmps_t])
                    CMC = CMPS[0:32, 0:48].rearrange("p (e c) -> p e c", c=3)
                    S.op("dve", lambda e: e.scalar_tensor_tensor(out=IDXF[0:32, :, 0], in0=CMC[:, :, 1], scalar=128.0, in1=CMC[:, :, 0],
                                                                 op0=ALU.mult, op1=ALU.add), [cmps_t, idx_t], [idx_t])
                    S.op("dve", lambda e: e.tensor_copy(out=IDXC[L][0:32, :], in_=IDXF[0:32, :, 0]), [idx_t], [idx_t])
                    S.op("dve", lambda e: e.tensor_copy(out=GATEC[L][0:32, :], in_=CMC[:, :, 2]), [cmps_t], [idx_t])
                if ("idx%d" % L) in dbg_d:
                    S.dma("sp", lambda e: e.dma_start(out=dbg_d["idx%d" % L], in_=IDXF[:].rearrange("p e a -> p (e a)")), [idx_t], [dbg_tok])
                    S.dma("sp", lambda e: e.dma_start(out=dbg_d["gate%d" % L], in_=GATE[L][:].rearrange("p e a -> p (e a)")), [idx_t], [dbg_tok])
                S.barrier()
            if stop_after == ("F", L):
                break
            with ExitStack() as pg:
                idx_t = idx_tok[L]
                do_ctx = not last
                NS = 544 if do_ctx else 512
                G2 = sb(pg, "G2", [128, D], F32)
                G2C = sb(pg, "G2C", [128, D], F32)
                g2_t = Tok()
                modrow(G2, 0, 5, g2_t)
                if do_ctx:
                    modrow(G2C, 1, 5, g2_t)
                XG = [[sb(pg, "XG%d_%d" % (i, a), [128, D], BF16) for a in range(5)] for i in range(2)]
                xg_t = [toks(5) for _ in range(2)]
                XGT = [sb(pg, "XGT%d" % i, [128, 8, NS], BF16) for i in range(2)]
                xgt_t = toks(2)
                HID = sb(pg, "HID", [128, 16, NS], BF16)
                hid_t = Tok()
                WG = [sb(pg, "WG%d" % i, [128, 8, 512], BF16) for i in range(4)]
                WU = [sb(pg, "WU%d" % i, [128, 8, 512], BF16) for i in range(4)]
                wg_t, wu_t = toks(4), toks(4)
                WD = [sb(pg, "WD%d" % i, [128, 16, D], BF16) for i in range(2)]
                wd_t = toks(2)
                SIL = [sb(pg, "SIL%d" % i, [128, 512], F32) for i in range(2)]
                sil_t = toks(2)
                SILC = sb(pg, "SILC", [128, 32], F32)
                silc_t = Tok()
                YS = [sb(pg, "YS%d" % i, [128, D], F32) for i in range(2)]
                ys_t = toks(2)
                ysi = 0
                fcc = 0

                def load_gather(ex):
                    xb = ex % 2
                    for a in range(4):
                        S.dma("pool", lambda e: e.indirect_dma_start(
                            out=XG[xb][a][:], out_offset=None, in_=h2_d[:, :],
                            in_offset=bass.IndirectOffsetOnAxis(ap=IDX[L][:, ex, a:a + 1], axis=0)), h2_tok + [idx_t], [xg_t[xb][a]])
                    if do_ctx:
                        S.dma("pool", lambda e: e.indirect_dma_start(
                            out=XG[xb][4][0:32, :], out_offset=None, in_=h2_d[:, :],
                            in_offset=bass.IndirectOffsetOnAxis(ap=IDXC[L][0:32, ex:ex + 1], axis=0)), h2_tok + [idx_t], [xg_t[xb][4]])

                def load_wd(ex):
                    db = ex % 2
                    wdsrc = wd_d[L, ex].rearrange("(f p) d -> p f d", p=128)
                    for q in range(4):
                        S.dma("pool", lambda e: e.dma_start(out=WD[db][:, q * 4:(q + 1) * 4, :], in_=wdsrc[:, q * 4:(q + 1) * 4, :]),
                              [], [wd_t[db]])

                def load_gu(ex, fq):
                    wgsrc = wg_d[L, ex].rearrange("(k p) f -> p k f", p=128)
                    wusrc = wu_d[L, ex].rearrange("(k p) f -> p k f", p=128)
                    S.dma("pool", lambda e: e.dma_start(out=WG[fq][:], in_=wgsrc[:, :, fq * 512:(fq + 1) * 512]), [], [wg_t[fq]])
                    S.dma("pool", lambda e: e.dma_start(out=WU[fq][:], in_=wusrc[:, :, fq * 512:(fq + 1) * 512]), [], [wu_t[fq]])

                load_gather(0)
                load_wd(0)
                for fq in range(4):
                    load_gu(0, fq)
                load_gather(1)
                load_wd(1)
                for ex in range(NE):
                    xb = ex % 2
                    db = ex % 2
                    for a in range(4):
                        for kh in range(2):
                            for kk in range(4):
                                k = kh * 4 + kk
                                S.op("pe", lambda e: e.matmul(bank(6, 0, 128, kk * 128, (kk + 1) * 128),
                                                              XG[xb][a][:, k * 128:(k + 1) * 128], ident[:], start=True, stop=True),
                                     [xg_t[xb][a], k_tok], [pst[6]])
                            eng = "act" if (a + kh) % 2 == 0 else "dve"
                            src = bank(6).rearrange("p (k t) -> p k t", k=4)
                            dst = XGT[xb][:, kh * 4:(kh + 1) * 4, a * 128:(a + 1) * 128]
                            if eng == "act":
                                S.op("act", lambda e: e.copy(out=dst, in_=src), [pst[6]], [xgt_t[xb]])
                            else:
                                S.op("dve", lambda e: e.tensor_copy(out=dst, in_=src), [pst[6]], [xgt_t[xb]])
                    if do_ctx:
                        for k in range(8):
                            S.op("pe", lambda e: e.matmul(bank(7, 0, 128, k * 32, (k + 1) * 32), XG[xb][4][0:32, k * 128:(k + 1) * 128],
                                                          ident[0:32, 0:32], start=True, stop=True), [xg_t[xb][4], k_tok], [pst[7]])
                        S.op("dve", lambda e: e.tensor_copy(out=XGT[xb][:, :, 512:544],
                                                            in_=bank(7, 0, 128, 0, 256).rearrange("p (k t) -> p k t", k=8)),
                             [pst[7]], [xgt_t[xb]])
                    if ex + 2 < NE:
                        load_gather(ex + 2)
                    for fq in range(4):
                        wb = fq
                        for fi in range(4):
                            fc = fq * 4 + fi
                            pa = fcc % 2
                            fcc += 1
                            for (Wt, wt_t, bk) in ((WG[wb], wg_t[wb], pa), (WU[wb], wu_t[wb], 2 + pa)):
                                for k in range(8):
                                    S.op("pe", lambda e: e.matmul(bank(bk), Wt[:, k, fi * 128:(fi + 1) * 128], XGT[xb][:, k, 0:512],
                                                                  start=(k == 0), stop=(k == 7)), [wt_t, xgt_t[xb]], [pst[bk]])
                            if do_ctx:
                                for (Wt, wt_t, c0) in ((WG[wb], wg_t[wb], 256), (WU[wb], wu_t[wb], 288)):
                                    for k in range(8):
                                        S.op("pe", lambda e: e.matmul(bank(7, 0, 128, c0, c0 + 32), Wt[:, k, fi * 128:(fi + 1) * 128],
                                                                      XGT[xb][:, k, 512:544], start=(k == 0), stop=(k == 7)),
                                             [wt_t, xgt_t[xb]], [pst[7]])
                            S.op("act", lambda e: e.activation(out=SIL[pa][:], in_=bank(pa), func=AF.Silu), [pst[pa]], [sil_t[pa]])
                            S.op("dve", lambda e: e.tensor_tensor(out=HID[:, fc, 0:512], in0=bank(2 + pa), in1=SIL[pa][:], op=ALU.mult),
                                 [pst[2 + pa], sil_t[pa]], [hid_t])
                            if do_ctx:
                                S.op("act", lambda e: e.activation(out=SILC[:], in_=bank(7, 0, 128, 256, 288), func=AF.Silu),
                                     [pst[7]], [silc_t])
                                S.op("dve", lambda e: e.tensor_tensor(out=HID[:, fc, 512:544], in0=bank(7, 0, 128, 288, 320), in1=SILC[:],
                                                                      op=ALU.mult), [pst[7], silc_t], [hid_t])
                        if ex + 1 < NE:
                            load_gu(ex + 1, fq)
                    slots = [(s * 128, 128, IDX[L][:, ex, s:s + 1], GATE[L][:, ex, s:s + 1], G2) for s in range(4)]
                    if do_ctx:
                        slots.append((512, 32, IDXC[L][0:32, ex:ex + 1], GATEC[L][0:32, ex:ex + 1], G2C))
                    for (s0, sn, idx_ap, gate_ap, g2row) in slots:
                        for n in range(2):
                            for fc in range(16):
                                S.op("pe", lambda e: e.matmul(bank(4 + n, 0, sn), HID[:, fc, s0:s0 + sn], WD[db][:, fc, n * 512:(n + 1) * 512],
                                                              start=(fc == 0), stop=(fc == 15)), [hid_t, wd_t[db]], [pst[4 + n]])
                        yb = ysi % 2
                        ysi += 1
                        S.op("dve", lambda e: e.scalar_tensor_tensor(out=YS[yb][0:sn, :], in0=ps[0:sn, 4 * 512:6 * 512], scalar=gate_ap,
                                                                     in1=g2row[0:sn, :], op0=ALU.mult, op1=ALU.mult),
                             [pst[4], pst[5], idx_t, g2_t], [ys_t[yb]])
                        S.dma("pool", lambda e: e.indirect_dma_start(
                            out=acc_d[:, :], out_offset=bass.IndirectOffsetOnAxis(ap=idx_ap, axis=0), in_=YS[yb][0:sn, :], in_offset=None,
                            compute_op=ALU.add),
                            [ys_t[yb], idx_t] + acc_tok, [accs_tok])
                    if ex + 2 < NE:
                        load_wd(ex + 2)
                S.barrier()
            with ExitStack() as phh:
                LN2G = sb(phh, "LN2G", [128, D], F32)
                LN2B = sb(phh, "LN2B", [128, D], F32)
                l2_t = Tok()
                rowb(LN2G, ln2g_d[L], l2_t)
                rowb(LN2B, ln2b_d[L], l2_t)
                AT = [sb(phh, "AT%d" % i, [128, D], F32) for i in range(2)]
                at_t = toks(2)
                XO = [sb(phh, "XO%d" % i, [128, D], F32) for i in range(2)]
                xo_t = toks(2)
                ST3 = sb(phh, "ST3", [128, 2, 6], F32)
                MV3 = sb(phh, "MV3", [128, 8], F32)
                st3_t, mv3_t = Tok(), Tok()
                for j in (range(2, NT) if last else range(NT)):
                    b = j % 2
                    S.dma("sp", lambda e: e.dma_start(out=AT[b][:], in_=acc_d[j * 128:(j + 1) * 128, :]), [acc_tok[j], accs_tok], [at_t[b]])
                    for hh in range(2):
                        S.op("dve", lambda e: e.bn_stats(out=ST3[:, hh, :], in_=AT[b][:, hh * 512:(hh + 1) * 512]), [at_t[b]], [st3_t])
                    S.op("dve", lambda e: e.bn_aggr(out=MV3[:, 0:2], in_=ST3[:].rearrange("p a b -> p (a b)")), [st3_t], [mv3_t])
                    S.op("act", lambda e: e.activation(out=MV3[:, 2:3], in_=MV3[:, 1:2], func=AF.Sqrt, bias=LN_EPS, scale=1.0), [mv3_t], [mv3_t])
                    S.op("dve", lambda e: e.reciprocal(out=MV3[:, 3:4], in_=MV3[:, 2:3]), [mv3_t], [mv3_t])
                    S.op("dve", lambda e: e.tensor_scalar(out=MV3[:, 4:5], in0=MV3[:, 0:1], scalar1=MV3[:, 3:4], scalar2=-1.0,
                                                          op0=ALU.mult, op1=ALU.mult), [mv3_t], [mv3_t])
                    S.op("act", lambda e: e.activation(out=XO[b][:], in_=AT[b][:], func=AF.Identity, bias=MV3[:, 4:5], scale=MV3[:, 3:4]),
                         [mv3_t, at_t[b]], [xo_t[b]])
                    S.op("dve", lambda e: e.tensor_tensor(out=XO[b][:], in0=XO[b][:], in1=LN2G[:], op=ALU.mult), [xo_t[b], l2_t], [xo_t[b]])
                    S.op("pool", lambda e: e.tensor_tensor(out=XO[b][:], in0=XO[b][:], in1=LN2B[:], op=ALU.add), [xo_t[b], l2_t], [xo_t[b]])
                    if last:
                        S.dma("sp", lambda e: e.dma_start(out=out_d[(j - 2) * 128:(j - 1) * 128, :], in_=XO[b][:]), [xo_t[b]], [out_tok[j]])
                    else:
                        S.dma("sp", lambda e: e.dma_start(out=xcur_d[j * 128:(j + 1) * 128, :], in_=XO[b][:]), [xo_t[b]], [xcur_tok[j]])
                        if ("x2_%d" % L) in dbg_d:
                            S.dma("sp", lambda e: e.dma_start(out=dbg_d["x2_%d" % L][j * 128:(j + 1) * 128, :], in_=XO[b][:]), [xo_t[b]], [dbg_tok])
                S.barrier()
            if stop_after == ("H", L):
                break
        except _Stop:
            pass
        S.enabled = True
        S.barrier()
        print("instructions:", S.nins, "waits:", S.nwait)
    return nc


_CONST = None


def _consts():
    global _CONST
    if _CONST is not None:
        return _CONST
    bf = ml_dtypes.bfloat16
    c = {}
    c["k_ident"] = np.eye(128, dtype=np.float32).astype(bf)
    c["k_ones"] = np.ones((128, 128), np.float32)
    pp = np.arange(128)
    c["k_tri"] = (pp[:, None] < pp[None, :]).astype(np.float32)
    c["k_maskL"] = (pp[:, None] >= pp[None, :]).astype(np.float32).astype(bf)
    c["k_maskU"] = (pp[:, None] <= pp[None, :]).astype(np.float32).astype(bf)
    c["k_iotaB"] = np.ascontiguousarray(np.broadcast_to(np.arange(128, dtype=np.float32)[None, None, :], (128, 16, 128))).reshape(128, 2048)
    c["k_iotaA"] = np.ascontiguousarray(np.broadcast_to(np.arange(4, dtype=np.float32)[None, None, :], (128, 16, 4))).reshape(128, 64)
    c["k_pcol"] = np.arange(128, dtype=np.float32).reshape(128, 1)
    rows = 4096 // 64
    row = np.repeat(np.arange(rows), 64).astype(np.float32)
    col = np.tile(np.arange(64), rows).astype(np.float32)
    inv_freq = (10000.0 ** (-np.arange(0, 32, 2, dtype=np.float32) / np.float32(32))).astype(np.float32)
    ang = np.stack([row[:, None] * inv_freq, col[:, None] * inv_freq], axis=1).astype(np.float32)
    cos = np.ones((NT * 128, 32), np.float32)
    sin = np.zeros((NT * 128, 32), np.float32)
    cos[256:] = np.cos(ang).astype(np.float32).reshape(4096, 32)
    sin[256:] = np.sin(ang).astype(np.float32).reshape(4096, 32)
    c["k_cos"], c["k_sin"] = cos, sin
    t = np.arange(4096, dtype=np.int64)
    ph = (t[:, None] * t[None, :]) % 4096
    a = ph.astype(np.float64) * (2 * np.pi / 4096)
    c["k_ct"] = (np.cos(a) / 64.0).astype(np.float32).astype(bf)
    c["k_sn"] = (-np.sin(a) / 64.0).astype(np.float32).astype(bf)
    t2 = np.arange(256, dtype=np.int64)
    a2 = ((t2[:, None] * t2[None, :]) % 256).astype(np.float64) * (2 * np.pi / 256)
    c["k_ctc"] = np.concatenate([np.cos(a2) / 16.0, -np.sin(a2) / 16.0], axis=1).astype(np.float32).astype(bf)
    t3 = np.arange(64, dtype=np.int64)
    a3 = ((t3[:, None] * t3[None, :]) % 64).astype(np.float64) * (2 * np.pi / 64)
    c["k_cc"] = np.concatenate([np.cos(a3) / 8.0, np.sin(a3) / 8.0], axis=1).astype(np.float32)
    tg = np.zeros((128, 32), np.float32)
    tg[:, :16] = 512.0
    tg[:, 16:] = 32.0
    c["k_tgt"] = tg
    _CONST = c
    return c


def make_in_map(inputs, b):
    m = dict(_consts())
    m["x"] = np.ascontiguousarray(inputs["x"][b])
    m["ctx"] = np.ascontiguousarray(inputs["ctx"][b])
    cT = np.zeros((128, 8, 2), np.float32)
    cT[:, :, 0] = np.asarray(inputs["c"][b]).reshape(8, 128).T
    cT[:, :, 1] = np.asarray(inputs["c_ctx"]).reshape(8, 128).T
    m["cT"] = cT.reshape(128, 16)
    for k in ("w_mod", "b_mod", "w_in", "q_norm_a", "k_norm_a", "w_fourier", "sink_c", "w_out", "ln1_g", "ln1_b",
              "w_router", "w_gate", "w_up", "w_down", "ln2_g", "ln2_b"):
        m[k] = np.ascontiguousarray(inputs[k])
    m["b_fourier"] = np.ascontiguousarray(np.asarray(inputs["b_fourier"]).reshape(2, 256))
    m["w_in_uT"] = np.ascontiguousarray(np.transpose(np.asarray(inputs["w_in"])[:, :, 768:1024], (0, 2, 1)))
    return m


_NC = None


def _get_nc():
    global _NC
    if _NC is None:
        nc = bass.Bass("TRN2", target_bir_lowering=False)
        build(nc, n_layers=2)
        _NC = nc
    return _NC


def kernel(**inputs):
    inputs = {k: np.asarray(v) for k, v in inputs.items()}
    nc = _get_nc()
    n = 8
    in_maps = [make_in_map(inputs, b) for b in range(n)]
    res = run_bass_kernel_spmd(nc, in_maps, core_ids=list(range(n)))
    out = np.stack([np.asarray(res.results[b]["out"], dtype=np.float32) for b in range(n)], axis=0)
    return out
```

```python
import numpy as np
import ml_dtypes
from contextlib import ExitStack
import concourse.bass as bass
import concourse.mybir as mybir
from concourse.bass_utils import run_bass_kernel_spmd

F32 = mybir.dt.float32
BF16 = mybir.dt.bfloat16
I32 = mybir.dt.int32
ALU = mybir.AluOpType
AF = mybir.ActivationFunctionType
AX = mybir.AxisListType

D = 1024
NT = 34
NE = 16
FF = 2048
ALPHA = 4.0 ** 0.25
LN_EPS = 1e-5
RMS_EPS = 1e-6
C_QA, C_KA, C_QC, C_KC, C_UC, C_US, C_VA, C_VC = 0, 512, 640, 896, 1024, 1280, 1536, 1664
NCOL = 1792


class _Stop(Exception):
    pass


class Tok:
    __slots__ = ("w", "r")

    def __init__(self):
        self.w = None
        self.r = {}


def toks(n):
    return [Tok() for _ in range(n)]


class Sched:
    def __init__(self, nc, es, n_dma=44):
        self.nc = nc
        self.E = {"pe": nc.tensor, "act": nc.scalar, "dve": nc.vector, "pool": nc.gpsimd, "sp": nc.sync}
        self.sem = {k: es.enter_context(nc.semaphore("c_" + k)) for k in ("pe", "act", "dve", "pool")}
        self.cnt = {k: 0 for k in self.sem}
        self.dsem = [es.enter_context(nc.semaphore("d%d" % i)) for i in range(n_dma)]
        self.dcnt = [0] * n_dma
        self.dnext = 0
        self.dnext_sw = 0
        self.known = {k: {} for k in self.E}
        self.nwait = 0
        self.nins = 0
        self.enabled = True

    def _semobj(self, key):
        return self.sem[key] if isinstance(key, str) else self.dsem[key]

    def _deps(self, e, reads, writes):
        need = {}
        for t in reads:
            if t.w is not None:
                k, v = t.w
                if not (k == e and e == "pe"):
                    if need.get(k, 0) < v:
                        need[k] = v
        for t in writes:
            if t.w is not None:
                k, v = t.w
                if k != e and need.get(k, 0) < v:
                    need[k] = v
            for k, v in t.r.items():
                if k != e and need.get(k, 0) < v:
                    need[k] = v
        return need

    def _wait(self, e, need):
        kn = self.known[e]
        for k, v in need.items():
            if kn.get(k, 0) >= v:
                continue
            self.E[e].wait_ge(self._semobj(k), v)
            kn[k] = v
            self.nwait += 1

    def _mark(self, me, reads, writes):
        k, v = me
        for t in reads:
            if t.r.get(k, 0) < v:
                t.r[k] = v
        for t in writes:
            t.w = me
            t.r = {}

    def op(self, e, fn, reads=(), writes=()):
        if not self.enabled:
            return
        self._wait(e, self._deps(e, reads, writes))
        ins = fn(self.E[e])
        self.cnt[e] += 1
        ins.then_inc(self.sem[e], 1)
        self.nins += 1
        self._mark((e, self.cnt[e]), reads, writes)

    def dma(self, e, fn, reads=(), writes=()):
        if not self.enabled:
            return
        half = len(self.dsem) // 2
        if e == "pool":
            k = half + self.dnext_sw
            self.dnext_sw = (self.dnext_sw + 1) % (len(self.dsem) - half)
        else:
            k = self.dnext
            self.dnext = (self.dnext + 1) % half
        need = self._deps(e, reads, writes)
        if self.dcnt[k] > 0:
            need[k] = max(need.get(k, 0), 16 * self.dcnt[k])
        self._wait(e, need)
        ins = fn(self.E[e])
        self.dcnt[k] += 1
        ins.then_inc(self.dsem[k], 16)
        self.nins += 1
        self._mark((k, 16 * self.dcnt[k]), reads, writes)

    def barrier(self):
        if not self.enabled:
            return
        need = {k: v for k, v in self.cnt.items() if v > 0}
        for k in range(len(self.dsem)):
            if self.dcnt[k] > 0:
                need[k] = 16 * self.dcnt[k]
        for e in self.E:
            self._wait(e, {k: v for k, v in need.items() if k != e})


def build(nc, n_layers=2, dbg=None, stop_after=None, ml_tiles=None, stop_at=None, bc_bf16=False):
    dbg = dbg or {}
    es_top = ExitStack()
    with es_top as es:
        S = Sched(nc, es)
        E = S.E

        def chk(tag):
            if stop_at == tag:
                S.enabled = False

        def dram(name, shape, dt, kind="ExternalInput"):
            return nc.dram_tensor(name, list(shape), dt, kind=kind).ap()

        x_d = dram("x", [4096, D], F32)
        ctx_d = dram("ctx", [256, D], F32)
        cT_d = dram("cT", [128, 16], F32)
        w_mod_d = dram("w_mod", [2, D, 6 * D], F32)
        b_mod_d = dram("b_mod", [2, 6 * D], F32)
        w_in_d = dram("w_in", [2, D, 1536], F32)
        w_in_uT_d = dram("w_in_uT", [2, 256, D], F32)
        qn_d = dram("q_norm_a", [2, 64], F32)
        kn_d = dram("k_norm_a", [2, 64], F32)
        wf_d = dram("w_fourier", [2, 4, 64, 64], F32)
        bf_d = dram("b_fourier", [2, 256], F32)
        sink_d = dram("sink_c", [2, 4], F32)
        w_out_d = dram("w_out", [2, D, D], F32)
        ln1g_d = dram("ln1_g", [2, D], F32)
        ln1b_d = dram("ln1_b", [2, D], F32)
        wr_d = dram("w_router", [2, D, NE], F32)
        wg_d = dram("w_gate", [2, NE, D, FF], F32)
        wu_d = dram("w_up", [2, NE, D, FF], F32)
        wd_d = dram("w_down", [2, NE, FF, D], F32)
        ln2g_d = dram("ln2_g", [2, D], F32)
        ln2b_d = dram("ln2_b", [2, D], F32)
        ident_d = dram("k_ident", [128, 128], BF16)
        onesf_d = dram("k_ones", [128, 128], F32)
        tri_d = dram("k_tri", [128, 128], F32)
        maskL_d = dram("k_maskL", [128, 128], BF16)
        maskU_d = dram("k_maskU", [128, 128], BF16)
        iotaB_d = dram("k_iotaB", [128, 16 * 128], F32)
        iotaA_d = dram("k_iotaA", [128, 64], F32)
        pcol_d = dram("k_pcol", [128, 1], F32)
        cos_d = dram("k_cos", [NT * 128, 32], F32)
        sin_d = dram("k_sin", [NT * 128, 32], F32)
        ct_d = dram("k_ct", [4096, 4096], BF16)
        sn_d = dram("k_sn", [4096, 4096], BF16)
        ctc_d = dram("k_ctc", [256, 512], BF16)
        cc_d = dram("k_cc", [64, 128], F32)
        tgt_d = dram("k_tgt", [128, 32], F32)
        out_d = dram("out", [4096, D], F32, kind="ExternalOutput")
        dbg_d = {k: dram("dbg_" + k, shp, F32, kind="ExternalOutput") for k, shp in dbg.items()}
        mod_d = dram("s_mod", [2, 2, 6 * D], F32, kind="Internal")
        xcur_d = dram("s_xcur", [NT * 128, D], F32, kind="Internal")
        h2_d = dram("s_h2", [NT * 128, D], BF16, kind="Internal")
        acc_d = dram("s_acc", [NT * 128, D], F32, kind="Internal")
        mod_tok = toks(2)
        xcur_tok = toks(NT)
        h2_tok = toks(NT)
        acc_tok = toks(NT)
        accs_tok = Tok()
        out_tok = toks(NT)
        dbg_tok = Tok()

        uid = [0]

        def sb(stack, name, shape, dt):
            uid[0] += 1
            return stack.enter_context(nc.sbuf_tensor("sb%d_%s" % (uid[0], name), list(shape), dt))

        ps = es.enter_context(nc.psum_tensor("ps", [128, 4096], F32))
        pst = toks(8)

        def bank(b, p0=0, p1=128, c0=0, c1=512):
            return ps[p0:p1, b * 512 + c0:b * 512 + c1]

        ident = sb(es, "ident", [128, 128], BF16)
        onesf = sb(es, "onesf", [128, 128], F32)
        cT = sb(es, "cT", [128, 16], F32)
        cTs = sb(es, "cTs", [128, 16], F32)
        k_tok = Tok()
        S.dma("sp", lambda e: e.dma_start(out=ident[:], in_=ident_d), [], [k_tok])
        S.dma("sp", lambda e: e.dma_start(out=onesf[:], in_=onesf_d), [], [k_tok])
        S.dma("sp", lambda e: e.dma_start(out=cT[:], in_=cT_d), [], [k_tok])
        S.op("act", lambda e: e.activation(out=cTs[:], in_=cT[:], func=AF.Silu), [k_tok], [k_tok])

        AFF = [sb(es, "AFF", [128, NT, NE], F32)] * 2
        IDX = [sb(es, "IDX", [128, 16, 4], I32)] * 2
        GATE = [sb(es, "GATE", [128, 16, 4], F32)] * 2
        IDXC = [sb(es, "IDXC", [128, 16], I32)] * 2
        GATEC = [sb(es, "GATEC", [128, 16], F32)] * 2
        idx_tok = toks(2)
        aff_t = Tok()

        def dump(name, src_ap, dst_ap, rtoks):
            if name in dbg_d:
                S.dma("sp", lambda e: e.dma_start(out=dst_ap, in_=src_ap), list(rtoks), [dbg_tok])

        try:
          for L in range(n_layers):
            last = (L == n_layers - 1)
            with ExitStack() as ph:
                wm = [sb(ph, "wm%d" % i, [128, 8, 512], F32) for i in range(2)]
                wm_t = toks(2)
                bm = sb(ph, "bm", [2, 6 * D], F32)
                md = sb(ph, "md", [2, 6 * D], F32)
                bm_t, md_t = Tok(), Tok()
                S.dma("sp", lambda e: e.dma_start(out=bm[:], in_=b_mod_d[L].partition_broadcast(2)), [], [bm_t])
                wsrc = w_mod_d[L].rearrange("(k p) n -> p k n", p=128)
                for n in range(12):
                    b = n % 2
                    S.dma("sp", lambda e: e.dma_start(out=wm[b][:], in_=wsrc[:, :, n * 512:(n + 1) * 512]),
                          [], [wm_t[b]])
                    for k in range(8):
                        S.op("pe", lambda e: e.matmul(bank(b, 0, 2), cTs[:, 2 * k:2 * k + 2], wm[b][:, k, :],
                                                      start=(k == 0), stop=(k == 7)),
                             [k_tok, wm_t[b]], [pst[b]])
                    S.op("dve", lambda e: e.tensor_tensor(out=md[:, n * 512:(n + 1) * 512], in0=bank(b, 0, 2),
                                                          in1=bm[:, n * 512:(n + 1) * 512], op=ALU.add),
                         [pst[b], bm_t], [md_t])
                for c in (1, 4):
                    S.op("dve", lambda e: e.tensor_scalar(out=md[:, c * D:(c + 1) * D], in0=md[:, c * D:(c + 1) * D],
                                                          scalar1=1.0, scalar2=None, op0=ALU.add), [md_t], [md_t])
                S.dma("sp", lambda e: e.dma_start(out=mod_d[L], in_=md[:]), [md_t], [mod_tok[L]])
                dump("mod%d" % L, md[:], dbg_d.get("mod%d" % L), [md_t])
                S.barrier()
            if stop_after == ("mod", L):
                break

            def modrow(dst, which, chunk, tok):
                src = mod_d[L, which, chunk * D:(chunk + 1) * D].partition_broadcast(128)
                S.dma("sp", lambda e: e.dma_start(out=dst[:], in_=src), [mod_tok[L]], [tok])

            def rowb(dst, src_row, tok):
                S.dma("sp", lambda e: e.dma_start(out=dst[:], in_=src_row.partition_broadcast(128)), [], [tok])

            def src_rows(j):
                if L == 0:
                    return (ctx_d[j * 128:(j + 1) * 128, :] if j < 2 else x_d[(j - 2) * 128:(j - 1) * 128, :]), []
                return xcur_d[j * 128:(j + 1) * 128, :], [xcur_tok[j]]

            with ExitStack() as lay:
                QA = sb(lay, "QA", [128, NT, 4, 128], BF16)
                KA = sb(lay, "KA", [128, NT * 128], BF16)
                VA = sb(lay, "VA", [128, NT, 2, 65], BF16)
                QC = sb(lay, "QC", [128, NT, 2, 128], BF16)
                KC = sb(lay, "KC", [128, NT * 128], BF16)
                VC = sb(lay, "VC", [128, NT, 2, 65], BF16)
                OBs = sb(lay, "OBs", [128, NT, 256], BF16)
                qa_t, ka_t, va_t, qc_t, kc_t, vc_t, ob_t = (toks(NT) for _ in range(7))
                vinit = Tok()
                S.op("pool", lambda e: e.memset(VA[:], 1.0), [], [vinit])
                S.op("pool", lambda e: e.memset(VC[:], 1.0), [], [vinit])

                with ExitStack() as ph_outer:
                  U2 = sb(ph_outer, "U2", [128, NT, 512], BF16)
                  u2_t = toks(NT)
                  with ExitStack() as ph:
                    WINB = sb(ph, "WINB", [128, 8, NCOL], BF16)
                    winb_t = Tok()
                    wsrc = w_in_d[L].rearrange("(k p) n -> p k n", p=128)
                    for (dc, sc, wd) in ((C_QA, 0, 512), (C_KA, 1024, 128), (C_QC, 512, 256), (C_KC, 1280, 128),
                                         (C_VA, 1152, 128), (C_VC, 1408, 128)):
                        S.dma("pool", lambda e: e.dma_start(out=WINB[:, :, dc:dc + wd], in_=wsrc[:, :, sc:sc + wd]),
                              [], [winb_t])
                    with ExitStack() as ff:
                        CC = sb(ff, "CC", [64, 128], F32)
                        WF = sb(ff, "WF", [64, 4, 64], F32)
                        MCS = sb(ff, "MCS", [64, 2, 4, 64], F32)
                        WUT = sb(ff, "WUT", [64, 4, D], F32)
                        f_t = Tok()
                        S.dma("sp", lambda e: e.dma_start(out=CC[:], in_=cc_d), [], [f_t])
                        S.dma("sp", lambda e: e.dma_start(out=WF[:], in_=wf_d[L].rearrange("g c d -> c g d")), [], [f_t])
                        S.dma("sp", lambda e: e.dma_start(out=WUT[:], in_=w_in_uT_d[L].rearrange("(g c) d -> c g d", c=64)),
                              [], [f_t])
                        for cs in range(2):
                            S.op("pe", lambda e: e.matmul(bank(cs, 0, 64, 0, 256), CC[:, cs * 64:(cs + 1) * 64],
                                                          WF[:].rearrange("c g d -> c (g d)"), start=True, stop=True),
                                 [f_t], [pst[cs]])
                            S.op("dve", lambda e: e.tensor_copy(out=MCS[:, cs].rearrange("c g d -> c (g d)"),
                                                                in_=bank(cs, 0, 64, 0, 256)), [pst[cs]], [f_t])
                        for k in range(8):
                            b = 2 + (k % 2)
                            for cs in range(2):
                                for g in range(4):
                                    c0 = cs * 256 + g * 64
                                    S.op("pe", lambda e: e.matmul(bank(b, 0, 128, c0, c0 + 64),
                                                                  WUT[:, g, k * 128:(k + 1) * 128], MCS[:, cs, g, :],
                                                                  start=True, stop=True), [f_t], [pst[b]])
                            S.op("dve", lambda e: e.tensor_copy(out=WINB[:, k, C_UC:C_UC + 512], in_=bank(b)),
                                 [pst[b]], [winb_t])
                        S.barrier()
                    SH1 = sb(ph, "SH1", [128, D], F32)
                    SC1 = sb(ph, "SC1", [128, D], F32)
                    GQK = sb(ph, "GQK", [128, 10, 64], F32)
                    mr_t = Tok()
                    g_t = Tok()
                    for h in range(8):
                        rowb(GQK[:, h, :], qn_d[L], g_t)
                    for h in range(8, 10):
                        rowb(GQK[:, h, :], kn_d[L], g_t)
                    XT = [sb(ph, "XT%d" % i, [128, D], F32) for i in range(2)]
                    xt_t = toks(2)
                    XN = sb(ph, "XN", [128, D], F32)
                    Hb = sb(ph, "Hb", [128, D], BF16)
                    HT = sb(ph, "HT", [128, 8, 128], BF16)
                    ST = sb(ph, "ST", [128, 2, 6], F32)
                    MV = sb(ph, "MV", [128, 8], F32)
                    SQ = sb(ph, "SQ", [128, 640], F32)
                    MS = sb(ph, "MS", [128, 16], F32)
                    NRM = SQ[:].rearrange("p (h d) -> p h d", d=64)
                    R = sb(ph, "R", [128, 16, 64], F32)
                    RO = sb(ph, "RO", [128, 16, 64], BF16)
                    T1 = sb(ph, "T1", [128, 16, 2, 16], F32)
                    T2 = sb(ph, "T2", [128, 16, 2, 16], F32)
                    CS = [sb(ph, "CS%d" % i, [128, 2, 32], F32) for i in range(2)]
                    cs_t = toks(2)
                    xn_t, hb_t, ht_t, st_t, mv_t, sq_t, ms_t, nrm_t, r_t, ro_t, tt_t = (Tok() for _ in range(11))
                    for j in range(NT):
                        if j == 0 or j == 2:
                            w = 1 if j == 0 else 0
                            modrow(SH1, w, 0, mr_t)
                            modrow(SC1, w, 1, mr_t)
                        b = j % 2
                        rows, rt = src_rows(j)
                        S.dma("sp", lambda e: e.dma_start(out=XT[b][:], in_=rows), rt, [xt_t[b]])
                        S.dma("sp", lambda e: e.dma_start(out=CS[b][:, 0, :], in_=cos_d[j * 128:(j + 1) * 128, :]), [], [cs_t[b]])
                        S.dma("sp", lambda e: e.dma_start(out=CS[b][:, 1, :], in_=sin_d[j * 128:(j + 1) * 128, :]), [], [cs_t[b]])
                        xt = XT[b]
                        for hh in range(2):
                            S.op("dve", lambda e: e.bn_stats(out=ST[:, hh, :], in_=xt[:, hh * 512:(hh + 1) * 512]),
                                 [xt_t[b]], [st_t])
                        S.op("dve", lambda e: e.bn_aggr(out=MV[:, 0:2], in_=ST[:].rearrange("p a b -> p (a b)")),
                             [st_t], [mv_t])
                        S.op("act", lambda e: e.activation(out=MV[:, 2:3], in_=MV[:, 1:2], func=AF.Sqrt, bias=LN_EPS,
                                                           scale=1.0), [mv_t], [mv_t])
                        S.op("dve", lambda e: e.reciprocal(out=MV[:, 3:4], in_=MV[:, 2:3]), [mv_t], [mv_t])
                        S.op("dve", lambda e: e.tensor_scalar(out=MV[:, 4:5], in0=MV[:, 0:1], scalar1=MV[:, 3:4],
                                                              scalar2=-1.0, op0=ALU.mult, op1=ALU.mult), [mv_t], [mv_t])
                        S.op("act", lambda e: e.activation(out=XN[:], in_=xt[:], func=AF.Identity, bias=MV[:, 4:5],
                                                           scale=MV[:, 3:4]), [mv_t, xt_t[b]], [xn_t])
                        S.op("dve", lambda e: e.tensor_tensor(out=XN[:], in0=XN[:], in1=SC1[:], op=ALU.mult),
                             [xn_t, mr_t], [xn_t])
                        S.op("dve", lambda e: e.tensor_tensor(out=Hb[:], in0=XN[:], in1=SH1[:], op=ALU.add),
                             [xn_t, mr_t], [hb_t])
                        for k in range(8):
                            S.op("pe", lambda e: e.matmul(bank(k // 4, 0, 128, (k % 4) * 128, (k % 4) * 128 + 128),
                                                          Hb[:, k * 128:(k + 1) * 128], ident[:], start=True, stop=True),
                                 [hb_t, k_tok], [pst[k // 4]])
                        S.op("act", lambda e: e.copy(out=HT[:, 0:4, :].rearrange("p a b -> p (a b)"), in_=bank(0)),
                             [pst[0]], [ht_t])
                        S.op("dve", lambda e: e.tensor_copy(out=HT[:, 4:8, :].rearrange("p a b -> p (a b)"), in_=bank(1)),
                             [pst[1]], [ht_t])
                        for cg in range(4):
                            c0, c1 = cg * 512, min(NCOL, cg * 512 + 512)
                            for k in range(8):
                                S.op("pe", lambda e: e.matmul(bank(2 + cg, 0, 128, 0, c1 - c0), HT[:, k, :],
                                                              WINB[:, k, c0:c1], start=(k == 0), stop=(k == 7)),
                                     [ht_t, winb_t], [pst[2 + cg]])
                        P = ps[:, 2 * 512:2 * 512 + NCOL]
                        if ("p%d" % L) in dbg_d and j == 2:
                            for (a0, a1) in ((0, 1024), (1024, NCOL)):
                                S.op("dve", lambda e: e.tensor_copy(out=XN[:, 0:a1 - a0], in_=P[:, a0:a1]),
                                     [pst[2], pst[3], pst[4], pst[5]], [xn_t])
                                S.dma("sp", lambda e: e.dma_start(out=dbg_d["p%d" % L][:, a0:a1], in_=XN[:, 0:a1 - a0]),
                                      [xn_t], [dbg_tok])
                        S.op("act", lambda e: e.activation(out=SQ[:], in_=P[:, 0:640], func=AF.Square),
                             [pst[2], pst[3]], [sq_t])
                        S.op("dve", lambda e: e.tensor_reduce(out=MS[:, 0:10], in_=SQ[:].rearrange("p (h d) -> p h d", d=64),
                                                              axis=AX.X, op=ALU.add), [sq_t], [ms_t])
                        S.op("act", lambda e: e.activation(out=MS[:, 0:10], in_=MS[:, 0:10], func=AF.Sqrt, bias=RMS_EPS,
                                                           scale=1.0 / 64.0), [ms_t], [ms_t])
                        S.op("dve", lambda e: e.reciprocal(out=MS[:, 0:10], in_=MS[:, 0:10]), [ms_t], [ms_t])
                        S.op("dve", lambda e: e.tensor_tensor(out=NRM, in0=P[:, 0:640].rearrange("p (h d) -> p h d", d=64),
                                                              in1=MS[:, 0:10].unsqueeze(2).to_broadcast([128, 10, 64]),
                                                              op=ALU.mult), [ms_t, pst[2], pst[3]], [nrm_t])
                        S.op("dve", lambda e: e.tensor_tensor(
                            out=R[:, 0:8, :].rearrange("p (g kv) d -> p kv g d", kv=2),
                            in0=NRM[:, 0:8, :].rearrange("p (kv g) d -> p kv g d", kv=2),
                            in1=GQK[:, 0:8, :].rearrange("p (kv g) d -> p kv g d", kv=2), op=ALU.mult),
                            [nrm_t, g_t], [r_t])
                        S.op("dve", lambda e: e.tensor_tensor(out=R[:, 8:10, :], in0=NRM[:, 8:10, :], in1=GQK[:, 8:10, :],
                                                              op=ALU.mult), [nrm_t, g_t], [r_t])
                        S.op("act", lambda e: e.copy(
                            out=R[:, 10:14, :].rearrange("p (g kv) d -> p kv g d", kv=2),
                            in_=P[:, C_QC:C_QC + 256].rearrange("p (kv g d) -> p kv g d", kv=2, g=2)), [pst[3]], [r_t])
                        S.op("act", lambda e: e.copy(out=R[:, 14:16, :].rearrange("p h d -> p (h d)"),
                                                     in_=P[:, C_KC:C_KC + 128]), [pst[3]], [r_t])
                        Rv = R[:].rearrange("p h (a b f) -> p h a b f", a=2, b=2)
                        ROv = RO[:].rearrange("p h (a b f) -> p h a b f", a=2, b=2)
                        x1, x2 = Rv[:, :, :, 0, :], Rv[:, :, :, 1, :]
                        cosb = CS[b][:, 0, :].rearrange("p (a f) -> p a f", a=2).unsqueeze(1).to_broadcast([128, 16, 2, 16])
                        sinb = CS[b][:, 1, :].rearrange("p (a f) -> p a f", a=2).unsqueeze(1).to_broadcast([128, 16, 2, 16])
                        S.op("dve", lambda e: e.tensor_tensor(out=T1[:], in0=x1, in1=cosb, op=ALU.mult), [r_t, cs_t[b]], [tt_t])
                        S.op("dve", lambda e: e.tensor_tensor(out=T2[:], in0=x2, in1=sinb, op=ALU.mult), [r_t, cs_t[b]], [tt_t])
                        S.op("dve", lambda e: e.tensor_tensor(out=ROv[:, :, :, 0, :], in0=T1[:], in1=T2[:], op=ALU.subtract),
                             [tt_t], [ro_t])
                        S.op("dve", lambda e: e.tensor_tensor(out=T1[:], in0=x2, in1=cosb, op=ALU.mult), [r_t, cs_t[b]], [tt_t])
                        S.op("dve", lambda e: e.tensor_tensor(out=T2[:], in0=x1, in1=sinb, op=ALU.mult), [r_t, cs_t[b]], [tt_t])
                        S.op("dve", lambda e: e.tensor_tensor(out=ROv[:, :, :, 1, :], in0=T1[:], in1=T2[:], op=ALU.add),
                             [tt_t], [ro_t])
                        for blk in range(8):
                            S.op("pe", lambda e: e.matmul(bank(blk // 4, 0, 128, (blk % 4) * 128, (blk % 4) * 128 + 128),
                                                          RO[:, 2 * blk:2 * blk + 2, :].rearrange("p h d -> p (h d)"),
                                                          ident[:], start=True, stop=True), [ro_t, k_tok], [pst[blk // 4]])
                        S.op("act", lambda e: e.copy(out=QA[:, j].rearrange("p g t -> p (g t)"), in_=bank(0)),
                             [pst[0]], [qa_t[j]])
                        S.op("dve", lambda e: e.tensor_copy(out=KA[:, j * 128:(j + 1) * 128], in_=bank(1, 0, 128, 0, 128)),
                             [pst[1]], [ka_t[j]])
                        S.op("dve", lambda e: e.tensor_copy(out=QC[:, j].rearrange("p g t -> p (g t)"),
                                                            in_=bank(1, 0, 128, 128, 384)), [pst[1]], [qc_t[j]])
                        S.op("dve", lambda e: e.tensor_copy(out=KC[:, j * 128:(j + 1) * 128], in_=bank(1, 0, 128, 384, 512)),
                             [pst[1]], [kc_t[j]])
                        S.op("act", lambda e: e.copy(out=VA[:, j, :, 1:65],
                                                     in_=P[:, C_VA:C_VA + 128].rearrange("p (h d) -> p h d", d=64)),
                             [pst[5], vinit], [va_t[j]])
                        S.op("act", lambda e: e.copy(out=VC[:, j, :, 1:65],
                                                     in_=P[:, C_VC:C_VC + 128].rearrange("p (h d) -> p h d", d=64)),
                             [pst[5], vinit], [vc_t[j]])
                        S.op("dve", lambda e: e.tensor_copy(out=U2[:, j, :], in_=P[:, C_UC:C_UC + 512]), [pst[4]], [u2_t[j]])
                    S.barrier()
                  if True:
                    with ExitStack() as pd:
                        BFR = sb(pd, "BFR", [128, 256], F32)
                        bfr_t = Tok()
                        rowb(BFR, bf_d[L], bfr_t)
                        TB = [sb(pd, "TB%d" % i, [128, 2, 2048], BF16) for i in range(3)]
                        tb_t = toks(3)
                        it = 0
                        for half in range(2):
                            for tc in range(32):
                                b = it % 3
                                it += 1
                                S.dma("sp", lambda e: e.dma_start(out=TB[b][:, 0, :],
                                                                  in_=ct_d[tc * 128:(tc + 1) * 128, half * 2048:(half + 1) * 2048]),
                                      [], [tb_t[b]])
                                S.dma("sp", lambda e: e.dma_start(out=TB[b][:, 1, :],
                                                                  in_=sn_d[tc * 128:(tc + 1) * 128, half * 2048:(half + 1) * 2048]),
                                      [], [tb_t[b]])
                                for kc in range(16):
                                    for cs in range(2):
                                        S.op("pe", lambda e: e.matmul(
                                            bank(kc // 2, 0, 128, (kc % 2) * 256, (kc % 2) * 256 + 256),
                                            TB[b][:, cs, kc * 128:(kc + 1) * 128], U2[:, 2 + tc, cs * 256:(cs + 1) * 256],
                                            start=(tc == 0 and cs == 0 and kc % 2 == 0), stop=(tc == 31 and cs == 1),
                                            skip_group_check=True),
                                            [tb_t[b], u2_t[2 + tc]], [pst[kc // 2]])
                            for kc in range(16):
                                jj = 2 + half * 16 + kc
                                S.op("dve", lambda e: e.tensor_tensor(
                                    out=OBs[:, jj, :], in0=bank(kc // 2, 0, 128, (kc % 2) * 256, (kc % 2) * 256 + 256),
                                    in1=BFR[:], op=ALU.add), [pst[kc // 2], bfr_t], [ob_t[jj]])
                        if not last:
                            TBC = sb(pd, "TBC", [128, 2, 512], BF16)
                            tbc_t = Tok()
                            for tc in range(2):
                                S.dma("sp", lambda e: e.dma_start(out=TBC[:, tc, :], in_=ctc_d[tc * 128:(tc + 1) * 128, :]),
                                      [], [tbc_t])
                            for kc in range(2):
                                for tc in range(2):
                                    for cs in range(2):
                                        S.op("pe", lambda e: e.matmul(
                                            bank(0, 0, 128, kc * 256, kc * 256 + 256),
                                            TBC[:, tc, cs * 256 + kc * 128:cs * 256 + kc * 128 + 128],
                                            U2[:, tc, cs * 256:(cs + 1) * 256],
                                            start=(tc == 0 and cs == 0 and kc == 0), stop=(tc == 1 and cs == 1),
                                            skip_group_check=True),
                                            [tbc_t, u2_t[tc]], [pst[0]])
                                S.op("dve", lambda e: e.tensor_tensor(out=OBs[:, kc, :], in0=bank(0, 0, 128, kc * 256, kc * 256 + 256),
                                                                      in1=BFR[:], op=ALU.add), [pst[0], bfr_t], [ob_t[kc]])
                        S.barrier()
                if ("QA%d" % L) in dbg_d:
                    with ExitStack() as dd:
                        TMPD = sb(dd, "TMPD", [128, 4352], F32)
                        td = Tok()
                        for nm, src in (("KA", KA[:]), ("KC", KC[:])):
                            S.op("dve", lambda e: e.tensor_copy(out=TMPD[:], in_=src), ka_t + kc_t, [td])
                            dump(nm + "%d" % L, TMPD[:], dbg_d.get(nm + "%d" % L), [td])
                        for nm, src in (("QA", QA[:, 2].rearrange("p g t -> p (g t)")), ("OB", OBs[:, 2, :]),
                                        ("VA", VA[:, 2].rearrange("p h d -> p (h d)"))):
                            n = src.shape[1]
                            S.op("dve", lambda e: e.tensor_copy(out=TMPD[:, 0:n], in_=src), qa_t + ob_t + va_t, [td])
                            dump(nm + "%d" % L, TMPD[:, 0:n], dbg_d.get(nm + "%d" % L), [td])
                        S.barrier()
                if stop_after == ("A", L):
                    break
                with ExitStack() as ml:
                    WOA = sb(ml, "WOA", [128, 8, D], BF16)
                    WOB = sb(ml, "WOB", [128, 2, D], BF16)
                    WOC = sb(ml, "WOC", [128, 4, D], BF16)
                    WR = sb(ml, "WR", [128, 8, NE], BF16)
                    w_t = Tok()
                    S.op("pool", lambda e: e.memset(WOA[:], 0.0), [], [w_t])
                    S.op("pool", lambda e: e.memset(WOC[:], 0.0), [], [w_t])
                    S.dma("pool", lambda e: e.dma_start(out=WOA[1:65], in_=w_out_d[L, 0:512, :].rearrange("(h p) n -> p h n", p=64)), [w_t], [w_t])
                    S.dma("pool", lambda e: e.dma_start(out=WOB[:], in_=w_out_d[L, 512:768, :].rearrange("(h p) n -> p h n", p=128)), [], [w_t])
                    S.dma("pool", lambda e: e.dma_start(out=WOC[1:65], in_=w_out_d[L, 768:1024, :].rearrange("(h p) n -> p h n", p=64)), [w_t], [w_t])
                    S.dma("pool", lambda e: e.dma_start(out=WR[:], in_=wr_d[L].rearrange("(k p) n -> p k n", p=128)), [], [w_t])
                    G1 = sb(ml, "G1", [128, D], F32)
                    SC2 = sb(ml, "SC2", [128, D], F32)
                    SH2 = sb(ml, "SH2", [128, D], F32)
                    LN1G = sb(ml, "LN1G", [128, D], F32)
                    LN1B = sb(ml, "LN1B", [128, D], F32)
                    mr_t, ln_t = Tok(), Tok()
                    rowb(LN1G, ln1g_d[L], ln_t)
                    rowb(LN1B, ln1b_d[L], ln_t)
                    MKL = sb(ml, "MKL", [128, 128], BF16)
                    MKU = sb(ml, "MKU", [128, 128], BF16)
                    SINKE = sb(ml, "SINKE", [128, 4, 128], F32)
                    SK4 = sb(ml, "SK4", [128, 4], F32)
                    mk_t, sk_t = Tok(), Tok()
                    S.dma("sp", lambda e: e.dma_start(out=MKL[:], in_=maskL_d), [], [mk_t])
                    S.dma("sp", lambda e: e.dma_start(out=MKU[:], in_=maskU_d), [], [mk_t])
                    S.dma("sp", lambda e: e.dma_start(out=SK4[0:1, :], in_=sink_d[L:L + 1, :]), [], [sk_t])
                    S.op("act", lambda e: e.activation(out=SK4[0:1, :], in_=SK4[0:1, :], func=AF.Exp), [sk_t], [sk_t])
                    S.op("dve", lambda e: e.tensor_copy(out=SINKE[0:1, :, :], in_=SK4[0:1, :].unsqueeze(2).to_broadcast([1, 4, 128])),
                         [sk_t], [sk_t])
                    PT = [sb(ml, "PT%d" % i, [128, 512], BF16) for i in range(3)]
                    pt_t = toks(3)
                    PTC = [sb(ml, "PTC%d" % i, [128, 512], BF16) for i in range(2)]
                    ptc_t = toks(2)
                    REC = sb(ml, "REC", [128, 1024], F32)
                    BCS = sb(ml, "BCS", [128, 1024], F32)
                    RECC = REC[:, 0:512]
                    BCC = BCS[:, 0:512]
                    CATA = sb(ml, "CATA", [128, 8, 128], BF16)
                    CATC = sb(ml, "CATC", [128, 4, 128], BF16)
                    CATB = sb(ml, "CATB", [128, 2, 128], BF16)
                    rec_t, bcs_t, cata_t, catc_t, catb_t = (Tok() for _ in range(5))
                    recc_t, bcc_t = rec_t, bcs_t
                    S.op("pool", lambda e: e.memset(CATA[:], 0.0), [], [cata_t])
                    S.op("pool", lambda e: e.memset(CATC[:], 0.0), [], [catc_t])
                    QZ = [sb(ml, "QZ%d" % i, [128, 2, 512], BF16) for i in range(2)]
                    QCZ = [sb(ml, "QCZ%d" % i, [128, 2, 256], BF16) for i in range(2)]
                    qz_t, qcz_t = toks(2), toks(2)
                    for i in range(2):
                        S.op("pool", lambda e: e.memset(QZ[i][:], 0.0), [], [qz_t[i]])
                        S.op("pool", lambda e: e.memset(QCZ[i][:], 0.0), [], [qcz_t[i]])
                    XT2 = [sb(ml, "XT20", [128, D], F32)] * 2
                    xt2_t = [Tok()] * 2
                    TMP = sb(ml, "TMP", [128, D], F32)
                    RR = sb(ml, "RR", [128, D], F32)
                    XN2 = sb(ml, "XN2", [128, D], F32)
                    ACC = TMP
                    H2 = sb(ml, "H2", [128, D], BF16)
                    HT2 = sb(ml, "HT2", [128, 8, 128], BF16)
                    ST2 = sb(ml, "ST2", [128, 2, 6], F32)
                    MV2 = sb(ml, "MV2", [128, 8], F32)
                    LG = sb(ml, "LG", [128, NE], F32)
                    SM = sb(ml, "SM", [128, 4], F32)
                    tmp_t, rr_t, xn2_t, h2b_t, ht2_t, st2_t, mv2_t, lg_t, sm_t = (Tok() for _ in range(9))
                    accb_t = tmp_t

                    NEGH = sb(ml, "NEGH", [128, 1], F32)
                    ngh_t = Tok()
                    S.op("pool", lambda e: e.memset(NEGH[:], -0.5), [], [ngh_t])

                    def ln_norm(dst, src, src_toks, dst_tok):
                        for hh in range(2):
                            S.op("dve", lambda e: e.bn_stats(out=ST2[:, hh, :], in_=src[:, hh * 512:(hh + 1) * 512]),
                                 src_toks, [st2_t])
                        S.op("dve", lambda e: e.bn_aggr(out=MV2[:, 0:2], in_=ST2[:].rearrange("p a b -> p (a b)")),
                             [st2_t], [mv2_t])
                        S.op("pool", lambda e: e.tensor_scalar(out=MV2[:, 2:3], in0=MV2[:, 1:2], scalar1=LN_EPS, scalar2=None,
                                                               op0=ALU.add), [mv2_t], [mv2_t])
                        S.op("pool", lambda e: e.tensor_tensor(out=MV2[:, 3:4], in0=MV2[:, 2:3], in1=NEGH[:], op=ALU.pow),
                             [mv2_t, ngh_t], [mv2_t])
                        S.op("dve", lambda e: e.tensor_scalar(out=dst[:], in0=src[:], scalar1=MV2[:, 0:1], scalar2=MV2[:, 3:4],
                                                              op0=ALU.subtract, op1=ALU.mult), [mv2_t] + list(src_toks), [dst_tok])

                    chk("ML_SETUP")
                    tiles = list(range(2, NT)) if last else list(range(NT))
                    if ml_tiles is not None:
                        tiles = list(ml_tiles)
                    CATA2 = [CATA, sb(ml, "CATA1", [128, 8, 128], BF16)]
                    CATC2 = [CATC, sb(ml, "CATC1", [128, 4, 128], BF16)]
                    CATB2 = [CATB, sb(ml, "CATB1", [128, 2, 128], BF16)]
                    H22 = [H2, sb(ml, "H21", [128, D], BF16)]
                    cata2_t, catc2_t, catb2_t, h22_t = [cata_t, Tok()], [catc_t, Tok()], [catb_t, Tok()], [h2b_t, Tok()]
                    S.op("pool", lambda e: e.memset(CATA2[1][:], 0.0), [], [cata2_t[1]])
                    S.op("pool", lambda e: e.memset(CATC2[1][:], 0.0), [], [catc2_t[1]])
                    state = {"pi": 0, "pc": 0}

                    def emit_QZ(j, b):
                        for h in range(2):
                            S.op("pool", lambda e: e.tensor_copy(out=QZ[b][64 * h:64 * h + 64, h, :],
                                                                 in_=QA[64 * h:64 * h + 64, j].rearrange("p g t -> p (g t)")),
                                 [qa_t[j]], [qz_t[b]])
                            S.op("pool", lambda e: e.tensor_copy(out=QCZ[b][64 * h:64 * h + 64, h, :],
                                                                 in_=QC[64 * h:64 * h + 64, j].rearrange("p g t -> p (g t)")),
                                 [qc_t[j]], [qcz_t[b]])

                    def emit_BC(j, b):
                        CA, CC = CATA2[b], CATC2[b]
                        ca_t, cc_t = cata2_t[b], catc2_t[b]
                        chunks = [0, 1] if j < 2 else list(range(NT))
                        steps = [(c, h) for c in chunks for h in range(2)]
                        ns = len(steps)

                        def emit_S(n):
                            c, h = steps[n]
                            S.op("pe", lambda e: e.matmul(bank(n % 2), KA[:, c * 128:(c + 1) * 128], QZ[b][:, h, :],
                                                          start=True, stop=True), [ka_t[c], qz_t[b]], [pst[n % 2]])
                        emit_S(0)
                        emit_S(1)
                        for n in range(ns):
                            c, h = steps[n]
                            p_ = state["pi"]
                            state["pi"] = (p_ + 1) % 3
                            S.op("act", lambda e: e.activation(out=PT[p_][:], in_=bank(n % 2), func=AF.Exp, scale=0.125),
                                 [pst[n % 2]], [pt_t[p_]])
                            S.op("pe", lambda e: e.matmul(bank(2 + h, 0, 65), VA[:, c, h, :], PT[p_][:],
                                                          start=(n < 2), stop=(n >= ns - 2)),
                                 [va_t[c], pt_t[p_]], [pst[2 + h]])
                            if n + 2 < ns:
                                emit_S(n + 2)
                        chk("B_LOOP")
                        for h in range(2):
                            S.op("dve", lambda e: e.reciprocal(out=REC[0:1, h * 512:(h + 1) * 512], in_=bank(2 + h, 0, 1)),
                                 [pst[2 + h]], [rec_t])
                            S.op("pe", lambda e: e.matmul(bank(h, 0, 65), onesf[0:1, 0:65], REC[0:1, h * 512:(h + 1) * 512],
                                                          start=True, stop=True), [rec_t, k_tok], [pst[h]])
                            S.op("dve", lambda e: e.tensor_copy(out=BCS[0:65, h * 512:(h + 1) * 512], in_=bank(h, 0, 65)), [pst[h]], [bcs_t])
                            S.op("dve", lambda e: e.tensor_tensor(out=CA[0:65, 4 * h:4 * h + 4, :].rearrange("p g t -> p (g t)"),
                                                                  in0=bank(2 + h, 0, 65), in1=BCS[0:65, h * 512:(h + 1) * 512],
                                                                  op=ALU.mult), [pst[2 + h], bcs_t], [ca_t])
                        chk("B_NORM")
                        cks = [(0, None), (1, None)]
                        if j >= 2:
                            if j - 1 >= 2:
                                cks.append((j - 1, MKL))
                            cks.append((j, None))
                            if j + 1 < NT:
                                cks.append((j + 1, MKU))
                        for ci, (c, mk) in enumerate(cks):
                            for h in range(2):
                                S.op("pe", lambda e: e.matmul(bank(h, 0, 128, 0, 256), KC[:, c * 128:(c + 1) * 128], QCZ[b][:, h, :],
                                                              start=True, stop=True), [kc_t[c], qcz_t[b]], [pst[h]])
                            q_ = state["pc"]
                            state["pc"] = (q_ + 1) % 2
                            S.op("act", lambda e: e.activation(out=PTC[q_][:].rearrange("p (b c) -> p b c", b=2),
                                                               in_=ps[:, 0:1024].rearrange("p (b c) -> p b c", b=2)[:, :, 0:256],
                                                               func=AF.Exp, scale=0.125),
                                 [pst[0], pst[1]], [ptc_t[q_]])
                            if mk is not None:
                                S.op("pool", lambda e: e.tensor_tensor(
                                    out=PTC[q_][:].rearrange("p (a t) -> p a t", a=4),
                                    in0=PTC[q_][:].rearrange("p (a t) -> p a t", a=4),
                                    in1=mk[:].unsqueeze(1).to_broadcast([128, 4, 128]), op=ALU.mult),
                                    [ptc_t[q_], mk_t], [ptc_t[q_]])
                            for h in range(2):
                                S.op("pe", lambda e: e.matmul(bank(5, 0, 65, h * 256, (h + 1) * 256), VC[:, c, h, :],
                                                              PTC[q_][:, h * 256:(h + 1) * 256],
                                                              start=(ci == 0 and h == 0), stop=(ci == len(cks) - 1),
                                                              skip_group_check=True), [vc_t[c], ptc_t[q_]], [pst[5]])
                        S.op("dve", lambda e: e.tensor_tensor(out=RECC[0:1, :], in0=bank(5, 0, 1),
                                                              in1=SINKE[0:1].rearrange("p a t -> p (a t)"), op=ALU.add),
                             [pst[5], sk_t], [recc_t])
                        S.op("dve", lambda e: e.reciprocal(out=RECC[0:1, :], in_=RECC[0:1, :]), [recc_t], [recc_t])
                        S.op("pe", lambda e: e.matmul(bank(4, 0, 65), onesf[0:1, 0:65], RECC[0:1, :], start=True, stop=True),
                             [recc_t, k_tok], [pst[4]])
                        S.op("dve", lambda e: e.tensor_copy(out=BCC[0:65, :], in_=bank(4, 0, 65)), [pst[4]], [bcc_t])
                        S.op("dve", lambda e: e.tensor_tensor(out=CC[0:65].rearrange("p a t -> p (a t)"), in0=bank(5, 0, 65),
                                                              in1=BCC[0:65, :], op=ALU.mult), [pst[5], bcc_t], [cc_t])
                        chk("C")

                    def emit_E1(j, b):
                        CA, CC, CB, HH = CATA2[b], CATC2[b], CATB2[b], H22[b]
                        ca_t, cc_t, cb_t, hh_t = cata2_t[b], catc2_t[b], catb2_t[b], h22_t[b]
                        if j == tiles[0] or j == 2:
                            w = 1 if j < 2 else 0
                            modrow(G1, w, 2, mr_t)
                            modrow(SH2, w, 3, mr_t)
                            modrow(SC2, w, 4, mr_t)
                        rows, rt = src_rows(j)
                        S.dma("sp", lambda e: e.dma_start(out=XT2[b][:], in_=rows), rt, [xt2_t[b]])
                        for m in range(2):
                            S.op("pe", lambda e: e.matmul(bank(6, 0, 128, m * 128, (m + 1) * 128), OBs[:, j, m * 128:(m + 1) * 128],
                                                          ident[:], start=True, stop=True), [ob_t[j], k_tok], [pst[6]])
                        S.op("dve", lambda e: e.tensor_copy(out=CB[:].rearrange("p m t -> p (m t)"), in_=bank(6, 0, 128, 0, 256)),
                             [pst[6]], [cb_t])
                        for n in range(2):
                            mms = []
                            for hd in range(8):
                                mms.append((CA[:, hd, :], WOA[:, hd, n * 512:(n + 1) * 512], ca_t))
                            for m in range(2):
                                mms.append((CB[:, m, :], WOB[:, m, n * 512:(n + 1) * 512], cb_t))
                            for hd in range(4):
                                mms.append((CC[:, hd, :], WOC[:, hd, n * 512:(n + 1) * 512], cc_t))
                            for i, (l_, r_, t_) in enumerate(mms):
                                S.op("pe", lambda e: e.matmul(bank(6 + n), l_, r_, start=(i == 0), stop=(i == len(mms) - 1)),
                                     [t_, w_t], [pst[6 + n]])
                        chk("E_PROJ")
                        O = ps[:, 6 * 512:8 * 512]
                        S.op("dve", lambda e: e.tensor_tensor(out=TMP[:], in0=O, in1=G1[:], op=ALU.mult),
                             [pst[6], pst[7], mr_t], [tmp_t])
                        S.op("dve", lambda e: e.scalar_tensor_tensor(out=RR[:], in0=XT2[b][:], scalar=ALPHA, in1=TMP[:],
                                                                     op0=ALU.mult, op1=ALU.add), [xt2_t[b], tmp_t], [rr_t])
                        ln_norm(XN2, RR, [rr_t], xn2_t)
                        S.op("dve", lambda e: e.tensor_tensor(out=XN2[:], in0=XN2[:], in1=LN1G[:], op=ALU.mult), [xn2_t, ln_t], [xn2_t])
                        S.op("pool", lambda e: e.tensor_tensor(out=RR[:], in0=XN2[:], in1=LN1B[:], op=ALU.add), [xn2_t, ln_t], [rr_t])
                        if ("x1_%d" % L) in dbg_d and j in (0, 2):
                            S.dma("sp", lambda e: e.dma_start(out=dbg_d["x1_%d" % L][(0 if j == 0 else 128):(128 if j == 0 else 256), :],
                                                              in_=RR[:]), [rr_t], [dbg_tok])
                        S.op("dve", lambda e: e.tensor_scalar(out=ACC[:], in0=RR[:], scalar1=ALPHA, scalar2=None, op0=ALU.mult),
                             [rr_t], [accb_t])
                        S.dma("sp", lambda e: e.dma_start(out=acc_d[j * 128:(j + 1) * 128, :], in_=ACC[:]), [accb_t], [acc_tok[j]])
                        ln_norm(XN2, RR, [rr_t], xn2_t)
                        S.op("dve", lambda e: e.tensor_tensor(out=XN2[:], in0=XN2[:], in1=SC2[:], op=ALU.mult), [xn2_t, mr_t], [xn2_t])
                        S.op("pool", lambda e: e.tensor_tensor(out=HH[:], in0=XN2[:], in1=SH2[:], op=ALU.add), [xn2_t, mr_t], [hh_t])
                        S.dma("sp", lambda e: e.dma_start(out=h2_d[j * 128:(j + 1) * 128, :], in_=HH[:]), [hh_t], [h2_tok[j]])
                        chk("E_LN")

                    def emit_E2(j, b):
                        HH, hh_t = H22[b], h22_t[b]
                        for k in range(8):
                            S.op("pe", lambda e: e.matmul(bank(6 + k // 4, 0, 128, (k % 4) * 128, (k % 4) * 128 + 128),
                                                          HH[:, k * 128:(k + 1) * 128], ident[:], start=True, stop=True),
                                 [hh_t, k_tok], [pst[6 + k // 4]])
                        S.op("dve", lambda e: e.tensor_copy(out=HT2[:, 0:4, :].rearrange("p a b -> p (a b)"), in_=bank(6)), [pst[6]], [ht2_t])
                        S.op("dve", lambda e: e.tensor_copy(out=HT2[:, 4:8, :].rearrange("p a b -> p (a b)"), in_=bank(7)), [pst[7]], [ht2_t])
                        for k in range(8):
                            S.op("pe", lambda e: e.matmul(bank(6, 0, 128, 0, NE), HT2[:, k, :], WR[:, k, :], start=(k == 0), stop=(k == 7)),
                                 [ht2_t, w_t], [pst[6]])
                        S.op("dve", lambda e: e.reduce_max(out=SM[:, 0:1], in_=bank(6, 0, 128, 0, NE), axis=AX.X), [pst[6]], [sm_t])
                        S.op("dve", lambda e: e.tensor_scalar(out=SM[:, 1:2], in0=SM[:, 0:1], scalar1=-1.0, scalar2=None, op0=ALU.mult),
                             [sm_t], [sm_t])
                        S.op("act", lambda e: e.activation(out=LG[:], in_=bank(6, 0, 128, 0, NE), func=AF.Exp, bias=SM[:, 1:2], scale=1.0,
                                                           accum_out=SM[:, 2:3]), [pst[6], sm_t], [lg_t, sm_t])
                        S.op("dve", lambda e: e.reciprocal(out=SM[:, 3:4], in_=SM[:, 2:3]), [sm_t], [sm_t])
                        S.op("dve", lambda e: e.tensor_scalar(out=AFF[L][:, j, :], in0=LG[:], scalar1=SM[:, 3:4], scalar2=None, op0=ALU.mult),
                             [lg_t, sm_t], [aff_t])

                    nt_ = len(tiles)
                    emit_QZ(tiles[0], 0)
                    for idx in range(nt_ + 2):
                        if idx + 1 < nt_:
                            emit_QZ(tiles[idx + 1], (idx + 1) % 2)
                        if idx < nt_:
                            emit_BC(tiles[idx], idx % 2)
                        if idx >= 2:
                            emit_E2(tiles[idx - 2], (idx - 2) % 2)
                        if 1 <= idx <= nt_:
                            emit_E1(tiles[idx - 1], (idx - 1) % 2)
                    S.barrier()
                    for nm, src in (("cata", CATA[0:65].rearrange("p a t -> p (a t)")), ("catc", CATC[0:65].rearrange("p a t -> p (a t)"))):
                        if (nm + "%d" % L) in dbg_d:
                            n_ = src.shape[1]
                            S.op("dve", lambda e: e.tensor_copy(out=TMP[0:65, 0:n_], in_=src), [cata_t, catc_t], [tmp_t])
                            S.dma("sp", lambda e: e.dma_start(out=dbg_d[nm + "%d" % L], in_=TMP[0:65, 0:n_]), [tmp_t], [dbg_tok])
                            S.barrier()
            if stop_after == ("ML", L):
                break
            with ExitStack() as pf:
                TRI = sb(pf, "TRI", [128, 128], F32)
                IOB = sb(pf, "IOB", [128, 16, 128], F32)
                IOA = sb(pf, "IOA", [128, 16, 4], F32)
                PCOL = sb(pf, "PCOL", [128, 1], F32)
                TGT = sb(pf, "TGT", [128, 32], F32)
                kf_t = Tok()
                S.dma("sp", lambda e: e.dma_start(out=TRI[:], in_=tri_d), [], [kf_t])
                S.dma("sp", lambda e: e.dma_start(out=IOB[:].rearrange("p a b -> p (a b)"), in_=iotaB_d), [], [kf_t])
                S.dma("sp", lambda e: e.dma_start(out=IOA[:].rearrange("p a b -> p (a b)"), in_=iotaA_d), [], [kf_t])
                S.dma("sp", lambda e: e.dma_start(out=PCOL[:], in_=pcol_d), [], [kf_t])
                S.dma("sp", lambda e: e.dma_start(out=TGT[:], in_=tgt_d), [], [kf_t])
                THR = sb(pf, "THR", [128, 32], F32)
                LO = sb(pf, "LO", [128, 32], F32)
                CNTP = sb(pf, "CNTP", [128, 32], F32)
                IND = sb(pf, "IND", [128, 32], F32)
                MSK = sb(pf, "MSK", [128, NT, NE], F32)
                thr_t, lo_t, cntp_t, ind_t, msk_t = (Tok() for _ in range(5))
                S.op("dve", lambda e: e.memset(LO[:], 0.0), [], [lo_t])
                S.op("dve", lambda e: e.memset(CNTP[:], 0.0), [], [cntp_t])
                S.op("dve", lambda e: e.memset(MSK[:], 0.0), [], [msk_t])
                A_ = AFF[L]
                do_ctx = not last

                def make_mask(thr):
                    S.op("dve", lambda e: e.tensor_tensor(out=MSK[:, 2:NT, :], in0=A_[:, 2:NT, :],
                                                          in1=thr[:, 0:16].unsqueeze(1).to_broadcast([128, 32, 16]), op=ALU.is_ge),
                         [aff_t, thr_t, lo_t], [msk_t])
                    if do_ctx:
                        S.op("dve", lambda e: e.tensor_tensor(out=MSK[:, 0:2, :], in0=A_[:, 0:2, :],
                                                              in1=thr[:, 16:32].unsqueeze(1).to_broadcast([128, 2, 16]), op=ALU.is_ge),
                             [aff_t, thr_t, lo_t], [msk_t])

                for it in range(28):
                    wv = 2.0 ** -(it + 1)
                    S.op("dve", lambda e: e.tensor_scalar(out=THR[:], in0=LO[:], scalar1=wv, scalar2=None, op0=ALU.add), [lo_t], [thr_t])
                    make_mask(THR)
                    S.op("dve", lambda e: e.tensor_reduce(out=CNTP[:, 0:16], in_=MSK[:, 2:NT, :].rearrange("p j e -> p e j"),
                                                          axis=AX.X, op=ALU.add), [msk_t], [cntp_t])
                    if do_ctx:
                        S.op("dve", lambda e: e.tensor_reduce(out=CNTP[:, 16:32], in_=MSK[:, 0:2, :].rearrange("p j e -> p e j"),
                                                              axis=AX.X, op=ALU.add), [msk_t], [cntp_t])
                    S.op("pe", lambda e: e.matmul(bank(0, 0, 128, 0, 32), onesf[:], CNTP[:], start=True, stop=True),
                         [cntp_t, k_tok], [pst[0]])
                    S.op("dve", lambda e: e.tensor_tensor(out=IND[:], in0=bank(0, 0, 128, 0, 32), in1=TGT[:], op=ALU.is_ge),
                         [pst[0], kf_t], [ind_t])
                    S.op("dve", lambda e: e.scalar_tensor_tensor(out=LO[:], in0=IND[:], scalar=wv, in1=LO[:], op0=ALU.mult, op1=ALU.add),
                         [ind_t, lo_t], [lo_t])
                make_mask(LO)
                POS = sb(pf, "POS", [128, NT, NE], F32)
                OFF = sb(pf, "OFF", [128, NT, NE], F32)
                POSI = sb(pf, "POSI", [128, NT, NE], I32)
                BI = sb(pf, "BI", [128, NT, NE], I32)
                AI = sb(pf, "AI", [128, NT, NE], I32)
                BFl = sb(pf, "BFl", [128, NT, NE], F32)
                AFl = sb(pf, "AFl", [128, NT, NE], F32)
                pos_t, off_t = Tok(), Tok()
                groups = [(2, NT, 1, 2)] + ([(0, 2, 3, 3)] if do_ctx else [])
                for (j0, j1, bpw, btot) in groups:
                    n_ = (j1 - j0) * NE
                    c0 = 0 if j0 == 2 else 0
                    c1 = 0 if j0 == 2 else 64
                    mview = MSK[:, j0:j1, :].rearrange("p j e -> p (j e)")
                    S.op("pe", lambda e: e.matmul(bank(bpw, 0, 128, c0, c0 + n_), TRI[:], mview, start=True, stop=True),
                         [msk_t, kf_t], [pst[bpw]])
                    S.op("pe", lambda e: e.matmul(bank(btot, 0, 128, c1, c1 + n_), onesf[:], mview, start=True, stop=True),
                         [msk_t, k_tok], [pst[btot]])
                    S.op("dve", lambda e: e.memset(OFF[:, j0, :], 0.0), [], [off_t])
                    for jj in range(j0 + 1, j1):
                        S.op("dve", lambda e: e.tensor_tensor(out=OFF[:, jj, :], in0=OFF[:, jj - 1, :],
                                                              in1=bank(btot, 0, 128, c1 + (jj - 1 - j0) * NE, c1 + (jj - j0) * NE),
                                                              op=ALU.add), [off_t, pst[btot]], [off_t])
                    pv = POS[:, j0:j1, :].rearrange("p j e -> p (j e)")
                    S.op("dve", lambda e: e.tensor_tensor(out=pv, in0=bank(bpw, 0, 128, c0, c0 + n_),
                                                          in1=OFF[:, j0:j1, :].rearrange("p j e -> p (j e)"), op=ALU.add),
                         [pst[bpw], off_t], [pos_t])
                    S.op("dve", lambda e: e.scalar_tensor_tensor(out=pv, in0=pv, scalar=1.0, in1=mview, op0=ALU.add, op1=ALU.mult),
                         [pos_t, msk_t], [pos_t])
                    S.op("dve", lambda e: e.tensor_scalar(out=pv, in0=pv, scalar1=-1.0, scalar2=None, op0=ALU.add), [pos_t], [pos_t])
                    for (o_, i_, fn) in ((POSI, POS, None), (BI, POSI, ("and", 127)), (AI, POSI, ("shr", 7)), (BFl, BI, None), (AFl, AI, None)):
                        ov = o_[:, j0:j1, :].rearrange("p j e -> p (j e)")
                        iv = i_[:, j0:j1, :].rearrange("p j e -> p (j e)")
                        if fn is None:
                            S.op("dve", lambda e: e.tensor_copy(out=ov, in_=iv), [pos_t], [pos_t])
                        else:
                            opx = ALU.bitwise_and if fn[0] == "and" else ALU.arith_shift_right
                            S.op("dve", lambda e: e.tensor_scalar(out=ov, in0=iv, scalar1=fn[1], scalar2=None, op0=opx), [pos_t], [pos_t])
                if ("pos%d" % L) in dbg_d:
                    S.dma("sp", lambda e: e.dma_start(out=dbg_d["pos%d" % L], in_=POS[:].rearrange("p j e -> p (j e)")), [pos_t], [dbg_tok])
                    S.dma("sp", lambda e: e.dma_start(out=dbg_d["aff%d" % L], in_=A_[:].rearrange("p j e -> p (j e)")), [aff_t], [dbg_tok])
                OHB = [sb(pf, "OHB%d" % i, [128, 16, 128], F32) for i in range(2)]
                ohb_t = toks(2)
                RA = sb(pf, "RA", [128, 16, 4], F32)
                R3 = [sb(pf, "R3%d" % i, [128, 16, 3, 4], F32) for i in range(2)]
                ra_t = Tok()
                r3_t = toks(2)
                for jj in range(32):
                    b = jj % 2
                    j = 2 + jj
                    S.op("dve", lambda e: e.tensor_tensor(out=OHB[b][:], in0=IOB[:], in1=BFl[:, j, :].unsqueeze(2).to_broadcast([128, 16, 128]),
                                                          op=ALU.is_equal), [kf_t, pos_t], [ohb_t[b]])
                    S.op("dve", lambda e: e.tensor_tensor(out=RA[:], in0=IOA[:], in1=AFl[:, j, :].unsqueeze(2).to_broadcast([128, 16, 4]),
                                                          op=ALU.is_equal), [kf_t, pos_t], [ra_t])
                    S.op("dve", lambda e: e.tensor_scalar(out=R3[b][:, :, 0, :], in0=RA[:], scalar1=PCOL[:, 0:1], scalar2=None, op0=ALU.mult),
                         [ra_t, kf_t], [r3_t[b]])
                    S.op("dve", lambda e: e.tensor_scalar(out=R3[b][:, :, 1, :], in0=RA[:], scalar1=float(j), scalar2=None, op0=ALU.mult),
                         [ra_t], [r3_t[b]])
                    S.op("dve", lambda e: e.tensor_tensor(out=R3[b][:, :, 2, :], in0=RA[:],
                                                          in1=A_[:, j, :].unsqueeze(2).to_broadcast([128, 16, 4]), op=ALU.mult),
                         [ra_t, aff_t], [r3_t[b]])
                    for ex in range(NE):
                        S.op("pe", lambda e: e.matmul(bank(4, 0, 128, ex * 12, ex * 12 + 12), OHB[b][:, ex, :],
                                                      R3[b][:, ex].rearrange("p c a -> p (c a)"),
                                                      start=(jj == 0 and ex == 0), stop=(jj == 31), skip_group_check=True),
                             [ohb_t[b], r3_t[b]], [pst[4]])
                CMPS = sb(pf, "CMPS", [128, 192], F32)
                cmps_t = Tok()
                S.op("dve", lambda e: e.tensor_copy(out=CMPS[:], in_=bank(4, 0, 128, 0, 192)), [pst[4]], [cmps_t])
                CMP = CMPS[:].rearrange("p (e c a) -> p e c a", e=16, c=3)
                IDXF = sb(pf, "IDXF", [128, 16, 4], F32)
                idx_t = idx_tok[L]
                S.op("dve", lambda e: e.scalar_tensor_tensor(out=IDXF[:], in0=CMP[:, :, 1, :], scalar=128.0, in1=CMP[:, :, 0, :],
                                                             op0=ALU.mult, op1=ALU.add), [cmps_t], [idx_t])
                S.op("dve", lambda e: e.tensor_copy(out=IDX[L][:], in_=IDXF[:]), [idx_t], [idx_t])
                S.op("dve", lambda e: e.tensor_copy(out=GATE[L][:], in_=CMP[:, :, 2, :]), [cmps_t], [idx_t])
                if do_ctx:
                    OHC = sb(pf, "OHC", [128, 16, 32], F32)
                    R3C = sb(pf, "R3C", [128, 16, 3], F32)
                    ohc_t, r3c_t = Tok(), Tok()
                    for j in range(2):
                        S.op("dve", lambda e: e.tensor_tensor(out=OHC[:], in0=IOB[:, :, 0:32],
                                                              in1=BFl[:, j, :].unsqueeze(2).to_broadcast([128, 16, 32]), op=ALU.is_equal),
                             [kf_t, pos_t], [ohc_t])
                        S.op("dve", lambda e: e.tensor_copy(out=R3C[:, :, 0], in_=PCOL[:, 0:1].to_broadcast([128, 16])), [kf_t], [r3c_t])
                        S.op("dve", lambda e: e.memset(R3C[:, :, 1], float(j)), [], [r3c_t])
                        S.op("dve", lambda e: e.tensor_copy(out=R3C[:, :, 2], in_=A_[:, j, :]), [aff_t], [r3c_t])
                        for ex in range(NE):
                            S.op("pe", lambda e: e.matmul(bank(5, 0, 32, ex * 3, ex * 3 + 3), OHC[:, ex, :], R3C[:, ex, :],
                                                          start=(j == 0 and ex == 0), stop=(j == 1), skip_group_check=True),
                                 [ohc_t, r3c_t], [pst[5]])
                    S.op("dve", lambda e: e.tensor_copy(out=CMPS[0:32, 0:48], in_=bank(5, 0, 32, 0, 48)), [pst[5], cmps_t, idx_t], [cmps_t])
                    CMC = CMPS[0:32, 0:48].rearrange("p (e c) -> p e c", c=3)
                    S.op("dve", lambda e: e.scalar_tensor_tensor(out=IDXF[0:32, :, 0], in0=CMC[:, :, 1], scalar=128.0, in1=CMC[:, :, 0],
                                                                 op0=ALU.mult, op1=ALU.add), [cmps_t, idx_t], [idx_t])
                    S.op("dve", lambda e: e.tensor_copy(out=IDXC[L][0:32, :], in_=IDXF[0:32, :, 0]), [idx_t], [idx_t])
                    S.op("dve", lambda e: e.tensor_copy(out=GATEC[L][0:32, :], in_=CMC[:, :, 2]), [cmps_t], [idx_t])
                if ("idx%d" % L) in dbg_d:
                    S.dma("sp", lambda e: e.dma_start(out=dbg_d["idx%d" % L], in_=IDXF[:].rearrange("p e a -> p (e a)")), [idx_t], [dbg_tok])
                    S.dma("sp", lambda e: e.dma_start(out=dbg_d["gate%d" % L], in_=GATE[L][:].rearrange("p e a -> p (e a)")), [idx_t], [dbg_tok])
                S.barrier()
            if stop_after == ("F", L):
                break
            with ExitStack() as pg:
                idx_t = idx_tok[L]
                do_ctx = not last
                NS = 544 if do_ctx else 512
                G2 = sb(pg, "G2", [128, D], F32)
                G2C = sb(pg, "G2C", [128, D], F32)
                g2_t = Tok()
                modrow(G2, 0, 5, g2_t)
                if do_ctx:
                    modrow(G2C, 1, 5, g2_t)
                XG = [[sb(pg, "XG%d_%d" % (i, a), [128, D], BF16) for a in range(5)] for i in range(2)]
                xg_t = [toks(5) for _ in range(2)]
                XGT = [sb(pg, "XGT%d" % i, [128, 8, NS], BF16) for i in range(2)]
                xgt_t = toks(2)
                HID = sb(pg, "HID", [128, 16, NS], BF16)
                hid_t = Tok()
                WG = [sb(pg, "WG%d" % i, [128, 8, 512], BF16) for i in range(4)]
                WU = [sb(pg, "WU%d" % i, [128, 8, 512], BF16) for i in range(4)]
                wg_t, wu_t = toks(4), toks(4)
                WD = [sb(pg, "WD%d" % i, [128, 16, D], BF16) for i in range(2)]
                wd_t = toks(2)
                SIL = [sb(pg, "SIL%d" % i, [128, 512], F32) for i in range(2)]
                sil_t = toks(2)
                SILC = sb(pg, "SILC", [128, 32], F32)
                silc_t = Tok()
                YS = [sb(pg, "YS%d" % i, [128, D], F32) for i in range(2)]
                ys_t = toks(2)
                ysi = 0
                fcc = 0

                def load_gather(ex):
                    xb = ex % 2
                    for a in range(4):
                        S.dma("pool", lambda e: e.indirect_dma_start(
                            out=XG[xb][a][:], out_offset=None, in_=h2_d[:, :],
                            in_offset=bass.IndirectOffsetOnAxis(ap=IDX[L][:, ex, a:a + 1], axis=0)), h2_tok + [idx_t], [xg_t[xb][a]])
                    if do_ctx:
                        S.dma("pool", lambda e: e.indirect_dma_start(
                            out=XG[xb][4][0:32, :], out_offset=None, in_=h2_d[:, :],
                            in_offset=bass.IndirectOffsetOnAxis(ap=IDXC[L][0:32, ex:ex + 1], axis=0)), h2_tok + [idx_t], [xg_t[xb][4]])

                def load_wd(ex):
                    db = ex % 2
                    wdsrc = wd_d[L, ex].rearrange("(f p) d -> p f d", p=128)
                    for q in range(4):
                        S.dma("pool", lambda e: e.dma_start(out=WD[db][:, q * 4:(q + 1) * 4, :], in_=wdsrc[:, q * 4:(q + 1) * 4, :]),
                              [], [wd_t[db]])

                def load_gu(ex, fq):
                    wgsrc = wg_d[L, ex].rearrange("(k p) f -> p k f", p=128)
                    wusrc = wu_d[L, ex].rearrange("(k p) f -> p k f", p=128)
                    S.dma("pool", lambda e: e.dma_start(out=WG[fq][:], in_=wgsrc[:, :, fq * 512:(fq + 1) * 512]), [], [wg_t[fq]])
                    S.dma("pool", lambda e: e.dma_start(out=WU[fq][:], in_=wusrc[:, :, fq * 512:(fq + 1) * 512]), [], [wu_t[fq]])

                load_gather(0)
                load_wd(0)
                for fq in range(4):
                    load_gu(0, fq)
                load_gather(1)
                load_wd(1)
                for ex in range(NE):
                    xb = ex % 2
                    db = ex % 2
                    for a in range(4):
                        for kh in range(2):
                            for kk in range(4):
                                k = kh * 4 + kk
                                S.op("pe", lambda e: e.matmul(bank(6, 0, 128, kk * 128, (kk + 1) * 128),
                                                              XG[xb][a][:, k * 128:(k + 1) * 128], ident[:], start=True, stop=True),
                                     [xg_t[xb][a], k_tok], [pst[6]])
                            eng = "act" if (a + kh) % 2 == 0 else "dve"
                            src = bank(6).rearrange("p (k t) -> p k t", k=4)
                            dst = XGT[xb][:, kh * 4:(kh + 1) * 4, a * 128:(a + 1) * 128]
                            if eng == "act":
                                S.op("act", lambda e: e.copy(out=dst, in_=src), [pst[6]], [xgt_t[xb]])
                            else:
                                S.op("dve", lambda e: e.tensor_copy(out=dst, in_=src), [pst[6]], [xgt_t[xb]])
                    if do_ctx:
                        for k in range(8):
                            S.op("pe", lambda e: e.matmul(bank(7, 0, 128, k * 32, (k + 1) * 32), XG[xb][4][0:32, k * 128:(k + 1) * 128],
                                                          ident[0:32, 0:32], start=True, stop=True), [xg_t[xb][4], k_tok], [pst[7]])
                        S.op("dve", lambda e: e.tensor_copy(out=XGT[xb][:, :, 512:544],
                                                            in_=bank(7, 0, 128, 0, 256).rearrange("p (k t) -> p k t", k=8)),
                             [pst[7]], [xgt_t[xb]])
                    if ex + 2 < NE:
                        load_gather(ex + 2)
                    for fq in range(4):
                        wb = fq
                        for fi in range(4):
                            fc = fq * 4 + fi
                            pa = fcc % 2
                            fcc += 1
                            for (Wt, wt_t, bk) in ((WG[wb], wg_t[wb], pa), (WU[wb], wu_t[wb], 2 + pa)):
                                for k in range(8):
                                    S.op("pe", lambda e: e.matmul(bank(bk), Wt[:, k, fi * 128:(fi + 1) * 128], XGT[xb][:, k, 0:512],
                                                                  start=(k == 0), stop=(k == 7)), [wt_t, xgt_t[xb]], [pst[bk]])
                            if do_ctx:
                                for (Wt, wt_t, c0) in ((WG[wb], wg_t[wb], 256), (WU[wb], wu_t[wb], 288)):
                                    for k in range(8):
                                        S.op("pe", lambda e: e.matmul(bank(7, 0, 128, c0, c0 + 32), Wt[:, k, fi * 128:(fi + 1) * 128],
                                                                      XGT[xb][:, k, 512:544], start=(k == 0), stop=(k == 7)),
                                             [wt_t, xgt_t[xb]], [pst[7]])
                            S.op("act", lambda e: e.activation(out=SIL[pa][:], in_=bank(pa), func=AF.Silu), [pst[pa]], [sil_t[pa]])
                            S.op("dve", lambda e: e.tensor_tensor(out=HID[:, fc, 0:512], in0=bank(2 + pa), in1=SIL[pa][:], op=ALU.mult),
                                 [pst[2 + pa], sil_t[pa]], [hid_t])
                            if do_ctx:
                                S.op("act", lambda e: e.activation(out=SILC[:], in_=bank(7, 0, 128, 256, 288), func=AF.Silu),
                                     [pst[7]], [silc_t])
                                S.op("dve", lambda e: e.tensor_tensor(out=HID[:, fc, 512:544], in0=bank(7, 0, 128, 288, 320), in1=SILC[:],
                                                                      op=ALU.mult), [pst[7], silc_t], [hid_t])
                        if ex + 1 < NE:
                            load_gu(ex + 1, fq)
                    slots = [(s * 128, 128, IDX[L][:, ex, s:s + 1], GATE[L][:, ex, s:s + 1], G2) for s in range(4)]
                    if do_ctx:
                        slots.append((512, 32, IDXC[L][0:32, ex:ex + 1], GATEC[L][0:32, ex:ex + 1], G2C))
                    for (s0, sn, idx_ap, gate_ap, g2row) in slots:
                        for n in range(2):
                            for fc in range(16):
                                S.op("pe", lambda e: e.matmul(bank(4 + n, 0, sn), HID[:, fc, s0:s0 + sn], WD[db][:, fc, n * 512:(n + 1) * 512],
                                                              start=(fc == 0), stop=(fc == 15)), [hid_t, wd_t[db]], [pst[4 + n]])
                        yb = ysi % 2
                        ysi += 1
                        S.op("dve", lambda e: e.scalar_tensor_tensor(out=YS[yb][0:sn, :], in0=ps[0:sn, 4 * 512:6 * 512], scalar=gate_ap,
                                                                     in1=g2row[0:sn, :], op0=ALU.mult, op1=ALU.mult),
                             [pst[4], pst[5], idx_t, g2_t], [ys_t[yb]])
                        S.dma("pool", lambda e: e.indirect_dma_start(
                            out=acc_d[:, :], out_offset=bass.IndirectOffsetOnAxis(ap=idx_ap, axis=0), in_=YS[yb][0:sn, :], in_offset=None,
                            compute_op=ALU.add),
                            [ys_t[yb], idx_t] + acc_tok, [accs_tok])
                    if ex + 2 < NE:
                        load_wd(ex + 2)
                S.barrier()
            with ExitStack() as phh:
                LN2G = sb(phh, "LN2G", [128, D], F32)
                LN2B = sb(phh, "LN2B", [128, D], F32)
                l2_t = Tok()
                rowb(LN2G, ln2g_d[L], l2_t)
                rowb(LN2B, ln2b_d[L], l2_t)
                AT = [sb(phh, "AT%d" % i, [128, D], F32) for i in range(2)]
                at_t = toks(2)
                XO = [sb(phh, "XO%d" % i, [128, D], F32) for i in range(2)]
                xo_t = toks(2)
                ST3 = sb(phh, "ST3", [128, 2, 6], F32)
                MV3 = sb(phh, "MV3", [128, 8], F32)
                st3_t, mv3_t = Tok(), Tok()
                for j in (range(2, NT) if last else range(NT)):
                    b = j % 2
                    S.dma("sp", lambda e: e.dma_start(out=AT[b][:], in_=acc_d[j * 128:(j + 1) * 128, :]), [acc_tok[j], accs_tok], [at_t[b]])
                    for hh in range(2):
                        S.op("dve", lambda e: e.bn_stats(out=ST3[:, hh, :], in_=AT[b][:, hh * 512:(hh + 1) * 512]), [at_t[b]], [st3_t])
                    S.op("dve", lambda e: e.bn_aggr(out=MV3[:, 0:2], in_=ST3[:].rearrange("p a b -> p (a b)")), [st3_t], [mv3_t])
                    S.op("act", lambda e: e.activation(out=MV3[:, 2:3], in_=MV3[:, 1:2], func=AF.Sqrt, bias=LN_EPS, scale=1.0), [mv3_t], [mv3_t])
                    S.op("dve", lambda e: e.reciprocal(out=MV3[:, 3:4], in_=MV3[:, 2:3]), [mv3_t], [mv3_t])
                    S.op("dve", lambda e: e.tensor_scalar(out=MV3[:, 4:5], in0=MV3[:, 0:1], scalar1=MV3[:, 3:4], scalar2=-1.0,
                                                          op0=ALU.mult, op1=ALU.mult), [mv3_t], [mv3_t])
                    S.op("act", lambda e: e.activation(out=XO[b][:], in_=AT[b][:], func=AF.Identity, bias=MV3[:, 4:5], scale=MV3[:, 3:4]),
                         [mv3_t, at_t[b]], [xo_t[b]])
                    S.op("dve", lambda e: e.tensor_tensor(out=XO[b][:], in0=XO[b][:], in1=LN2G[:], op=ALU.mult), [xo_t[b], l2_t], [xo_t[b]])
                    S.op("pool", lambda e: e.tensor_tensor(out=XO[b][:], in0=XO[b][:], in1=LN2B[:], op=ALU.add), [xo_t[b], l2_t], [xo_t[b]])
                    if last:
                        S.dma("sp", lambda e: e.dma_start(out=out_d[(j - 2) * 128:(j - 1) * 128, :], in_=XO[b][:]), [xo_t[b]], [out_tok[j]])
                    else:
                        S.dma("sp", lambda e: e.dma_start(out=xcur_d[j * 128:(j + 1) * 128, :], in_=XO[b][:]), [xo_t[b]], [xcur_tok[j]])
                        if ("x2_%d" % L) in dbg_d:
                            S.dma("sp", lambda e: e.dma_start(out=dbg_d["x2_%d" % L][j * 128:(j + 1) * 128, :], in_=XO[b][:]), [xo_t[b]], [dbg_tok])
                S.barrier()
            if stop_after == ("H", L):
                break
        except _Stop:
            pass
        S.enabled = True
        S.barrier()
        print("instructions:", S.nins, "waits:", S.nwait)
    return nc


_CONST = None


def _consts():
    global _CONST
    if _CONST is not None:
        return _CONST
    bf = ml_dtypes.bfloat16
    c = {}
    c["k_ident"] = np.eye(128, dtype=np.float32).astype(bf)
    c["k_ones"] = np.ones((128, 128), np.float32)
    pp = np.arange(128)
    c["k_tri"] = (pp[:, None] < pp[None, :]).astype(np.float32)
    c["k_maskL"] = (pp[:, None] >= pp[None, :]).astype(np.float32).astype(bf)
    c["k_maskU"] = (pp[:, None] <= pp[None, :]).astype(np.float32).astype(bf)
    c["k_iotaB"] = np.ascontiguousarray(np.broadcast_to(np.arange(128, dtype=np.float32)[None, None, :], (128, 16, 128))).reshape(128, 2048)
    c["k_iotaA"] = np.ascontiguousarray(np.broadcast_to(np.arange(4, dtype=np.float32)[None, None, :], (128, 16, 4))).reshape(128, 64)
    c["k_pcol"] = np.arange(128, dtype=np.float32).reshape(128, 1)
    rows = 4096 // 64
    row = np.repeat(np.arange(rows), 64).astype(np.float32)
    col = np.tile(np.arange(64), rows).astype(np.float32)
    inv_freq = (10000.0 ** (-np.arange(0, 32, 2, dtype=np.float32) / np.float32(32))).astype(np.float32)
    ang = np.stack([row[:, None] * inv_freq, col[:, None] * inv_freq], axis=1).astype(np.float32)
    cos = np.ones((NT * 128, 32), np.float32)
    sin = np.zeros((NT * 128, 32), np.float32)
    cos[256:] = np.cos(ang).astype(np.float32).reshape(4096, 32)
    sin[256:] = np.sin(ang).astype(np.float32).reshape(4096, 32)
    c["k_cos"], c["k_sin"] = cos, sin
    t = np.arange(4096, dtype=np.int64)
    ph = (t[:, None] * t[None, :]) % 4096
    a = ph.astype(np.float64) * (2 * np.pi / 4096)
    c["k_ct"] = (np.cos(a) / 64.0).astype(np.float32).astype(bf)
    c["k_sn"] = (-np.sin(a) / 64.0).astype(np.float32).astype(bf)
    t2 = np.arange(256, dtype=np.int64)
    a2 = ((t2[:, None] * t2[None, :]) % 256).astype(np.float64) * (2 * np.pi / 256)
    c["k_ctc"] = np.concatenate([np.cos(a2) / 16.0, -np.sin(a2) / 16.0], axis=1).astype(np.float32).astype(bf)
    t3 = np.arange(64, dtype=np.int64)
    a3 = ((t3[:, None] * t3[None, :]) % 64).astype(np.float64) * (2 * np.pi / 64)
    c["k_cc"] = np.concatenate([np.cos(a3) / 8.0, np.sin(a3) / 8.0], axis=1).astype(np.float32)
    tg = np.zeros((128, 32), np.float32)
    tg[:, :16] = 512.0
    tg[:, 16:] = 32.0
    c["k_tgt"] = tg
    _CONST = c
    return c


def make_in_map(inputs, b):
    m = dict(_consts())
    m["x"] = np.ascontiguousarray(inputs["x"][b])
    m["ctx"] = np.ascontiguousarray(inputs["ctx"][b])
    cT = np.zeros((128, 8, 2), np.float32)
    cT[:, :, 0] = np.asarray(inputs["c"][b]).reshape(8, 128).T
    cT[:, :, 1] = np.asarray(inputs["c_ctx"]).reshape(8, 128).T
    m["cT"] = cT.reshape(128, 16)
    for k in ("w_mod", "b_mod", "w_in", "q_norm_a", "k_norm_a", "w_fourier", "sink_c", "w_out", "ln1_g", "ln1_b",
              "w_router", "w_gate", "w_up", "w_down", "ln2_g", "ln2_b"):
        m[k] = np.ascontiguousarray(inputs[k])
    m["b_fourier"] = np.ascontiguousarray(np.asarray(inputs["b_fourier"]).reshape(2, 256))
    m["w_in_uT"] = np.ascontiguousarray(np.transpose(np.asarray(inputs["w_in"])[:, :, 768:1024], (0, 2, 1)))
    return m


_NC = None


def _get_nc():
    global _NC
    if _NC is None:
        nc = bass.Bass("TRN2", target_bir_lowering=False)
        build(nc, n_layers=2)
        _NC = nc
    return _NC


def kernel(**inputs):
    inputs = {k: np.asarray(v) for k, v in inputs.items()}
    nc = _get_nc()
    n = 8
    in_maps = [make_in_map(inputs, b) for b in range(n)]
    res = run_bass_kernel_spmd(nc, in_maps, core_ids=list(range(n)))
    out = np.stack([np.asarray(res.results[b]["out"], dtype=np.float32) for b in range(n)], axis=0)
    return out
```

```python
import numpy as np
import ml_dtypes
from contextlib import ExitStack
import concourse.bass as bass
import concourse.mybir as mybir
from concourse.bass_utils import run_bass_kernel_spmd

F32 = mybir.dt.float32
BF16 = mybir.dt.bfloat16
I32 = mybir.dt.int32
ALU = mybir.AluOpType
AF = mybir.ActivationFunctionType
AX = mybir.AxisListType

D = 1024
NT = 34
NE = 16
FF = 2048
ALPHA = 4.0 ** 0.25
LN_EPS = 1e-5
RMS_EPS = 1e-6
C_QA, C_KA, C_QC, C_KC, C_UC, C_US, C_VA, C_VC = 0, 512, 640, 896, 1024, 1280, 1536, 1664
NCOL = 1792


class _Stop(Exception):
    pass


class Tok:
    __slots__ = ("w", "r")

    def __init__(self):
        self.w = None
        self.r = {}


def toks(n):
    return [Tok() for _ in range(n)]


class Sched:
    def __init__(self, nc, es, n_dma=44):
        self.nc = nc
        self.E = {"pe": nc.tensor, "act": nc.scalar, "dve": nc.vector, "pool": nc.gpsimd, "sp": nc.sync}
        self.sem = {k: es.enter_context(nc.semaphore("c_" + k)) for k in ("pe", "act", "dve", "pool")}
        self.cnt = {k: 0 for k in self.sem}
        self.dsem = [es.enter_context(nc.semaphore("d%d" % i)) for i in range(n_dma)]
        self.dcnt = [0] * n_dma
        self.dnext = 0
        self.dnext_sw = 0
        self.known = {k: {} for k in self.E}
        self.nwait = 0
        self.nins = 0
        self.enabled = True

    def _semobj(self, key):
        return self.sem[key] if isinstance(key, str) else self.dsem[key]

    def _deps(self, e, reads, writes):
        need = {}
        for t in reads:
            if t.w is not None:
                k, v = t.w
                if not (k == e and e == "pe"):
                    if need.get(k, 0) < v:
                        need[k] = v
        for t in writes:
            if t.w is not None:
                k, v = t.w
                if k != e and need.get(k, 0) < v:
                    need[k] = v
            for k, v in t.r.items():
                if k != e and need.get(k, 0) < v:
                    need[k] = v
        return need

    def _wait(self, e, need):
        kn = self.known[e]
        for k, v in need.items():
            if kn.get(k, 0) >= v:
                continue
            self.E[e].wait_ge(self._semobj(k), v)
            kn[k] = v
            self.nwait += 1

    def _mark(self, me, reads, writes):
        k, v = me
        for t in reads:
            if t.r.get(k, 0) < v:
                t.r[k] = v
        for t in writes:
            t.w = me
            t.r = {}

    def op(self, e, fn, reads=(), writes=()):
        if not self.enabled:
            return
        self._wait(e, self._deps(e, reads, writes))
        ins = fn(self.E[e])
        self.cnt[e] += 1
        ins.then_inc(self.sem[e], 1)
        self.nins += 1
        self._mark((e, self.cnt[e]), reads, writes)

    def dma(self, e, fn, reads=(), writes=()):
        if not self.enabled:
            return
        half = len(self.dsem) // 2
        if e == "pool":
            k = half + self.dnext_sw
            self.dnext_sw = (self.dnext_sw + 1) % (len(self.dsem) - half)
        else:
            k = self.dnext
            self.dnext = (self.dnext + 1) % half
        need = self._deps(e, reads, writes)
        if self.dcnt[k] > 0:
            need[k] = max(need.get(k, 0), 16 * self.dcnt[k])
        self._wait(e, need)
        ins = fn(self.E[e])
        self.dcnt[k] += 1
        ins.then_inc(self.dsem[k], 16)
        self.nins += 1
        self._mark((k, 16 * self.dcnt[k]), reads, writes)

    def barrier(self):
        if not self.enabled:
            return
        need = {k: v for k, v in self.cnt.items() if v > 0}
        for k in range(len(self.dsem)):
            if self.dcnt[k] > 0:
                need[k] = 16 * self.dcnt[k]
        for e in self.E:
            self._wait(e, {k: v for k, v in need.items() if k != e})


def build(nc, n_layers=2, dbg=None, stop_after=None, ml_tiles=None, stop_at=None, bc_bf16=False):
    dbg = dbg or {}
    es_top = ExitStack()
    with es_top as es:
        S = Sched(nc, es)
        E = S.E

        def chk(tag):
            if stop_at == tag:
                S.enabled = False

        def dram(name, shape, dt, kind="ExternalInput"):
            return nc.dram_tensor(name, list(shape), dt, kind=kind).ap()

        x_d = dram("x", [4096, D], F32)
        ctx_d = dram("ctx", [256, D], F32)
        cT_d = dram("cT", [128, 16], F32)
        w_mod_d = dram("w_mod", [2, D, 6 * D], F32)
        b_mod_d = dram("b_mod", [2, 6 * D], F32)
        w_in_d = dram("w_in", [2, D, 1536], F32)
        w_in_uT_d = dram("w_in_uT", [2, 256, D], F32)
        qn_d = dram("q_norm_a", [2, 64], F32)
        kn_d = dram("k_norm_a", [2, 64], F32)
        wf_d = dram("w_fourier", [2, 4, 64, 64], F32)
        bf_d = dram("b_fourier", [2, 256], F32)
        sink_d = dram("sink_c", [2, 4], F32)
        w_out_d = dram("w_out", [2, D, D], F32)
        ln1g_d = dram("ln1_g", [2, D], F32)
        ln1b_d = dram("ln1_b", [2, D], F32)
        wr_d = dram("w_router", [2, D, NE], F32)
        wg_d = dram("w_gate", [2, NE, D, FF], F32)
        wu_d = dram("w_up", [2, NE, D, FF], F32)
        wd_d = dram("w_down", [2, NE, FF, D], F32)
        ln2g_d = dram("ln2_g", [2, D], F32)
        ln2b_d = dram("ln2_b", [2, D], F32)
        ident_d = dram("k_ident", [128, 128], BF16)
        onesf_d = dram("k_ones", [128, 128], F32)
        tri_d = dram("k_tri", [128, 128], F32)
        maskL_d = dram("k_maskL", [128, 128], BF16)
        maskU_d = dram("k_maskU", [128, 128], BF16)
        iotaB_d = dram("k_iotaB", [128, 16 * 128], F32)
        iotaA_d = dram("k_iotaA", [128, 64], F32)
        pcol_d = dram("k_pcol", [128, 1], F32)
        cos_d = dram("k_cos", [NT * 128, 32], F32)
        sin_d = dram("k_sin", [NT * 128, 32], F32)
        ct_d = dram("k_ct", [4096, 4096], BF16)
        sn_d = dram("k_sn", [4096, 4096], BF16)
        ctc_d = dram("k_ctc", [256, 512], BF16)
        cc_d = dram("k_cc", [64, 128], F32)
        tgt_d = dram("k_tgt", [128, 32], F32)
        out_d = dram("out", [4096, D], F32, kind="ExternalOutput")
        dbg_d = {k: dram("dbg_" + k, shp, F32, kind="ExternalOutput") for k, shp in dbg.items()}
        mod_d = dram("s_mod", [2, 2, 6 * D], F32, kind="Internal")
        xcur_d = dram("s_xcur", [NT * 128, D], F32, kind="Internal")
        h2_d = dram("s_h2", [NT * 128, D], BF16, kind="Internal")
        acc_d = dram("s_acc", [NT * 128, D], F32, kind="Internal")
        mod_tok = toks(2)
        xcur_tok = toks(NT)
        h2_tok = toks(NT)
        acc_tok = toks(NT)
        accs_tok = Tok()
        out_tok = toks(NT)
        dbg_tok = Tok()

        uid = [0]

        def sb(stack, name, shape, dt):
            uid[0] += 1
            return stack.enter_context(nc.sbuf_tensor("sb%d_%s" % (uid[0], name), list(shape), dt))

        ps = es.enter_context(nc.psum_tensor("ps", [128, 4096], F32))
        pst = toks(8)

        def bank(b, p0=0, p1=128, c0=0, c1=512):
            return ps[p0:p1, b * 512 + c0:b * 512 + c1]

        ident = sb(es, "ident", [128, 128], BF16)
        onesf = sb(es, "onesf", [128, 128], F32)
        cT = sb(es, "cT", [128, 16], F32)
        cTs = sb(es, "cTs", [128, 16], F32)
        k_tok = Tok()
        S.dma("sp", lambda e: e.dma_start(out=ident[:], in_=ident_d), [], [k_tok])
        S.dma("sp", lambda e: e.dma_start(out=onesf[:], in_=onesf_d), [], [k_tok])
        S.dma("sp", lambda e: e.dma_start(out=cT[:], in_=cT_d), [], [k_tok])
        S.op("act", lambda e: e.activation(out=cTs[:], in_=cT[:], func=AF.Silu), [k_tok], [k_tok])

        AFF = [sb(es, "AFF", [128, NT, NE], F32)] * 2
        IDX = [sb(es, "IDX", [128, 16, 4], I32)] * 2
        GATE = [sb(es, "GATE", [128, 16, 4], F32)] * 2
        IDXC = [sb(es, "IDXC", [128, 16], I32)] * 2
        GATEC = [sb(es, "GATEC", [128, 16], F32)] * 2
        idx_tok = toks(2)
        aff_t = Tok()

        def dump(name, src_ap, dst_ap, rtoks):
            if name in dbg_d:
                S.dma("sp", lambda e: e.dma_start(out=dst_ap, in_=src_ap), list(rtoks), [dbg_tok])

        try:
          for L in range(n_layers):
            last = (L == n_layers - 1)
            with ExitStack() as ph:
                wm = [sb(ph, "wm%d" % i, [128, 8, 512], F32) for i in range(6)]
                wm_t = toks(6)
                bm = sb(ph, "bm", [2, 6 * D], F32)
                md = sb(ph, "md", [2, 6 * D], F32)
                bm_t, md_t = Tok(), Tok()
                S.dma("sp", lambda e: e.dma_start(out=bm[:], in_=b_mod_d[L].partition_broadcast(2)), [], [bm_t])
                wsrc = w_mod_d[L].rearrange("(k p) n -> p k n", p=128)
                for n in range(6):
                    S.dma("sp", lambda e: e.dma_start(out=wm[n][:], in_=wsrc[:, :, n * 512:(n + 1) * 512]), [], [wm_t[n]])
                for n in range(12):
                    b = n % 2
                    wb_ = n % 6
                    for k in range(8):
                        S.op("pe", lambda e: e.matmul(bank(b, 0, 2), cTs[:, 2 * k:2 * k + 2], wm[wb_][:, k, :],
                                                      start=(k == 0), stop=(k == 7)),
                             [k_tok, wm_t[wb_]], [pst[b]])
                    if n + 6 < 12:
                        S.dma("sp", lambda e: e.dma_start(out=wm[wb_][:], in_=wsrc[:, :, (n + 6) * 512:(n + 7) * 512]), [], [wm_t[wb_]])
                    S.op("dve", lambda e: e.tensor_tensor(out=md[:, n * 512:(n + 1) * 512], in0=bank(b, 0, 2),
                                                          in1=bm[:, n * 512:(n + 1) * 512], op=ALU.add),
                         [pst[b], bm_t], [md_t])
                for c in (1, 4):
                    S.op("dve", lambda e: e.tensor_scalar(out=md[:, c * D:(c + 1) * D], in0=md[:, c * D:(c + 1) * D],
                                                          scalar1=1.0, scalar2=None, op0=ALU.add), [md_t], [md_t])
                S.dma("sp", lambda e: e.dma_start(out=mod_d[L], in_=md[:]), [md_t], [mod_tok[L]])
                dump("mod%d" % L, md[:], dbg_d.get("mod%d" % L), [md_t])
                S.barrier()
            if stop_after == ("mod", L):
                break

            def modrow(dst, which, chunk, tok):
                src = mod_d[L, which, chunk * D:(chunk + 1) * D].partition_broadcast(128)
                S.dma("sp", lambda e: e.dma_start(out=dst[:], in_=src), [mod_tok[L]], [tok])

            def rowb(dst, src_row, tok):
                S.dma("sp", lambda e: e.dma_start(out=dst[:], in_=src_row.partition_broadcast(128)), [], [tok])

            def src_rows(j):
                if L == 0:
                    return (ctx_d[j * 128:(j + 1) * 128, :] if j < 2 else x_d[(j - 2) * 128:(j - 1) * 128, :]), []
                return xcur_d[j * 128:(j + 1) * 128, :], [xcur_tok[j]]

            with ExitStack() as lay:
                QA = sb(lay, "QA", [128, NT, 4, 128], BF16)
                KA = sb(lay, "KA", [128, NT * 128], BF16)
                VA = sb(lay, "VA", [128, NT, 2, 65], BF16)
                QC = sb(lay, "QC", [128, NT, 2, 128], BF16)
                KC = sb(lay, "KC", [128, NT * 128], BF16)
                VC = sb(lay, "VC", [128, NT, 2, 65], BF16)
                OBs = sb(lay, "OBs", [128, NT, 256], BF16)
                qa_t, ka_t, va_t, qc_t, kc_t, vc_t, ob_t = (toks(NT) for _ in range(7))
                vinit = Tok()
                S.op("pool", lambda e: e.memset(VA[:], 1.0), [], [vinit])
                S.op("pool", lambda e: e.memset(VC[:], 1.0), [], [vinit])

                with ExitStack() as ph_outer:
                  U2 = sb(ph_outer, "U2", [128, NT, 512], BF16)
                  u2_t = toks(NT)
                  with ExitStack() as ph:
                    WINB = sb(ph, "WINB", [128, 8, NCOL], BF16)
                    winb_t = Tok()
                    wsrc = w_in_d[L].rearrange("(k p) n -> p k n", p=128)
                    for (dc, sc, wd) in ((C_QA, 0, 512), (C_KA, 1024, 128), (C_QC, 512, 256), (C_KC, 1280, 128),
                                         (C_VA, 1152, 128), (C_VC, 1408, 128)):
                        S.dma("pool", lambda e: e.dma_start(out=WINB[:, :, dc:dc + wd], in_=wsrc[:, :, sc:sc + wd]),
                              [], [winb_t])
                    with ExitStack() as ff:
                        CC = sb(ff, "CC", [64, 128], F32)
                        WF = sb(ff, "WF", [64, 4, 64], F32)
                        MCS = sb(ff, "MCS", [64, 2, 4, 64], F32)
                        WUT = sb(ff, "WUT", [64, 4, D], F32)
                        f_t = Tok()
                        S.dma("sp", lambda e: e.dma_start(out=CC[:], in_=cc_d), [], [f_t])
                        S.dma("sp", lambda e: e.dma_start(out=WF[:], in_=wf_d[L].rearrange("g c d -> c g d")), [], [f_t])
                        S.dma("sp", lambda e: e.dma_start(out=WUT[:], in_=w_in_uT_d[L].rearrange("(g c) d -> c g d", c=64)),
                              [], [f_t])
                        for cs in range(2):
                            S.op("pe", lambda e: e.matmul(bank(cs, 0, 64, 0, 256), CC[:, cs * 64:(cs + 1) * 64],
                                                          WF[:].rearrange("c g d -> c (g d)"), start=True, stop=True),
                                 [f_t], [pst[cs]])
                            S.op("dve", lambda e: e.tensor_copy(out=MCS[:, cs].rearrange("c g d -> c (g d)"),
                                                                in_=bank(cs, 0, 64, 0, 256)), [pst[cs]], [f_t])
                        for k in range(8):
                            b = 2 + (k % 2)
                            for cs in range(2):
                                for g in range(4):
                                    c0 = cs * 256 + g * 64
                                    S.op("pe", lambda e: e.matmul(bank(b, 0, 128, c0, c0 + 64),
                                                                  WUT[:, g, k * 128:(k + 1) * 128], MCS[:, cs, g, :],
                                                                  start=True, stop=True), [f_t], [pst[b]])
                            S.op("dve", lambda e: e.tensor_copy(out=WINB[:, k, C_UC:C_UC + 512], in_=bank(b)),
                                 [pst[b]], [winb_t])
                        S.barrier()
                    SH1 = sb(ph, "SH1", [128, D], F32)
                    SC1 = sb(ph, "SC1", [128, D], F32)
                    GQK = sb(ph, "GQK", [128, 10, 64], F32)
                    mr_t = Tok()
                    g_t = Tok()
                    for h in range(8):
                        rowb(GQK[:, h, :], qn_d[L], g_t)
                    for h in range(8, 10):
                        rowb(GQK[:, h, :], kn_d[L], g_t)
                    XT = [sb(ph, "XT%d" % i, [128, D], F32) for i in range(2)]
                    xt_t = toks(2)
                    XN = sb(ph, "XN", [128, D], F32)
                    Hb = sb(ph, "Hb", [128, D], BF16)
                    HT = sb(ph, "HT", [128, 8, 128], BF16)
                    ST = sb(ph, "ST", [128, 2, 6], F32)
                    MV = sb(ph, "MV", [128, 8], F32)
                    SQ = sb(ph, "SQ", [128, 640], F32)
                    MS = sb(ph, "MS", [128, 16], F32)
                    NRM = SQ[:].rearrange("p (h d) -> p h d", d=64)
                    R = sb(ph, "R", [128, 16, 64], F32)
                    RO = sb(ph, "RO", [128, 16, 64], BF16)
                    T1 = sb(ph, "T1", [128, 16, 2, 16], F32)
                    T2 = sb(ph, "T2", [128, 16, 2, 16], F32)
                    CS = [sb(ph, "CS%d" % i, [128, 2, 32], F32) for i in range(2)]
                    cs_t = toks(2)
                    xn_t, hb_t, ht_t, st_t, mv_t, sq_t, ms_t, nrm_t, r_t, ro_t, tt_t = (Tok() for _ in range(11))
                    for j in range(NT):
                        if j == 0 or j == 2:
                            w = 1 if j == 0 else 0
                            modrow(SH1, w, 0, mr_t)
                            modrow(SC1, w, 1, mr_t)
                        b = j % 2
                        rows, rt = src_rows(j)
                        S.dma("sp", lambda e: e.dma_start(out=XT[b][:], in_=rows), rt, [xt_t[b]])
                        S.dma("sp", lambda e: e.dma_start(out=CS[b][:, 0, :], in_=cos_d[j * 128:(j + 1) * 128, :]), [], [cs_t[b]])
                        S.dma("sp", lambda e: e.dma_start(out=CS[b][:, 1, :], in_=sin_d[j * 128:(j + 1) * 128, :]), [], [cs_t[b]])
                        xt = XT[b]
                        for hh in range(2):
                            S.op("dve", lambda e: e.bn_stats(out=ST[:, hh, :], in_=xt[:, hh * 512:(hh + 1) * 512]),
                                 [xt_t[b]], [st_t])
                        S.op("dve", lambda e: e.bn_aggr(out=MV[:, 0:2], in_=ST[:].rearrange("p a b -> p (a b)")),
                             [st_t], [mv_t])
                        S.op("act", lambda e: e.activation(out=MV[:, 2:3], in_=MV[:, 1:2], func=AF.Sqrt, bias=LN_EPS,
                                                           scale=1.0), [mv_t], [mv_t])
                        S.op("dve", lambda e: e.reciprocal(out=MV[:, 3:4], in_=MV[:, 2:3]), [mv_t], [mv_t])
                        S.op("dve", lambda e: e.tensor_scalar(out=MV[:, 4:5], in0=MV[:, 0:1], scalar1=MV[:, 3:4],
                                                              scalar2=-1.0, op0=ALU.mult, op1=ALU.mult), [mv_t], [mv_t])
                        S.op("act", lambda e: e.activation(out=XN[:], in_=xt[:], func=AF.Identity, bias=MV[:, 4:5],
                                                           scale=MV[:, 3:4]), [mv_t, xt_t[b]], [xn_t])
                        S.op("dve", lambda e: e.tensor_tensor(out=XN[:], in0=XN[:], in1=SC1[:], op=ALU.mult),
                             [xn_t, mr_t], [xn_t])
                        S.op("dve", lambda e: e.tensor_tensor(out=Hb[:], in0=XN[:], in1=SH1[:], op=ALU.add),
                             [xn_t, mr_t], [hb_t])
                        for k in range(8):
                            S.op("pe", lambda e: e.matmul(bank(k // 4, 0, 128, (k % 4) * 128, (k % 4) * 128 + 128),
                                                          Hb[:, k * 128:(k + 1) * 128], ident[:], start=True, stop=True),
                                 [hb_t, k_tok], [pst[k // 4]])
                        S.op("act", lambda e: e.copy(out=HT[:, 0:4, :].rearrange("p a b -> p (a b)"), in_=bank(0)),
                             [pst[0]], [ht_t])
                        S.op("dve", lambda e: e.tensor_copy(out=HT[:, 4:8, :].rearrange("p a b -> p (a b)"), in_=bank(1)),
                             [pst[1]], [ht_t])
                        for cg in range(4):
                            c0, c1 = cg * 512, min(NCOL, cg * 512 + 512)
                            for k in range(8):
                                S.op("pe", lambda e: e.matmul(bank(2 + cg, 0, 128, 0, c1 - c0), HT[:, k, :],
                                                              WINB[:, k, c0:c1], start=(k == 0), stop=(k == 7)),
                                     [ht_t, winb_t], [pst[2 + cg]])
                        P = ps[:, 2 * 512:2 * 512 + NCOL]
                        if ("p%d" % L) in dbg_d and j == 2:
                            for (a0, a1) in ((0, 1024), (1024, NCOL)):
                                S.op("dve", lambda e: e.tensor_copy(out=XN[:, 0:a1 - a0], in_=P[:, a0:a1]),
                                     [pst[2], pst[3], pst[4], pst[5]], [xn_t])
                                S.dma("sp", lambda e: e.dma_start(out=dbg_d["p%d" % L][:, a0:a1], in_=XN[:, 0:a1 - a0]),
                                      [xn_t], [dbg_tok])
                        S.op("act", lambda e: e.activation(out=SQ[:], in_=P[:, 0:640], func=AF.Square),
                             [pst[2], pst[3]], [sq_t])
                        S.op("dve", lambda e: e.tensor_reduce(out=MS[:, 0:10], in_=SQ[:].rearrange("p (h d) -> p h d", d=64),
                                                              axis=AX.X, op=ALU.add), [sq_t], [ms_t])
                        S.op("act", lambda e: e.activation(out=MS[:, 0:10], in_=MS[:, 0:10], func=AF.Sqrt, bias=RMS_EPS,
                                                           scale=1.0 / 64.0), [ms_t], [ms_t])
                        S.op("dve", lambda e: e.reciprocal(out=MS[:, 0:10], in_=MS[:, 0:10]), [ms_t], [ms_t])
                        S.op("dve", lambda e: e.tensor_tensor(out=NRM, in0=P[:, 0:640].rearrange("p (h d) -> p h d", d=64),
                                                              in1=MS[:, 0:10].unsqueeze(2).to_broadcast([128, 10, 64]),
                                                              op=ALU.mult), [ms_t, pst[2], pst[3]], [nrm_t])
                        S.op("dve", lambda e: e.tensor_tensor(
                            out=R[:, 0:8, :].rearrange("p (g kv) d -> p kv g d", kv=2),
                            in0=NRM[:, 0:8, :].rearrange("p (kv g) d -> p kv g d", kv=2),
                            in1=GQK[:, 0:8, :].rearrange("p (kv g) d -> p kv g d", kv=2), op=ALU.mult),
                            [nrm_t, g_t], [r_t])
                        S.op("dve", lambda e: e.tensor_tensor(out=R[:, 8:10, :], in0=NRM[:, 8:10, :], in1=GQK[:, 8:10, :],
                                                              op=ALU.mult), [nrm_t, g_t], [r_t])
                        S.op("act", lambda e: e.copy(
                            out=R[:, 10:14, :].rearrange("p (g kv) d -> p kv g d", kv=2),
                            in_=P[:, C_QC:C_QC + 256].rearrange("p (kv g d) -> p kv g d", kv=2, g=2)), [pst[3]], [r_t])
                        S.op("act", lambda e: e.copy(out=R[:, 14:16, :].rearrange("p h d -> p (h d)"),
                                                     in_=P[:, C_KC:C_KC + 128]), [pst[3]], [r_t])
                        Rv = R[:].rearrange("p h (a b f) -> p h a b f", a=2, b=2)
                        ROv = RO[:].rearrange("p h (a b f) -> p h a b f", a=2, b=2)
                        x1, x2 = Rv[:, :, :, 0, :], Rv[:, :, :, 1, :]
                        cosb = CS[b][:, 0, :].rearrange("p (a f) -> p a f", a=2).unsqueeze(1).to_broadcast([128, 16, 2, 16])
                        sinb = CS[b][:, 1, :].rearrange("p (a f) -> p a f", a=2).unsqueeze(1).to_broadcast([128, 16, 2, 16])
                        S.op("dve", lambda e: e.tensor_tensor(out=T1[:], in0=x1, in1=cosb, op=ALU.mult), [r_t, cs_t[b]], [tt_t])
                        S.op("dve", lambda e: e.tensor_tensor(out=T2[:], in0=x2, in1=sinb, op=ALU.mult), [r_t, cs_t[b]], [tt_t])
                        S.op("dve", lambda e: e.tensor_tensor(out=ROv[:, :, :, 0, :], in0=T1[:], in1=T2[:], op=ALU.subtract),
                             [tt_t], [ro_t])
                        S.op("dve", lambda e: e.tensor_tensor(out=T1[:], in0=x2, in1=cosb, op=ALU.mult), [r_t, cs_t[b]], [tt_t])
                        S.op("dve", lambda e: e.tensor_tensor(out=T2[:], in0=x1, in1=sinb, op=ALU.mult), [r_t, cs_t[b]], [tt_t])
                        S.op("dve", lambda e: e.tensor_tensor(out=ROv[:, :, :, 1, :], in0=T1[:], in1=T2[:], op=ALU.add),
                             [tt_t], [ro_t])
                        for blk in range(8):
                            S.op("pe", lambda e: e.matmul(bank(blk // 4, 0, 128, (blk % 4) * 128, (blk % 4) * 128 + 128),
                                                          RO[:, 2 * blk:2 * blk + 2, :].rearrange("p h d -> p (h d)"),
                                                          ident[:], start=True, stop=True), [ro_t, k_tok], [pst[blk // 4]])
                        S.op("act", lambda e: e.copy(out=QA[:, j].rearrange("p g t -> p (g t)"), in_=bank(0)),
                             [pst[0]], [qa_t[j]])
                        S.op("dve", lambda e: e.tensor_copy(out=KA[:, j * 128:(j + 1) * 128], in_=bank(1, 0, 128, 0, 128)),
                             [pst[1]], [ka_t[j]])
                        S.op("dve", lambda e: e.tensor_copy(out=QC[:, j].rearrange("p g t -> p (g t)"),
                                                            in_=bank(1, 0, 128, 128, 384)), [pst[1]], [qc_t[j]])
                        S.op("dve", lambda e: e.tensor_copy(out=KC[:, j * 128:(j + 1) * 128], in_=bank(1, 0, 128, 384, 512)),
                             [pst[1]], [kc_t[j]])
                        S.op("act", lambda e: e.copy(out=VA[:, j, :, 1:65],
                                                     in_=P[:, C_VA:C_VA + 128].rearrange("p (h d) -> p h d", d=64)),
                             [pst[5], vinit], [va_t[j]])
                        S.op("act", lambda e: e.copy(out=VC[:, j, :, 1:65],
                                                     in_=P[:, C_VC:C_VC + 128].rearrange("p (h d) -> p h d", d=64)),
                             [pst[5], vinit], [vc_t[j]])
                        S.op("dve", lambda e: e.tensor_copy(out=U2[:, j, :], in_=P[:, C_UC:C_UC + 512]), [pst[4]], [u2_t[j]])
                    S.barrier()
                  if True:
                    with ExitStack() as pd:
                        BFR = sb(pd, "BFR", [128, 256], F32)
                        bfr_t = Tok()
                        rowb(BFR, bf_d[L], bfr_t)
                        TB = [sb(pd, "TB%d" % i, [128, 2, 2048], BF16) for i in range(3)]
                        tb_t = toks(3)
                        it = 0
                        for half in range(2):
                            for tc in range(32):
                                b = it % 3
                                it += 1
                                S.dma("sp", lambda e: e.dma_start(out=TB[b][:, 0, :],
                                                                  in_=ct_d[tc * 128:(tc + 1) * 128, half * 2048:(half + 1) * 2048]),
                                      [], [tb_t[b]])
                                S.dma("sp", lambda e: e.dma_start(out=TB[b][:, 1, :],
                                                                  in_=sn_d[tc * 128:(tc + 1) * 128, half * 2048:(half + 1) * 2048]),
                                      [], [tb_t[b]])
                                for kc in range(16):
                                    for cs in range(2):
                                        S.op("pe", lambda e: e.matmul(
                                            bank(kc // 2, 0, 128, (kc % 2) * 256, (kc % 2) * 256 + 256),
                                            TB[b][:, cs, kc * 128:(kc + 1) * 128], U2[:, 2 + tc, cs * 256:(cs + 1) * 256],
                                            start=(tc == 0 and cs == 0 and kc % 2 == 0), stop=(tc == 31 and cs == 1),
                                            skip_group_check=True),
                                            [tb_t[b], u2_t[2 + tc]], [pst[kc // 2]])
                            for kc in range(16):
                                jj = 2 + half * 16 + kc
                                S.op("dve", lambda e: e.tensor_tensor(
                                    out=OBs[:, jj, :], in0=bank(kc // 2, 0, 128, (kc % 2) * 256, (kc % 2) * 256 + 256),
                                    in1=BFR[:], op=ALU.add), [pst[kc // 2], bfr_t], [ob_t[jj]])
                        if not last:
                            TBC = sb(pd, "TBC", [128, 2, 512], BF16)
                            tbc_t = Tok()
                            for tc in range(2):
                                S.dma("sp", lambda e: e.dma_start(out=TBC[:, tc, :], in_=ctc_d[tc * 128:(tc + 1) * 128, :]),
                                      [], [tbc_t])
                            for kc in range(2):
                                for tc in range(2):
                                    for cs in range(2):
                                        S.op("pe", lambda e: e.matmul(
                                            bank(0, 0, 128, kc * 256, kc * 256 + 256),
                                            TBC[:, tc, cs * 256 + kc * 128:cs * 256 + kc * 128 + 128],
                                            U2[:, tc, cs * 256:(cs + 1) * 256],
                                            start=(tc == 0 and cs == 0 and kc == 0), stop=(tc == 1 and cs == 1),
                                            skip_group_check=True),
                                            [tbc_t, u2_t[tc]], [pst[0]])
                                S.op("dve", lambda e: e.tensor_tensor(out=OBs[:, kc, :], in0=bank(0, 0, 128, kc * 256, kc * 256 + 256),
                                                                      in1=BFR[:], op=ALU.add), [pst[0], bfr_t], [ob_t[kc]])
                        S.barrier()
                if ("QA%d" % L) in dbg_d:
                    with ExitStack() as dd:
                        TMPD = sb(dd, "TMPD", [128, 4352], F32)
                        td = Tok()
                        for nm, src in (("KA", KA[:]), ("KC", KC[:])):
                            S.op("dve", lambda e: e.tensor_copy(out=TMPD[:], in_=src), ka_t + kc_t, [td])
                            dump(nm + "%d" % L, TMPD[:], dbg_d.get(nm + "%d" % L), [td])
                        for nm, src in (("QA", QA[:, 2].rearrange("p g t -> p (g t)")), ("OB", OBs[:, 2, :]),
                                        ("VA", VA[:, 2].rearrange("p h d -> p (h d)"))):
                            n = src.shape[1]
                            S.op("dve", lambda e: e.tensor_copy(out=TMPD[:, 0:n], in_=src), qa_t + ob_t + va_t, [td])
                            dump(nm + "%d" % L, TMPD[:, 0:n], dbg_d.get(nm + "%d" % L), [td])
                        S.barrier()
                if stop_after == ("A", L):
                    break
                with ExitStack() as ml:
                    WOA = sb(ml, "WOA", [128, 8, D], BF16)
                    WOB = sb(ml, "WOB", [128, 2, D], BF16)
                    WOC = sb(ml, "WOC", [128, 4, D], BF16)
                    WR = sb(ml, "WR", [128, 8, NE], BF16)
                    w_t = Tok()
                    S.op("pool", lambda e: e.memset(WOA[:], 0.0), [], [w_t])
                    S.op("pool", lambda e: e.memset(WOC[:], 0.0), [], [w_t])
                    S.dma("pool", lambda e: e.dma_start(out=WOA[1:65], in_=w_out_d[L, 0:512, :].rearrange("(h p) n -> p h n", p=64)), [w_t], [w_t])
                    S.dma("pool", lambda e: e.dma_start(out=WOB[:], in_=w_out_d[L, 512:768, :].rearrange("(h p) n -> p h n", p=128)), [], [w_t])
                    S.dma("pool", lambda e: e.dma_start(out=WOC[1:65], in_=w_out_d[L, 768:1024, :].rearrange("(h p) n -> p h n", p=64)), [w_t], [w_t])
                    S.dma("pool", lambda e: e.dma_start(out=WR[:], in_=wr_d[L].rearrange("(k p) n -> p k n", p=128)), [], [w_t])
                    G1 = sb(ml, "G1", [128, D], F32)
                    SC2 = sb(ml, "SC2", [128, D], F32)
                    SH2 = sb(ml, "SH2", [128, D], F32)
                    LN1G = sb(ml, "LN1G", [128, D], F32)
                    LN1B = sb(ml, "LN1B", [128, D], F32)
                    mr_t, ln_t = Tok(), Tok()
                    rowb(LN1G, ln1g_d[L], ln_t)
                    rowb(LN1B, ln1b_d[L], ln_t)
                    MKL = sb(ml, "MKL", [128, 128], BF16)
                    MKU = sb(ml, "MKU", [128, 128], BF16)
                    SINKE = sb(ml, "SINKE", [128, 4, 128], F32)
                    SK4 = sb(ml, "SK4", [128, 4], F32)
                    mk_t, sk_t = Tok(), Tok()
                    S.dma("sp", lambda e: e.dma_start(out=MKL[:], in_=maskL_d), [], [mk_t])
                    S.dma("sp", lambda e: e.dma_start(out=MKU[:], in_=maskU_d), [], [mk_t])
                    S.dma("sp", lambda e: e.dma_start(out=SK4[0:1, :], in_=sink_d[L:L + 1, :]), [], [sk_t])
                    S.op("act", lambda e: e.activation(out=SK4[0:1, :], in_=SK4[0:1, :], func=AF.Exp), [sk_t], [sk_t])
                    S.op("dve", lambda e: e.tensor_copy(out=SINKE[0:1, :, :], in_=SK4[0:1, :].unsqueeze(2).to_broadcast([1, 4, 128])),
                         [sk_t], [sk_t])
                    PT = [sb(ml, "PT%d" % i, [128, 512], BF16) for i in range(3)]
                    pt_t = toks(3)
                    PTC = [sb(ml, "PTC%d" % i, [128, 512], BF16) for i in range(2)]
                    ptc_t = toks(2)
                    REC = sb(ml, "REC", [128, 1024], F32)
                    BCS = sb(ml, "BCS", [128, 1024], F32)
                    RECC = REC[:, 0:512]
                    BCC = BCS[:, 0:512]
                    CATA = sb(ml, "CATA", [128, 8, 128], BF16)
                    CATC = sb(ml, "CATC", [128, 4, 128], BF16)
                    CATB = sb(ml, "CATB", [128, 2, 128], BF16)
                    rec_t, bcs_t, cata_t, catc_t, catb_t = (Tok() for _ in range(5))
                    recc_t, bcc_t = rec_t, bcs_t
                    S.op("pool", lambda e: e.memset(CATA[:], 0.0), [], [cata_t])
                    S.op("pool", lambda e: e.memset(CATC[:], 0.0), [], [catc_t])
                    QZ = [sb(ml, "QZ%d" % i, [128, 2, 512], BF16) for i in range(2)]
                    QCZ = [sb(ml, "QCZ%d" % i, [128, 2, 256], BF16) for i in range(2)]
                    qz_t, qcz_t = toks(2), toks(2)
                    for i in range(2):
                        S.op("pool", lambda e: e.memset(QZ[i][:], 0.0), [], [qz_t[i]])
                        S.op("pool", lambda e: e.memset(QCZ[i][:], 0.0), [], [qcz_t[i]])
                    XT2 = [sb(ml, "XT20", [128, D], F32)] * 2
                    xt2_t = [Tok()] * 2
                    TMP = sb(ml, "TMP", [128, D], F32)
                    RR = sb(ml, "RR", [128, D], F32)
                    XN2 = sb(ml, "XN2", [128, D], F32)
                    ACC = TMP
                    H2 = sb(ml, "H2", [128, D], BF16)
                    HT2 = sb(ml, "HT2", [128, 8, 128], BF16)
                    ST2 = sb(ml, "ST2", [128, 2, 6], F32)
                    MV2 = sb(ml, "MV2", [128, 8], F32)
                    LG = sb(ml, "LG", [128, NE], F32)
                    SM = sb(ml, "SM", [128, 4], F32)
                    tmp_t, rr_t, xn2_t, h2b_t, ht2_t, st2_t, mv2_t, lg_t, sm_t = (Tok() for _ in range(9))
                    accb_t = tmp_t

                    NEGH = sb(ml, "NEGH", [128, 1], F32)
                    ngh_t = Tok()
                    S.op("pool", lambda e: e.memset(NEGH[:], -0.5), [], [ngh_t])

                    def ln_norm(dst, src, src_toks, dst_tok):
                        for hh in range(2):
                            S.op("dve", lambda e: e.bn_stats(out=ST2[:, hh, :], in_=src[:, hh * 512:(hh + 1) * 512]),
                                 src_toks, [st2_t])
                        S.op("dve", lambda e: e.bn_aggr(out=MV2[:, 0:2], in_=ST2[:].rearrange("p a b -> p (a b)")),
                             [st2_t], [mv2_t])
                        S.op("pool", lambda e: e.tensor_scalar(out=MV2[:, 2:3], in0=MV2[:, 1:2], scalar1=LN_EPS, scalar2=None,
                                                               op0=ALU.add), [mv2_t], [mv2_t])
                        S.op("pool", lambda e: e.tensor_tensor(out=MV2[:, 3:4], in0=MV2[:, 2:3], in1=NEGH[:], op=ALU.pow),
                             [mv2_t, ngh_t], [mv2_t])
                        S.op("dve", lambda e: e.tensor_scalar(out=dst[:], in0=src[:], scalar1=MV2[:, 0:1], scalar2=MV2[:, 3:4],
                                                              op0=ALU.subtract, op1=ALU.mult), [mv2_t] + list(src_toks), [dst_tok])

                    chk("ML_SETUP")
                    tiles = list(range(2, NT)) if last else list(range(NT))
                    if ml_tiles is not None:
                        tiles = list(ml_tiles)
                    CATA2 = [CATA, sb(ml, "CATA1", [128, 8, 128], BF16)]
                    CATC2 = [CATC, sb(ml, "CATC1", [128, 4, 128], BF16)]
                    CATB2 = [CATB, sb(ml, "CATB1", [128, 2, 128], BF16)]
                    H22 = [H2, sb(ml, "H21", [128, D], BF16)]
                    cata2_t, catc2_t, catb2_t, h22_t = [cata_t, Tok()], [catc_t, Tok()], [catb_t, Tok()], [h2b_t, Tok()]
                    S.op("pool", lambda e: e.memset(CATA2[1][:], 0.0), [], [cata2_t[1]])
                    S.op("pool", lambda e: e.memset(CATC2[1][:], 0.0), [], [catc2_t[1]])
                    state = {"pi": 0, "pc": 0}

                    def emit_QZ(j, b):
                        for h in range(2):
                            S.op("pool", lambda e: e.tensor_copy(out=QZ[b][64 * h:64 * h + 64, h, :],
                                                                 in_=QA[64 * h:64 * h + 64, j].rearrange("p g t -> p (g t)")),
                                 [qa_t[j]], [qz_t[b]])
                            S.op("pool", lambda e: e.tensor_copy(out=QCZ[b][64 * h:64 * h + 64, h, :],
                                                                 in_=QC[64 * h:64 * h + 64, j].rearrange("p g t -> p (g t)")),
                                 [qc_t[j]], [qcz_t[b]])

                    def emit_BC(j, b):
                        CA, CC = CATA2[b], CATC2[b]
                        ca_t, cc_t = cata2_t[b], catc2_t[b]
                        chunks = [0, 1] if j < 2 else list(range(NT))
                        steps = [(c, h) for c in chunks for h in range(2)]
                        ns = len(steps)

                        def emit_S(n):
                            c, h = steps[n]
                            S.op("pe", lambda e: e.matmul(bank(n % 2), KA[:, c * 128:(c + 1) * 128], QZ[b][:, h, :],
                                                          start=True, stop=True), [ka_t[c], qz_t[b]], [pst[n % 2]])
                        emit_S(0)
                        emit_S(1)
                        for n in range(ns):
                            c, h = steps[n]
                            p_ = state["pi"]
                            state["pi"] = (p_ + 1) % 3
                            S.op("act", lambda e: e.activation(out=PT[p_][:], in_=bank(n % 2), func=AF.Exp, scale=0.125),
                                 [pst[n % 2]], [pt_t[p_]])
                            S.op("pe", lambda e: e.matmul(bank(2 + h, 0, 65), VA[:, c, h, :], PT[p_][:],
                                                          start=(n < 2), stop=(n >= ns - 2)),
                                 [va_t[c], pt_t[p_]], [pst[2 + h]])
                            if n + 2 < ns:
                                emit_S(n + 2)
                        chk("B_LOOP")
                        for h in range(2):
                            S.op("act", lambda e: e.activation(out=REC[0:1, h * 512:(h + 1) * 512], in_=bank(2 + h, 0, 1), func=AF.Ln),
                                 [pst[2 + h]], [rec_t])
                            S.op("act", lambda e: e.activation(out=REC[0:1, h * 512:(h + 1) * 512], in_=REC[0:1, h * 512:(h + 1) * 512],
                                                               func=AF.Exp, scale=-1.0), [rec_t], [rec_t])
                            S.op("pe", lambda e: e.matmul(bank(6 + h, 0, 65), onesf[0:1, 0:65], REC[0:1, h * 512:(h + 1) * 512],
                                                          start=True, stop=True), [rec_t, k_tok], [pst[6 + h]])
                            S.op("dve", lambda e: e.tensor_copy(out=BCS[0:65, h * 512:(h + 1) * 512], in_=bank(6 + h, 0, 65)), [pst[6 + h]], [bcs_t])
                            S.op("dve", lambda e: e.tensor_tensor(out=CA[0:65, 4 * h:4 * h + 4, :].rearrange("p g t -> p (g t)"),
                                                                  in0=bank(2 + h, 0, 65), in1=BCS[0:65, h * 512:(h + 1) * 512],
                                                                  op=ALU.mult), [pst[2 + h], bcs_t], [ca_t])
                        chk("B_NORM")
                        cks = [(0, None), (1, None)]
                        if j >= 2:
                            if j - 1 >= 2:
                                cks.append((j - 1, MKL))
                            cks.append((j, None))
                            if j + 1 < NT:
                                cks.append((j + 1, MKU))
                        for ci, (c, mk) in enumerate(cks):
                            for h in range(2):
                                S.op("pe", lambda e: e.matmul(bank(h, 0, 128, 0, 256), KC[:, c * 128:(c + 1) * 128], QCZ[b][:, h, :],
                                                              start=True, stop=True), [kc_t[c], qcz_t[b]], [pst[h]])
                            q_ = state["pc"]
                            state["pc"] = (q_ + 1) % 2
                            S.op("act", lambda e: e.activation(out=PTC[q_][:].rearrange("p (b c) -> p b c", b=2),
                                                               in_=ps[:, 0:1024].rearrange("p (b c) -> p b c", b=2)[:, :, 0:256],
                                                               func=AF.Exp, scale=0.125),
                                 [pst[0], pst[1]], [ptc_t[q_]])
                            if mk is not None:
                                S.op("pool", lambda e: e.tensor_tensor(
                                    out=PTC[q_][:].rearrange("p (a t) -> p a t", a=4),
                                    in0=PTC[q_][:].rearrange("p (a t) -> p a t", a=4),
                                    in1=mk[:].unsqueeze(1).to_broadcast([128, 4, 128]), op=ALU.mult),
                                    [ptc_t[q_], mk_t], [ptc_t[q_]])
                            for h in range(2):
                                S.op("pe", lambda e: e.matmul(bank(5, 0, 65, h * 256, (h + 1) * 256), VC[:, c, h, :],
                                                              PTC[q_][:, h * 256:(h + 1) * 256],
                                                              start=(ci == 0 and h == 0), stop=(ci == len(cks) - 1),
                                                              skip_group_check=True), [vc_t[c], ptc_t[q_]], [pst[5]])
                        for a4 in range(4):
                            S.op("act", lambda e: e.activation(out=RECC[0:1, a4 * 128:(a4 + 1) * 128], in_=bank(5, 0, 1, a4 * 128, (a4 + 1) * 128),
                                                               func=AF.Ln, bias=SK4[0:1, a4:a4 + 1], scale=1.0), [pst[5], sk_t], [recc_t])
                        S.op("act", lambda e: e.activation(out=RECC[0:1, :], in_=RECC[0:1, :], func=AF.Exp, scale=-1.0), [recc_t], [recc_t])
                        S.op("pe", lambda e: e.matmul(bank(4, 0, 65), onesf[0:1, 0:65], RECC[0:1, :], start=True, stop=True),
                             [recc_t, k_tok], [pst[4]])
                        S.op("dve", lambda e: e.tensor_copy(out=BCC[0:65, :], in_=bank(4, 0, 65)), [pst[4]], [bcc_t])
                        S.op("dve", lambda e: e.tensor_tensor(out=CC[0:65].rearrange("p a t -> p (a t)"), in0=bank(5, 0, 65),
                                                              in1=BCC[0:65, :], op=ALU.mult), [pst[5], bcc_t], [cc_t])
                        chk("C")

                    def emit_E1(j, b):
                        CA, CC, CB, HH = CATA2[b], CATC2[b], CATB2[b], H22[b]
                        ca_t, cc_t, cb_t, hh_t = cata2_t[b], catc2_t[b], catb2_t[b], h22_t[b]
                        if j == tiles[0] or j == 2:
                            w = 1 if j < 2 else 0
                            modrow(G1, w, 2, mr_t)
                            modrow(SH2, w, 3, mr_t)
                            modrow(SC2, w, 4, mr_t)
                        rows, rt = src_rows(j)
                        S.dma("sp", lambda e: e.dma_start(out=XT2[b][:], in_=rows), rt, [xt2_t[b]])
                        for m in range(2):
                            S.op("pe", lambda e: e.matmul(bank(6, 0, 128, m * 128, (m + 1) * 128), OBs[:, j, m * 128:(m + 1) * 128],
                                                          ident[:], start=True, stop=True), [ob_t[j], k_tok], [pst[6]])
                        S.op("dve", lambda e: e.tensor_copy(out=CB[:].rearrange("p m t -> p (m t)"), in_=bank(6, 0, 128, 0, 256)),
                             [pst[6]], [cb_t])
                        for n in range(2):
                            mms = []
                            for hd in range(8):
                                mms.append((CA[:, hd, :], WOA[:, hd, n * 512:(n + 1) * 512], ca_t))
                            for m in range(2):
                                mms.append((CB[:, m, :], WOB[:, m, n * 512:(n + 1) * 512], cb_t))
                            for hd in range(4):
                                mms.append((CC[:, hd, :], WOC[:, hd, n * 512:(n + 1) * 512], cc_t))
                            for i, (l_, r_, t_) in enumerate(mms):
                                S.op("pe", lambda e: e.matmul(bank(6 + n), l_, r_, start=(i == 0), stop=(i == len(mms) - 1)),
                                     [t_, w_t], [pst[6 + n]])
                        chk("E_PROJ")
                        O = ps[:, 6 * 512:8 * 512]
                        S.op("dve", lambda e: e.tensor_tensor(out=TMP[:], in0=O, in1=G1[:], op=ALU.mult),
                             [pst[6], pst[7], mr_t], [tmp_t])
                        S.op("dve", lambda e: e.scalar_tensor_tensor(out=RR[:], in0=XT2[b][:], scalar=ALPHA, in1=TMP[:],
                                                                     op0=ALU.mult, op1=ALU.add), [xt2_t[b], tmp_t], [rr_t])
                        ln_norm(XN2, RR, [rr_t], xn2_t)
                        S.op("dve", lambda e: e.tensor_tensor(out=XN2[:], in0=XN2[:], in1=LN1G[:], op=ALU.mult), [xn2_t, ln_t], [xn2_t])
                        S.op("pool", lambda e: e.tensor_tensor(out=RR[:], in0=XN2[:], in1=LN1B[:], op=ALU.add), [xn2_t, ln_t], [rr_t])
                        if ("x1_%d" % L) in dbg_d and j in (0, 2):
                            S.dma("sp", lambda e: e.dma_start(out=dbg_d["x1_%d" % L][(0 if j == 0 else 128):(128 if j == 0 else 256), :],
                                                              in_=RR[:]), [rr_t], [dbg_tok])
                        S.op("dve", lambda e: e.tensor_scalar(out=ACC[:], in0=RR[:], scalar1=ALPHA, scalar2=None, op0=ALU.mult),
                             [rr_t], [accb_t])
                        S.dma("sp", lambda e: e.dma_start(out=acc_d[j * 128:(j + 1) * 128, :], in_=ACC[:]), [accb_t], [acc_tok[j]])
                        ln_norm(XN2, RR, [rr_t], xn2_t)
                        S.op("dve", lambda e: e.tensor_tensor(out=XN2[:], in0=XN2[:], in1=SC2[:], op=ALU.mult), [xn2_t, mr_t], [xn2_t])
                        S.op("pool", lambda e: e.tensor_tensor(out=HH[:], in0=XN2[:], in1=SH2[:], op=ALU.add), [xn2_t, mr_t], [hh_t])
                        S.dma("sp", lambda e: e.dma_start(out=h2_d[j * 128:(j + 1) * 128, :], in_=HH[:]), [hh_t], [h2_tok[j]])
                        chk("E_LN")

                    def emit_E2(j, b):
                        HH, hh_t = H22[b], h22_t[b]
                        for k in range(8):
                            S.op("pe", lambda e: e.matmul(bank(6 + k // 4, 0, 128, (k % 4) * 128, (k % 4) * 128 + 128),
                                                          HH[:, k * 128:(k + 1) * 128], ident[:], start=True, stop=True),
                                 [hh_t, k_tok], [pst[6 + k // 4]])
                        S.op("dve", lambda e: e.tensor_copy(out=HT2[:, 0:4, :].rearrange("p a b -> p (a b)"), in_=bank(6)), [pst[6]], [ht2_t])
                        S.op("dve", lambda e: e.tensor_copy(out=HT2[:, 4:8, :].rearrange("p a b -> p (a b)"), in_=bank(7)), [pst[7]], [ht2_t])
                        for k in range(8):
                            S.op("pe", lambda e: e.matmul(bank(6, 0, 128, 0, NE), HT2[:, k, :], WR[:, k, :], start=(k == 0), stop=(k == 7)),
                                 [ht2_t, w_t], [pst[6]])
                        S.op("dve", lambda e: e.reduce_max(out=SM[:, 0:1], in_=bank(6, 0, 128, 0, NE), axis=AX.X), [pst[6]], [sm_t])
                        S.op("dve", lambda e: e.tensor_scalar(out=SM[:, 1:2], in0=SM[:, 0:1], scalar1=-1.0, scalar2=None, op0=ALU.mult),
                             [sm_t], [sm_t])
                        S.op("act", lambda e: e.activation(out=LG[:], in_=bank(6, 0, 128, 0, NE), func=AF.Exp, bias=SM[:, 1:2], scale=1.0,
                                                           accum_out=SM[:, 2:3]), [pst[6], sm_t], [lg_t, sm_t])
                        S.op("dve", lambda e: e.reciprocal(out=SM[:, 3:4], in_=SM[:, 2:3]), [sm_t], [sm_t])
                        S.op("dve", lambda e: e.tensor_scalar(out=AFF[L][:, j, :], in0=LG[:], scalar1=SM[:, 3:4], scalar2=None, op0=ALU.mult),
                             [lg_t, sm_t], [aff_t])

                    nt_ = len(tiles)
                    emit_QZ(tiles[0], 0)
                    for idx in range(nt_ + 2):
                        if idx + 1 < nt_:
                            emit_QZ(tiles[idx + 1], (idx + 1) % 2)
                        if idx < nt_:
                            emit_BC(tiles[idx], idx % 2)
                        if idx >= 2:
                            emit_E2(tiles[idx - 2], (idx - 2) % 2)
                        if 1 <= idx <= nt_:
                            emit_E1(tiles[idx - 1], (idx - 1) % 2)
                    S.barrier()
                    for nm, src in (("cata", CATA[0:65].rearrange("p a t -> p (a t)")), ("catc", CATC[0:65].rearrange("p a t -> p (a t)"))):
                        if (nm + "%d" % L) in dbg_d:
                            n_ = src.shape[1]
                            S.op("dve", lambda e: e.tensor_copy(out=TMP[0:65, 0:n_], in_=src), [cata_t, catc_t], [tmp_t])
                            S.dma("sp", lambda e: e.dma_start(out=dbg_d[nm + "%d" % L], in_=TMP[0:65, 0:n_]), [tmp_t], [dbg_tok])
                            S.barrier()
            if stop_after == ("ML", L):
                break
            with ExitStack() as pf:
                TRI = sb(pf, "TRI", [128, 128], F32)
                IOB = sb(pf, "IOB", [128, 16, 128], F32)
                IOA = sb(pf, "IOA", [128, 16, 4], F32)
                PCOL = sb(pf, "PCOL", [128, 1], F32)
                TGT = sb(pf, "TGT", [128, 32], F32)
                kf_t = Tok()
                S.dma("sp", lambda e: e.dma_start(out=TRI[:], in_=tri_d), [], [kf_t])
                S.dma("sp", lambda e: e.dma_start(out=IOB[:].rearrange("p a b -> p (a b)"), in_=iotaB_d), [], [kf_t])
                S.dma("sp", lambda e: e.dma_start(out=IOA[:].rearrange("p a b -> p (a b)"), in_=iotaA_d), [], [kf_t])
                S.dma("sp", lambda e: e.dma_start(out=PCOL[:], in_=pcol_d), [], [kf_t])
                S.dma("sp", lambda e: e.dma_start(out=TGT[:], in_=tgt_d), [], [kf_t])
                THR = sb(pf, "THR", [128, 32], F32)
                LO = sb(pf, "LO", [128, 32], F32)
                CNTP = sb(pf, "CNTP", [128, 32], F32)
                IND = sb(pf, "IND", [128, 32], F32)
                MSK = sb(pf, "MSK", [128, NT, NE], F32)
                thr_t, lo_t, cntp_t, ind_t, msk_t = (Tok() for _ in range(5))
                S.op("dve", lambda e: e.memset(LO[:], 0.0), [], [lo_t])
                S.op("dve", lambda e: e.memset(CNTP[:], 0.0), [], [cntp_t])
                S.op("dve", lambda e: e.memset(MSK[:], 0.0), [], [msk_t])
                A_ = AFF[L]
                do_ctx = not last

                def make_mask(thr):
                    S.op("dve", lambda e: e.tensor_tensor(out=MSK[:, 2:NT, :], in0=A_[:, 2:NT, :],
                                                          in1=thr[:, 0:16].unsqueeze(1).to_broadcast([128, 32, 16]), op=ALU.is_ge),
                         [aff_t, thr_t, lo_t], [msk_t])
                    if do_ctx:
                        S.op("dve", lambda e: e.tensor_tensor(out=MSK[:, 0:2, :], in0=A_[:, 0:2, :],
                                                              in1=thr[:, 16:32].unsqueeze(1).to_broadcast([128, 2, 16]), op=ALU.is_ge),
                             [aff_t, thr_t, lo_t], [msk_t])

                for it in range(28):
                    wv = 2.0 ** -(it + 1)
                    S.op("dve", lambda e: e.tensor_scalar(out=THR[:], in0=LO[:], scalar1=wv, scalar2=None, op0=ALU.add), [lo_t], [thr_t])
                    make_mask(THR)
                    S.op("dve", lambda e: e.tensor_reduce(out=CNTP[:, 0:16], in_=MSK[:, 2:NT, :].rearrange("p j e -> p e j"),
                                                          axis=AX.X, op=ALU.add), [msk_t], [cntp_t])
                    if do_ctx:
                        S.op("dve", lambda e: e.tensor_reduce(out=CNTP[:, 16:32], in_=MSK[:, 0:2, :].rearrange("p j e -> p e j"),
                                                              axis=AX.X, op=ALU.add), [msk_t], [cntp_t])
                    S.op("pe", lambda e: e.matmul(bank(0, 0, 128, 0, 32), onesf[:], CNTP[:], start=True, stop=True),
                         [cntp_t, k_tok], [pst[0]])
                    S.op("dve", lambda e: e.tensor_tensor(out=IND[:], in0=bank(0, 0, 128, 0, 32), in1=TGT[:], op=ALU.is_ge),
                         [pst[0], kf_t], [ind_t])
                    S.op("dve", lambda e: e.scalar_tensor_tensor(out=LO[:], in0=IND[:], scalar=wv, in1=LO[:], op0=ALU.mult, op1=ALU.add),
                         [ind_t, lo_t], [lo_t])
                make_mask(LO)
                POS = sb(pf, "POS", [128, NT, NE], F32)
                OFF = sb(pf, "OFF", [128, NT, NE], F32)
                POSI = sb(pf, "POSI", [128, NT, NE], I32)
                BI = sb(pf, "BI", [128, NT, NE], I32)
                AI = sb(pf, "AI", [128, NT, NE], I32)
                BFl = sb(pf, "BFl", [128, NT, NE], F32)
                AFl = sb(pf, "AFl", [128, NT, NE], F32)
                pos_t, off_t = Tok(), Tok()
                groups = [(2, NT, 1, 2)] + ([(0, 2, 3, 3)] if do_ctx else [])
                for (j0, j1, bpw, btot) in groups:
                    n_ = (j1 - j0) * NE
                    c0 = 0 if j0 == 2 else 0
                    c1 = 0 if j0 == 2 else 64
                    mview = MSK[:, j0:j1, :].rearrange("p j e -> p (j e)")
                    S.op("pe", lambda e: e.matmul(bank(bpw, 0, 128, c0, c0 + n_), TRI[:], mview, start=True, stop=True),
                         [msk_t, kf_t], [pst[bpw]])
                    S.op("pe", lambda e: e.matmul(bank(btot, 0, 128, c1, c1 + n_), onesf[:], mview, start=True, stop=True),
                         [msk_t, k_tok], [pst[btot]])
                    S.op("dve", lambda e: e.memset(OFF[:, j0, :], 0.0), [], [off_t])
                    for jj in range(j0 + 1, j1):
                        S.op("dve", lambda e: e.tensor_tensor(out=OFF[:, jj, :], in0=OFF[:, jj - 1, :],
                                                              in1=bank(btot, 0, 128, c1 + (jj - 1 - j0) * NE, c1 + (jj - j0) * NE),
                                                              op=ALU.add), [off_t, pst[btot]], [off_t])
                    pv = POS[:, j0:j1, :].rearrange("p j e -> p (j e)")
                    S.op("dve", lambda e: e.tensor_tensor(out=pv, in0=bank(bpw, 0, 128, c0, c0 + n_),
                                                          in1=OFF[:, j0:j1, :].rearrange("p j e -> p (j e)"), op=ALU.add),
                         [pst[bpw], off_t], [pos_t])
                    S.op("dve", lambda e: e.scalar_tensor_tensor(out=pv, in0=pv, scalar=1.0, in1=mview, op0=ALU.add, op1=ALU.mult),
                         [pos_t, msk_t], [pos_t])
                    S.op("dve", lambda e: e.tensor_scalar(out=pv, in0=pv, scalar1=-1.0, scalar2=None, op0=ALU.add), [pos_t], [pos_t])
                    for (o_, i_, fn) in ((POSI, POS, None), (BI, POSI, ("and", 127)), (AI, POSI, ("shr", 7)), (BFl, BI, None), (AFl, AI, None)):
                        ov = o_[:, j0:j1, :].rearrange("p j e -> p (j e)")
                        iv = i_[:, j0:j1, :].rearrange("p j e -> p (j e)")
                        if fn is None:
                            S.op("dve", lambda e: e.tensor_copy(out=ov, in_=iv), [pos_t], [pos_t])
                        else:
                            opx = ALU.bitwise_and if fn[0] == "and" else ALU.arith_shift_right
                            S.op("dve", lambda e: e.tensor_scalar(out=ov, in0=iv, scalar1=fn[1], scalar2=None, op0=opx), [pos_t], [pos_t])
                if ("pos%d" % L) in dbg_d:
                    S.dma("sp", lambda e: e.dma_start(out=dbg_d["pos%d" % L], in_=POS[:].rearrange("p j e -> p (j e)")), [pos_t], [dbg_tok])
                    S.dma("sp", lambda e: e.dma_start(out=dbg_d["aff%d" % L], in_=A_[:].rearrange("p j e -> p (j e)")), [aff_t], [dbg_tok])
                OHB = [sb(pf, "OHB%d" % i, [128, 16, 128], F32) for i in range(2)]
                ohb_t = toks(2)
                RA = sb(pf, "RA", [128, 16, 4], F32)
                R3 = [sb(pf, "R3%d" % i, [128, 16, 3, 4], F32) for i in range(2)]
                ra_t = Tok()
                r3_t = toks(2)
                for jj in range(32):
                    b = jj % 2
                    j = 2 + jj
                    S.op("dve", lambda e: e.tensor_tensor(out=OHB[b][:], in0=IOB[:], in1=BFl[:, j, :].unsqueeze(2).to_broadcast([128, 16, 128]),
                                                          op=ALU.is_equal), [kf_t, pos_t], [ohb_t[b]])
                    S.op("dve", lambda e: e.tensor_tensor(out=RA[:], in0=IOA[:], in1=AFl[:, j, :].unsqueeze(2).to_broadcast([128, 16, 4]),
                                                          op=ALU.is_equal), [kf_t, pos_t], [ra_t])
                    S.op("dve", lambda e: e.tensor_scalar(out=R3[b][:, :, 0, :], in0=RA[:], scalar1=PCOL[:, 0:1], scalar2=None, op0=ALU.mult),
                         [ra_t, kf_t], [r3_t[b]])
                    S.op("dve", lambda e: e.tensor_scalar(out=R3[b][:, :, 1, :], in0=RA[:], scalar1=float(j), scalar2=None, op0=ALU.mult),
                         [ra_t], [r3_t[b]])
                    S.op("dve", lambda e: e.tensor_tensor(out=R3[b][:, :, 2, :], in0=RA[:],
                                                          in1=A_[:, j, :].unsqueeze(2).to_broadcast([128, 16, 4]), op=ALU.mult),
                         [ra_t, aff_t], [r3_t[b]])
                    for ex in range(NE):
                        S.op("pe", lambda e: e.matmul(bank(4, 0, 128, ex * 12, ex * 12 + 12), OHB[b][:, ex, :],
                                                      R3[b][:, ex].rearrange("p c a -> p (c a)"),
                                                      start=(jj == 0 and ex == 0), stop=(jj == 31), skip_group_check=True),
                             [ohb_t[b], r3_t[b]], [pst[4]])
                CMPS = sb(pf, "CMPS", [128, 192], F32)
                cmps_t = Tok()
                S.op("dve", lambda e: e.tensor_copy(out=CMPS[:], in_=bank(4, 0, 128, 0, 192)), [pst[4]], [cmps_t])
                CMP = CMPS[:].rearrange("p (e c a) -> p e c a", e=16, c=3)
                IDXF = sb(pf, "IDXF", [128, 16, 4], F32)
                idx_t = idx_tok[L]
                S.op("dve", lambda e: e.scalar_tensor_tensor(out=IDXF[:], in0=CMP[:, :, 1, :], scalar=128.0, in1=CMP[:, :, 0, :],
                                                             op0=ALU.mult, op1=ALU.add), [cmps_t], [idx_t])
                S.op("dve", lambda e: e.tensor_copy(out=IDX[L][:], in_=IDXF[:]), [idx_t], [idx_t])
                S.op("dve", lambda e: e.tensor_copy(out=GATE[L][:], in_=CMP[:, :, 2, :]), [cmps_t], [idx_t])
                if do_ctx:
                    OHC = sb(pf, "OHC", [128, 16, 32], F32)
                    R3C = sb(pf, "R3C", [128, 16, 3], F32)
                    ohc_t, r3c_t = Tok(), Tok()
                    for j in range(2):
                        S.op("dve", lambda e: e.tensor_tensor(out=OHC[:], in0=IOB[:, :, 0:32],
                                                              in1=BFl[:, j, :].unsqueeze(2).to_broadcast([128, 16, 32]), op=ALU.is_equal),
                             [kf_t, pos_t], [ohc_t])
                        S.op("dve", lambda e: e.tensor_copy(out=R3C[:, :, 0], in_=PCOL[:, 0:1].to_broadcast([128, 16])), [kf_t], [r3c_t])
                        S.op("dve", lambda e: e.memset(R3C[:, :, 1], float(j)), [], [r3c_t])
                        S.op("dve", lambda e: e.tensor_copy(out=R3C[:, :, 2], in_=A_[:, j, :]), [aff_t], [r3c_t])
                        for ex in range(NE):
                            S.op("pe", lambda e: e.matmul(bank(5, 0, 32, ex * 3, ex * 3 + 3), OHC[:, ex, :], R3C[:, ex, :],
                                                          start=(j == 0 and ex == 0), stop=(j == 1), skip_group_check=True),
                                 [ohc_t, r3c_t], [pst[5]])
                    S.op("dve", lambda e: e.tensor_copy(out=CMPS[0:32, 0:48], in_=bank(5, 0, 32, 0, 48)), [pst[5], cmps_t, idx_t], [cmps_t])
                    CMC = CMPS[0:32, 0:48].rearrange("p (e c) -> p e c", c=3)
                    S.op("dve", lambda e: e.scalar_tensor_tensor(out=IDXF[0:32, :, 0], in0=CMC[:, :, 1], scalar=128.0, in1=CMC[:, :, 0],
                                                                 op0=ALU.mult, op1=ALU.add), [cmps_t, idx_t], [idx_t])
                    S.op("dve", lambda e: e.tensor_copy(out=IDXC[L][0:32, :], in_=IDXF[0:32, :, 0]), [idx_t], [idx_t])
                    S.op("dve", lambda e: e.tensor_copy(out=GATEC[L][0:32, :], in_=CMC[:, :, 2]), [cmps_t], [idx_t])
                if ("idx%d" % L) in dbg_d:
                    S.dma("sp", lambda e: e.dma_start(out=dbg_d["idx%d" % L], in_=IDXF[:].rearrange("p e a -> p (e a)")), [idx_t], [dbg_tok])
                    S.dma("sp", lambda e: e.dma_start(out=dbg_d["gate%d" % L], in_=GATE[L][:].rearrange("p e a -> p (e a)")), [idx_t], [dbg_tok])
                S.barrier()
            if stop_after == ("F", L):
                break
            with ExitStack() as pg:
                idx_t = idx_tok[L]
                do_ctx = not last
                NS = 544 if do_ctx else 512
                G2 = sb(pg, "G2", [128, D], F32)
                G2C = sb(pg, "G2C", [128, D], F32)
                g2_t = Tok()
                modrow(G2, 0, 5, g2_t)
                if do_ctx:
                    modrow(G2C, 1, 5, g2_t)
                XG = [[sb(pg, "XG%d_%d" % (i, a), [128, D], BF16) for a in range(5)] for i in range(2)]
                xg_t = [toks(5) for _ in range(2)]
                XGT = [sb(pg, "XGT%d" % i, [128, 8, NS], BF16) for i in range(2)]
                xgt_t = toks(2)
                HID = sb(pg, "HID", [128, 16, NS], BF16)
                hid_t = Tok()
                WG = [sb(pg, "WG%d" % i, [128, 8, 512], BF16) for i in range(4)]
                WU = [sb(pg, "WU%d" % i, [128, 8, 512], BF16) for i in range(4)]
                wg_t, wu_t = toks(4), toks(4)
                WD = [sb(pg, "WD%d" % i, [128, 16, D], BF16) for i in range(2)]
                wd_t = toks(2)
                SIL = [sb(pg, "SIL%d" % i, [128, 512], F32) for i in range(2)]
                sil_t = toks(2)
                SILC = sb(pg, "SILC", [128, 32], F32)
                silc_t = Tok()
                YS = [sb(pg, "YS%d" % i, [128, D], F32) for i in range(2)]
                ys_t = toks(2)
                ysi = 0
                fcc = 0

                def load_gather(ex):
                    xb = ex % 2
                    for a in range(4):
                        S.dma("pool", lambda e: e.indirect_dma_start(
                            out=XG[xb][a][:], out_offset=None, in_=h2_d[:, :],
                            in_offset=bass.IndirectOffsetOnAxis(ap=IDX[L][:, ex, a:a + 1], axis=0)), h2_tok + [idx_t], [xg_t[xb][a]])
                    if do_ctx:
                        S.dma("pool", lambda e: e.indirect_dma_start(
                            out=XG[xb][4][0:32, :], out_offset=None, in_=h2_d[:, :],
                            in_offset=bass.IndirectOffsetOnAxis(ap=IDXC[L][0:32, ex:ex + 1], axis=0)), h2_tok + [idx_t], [xg_t[xb][4]])

                def load_wd(ex):
                    db = ex % 2
                    wdsrc = wd_d[L, ex].rearrange("(f p) d -> p f d", p=128)
                    for q in range(4):
                        S.dma("pool", lambda e: e.dma_start(out=WD[db][:, q * 4:(q + 1) * 4, :], in_=wdsrc[:, q * 4:(q + 1) * 4, :]),
                              [], [wd_t[db]])

                def load_gu(ex, fq):
                    wgsrc = wg_d[L, ex].rearrange("(k p) f -> p k f", p=128)
                    wusrc = wu_d[L, ex].rearrange("(k p) f -> p k f", p=128)
                    S.dma("pool", lambda e: e.dma_start(out=WG[fq][:], in_=wgsrc[:, :, fq * 512:(fq + 1) * 512]), [], [wg_t[fq]])
                    S.dma("pool", lambda e: e.dma_start(out=WU[fq][:], in_=wusrc[:, :, fq * 512:(fq + 1) * 512]), [], [wu_t[fq]])

                load_gather(0)
                load_wd(0)
                for fq in range(4):
                    load_gu(0, fq)
                load_gather(1)
                load_wd(1)
                for ex in range(NE):
                    xb = ex % 2
                    db = ex % 2
                    for a in range(4):
                        for kh in range(2):
                            for kk in range(4):
                                k = kh * 4 + kk
                                S.op("pe", lambda e: e.matmul(bank(6, 0, 128, kk * 128, (kk + 1) * 128),
                                                              XG[xb][a][:, k * 128:(k + 1) * 128], ident[:], start=True, stop=True),
                                     [xg_t[xb][a], k_tok], [pst[6]])
                            eng = "act" if (a + kh) % 2 == 0 else "dve"
                            src = bank(6).rearrange("p (k t) -> p k t", k=4)
                            dst = XGT[xb][:, kh * 4:(kh + 1) * 4, a * 128:(a + 1) * 128]
                            if eng == "act":
                                S.op("act", lambda e: e.copy(out=dst, in_=src), [pst[6]], [xgt_t[xb]])
                            else:
                                S.op("dve", lambda e: e.tensor_copy(out=dst, in_=src), [pst[6]], [xgt_t[xb]])
                    if do_ctx:
                        for k in range(8):
                            S.op("pe", lambda e: e.matmul(bank(7, 0, 128, k * 32, (k + 1) * 32), XG[xb][4][0:32, k * 128:(k + 1) * 128],
                                                          ident[0:32, 0:32], start=True, stop=True), [xg_t[xb][4], k_tok], [pst[7]])
                        S.op("dve", lambda e: e.tensor_copy(out=XGT[xb][:, :, 512:544],
                                                            in_=bank(7, 0, 128, 0, 256).rearrange("p (k t) -> p k t", k=8)),
                             [pst[7]], [xgt_t[xb]])
                    if ex + 2 < NE:
                        load_gather(ex + 2)
                    for fq in range(4):
                        wb = fq
                        for fi in range(4):
                            fc = fq * 4 + fi
                            pa = fcc % 2
                            fcc += 1
                            for (Wt, wt_t, bk) in ((WG[wb], wg_t[wb], pa), (WU[wb], wu_t[wb], 2 + pa)):
                                for k in range(8):
                                    S.op("pe", lambda e: e.matmul(bank(bk), Wt[:, k, fi * 128:(fi + 1) * 128], XGT[xb][:, k, 0:512],
                                                                  start=(k == 0), stop=(k == 7)), [wt_t, xgt_t[xb]], [pst[bk]])
                            if do_ctx:
                                for (Wt, wt_t, c0) in ((WG[wb], wg_t[wb], 256), (WU[wb], wu_t[wb], 288)):
                                    for k in range(8):
                                        S.op("pe", lambda e: e.matmul(bank(7, 0, 128, c0, c0 + 32), Wt[:, k, fi * 128:(fi + 1) * 128],
                                                                      XGT[xb][:, k, 512:544], start=(k == 0), stop=(k == 7)),
                                             [wt_t, xgt_t[xb]], [pst[7]])
                            S.op("act", lambda e: e.activation(out=SIL[pa][:], in_=bank(pa), func=AF.Silu), [pst[pa]], [sil_t[pa]])
                            S.op("dve", lambda e: e.tensor_tensor(out=HID[:, fc, 0:512], in0=bank(2 + pa), in1=SIL[pa][:], op=ALU.mult),
                                 [pst[2 + pa], sil_t[pa]], [hid_t])
                            if do_ctx:
                                S.op("act", lambda e: e.activation(out=SILC[:], in_=bank(7, 0, 128, 256, 288), func=AF.Silu),
                                     [pst[7]], [silc_t])
                                S.op("dve", lambda e: e.tensor_tensor(out=HID[:, fc, 512:544], in0=bank(7, 0, 128, 288, 320), in1=SILC[:],
                                                                      op=ALU.mult), [pst[7], silc_t], [hid_t])
                        if ex + 1 < NE:
                            load_gu(ex + 1, fq)
                    slots = [(s * 128, 128, IDX[L][:, ex, s:s + 1], GATE[L][:, ex, s:s + 1], G2) for s in range(4)]
                    if do_ctx:
                        slots.append((512, 32, IDXC[L][0:32, ex:ex + 1], GATEC[L][0:32, ex:ex + 1], G2C))
                    for (s0, sn, idx_ap, gate_ap, g2row) in slots:
                        for n in range(2):
                            for fc in range(16):
                                S.op("pe", lambda e: e.matmul(bank(4 + n, 0, sn), HID[:, fc, s0:s0 + sn], WD[db][:, fc, n * 512:(n + 1) * 512],
                                                              start=(fc == 0), stop=(fc == 15)), [hid_t, wd_t[db]], [pst[4 + n]])
                        yb = ysi % 2
                        ysi += 1
                        S.op("dve", lambda e: e.scalar_tensor_tensor(out=YS[yb][0:sn, :], in0=ps[0:sn, 4 * 512:6 * 512], scalar=gate_ap,
                                                                     in1=g2row[0:sn, :], op0=ALU.mult, op1=ALU.mult),
                             [pst[4], pst[5], idx_t, g2_t], [ys_t[yb]])
                        S.dma("pool", lambda e: e.indirect_dma_start(
                            out=acc_d[:, :], out_offset=bass.IndirectOffsetOnAxis(ap=idx_ap, axis=0), in_=YS[yb][0:sn, :], in_offset=None,
                            compute_op=ALU.add),
                            [ys_t[yb], idx_t] + acc_tok, [accs_tok])
                    if ex + 2 < NE:
                        load_wd(ex + 2)
                S.barrier()
            with ExitStack() as phh:
                LN2G = sb(phh, "LN2G", [128, D], F32)
                LN2B = sb(phh, "LN2B", [128, D], F32)
                l2_t = Tok()
                rowb(LN2G, ln2g_d[L], l2_t)
                rowb(LN2B, ln2b_d[L], l2_t)
                AT = [sb(phh, "AT%d" % i, [128, D], F32) for i in range(2)]
                at_t = toks(2)
                XO = [sb(phh, "XO%d" % i, [128, D], F32) for i in range(2)]
                xo_t = toks(2)
                ST3 = sb(phh, "ST3", [128, 2, 6], F32)
                MV3 = sb(phh, "MV3", [128, 8], F32)
                st3_t, mv3_t = Tok(), Tok()
                for j in (range(2, NT) if last else range(NT)):
                    b = j % 2
                    S.dma("sp", lambda e: e.dma_start(out=AT[b][:], in_=acc_d[j * 128:(j + 1) * 128, :]), [acc_tok[j], accs_tok], [at_t[b]])
                    for hh in range(2):
                        S.op("dve", lambda e: e.bn_stats(out=ST3[:, hh, :], in_=AT[b][:, hh * 512:(hh + 1) * 512]), [at_t[b]], [st3_t])
                    S.op("dve", lambda e: e.bn_aggr(out=MV3[:, 0:2], in_=ST3[:].rearrange("p a b -> p (a b)")), [st3_t], [mv3_t])
                    S.op("act", lambda e: e.activation(out=MV3[:, 2:3], in_=MV3[:, 1:2], func=AF.Sqrt, bias=LN_EPS, scale=1.0), [mv3_t], [mv3_t])
                    S.op("dve", lambda e: e.reciprocal(out=MV3[:, 3:4], in_=MV3[:, 2:3]), [mv3_t], [mv3_t])
                    S.op("dve", lambda e: e.tensor_scalar(out=MV3[:, 4:5], in0=MV3[:, 0:1], scalar1=MV3[:, 3:4], scalar2=-1.0,
                                                          op0=ALU.mult, op1=ALU.mult), [mv3_t], [mv3_t])
                    S.op("act", lambda e: e.activation(out=XO[b][:], in_=AT[b][:], func=AF.Identity, bias=MV3[:, 4:5], scale=MV3[:, 3:4]),
                         [mv3_t, at_t[b]], [xo_t[b]])
                    S.op("dve", lambda e: e.tensor_tensor(out=XO[b][:], in0=XO[b][:], in1=LN2G[:], op=ALU.mult), [xo_t[b], l2_t], [xo_t[b]])
                    S.op("pool", lambda e: e.tensor_tensor(out=XO[b][:], in0=XO[b][:], in1=LN2B[:], op=ALU.add), [xo_t[b], l2_t], [xo_t[b]])
                    if last:
                        S.dma("sp", lambda e: e.dma_start(out=out_d[(j - 2) * 128:(j - 1) * 128, :], in_=XO[b][:]), [xo_t[b]], [out_tok[j]])
                    else:
                        S.dma("sp", lambda e: e.dma_start(out=xcur_d[j * 128:(j + 1) * 128, :], in_=XO[b][:]), [xo_t[b]], [xcur_tok[j]])
                        if ("x2_%d" % L) in dbg_d:
                            S.dma("sp", lambda e: e.dma_start(out=dbg_d["x2_%d" % L][j * 128:(j + 1) * 128, :], in_=XO[b][:]), [xo_t[b]], [dbg_tok])
                S.barrier()
            if stop_after == ("H", L):
                break
        except _Stop:
            pass
        S.enabled = True
        S.barrier()
        print("instructions:", S.nins, "waits:", S.nwait)
    return nc


_CONST = None


def _consts():
    global _CONST
    if _CONST is not None:
        return _CONST
    bf = ml_dtypes.bfloat16
    c = {}
    c["k_ident"] = np.eye(128, dtype=np.float32).astype(bf)
    c["k_ones"] = np.ones((128, 128), np.float32)
    pp = np.arange(128)
    c["k_tri"] = (pp[:, None] < pp[None, :]).astype(np.float32)
    c["k_maskL"] = (pp[:, None] >= pp[None, :]).astype(np.float32).astype(bf)
    c["k_maskU"] = (pp[:, None] <= pp[None, :]).astype(np.float32).astype(bf)
    c["k_iotaB"] = np.ascontiguousarray(np.broadcast_to(np.arange(128, dtype=np.float32)[None, None, :], (128, 16, 128))).reshape(128, 2048)
    c["k_iotaA"] = np.ascontiguousarray(np.broadcast_to(np.arange(4, dtype=np.float32)[None, None, :], (128, 16, 4))).reshape(128, 64)
    c["k_pcol"] = np.arange(128, dtype=np.float32).reshape(128, 1)
    rows = 4096 // 64
    row = np.repeat(np.arange(rows), 64).astype(np.float32)
    col = np.tile(np.arange(64), rows).astype(np.float32)
    inv_freq = (10000.0 ** (-np.arange(0, 32, 2, dtype=np.float32) / np.float32(32))).astype(np.float32)
    ang = np.stack([row[:, None] * inv_freq, col[:, None] * inv_freq], axis=1).astype(np.float32)
    cos = np.ones((NT * 128, 32), np.float32)
    sin = np.zeros((NT * 128, 32), np.float32)
    cos[256:] = np.cos(ang).astype(np.float32).reshape(4096, 32)
    sin[256:] = np.sin(ang).astype(np.float32).reshape(4096, 32)
    c["k_cos"], c["k_sin"] = cos, sin
    t = np.arange(4096, dtype=np.int64)
    ph = (t[:, None] * t[None, :]) % 4096
    a = ph.astype(np.float64) * (2 * np.pi / 4096)
    c["k_ct"] = (np.cos(a) / 64.0).astype(np.float32).astype(bf)
    c["k_sn"] = (-np.sin(a) / 64.0).astype(np.float32).astype(bf)
    t2 = np.arange(256, dtype=np.int64)
    a2 = ((t2[:, None] * t2[None, :]) % 256).astype(np.float64) * (2 * np.pi / 256)
    c["k_ctc"] = np.concatenate([np.cos(a2) / 16.0, -np.sin(a2) / 16.0], axis=1).astype(np.float32).astype(bf)
    t3 = np.arange(64, dtype=np.int64)
    a3 = ((t3[:, None] * t3[None, :]) % 64).astype(np.float64) * (2 * np.pi / 64)
    c["k_cc"] = np.concatenate([np.cos(a3) / 8.0, np.sin(a3) / 8.0], axis=1).astype(np.float32)
    tg = np.zeros((128, 32), np.float32)
    tg[:, :16] = 512.0
    tg[:, 16:] = 32.0
    c["k_tgt"] = tg
    _CONST = c
    return c


def make_in_map(inputs, b):
    m = dict(_consts())
    m["x"] = np.ascontiguousarray(inputs["x"][b])
    m["ctx"] = np.ascontiguousarray(inputs["ctx"][b])
    cT = np.zeros((128, 8, 2), np.float32)
    cT[:, :, 0] = np.asarray(inputs["c"][b]).reshape(8, 128).T
    cT[:, :, 1] = np.asarray(inputs["c_ctx"]).reshape(8, 128).T
    m["cT"] = cT.reshape(128, 16)
    for k in ("w_mod", "b_mod", "w_in", "q_norm_a", "k_norm_a", "w_fourier", "sink_c", "w_out", "ln1_g", "ln1_b",
              "w_router", "w_gate", "w_up", "w_down", "ln2_g", "ln2_b"):
        m[k] = np.ascontiguousarray(inputs[k])
    m["b_fourier"] = np.ascontiguousarray(np.asarray(inputs["b_fourier"]).reshape(2, 256))
    m["w_in_uT"] = np.ascontiguousarray(np.transpose(np.asarray(inputs["w_in"])[:, :, 768:1024], (0, 2, 1)))
    return m


_NC = None


def _get_nc():
    global _NC
    if _NC is None:
        nc = bass.Bass("TRN2", target_bir_lowering=False)
        build(nc, n_layers=2)
        _NC = nc
    return _NC


def kernel(**inputs):
    inputs = {k: np.asarray(v) for k, v in inputs.items()}
    nc = _get_nc()
    n = 8
    in_maps = [make_in_map(inputs, b) for b in range(n)]
    res = run_bass_kernel_spmd(nc, in_maps, core_ids=list(range(n)))
    out = np.stack([np.asarray(res.results[b]["out"], dtype=np.float32) for b in range(n)], axis=0)
    return out
```

```python
import numpy as np
import ml_dtypes
from contextlib import ExitStack
import concourse.bass as bass
import concourse.mybir as mybir
from concourse.bass_utils import run_bass_kernel_spmd

F32 = mybir.dt.float32
BF16 = mybir.dt.bfloat16
I32 = mybir.dt.int32
ALU = mybir.AluOpType
AF = mybir.ActivationFunctionType
AX = mybir.AxisListType

D = 1024
NT = 34
NE = 16
FF = 2048
ALPHA = 4.0 ** 0.25
LN_EPS = 1e-5
RMS_EPS = 1e-6
C_QA, C_KA, C_QC, C_KC, C_UC, C_US, C_VA, C_VC = 0, 512, 640, 896, 1024, 1280, 1536, 1664
NCOL = 1792


class _Stop(Exception):
    pass


class Tok:
    __slots__ = ("w", "r")

    def __init__(self):
        self.w = None
        self.r = {}


def toks(n):
    return [Tok() for _ in range(n)]


class Sched:
    def __init__(self, nc, es, n_dma=44):
        self.nc = nc
        self.E = {"pe": nc.tensor, "act": nc.scalar, "dve": nc.vector, "pool": nc.gpsimd, "sp": nc.sync}
        self.sem = {k: es.enter_context(nc.semaphore("c_" + k)) for k in ("pe", "act", "dve", "pool")}
        self.cnt = {k: 0 for k in self.sem}
        self.dsem = [es.enter_context(nc.semaphore("d%d" % i)) for i in range(n_dma)]
        self.dcnt = [0] * n_dma
        self.dnext = 0
        self.dnext_sw = 0
        self.known = {k: {} for k in self.E}
        self.nwait = 0
        self.nins = 0
        self.enabled = True

    def _semobj(self, key):
        return self.sem[key] if isinstance(key, str) else self.dsem[key]

    def _deps(self, e, reads, writes):
        need = {}
        for t in reads:
            if t.w is not None:
                k, v = t.w
                if not (k == e and e == "pe"):
                    if need.get(k, 0) < v:
                        need[k] = v
        for t in writes:
            if t.w is not None:
                k, v = t.w
                if k != e and need.get(k, 0) < v:
                    need[k] = v
            for k, v in t.r.items():
                if k != e and need.get(k, 0) < v:
                    need[k] = v
        return need

    def _wait(self, e, need):
        kn = self.known[e]
        for k, v in need.items():
            if kn.get(k, 0) >= v:
                continue
            self.E[e].wait_ge(self._semobj(k), v)
            kn[k] = v
            self.nwait += 1

    def _mark(self, me, reads, writes):
        k, v = me
        for t in reads:
            if t.r.get(k, 0) < v:
                t.r[k] = v
        for t in writes:
            t.w = me
            t.r = {}

    def op(self, e, fn, reads=(), writes=()):
        if not self.enabled:
            return
        self._wait(e, self._deps(e, reads, writes))
        ins = fn(self.E[e])
        self.cnt[e] += 1
        ins.then_inc(self.sem[e], 1)
        self.nins += 1
        self._mark((e, self.cnt[e]), reads, writes)

    def dma(self, e, fn, reads=(), writes=()):
        if not self.enabled:
            return
        half = len(self.dsem) // 2
        if e == "pool":
            k = half + self.dnext_sw
            self.dnext_sw = (self.dnext_sw + 1) % (len(self.dsem) - half)
        else:
            k = self.dnext
            self.dnext = (self.dnext + 1) % half
        need = self._deps(e, reads, writes)
        if self.dcnt[k] > 0:
            need[k] = max(need.get(k, 0), 16 * self.dcnt[k])
        self._wait(e, need)
        ins = fn(self.E[e])
        self.dcnt[k] += 1
        ins.then_inc(self.dsem[k], 16)
        self.nins += 1
        self._mark((k, 16 * self.dcnt[k]), reads, writes)

    def barrier(self):
        if not self.enabled:
            return
        need = {k: v for k, v in self.cnt.items() if v > 0}
        for k in range(len(self.dsem)):
            if self.dcnt[k] > 0:
                need[k] = 16 * self.dcnt[k]
        for e in self.E:
            self._wait(e, {k: v for k, v in need.items() if k != e})


def build(nc, n_layers=2, dbg=None, stop_after=None, ml_tiles=None, stop_at=None, bc_bf16=False):
    dbg = dbg or {}
    es_top = ExitStack()
    with es_top as es:
        S = Sched(nc, es)
        E = S.E

        def chk(tag):
            if stop_at == tag:
                S.enabled = False

        def dram(name, shape, dt, kind="ExternalInput"):
            return nc.dram_tensor(name, list(shape), dt, kind=kind).ap()

        x_d = dram("x", [4096, D], F32)
        ctx_d = dram("ctx", [256, D], F32)
        cT_d = dram("cT", [128, 16], F32)
        w_mod_d = dram("w_mod", [2, D, 6 * D], F32)
        b_mod_d = dram("b_mod", [2, 6 * D], F32)
        w_in_d = dram("w_in", [2, D, 1536], F32)
        w_in_uT_d = dram("w_in_uT", [2, 256, D], F32)
        qn_d = dram("q_norm_a", [2, 64], F32)
        kn_d = dram("k_norm_a", [2, 64], F32)
        wf_d = dram("w_fourier", [2, 4, 64, 64], F32)
        bf_d = dram("b_fourier", [2, 256], F32)
        sink_d = dram("sink_c", [2, 4], F32)
        w_out_d = dram("w_out", [2, D, D], F32)
        ln1g_d = dram("ln1_g", [2, D], F32)
        ln1b_d = dram("ln1_b", [2, D], F32)
        wr_d = dram("w_router", [2, D, NE], F32)
        wg_d = dram("w_gate", [2, NE, D, FF], F32)
        wu_d = dram("w_up", [2, NE, D, FF], F32)
        wd_d = dram("w_down", [2, NE, FF, D], F32)
        ln2g_d = dram("ln2_g", [2, D], F32)
        ln2b_d = dram("ln2_b", [2, D], F32)
        ident_d = dram("k_ident", [128, 128], BF16)
        onesf_d = dram("k_ones", [128, 128], F32)
        tri_d = dram("k_tri", [128, 128], F32)
        maskL_d = dram("k_maskL", [128, 128], BF16)
        maskU_d = dram("k_maskU", [128, 128], BF16)
        iotaB_d = dram("k_iotaB", [128, 16 * 128], F32)
        iotaA_d = dram("k_iotaA", [128, 64], F32)
        pcol_d = dram("k_pcol", [128, 1], F32)
        cos_d = dram("k_cos", [NT * 128, 32], F32)
        sin_d = dram("k_sin", [NT * 128, 32], F32)
        ct_d = dram("k_ct", [4096, 4096], BF16)
        sn_d = dram("k_sn", [4096, 4096], BF16)
        ctc_d = dram("k_ctc", [256, 512], BF16)
        cc_d = dram("k_cc", [64, 128], F32)
        tgt_d = dram("k_tgt", [128, 32], F32)
        out_d = dram("out", [4096, D], F32, kind="ExternalOutput")
        dbg_d = {k: dram("dbg_" + k, shp, F32, kind="ExternalOutput") for k, shp in dbg.items()}
        mod_d = dram("s_mod", [2, 2, 6 * D], F32, kind="Internal")
        xcur_d = dram("s_xcur", [NT * 128, D], F32, kind="Internal")
        h2_d = dram("s_h2", [NT * 128, D], BF16, kind="Internal")
        acc_d = dram("s_acc", [NT * 128, D], F32, kind="Internal")
        mod_tok = toks(2)
        xcur_tok = toks(NT)
        h2_tok = toks(NT)
        acc_tok = toks(NT)
        accs_tok = Tok()
        out_tok = toks(NT)
        dbg_tok = Tok()

        uid = [0]

        def sb(stack, name, shape, dt):
            uid[0] += 1
            return stack.enter_context(nc.sbuf_tensor("sb%d_%s" % (uid[0], name), list(shape), dt))

        ps = es.enter_context(nc.psum_tensor("ps", [128, 4096], F32))
        pst = toks(8)

        def bank(b, p0=0, p1=128, c0=0, c1=512):
            return ps[p0:p1, b * 512 + c0:b * 512 + c1]

        ident = sb(es, "ident", [128, 128], BF16)
        onesf = sb(es, "onesf", [128, 128], F32)
        cT = sb(es, "cT", [128, 16], F32)
        cTs = sb(es, "cTs", [128, 16], F32)
        k_tok = Tok()
        S.dma("sp", lambda e: e.dma_start(out=ident[:], in_=ident_d), [], [k_tok])
        S.dma("sp", lambda e: e.dma_start(out=onesf[:], in_=onesf_d), [], [k_tok])
        S.dma("sp", lambda e: e.dma_start(out=cT[:], in_=cT_d), [], [k_tok])
        S.op("act", lambda e: e.activation(out=cTs[:], in_=cT[:], func=AF.Silu), [k_tok], [k_tok])

        AFF = [sb(es, "AFF", [128, NT, NE], F32)] * 2
        IDX = [sb(es, "IDX", [128, 16, 4], I32)] * 2
        GATE = [sb(es, "GATE", [128, 16, 4], F32)] * 2
        IDXC = [sb(es, "IDXC", [128, 16], I32)] * 2
        GATEC = [sb(es, "GATEC", [128, 16], F32)] * 2
        idx_tok = toks(2)
        aff_t = Tok()

        def dump(name, src_ap, dst_ap, rtoks):
            if name in dbg_d:
                S.dma("sp", lambda e: e.dma_start(out=dst_ap, in_=src_ap), list(rtoks), [dbg_tok])

        try:
          for L in range(n_layers):
            last = (L == n_layers - 1)
            with ExitStack() as ph:
                wm = [sb(ph, "wm%d" % i, [128, 8, 512], F32) for i in range(6)]
                wm_t = toks(6)
                bm = sb(ph, "bm", [2, 6 * D], F32)
                md = sb(ph, "md", [2, 6 * D], F32)
                bm_t, md_t = Tok(), Tok()
                S.dma("sp", lambda e: e.dma_start(out=bm[:], in_=b_mod_d[L].partition_broadcast(2)), [], [bm_t])
                wsrc = w_mod_d[L].rearrange("(k p) n -> p k n", p=128)
                for n in range(6):
                    S.dma("sp", lambda e: e.dma_start(out=wm[n][:], in_=wsrc[:, :, n * 512:(n + 1) * 512]), [], [wm_t[n]])
                for n in range(12):
                    b = n % 2
                    wb_ = n % 6
                    for k in range(8):
                        S.op("pe", lambda e: e.matmul(bank(b, 0, 2), cTs[:, 2 * k:2 * k + 2], wm[wb_][:, k, :],
                                                      start=(k == 0), stop=(k == 7)),
                             [k_tok, wm_t[wb_]], [pst[b]])
                    if n + 6 < 12:
                        S.dma("sp", lambda e: e.dma_start(out=wm[wb_][:], in_=wsrc[:, :, (n + 6) * 512:(n + 7) * 512]), [], [wm_t[wb_]])
                    S.op("dve", lambda e: e.tensor_tensor(out=md[:, n * 512:(n + 1) * 512], in0=bank(b, 0, 2),
                                                          in1=bm[:, n * 512:(n + 1) * 512], op=ALU.add),
                         [pst[b], bm_t], [md_t])
                for c in (1, 4):
                    S.op("dve", lambda e: e.tensor_scalar(out=md[:, c * D:(c + 1) * D], in0=md[:, c * D:(c + 1) * D],
                                                          scalar1=1.0, scalar2=None, op0=ALU.add), [md_t], [md_t])
                S.dma("sp", lambda e: e.dma_start(out=mod_d[L], in_=md[:]), [md_t], [mod_tok[L]])
                dump("mod%d" % L, md[:], dbg_d.get("mod%d" % L), [md_t])
                S.barrier()
            if stop_after == ("mod", L):
                break

            def modrow(dst, which, chunk, tok):
                src = mod_d[L, which, chunk * D:(chunk + 1) * D].partition_broadcast(128)
                S.dma("sp", lambda e: e.dma_start(out=dst[:], in_=src), [mod_tok[L]], [tok])

            def rowb(dst, src_row, tok):
                S.dma("sp", lambda e: e.dma_start(out=dst[:], in_=src_row.partition_broadcast(128)), [], [tok])

            def src_rows(j):
                if L == 0:
                    return (ctx_d[j * 128:(j + 1) * 128, :] if j < 2 else x_d[(j - 2) * 128:(j - 1) * 128, :]), []
                return xcur_d[j * 128:(j + 1) * 128, :], [xcur_tok[j]]

            with ExitStack() as lay:
                QA = sb(lay, "QA", [128, NT, 4, 128], BF16)
                KA = sb(lay, "KA", [128, NT * 128], BF16)
                VA = sb(lay, "VA", [128, NT, 2, 65], BF16)
                QC = sb(lay, "QC", [128, NT, 2, 128], BF16)
                KC = sb(lay, "KC", [128, NT * 128], BF16)
                VC = sb(lay, "VC", [128, NT, 2, 65], BF16)
                OBs = sb(lay, "OBs", [128, NT, 256], BF16)
                qa_t, ka_t, va_t, qc_t, kc_t, vc_t, ob_t = (toks(NT) for _ in range(7))
                vinit = Tok()
                S.op("pool", lambda e: e.memset(VA[:], 1.0), [], [vinit])
                S.op("pool", lambda e: e.memset(VC[:], 1.0), [], [vinit])

                with ExitStack() as ph_outer:
                  U2 = sb(ph_outer, "U2", [128, NT, 512], BF16)
                  u2_t = toks(NT)
                  with ExitStack() as ph:
                    WINB = sb(ph, "WINB", [128, 8, NCOL], BF16)
                    winb_t = Tok()
                    wsrc = w_in_d[L].rearrange("(k p) n -> p k n", p=128)
                    for (dc, sc, wd) in ((C_QA, 0, 512), (C_KA, 1024, 128), (C_QC, 512, 256), (C_KC, 1280, 128),
                                         (C_VA, 1152, 128), (C_VC, 1408, 128)):
                        S.dma("pool", lambda e: e.dma_start(out=WINB[:, :, dc:dc + wd], in_=wsrc[:, :, sc:sc + wd]),
                              [], [winb_t])
                    with ExitStack() as ff:
                        CC = sb(ff, "CC", [64, 128], F32)
                        WF = sb(ff, "WF", [64, 4, 64], F32)
                        MCS = sb(ff, "MCS", [64, 2, 4, 64], F32)
                        WUT = sb(ff, "WUT", [64, 4, D], F32)
                        f_t = Tok()
                        S.dma("sp", lambda e: e.dma_start(out=CC[:], in_=cc_d), [], [f_t])
                        S.dma("sp", lambda e: e.dma_start(out=WF[:], in_=wf_d[L].rearrange("g c d -> c g d")), [], [f_t])
                        S.dma("sp", lambda e: e.dma_start(out=WUT[:], in_=w_in_uT_d[L].rearrange("(g c) d -> c g d", c=64)),
                              [], [f_t])
                        for cs in range(2):
                            S.op("pe", lambda e: e.matmul(bank(cs, 0, 64, 0, 256), CC[:, cs * 64:(cs + 1) * 64],
                                                          WF[:].rearrange("c g d -> c (g d)"), start=True, stop=True),
                                 [f_t], [pst[cs]])
                            S.op("dve", lambda e: e.tensor_copy(out=MCS[:, cs].rearrange("c g d -> c (g d)"),
                                                                in_=bank(cs, 0, 64, 0, 256)), [pst[cs]], [f_t])
                        for k in range(8):
                            b = 2 + (k % 2)
                            for cs in range(2):
                                for g in range(4):
                                    c0 = cs * 256 + g * 64
                                    S.op("pe", lambda e: e.matmul(bank(b, 0, 128, c0, c0 + 64),
                                                                  WUT[:, g, k * 128:(k + 1) * 128], MCS[:, cs, g, :],
                                                                  start=True, stop=True), [f_t], [pst[b]])
                            S.op("dve", lambda e: e.tensor_copy(out=WINB[:, k, C_UC:C_UC + 512], in_=bank(b)),
                                 [pst[b]], [winb_t])
                        S.barrier()
                    SH1 = sb(ph, "SH1", [128, D], F32)
                    SC1 = sb(ph, "SC1", [128, D], F32)
                    GQK = sb(ph, "GQK", [128, 10, 64], F32)
                    mr_t = Tok()
                    g_t = Tok()
                    for h in range(8):
                        rowb(GQK[:, h, :], qn_d[L], g_t)
                    for h in range(8, 10):
                        rowb(GQK[:, h, :], kn_d[L], g_t)
                    XT = [sb(ph, "XT%d" % i, [128, D], F32) for i in range(2)]
                    xt_t = toks(2)
                    XN = sb(ph, "XN", [128, D], F32)
                    Hb = sb(ph, "Hb", [128, D], BF16)
                    HT = sb(ph, "HT", [128, 8, 128], BF16)
                    ST = sb(ph, "ST", [128, 2, 6], F32)
                    MV = sb(ph, "MV", [128, 8], F32)
                    SQ = sb(ph, "SQ", [128, 640], F32)
                    MS = sb(ph, "MS", [128, 16], F32)
                    NRM = SQ[:].rearrange("p (h d) -> p h d", d=64)
                    R = sb(ph, "R", [128, 16, 64], F32)
                    RO = sb(ph, "RO", [128, 16, 64], BF16)
                    T1 = sb(ph, "T1", [128, 16, 2, 16], F32)
                    T2 = sb(ph, "T2", [128, 16, 2, 16], F32)
                    CS = [sb(ph, "CS%d" % i, [128, 2, 32], F32) for i in range(2)]
                    cs_t = toks(2)
                    xn_t, hb_t, ht_t, st_t, mv_t, sq_t, ms_t, nrm_t, r_t, ro_t, tt_t = (Tok() for _ in range(11))
                    for j in range(NT):
                        if j == 0 or j == 2:
                            w = 1 if j == 0 else 0
                            modrow(SH1, w, 0, mr_t)
                            modrow(SC1, w, 1, mr_t)
                        b = j % 2
                        rows, rt = src_rows(j)
                        S.dma("sp", lambda e: e.dma_start(out=XT[b][:], in_=rows), rt, [xt_t[b]])
                        S.dma("sp", lambda e: e.dma_start(out=CS[b][:, 0, :], in_=cos_d[j * 128:(j + 1) * 128, :]), [], [cs_t[b]])
                        S.dma("sp", lambda e: e.dma_start(out=CS[b][:, 1, :], in_=sin_d[j * 128:(j + 1) * 128, :]), [], [cs_t[b]])
                        xt = XT[b]
                        for hh in range(2):
                            S.op("dve", lambda e: e.bn_stats(out=ST[:, hh, :], in_=xt[:, hh * 512:(hh + 1) * 512]),
                                 [xt_t[b]], [st_t])
                        S.op("dve", lambda e: e.bn_aggr(out=MV[:, 0:2], in_=ST[:].rearrange("p a b -> p (a b)")),
                             [st_t], [mv_t])
                        S.op("act", lambda e: e.activation(out=MV[:, 2:3], in_=MV[:, 1:2], func=AF.Sqrt, bias=LN_EPS,
                                                           scale=1.0), [mv_t], [mv_t])
                        S.op("dve", lambda e: e.reciprocal(out=MV[:, 3:4], in_=MV[:, 2:3]), [mv_t], [mv_t])
                        S.op("dve", lambda e: e.tensor_scalar(out=MV[:, 4:5], in0=MV[:, 0:1], scalar1=MV[:, 3:4],
                                                              scalar2=-1.0, op0=ALU.mult, op1=ALU.mult), [mv_t], [mv_t])
                        S.op("act", lambda e: e.activation(out=XN[:], in_=xt[:], func=AF.Identity, bias=MV[:, 4:5],
                                                           scale=MV[:, 3:4]), [mv_t, xt_t[b]], [xn_t])
                        S.op("dve", lambda e: e.tensor_tensor(out=XN[:], in0=XN[:], in1=SC1[:], op=ALU.mult),
                             [xn_t, mr_t], [xn_t])
                        S.op("dve", lambda e: e.tensor_tensor(out=Hb[:], in0=XN[:], in1=SH1[:], op=ALU.add),
                             [xn_t, mr_t], [hb_t])
                        for k in range(8):
                            S.op("pe", lambda e: e.matmul(bank(k // 4, 0, 128, (k % 4) * 128, (k % 4) * 128 + 128),
                                                          Hb[:, k * 128:(k + 1) * 128], ident[:], start=True, stop=True),
                                 [hb_t, k_tok], [pst[k // 4]])
                        S.op("act", lambda e: e.copy(out=HT[:, 0:4, :].rearrange("p a b -> p (a b)"), in_=bank(0)),
                             [pst[0]], [ht_t])
                        S.op("dve", lambda e: e.tensor_copy(out=HT[:, 4:8, :].rearrange("p a b -> p (a b)"), in_=bank(1)),
                             [pst[1]], [ht_t])
                        for cg in range(4):
                            c0, c1 = cg * 512, min(NCOL, cg * 512 + 512)
                            for k in range(8):
                                S.op("pe", lambda e: e.matmul(bank(2 + cg, 0, 128, 0, c1 - c0), HT[:, k, :],
                                                              WINB[:, k, c0:c1], start=(k == 0), stop=(k == 7)),
                                     [ht_t, winb_t], [pst[2 + cg]])
                        P = ps[:, 2 * 512:2 * 512 + NCOL]
                        if ("p%d" % L) in dbg_d and j == 2:
                            for (a0, a1) in ((0, 1024), (1024, NCOL)):
                                S.op("dve", lambda e: e.tensor_copy(out=XN[:, 0:a1 - a0], in_=P[:, a0:a1]),
                                     [pst[2], pst[3], pst[4], pst[5]], [xn_t])
                                S.dma("sp", lambda e: e.dma_start(out=dbg_d["p%d" % L][:, a0:a1], in_=XN[:, 0:a1 - a0]),
                                      [xn_t], [dbg_tok])
                        S.op("act", lambda e: e.activation(out=SQ[:], in_=P[:, 0:640], func=AF.Square),
                             [pst[2], pst[3]], [sq_t])
                        S.op("dve", lambda e: e.tensor_reduce(out=MS[:, 0:10], in_=SQ[:].rearrange("p (h d) -> p h d", d=64),
                                                              axis=AX.X, op=ALU.add), [sq_t], [ms_t])
                        S.op("act", lambda e: e.activation(out=MS[:, 0:10], in_=MS[:, 0:10], func=AF.Sqrt, bias=RMS_EPS,
                                                           scale=1.0 / 64.0), [ms_t], [ms_t])
                        S.op("dve", lambda e: e.reciprocal(out=MS[:, 0:10], in_=MS[:, 0:10]), [ms_t], [ms_t])
                        S.op("dve", lambda e: e.tensor_tensor(out=NRM, in0=P[:, 0:640].rearrange("p (h d) -> p h d", d=64),
                                                              in1=MS[:, 0:10].unsqueeze(2).to_broadcast([128, 10, 64]),
                                                              op=ALU.mult), [ms_t, pst[2], pst[3]], [nrm_t])
                        S.op("dve", lambda e: e.tensor_tensor(
                            out=R[:, 0:8, :].rearrange("p (g kv) d -> p kv g d", kv=2),
                            in0=NRM[:, 0:8, :].rearrange("p (kv g) d -> p kv g d", kv=2),
                            in1=GQK[:, 0:8, :].rearrange("p (kv g) d -> p kv g d", kv=2), op=ALU.mult),
                            [nrm_t, g_t], [r_t])
                        S.op("dve", lambda e: e.tensor_tensor(out=R[:, 8:10, :], in0=NRM[:, 8:10, :], in1=GQK[:, 8:10, :],
                                                              op=ALU.mult), [nrm_t, g_t], [r_t])
                        S.op("act", lambda e: e.copy(
                            out=R[:, 10:14, :].rearrange("p (g kv) d -> p kv g d", kv=2),
                            in_=P[:, C_QC:C_QC + 256].rearrange("p (kv g d) -> p kv g d", kv=2, g=2)), [pst[3]], [r_t])
                        S.op("act", lambda e: e.copy(out=R[:, 14:16, :].rearrange("p h d -> p (h d)"),
                                                     in_=P[:, C_KC:C_KC + 128]), [pst[3]], [r_t])
                        Rv = R[:].rearrange("p h (a b f) -> p h a b f", a=2, b=2)
                        ROv = RO[:].rearrange("p h (a b f) -> p h a b f", a=2, b=2)
                        x1, x2 = Rv[:, :, :, 0, :], Rv[:, :, :, 1, :]
                        cosb = CS[b][:, 0, :].rearrange("p (a f) -> p a f", a=2).unsqueeze(1).to_broadcast([128, 16, 2, 16])
                        sinb = CS[b][:, 1, :].rearrange("p (a f) -> p a f", a=2).unsqueeze(1).to_broadcast([128, 16, 2, 16])
                        S.op("dve", lambda e: e.tensor_tensor(out=T1[:], in0=x1, in1=cosb, op=ALU.mult), [r_t, cs_t[b]], [tt_t])
                        S.op("dve", lambda e: e.tensor_tensor(out=T2[:], in0=x2, in1=sinb, op=ALU.mult), [r_t, cs_t[b]], [tt_t])
                        S.op("dve", lambda e: e.tensor_tensor(out=ROv[:, :, :, 0, :], in0=T1[:], in1=T2[:], op=ALU.subtract),
                             [tt_t], [ro_t])
                        S.op("dve", lambda e: e.tensor_tensor(out=T1[:], in0=x2, in1=cosb, op=ALU.mult), [r_t, cs_t[b]], [tt_t])
                        S.op("dve", lambda e: e.tensor_tensor(out=T2[:], in0=x1, in1=sinb, op=ALU.mult), [r_t, cs_t[b]], [tt_t])
                        S.op("dve", lambda e: e.tensor_tensor(out=ROv[:, :, :, 1, :], in0=T1[:], in1=T2[:], op=ALU.add),
                             [tt_t], [ro_t])
                        for blk in range(8):
                            S.op("pe", lambda e: e.matmul(bank(blk // 4, 0, 128, (blk % 4) * 128, (blk % 4) * 128 + 128),
                                                          RO[:, 2 * blk:2 * blk + 2, :].rearrange("p h d -> p (h d)"),
                                                          ident[:], start=True, stop=True), [ro_t, k_tok], [pst[blk // 4]])
                        S.op("act", lambda e: e.copy(out=QA[:, j].rearrange("p g t -> p (g t)"), in_=bank(0)),
                             [pst[0]], [qa_t[j]])
                        S.op("dve", lambda e: e.tensor_copy(out=KA[:, j * 128:(j + 1) * 128], in_=bank(1, 0, 128, 0, 128)),
                             [pst[1]], [ka_t[j]])
                        S.op("dve", lambda e: e.tensor_copy(out=QC[:, j].rearrange("p g t -> p (g t)"),
                                                            in_=bank(1, 0, 128, 128, 384)), [pst[1]], [qc_t[j]])
                        S.op("dve", lambda e: e.tensor_copy(out=KC[:, j * 128:(j + 1) * 128], in_=bank(1, 0, 128, 384, 512)),
                             [pst[1]], [kc_t[j]])
                        S.op("act", lambda e: e.copy(out=VA[:, j, :, 1:65],
                                                     in_=P[:, C_VA:C_VA + 128].rearrange("p (h d) -> p h d", d=64)),
                             [pst[5], vinit], [va_t[j]])
                        S.op("act", lambda e: e.copy(out=VC[:, j, :, 1:65],
                                                     in_=P[:, C_VC:C_VC + 128].rearrange("p (h d) -> p h d", d=64)),
                             [pst[5], vinit], [vc_t[j]])
                        S.op("dve", lambda e: e.tensor_copy(out=U2[:, j, :], in_=P[:, C_UC:C_UC + 512]), [pst[4]], [u2_t[j]])
                    S.barrier()
                  if True:
                    with ExitStack() as pd:
                        BFR = sb(pd, "BFR", [128, 256], F32)
                        bfr_t = Tok()
                        rowb(BFR, bf_d[L], bfr_t)
                        TB = [sb(pd, "TB%d" % i, [128, 2, 2048], BF16) for i in range(6)]
                        tb_t = toks(6)
                        it = 0
                        for half in range(2):
                            for tc in range(32):
                                b = it % 6
                                it += 1
                                S.dma("sp", lambda e: e.dma_start(out=TB[b][:, 0, :],
                                                                  in_=ct_d[tc * 128:(tc + 1) * 128, half * 2048:(half + 1) * 2048]),
                                      [], [tb_t[b]])
                                S.dma("sp", lambda e: e.dma_start(out=TB[b][:, 1, :],
                                                                  in_=sn_d[tc * 128:(tc + 1) * 128, half * 2048:(half + 1) * 2048]),
                                      [], [tb_t[b]])
                                for kc in range(16):
                                    for cs in range(2):
                                        S.op("pe", lambda e: e.matmul(
                                            bank(kc // 2, 0, 128, (kc % 2) * 256, (kc % 2) * 256 + 256),
                                            TB[b][:, cs, kc * 128:(kc + 1) * 128], U2[:, 2 + tc, cs * 256:(cs + 1) * 256],
                                            start=(tc == 0 and cs == 0 and kc % 2 == 0), stop=(tc == 31 and cs == 1),
                                            skip_group_check=True),
                                            [tb_t[b], u2_t[2 + tc]], [pst[kc // 2]])
                            for kc in range(16):
                                jj = 2 + half * 16 + kc
                                S.op("dve", lambda e: e.tensor_tensor(
                                    out=OBs[:, jj, :], in0=bank(kc // 2, 0, 128, (kc % 2) * 256, (kc % 2) * 256 + 256),
                                    in1=BFR[:], op=ALU.add), [pst[kc // 2], bfr_t], [ob_t[jj]])
                        if not last:
                            TBC = sb(pd, "TBC", [128, 2, 512], BF16)
                            tbc_t = Tok()
                            for tc in range(2):
                                S.dma("sp", lambda e: e.dma_start(out=TBC[:, tc, :], in_=ctc_d[tc * 128:(tc + 1) * 128, :]),
                                      [], [tbc_t])
                            for kc in range(2):
                                for tc in range(2):
                                    for cs in range(2):
                                        S.op("pe", lambda e: e.matmul(
                                            bank(0, 0, 128, kc * 256, kc * 256 + 256),
                                            TBC[:, tc, cs * 256 + kc * 128:cs * 256 + kc * 128 + 128],
                                            U2[:, tc, cs * 256:(cs + 1) * 256],
                                            start=(tc == 0 and cs == 0 and kc == 0), stop=(tc == 1 and cs == 1),
                                            skip_group_check=True),
                                            [tbc_t, u2_t[tc]], [pst[0]])
                                S.op("dve", lambda e: e.tensor_tensor(out=OBs[:, kc, :], in0=bank(0, 0, 128, kc * 256, kc * 256 + 256),
                                                                      in1=BFR[:], op=ALU.add), [pst[0], bfr_t], [ob_t[kc]])
                        S.barrier()
                if ("QA%d" % L) in dbg_d:
                    with ExitStack() as dd:
                        TMPD = sb(dd, "TMPD", [128, 4352], F32)
                        td = Tok()
                        for nm, src in (("KA", KA[:]), ("KC", KC[:])):
                            S.op("dve", lambda e: e.tensor_copy(out=TMPD[:], in_=src), ka_t + kc_t, [td])
                            dump(nm + "%d" % L, TMPD[:], dbg_d.get(nm + "%d" % L), [td])
                        for nm, src in (("QA", QA[:, 2].rearrange("p g t -> p (g t)")), ("OB", OBs[:, 2, :]),
                                        ("VA", VA[:, 2].rearrange("p h d -> p (h d)"))):
                            n = src.shape[1]
                            S.op("dve", lambda e: e.tensor_copy(out=TMPD[:, 0:n], in_=src), qa_t + ob_t + va_t, [td])
                            dump(nm + "%d" % L, TMPD[:, 0:n], dbg_d.get(nm + "%d" % L), [td])
                        S.barrier()
                if stop_after == ("A", L):
                    break
                with ExitStack() as ml:
                    WOA = sb(ml, "WOA", [128, 8, D], BF16)
                    WOB = sb(ml, "WOB", [128, 2, D], BF16)
                    WOC = sb(ml, "WOC", [128, 4, D], BF16)
                    WR = sb(ml, "WR", [128, 8, NE], BF16)
                    w_t = Tok()
                    S.op("pool", lambda e: e.memset(WOA[:], 0.0), [], [w_t])
                    S.op("pool", lambda e: e.memset(WOC[:], 0.0), [], [w_t])
                    S.dma("pool", lambda e: e.dma_start(out=WOA[1:65], in_=w_out_d[L, 0:512, :].rearrange("(h p) n -> p h n", p=64)), [w_t], [w_t])
                    S.dma("pool", lambda e: e.dma_start(out=WOB[:], in_=w_out_d[L, 512:768, :].rearrange("(h p) n -> p h n", p=128)), [], [w_t])
                    S.dma("pool", lambda e: e.dma_start(out=WOC[1:65], in_=w_out_d[L, 768:1024, :].rearrange("(h p) n -> p h n", p=64)), [w_t], [w_t])
                    S.dma("pool", lambda e: e.dma_start(out=WR[:], in_=wr_d[L].rearrange("(k p) n -> p k n", p=128)), [], [w_t])
                    G1 = sb(ml, "G1", [128, D], F32)
                    SC2 = sb(ml, "SC2", [128, D], F32)
                    SH2 = sb(ml, "SH2", [128, D], F32)
                    LN1G = sb(ml, "LN1G", [128, D], F32)
                    LN1B = sb(ml, "LN1B", [128, D], F32)
                    mr_t, ln_t = Tok(), Tok()
                    rowb(LN1G, ln1g_d[L], ln_t)
                    rowb(LN1B, ln1b_d[L], ln_t)
                    MKL = sb(ml, "MKL", [128, 128], BF16)
                    MKU = sb(ml, "MKU", [128, 128], BF16)
                    SINKE = sb(ml, "SINKE", [128, 4, 128], F32)
                    SK4 = sb(ml, "SK4", [128, 4], F32)
                    mk_t, sk_t = Tok(), Tok()
                    S.dma("sp", lambda e: e.dma_start(out=MKL[:], in_=maskL_d), [], [mk_t])
                    S.dma("sp", lambda e: e.dma_start(out=MKU[:], in_=maskU_d), [], [mk_t])
                    S.dma("sp", lambda e: e.dma_start(out=SK4[0:1, :], in_=sink_d[L:L + 1, :]), [], [sk_t])
                    S.op("act", lambda e: e.activation(out=SK4[0:1, :], in_=SK4[0:1, :], func=AF.Exp), [sk_t], [sk_t])
                    S.op("dve", lambda e: e.tensor_copy(out=SINKE[0:1, :, :], in_=SK4[0:1, :].unsqueeze(2).to_broadcast([1, 4, 128])),
                         [sk_t], [sk_t])
                    PT = [sb(ml, "PT%d" % i, [128, 512], BF16) for i in range(3)]
                    pt_t = toks(3)
                    PTC = [sb(ml, "PTC%d" % i, [128, 512], BF16) for i in range(2)]
                    ptc_t = toks(2)
                    REC = sb(ml, "REC", [128, 1024], F32)
                    BCS = sb(ml, "BCS", [128, 1024], F32)
                    RECC = REC[:, 0:512]
                    BCC = BCS[:, 0:512]
                    CATA = sb(ml, "CATA", [128, 8, 128], BF16)
                    CATC = sb(ml, "CATC", [128, 4, 128], BF16)
                    CATB = sb(ml, "CATB", [128, 2, 128], BF16)
                    rec_t, bcs_t, cata_t, catc_t, catb_t = (Tok() for _ in range(5))
                    recc_t, bcc_t = rec_t, bcs_t
                    S.op("pool", lambda e: e.memset(CATA[:], 0.0), [], [cata_t])
                    S.op("pool", lambda e: e.memset(CATC[:], 0.0), [], [catc_t])
                    QZ = [sb(ml, "QZ%d" % i, [128, 2, 512], BF16) for i in range(2)]
                    QCZ = [sb(ml, "QCZ%d" % i, [128, 2, 256], BF16) for i in range(2)]
                    qz_t, qcz_t = toks(2), toks(2)
                    for i in range(2):
                        S.op("pool", lambda e: e.memset(QZ[i][:], 0.0), [], [qz_t[i]])
                        S.op("pool", lambda e: e.memset(QCZ[i][:], 0.0), [], [qcz_t[i]])
                    XT2 = [sb(ml, "XT20", [128, D], F32)] * 2
                    xt2_t = [Tok()] * 2
                    TMP = sb(ml, "TMP", [128, D], F32)
                    RR = sb(ml, "RR", [128, D], F32)
                    XN2 = sb(ml, "XN2", [128, D], F32)
                    ACC = TMP
                    H2 = sb(ml, "H2", [128, D], BF16)
                    HT2 = sb(ml, "HT2", [128, 8, 128], BF16)
                    ST2 = sb(ml, "ST2", [128, 2, 6], F32)
                    MV2 = sb(ml, "MV2", [128, 8], F32)
                    LG = sb(ml, "LG", [128, NE], F32)
                    SM = sb(ml, "SM", [128, 4], F32)
                    tmp_t, rr_t, xn2_t, h2b_t, ht2_t, st2_t, mv2_t, lg_t, sm_t = (Tok() for _ in range(9))
                    accb_t = tmp_t

                    NEGH = sb(ml, "NEGH", [128, 1], F32)
                    ngh_t = Tok()
                    S.op("pool", lambda e: e.memset(NEGH[:], -0.5), [], [ngh_t])

                    def ln_norm(dst, src, src_toks, dst_tok):
                        for hh in range(2):
                            S.op("dve", lambda e: e.bn_stats(out=ST2[:, hh, :], in_=src[:, hh * 512:(hh + 1) * 512]),
                                 src_toks, [st2_t])
                        S.op("dve", lambda e: e.bn_aggr(out=MV2[:, 0:2], in_=ST2[:].rearrange("p a b -> p (a b)")),
                             [st2_t], [mv2_t])
                        S.op("pool", lambda e: e.tensor_scalar(out=MV2[:, 2:3], in0=MV2[:, 1:2], scalar1=LN_EPS, scalar2=None,
                                                               op0=ALU.add), [mv2_t], [mv2_t])
                        S.op("pool", lambda e: e.tensor_tensor(out=MV2[:, 3:4], in0=MV2[:, 2:3], in1=NEGH[:], op=ALU.pow),
                             [mv2_t, ngh_t], [mv2_t])
                        S.op("dve", lambda e: e.tensor_scalar(out=dst[:], in0=src[:], scalar1=MV2[:, 0:1], scalar2=MV2[:, 3:4],
                                                              op0=ALU.subtract, op1=ALU.mult), [mv2_t] + list(src_toks), [dst_tok])

                    chk("ML_SETUP")
                    tiles = list(range(2, NT)) if last else list(range(NT))
                    if ml_tiles is not None:
                        tiles = list(ml_tiles)
                    CATA2 = [CATA, sb(ml, "CATA1", [128, 8, 128], BF16)]
                    CATC2 = [CATC, sb(ml, "CATC1", [128, 4, 128], BF16)]
                    CATB2 = [CATB, sb(ml, "CATB1", [128, 2, 128], BF16)]
                    H22 = [H2, sb(ml, "H21", [128, D], BF16)]
                    cata2_t, catc2_t, catb2_t, h22_t = [cata_t, Tok()], [catc_t, Tok()], [catb_t, Tok()], [h2b_t, Tok()]
                    S.op("pool", lambda e: e.memset(CATA2[1][:], 0.0), [], [cata2_t[1]])
                    S.op("pool", lambda e: e.memset(CATC2[1][:], 0.0), [], [catc2_t[1]])
                    state = {"pi": 0, "pc": 0}

                    def emit_QZ(j, b):
                        for h in range(2):
                            S.op("pool", lambda e: e.tensor_copy(out=QZ[b][64 * h:64 * h + 64, h, :],
                                                                 in_=QA[64 * h:64 * h + 64, j].rearrange("p g t -> p (g t)")),
                                 [qa_t[j]], [qz_t[b]])
                            S.op("pool", lambda e: e.tensor_copy(out=QCZ[b][64 * h:64 * h + 64, h, :],
                                                                 in_=QC[64 * h:64 * h + 64, j].rearrange("p g t -> p (g t)")),
                                 [qc_t[j]], [qcz_t[b]])

                    def emit_BC(j, b):
                        CA, CC = CATA2[b], CATC2[b]
                        ca_t, cc_t = cata2_t[b], catc2_t[b]
                        chunks = [0, 1] if j < 2 else list(range(NT))
                        steps = [(c, h) for c in chunks for h in range(2)]
                        ns = len(steps)

                        SB_ = (0, 1, 4)

                        def emit_S(n):
                            c, h = steps[n]
                            sbk = SB_[n % 3]
                            S.op("pe", lambda e: e.matmul(bank(sbk), KA[:, c * 128:(c + 1) * 128], QZ[b][:, h, :],
                                                          start=True, stop=True), [ka_t[c], qz_t[b]], [pst[sbk]])
                        emit_S(0)
                        emit_S(1)
                        emit_S(2)
                        for n in range(ns):
                            c, h = steps[n]
                            p_ = state["pi"]
                            state["pi"] = (p_ + 1) % 3
                            sbk = SB_[n % 3]
                            S.op("act", lambda e: e.activation(out=PT[p_][:], in_=bank(sbk), func=AF.Exp, scale=0.125),
                                 [pst[sbk]], [pt_t[p_]])
                            S.op("pe", lambda e: e.matmul(bank(2 + h, 0, 65), VA[:, c, h, :], PT[p_][:],
                                                          start=(n < 2), stop=(n >= ns - 2)),
                                 [va_t[c], pt_t[p_]], [pst[2 + h]])
                            if n + 3 < ns:
                                emit_S(n + 3)
                        chk("B_LOOP")
                        for h in range(2):
                            S.op("act", lambda e: e.activation(out=REC[0:1, h * 512:(h + 1) * 512], in_=bank(2 + h, 0, 1), func=AF.Ln),
                                 [pst[2 + h]], [rec_t])
                            S.op("act", lambda e: e.activation(out=REC[0:1, h * 512:(h + 1) * 512], in_=REC[0:1, h * 512:(h + 1) * 512],
                                                               func=AF.Exp, scale=-1.0), [rec_t], [rec_t])
                            S.op("pe", lambda e: e.matmul(bank(6 + h, 0, 65), onesf[0:1, 0:65], REC[0:1, h * 512:(h + 1) * 512],
                                                          start=True, stop=True), [rec_t, k_tok], [pst[6 + h]])
                            S.op("dve", lambda e: e.tensor_copy(out=BCS[0:65, h * 512:(h + 1) * 512], in_=bank(6 + h, 0, 65)), [pst[6 + h]], [bcs_t])
                            S.op("dve", lambda e: e.tensor_tensor(out=CA[0:65, 4 * h:4 * h + 4, :].rearrange("p g t -> p (g t)"),
                                                                  in0=bank(2 + h, 0, 65), in1=BCS[0:65, h * 512:(h + 1) * 512],
                                                                  op=ALU.mult), [pst[2 + h], bcs_t], [ca_t])
                        chk("B_NORM")
                        cks = [(0, None), (1, None)]
                        if j >= 2:
                            if j - 1 >= 2:
                                cks.append((j - 1, MKL))
                            cks.append((j, None))
                            if j + 1 < NT:
                                cks.append((j + 1, MKU))
                        for ci, (c, mk) in enumerate(cks):
                            for h in range(2):
                                S.op("pe", lambda e: e.matmul(bank(h, 0, 128, 0, 256), KC[:, c * 128:(c + 1) * 128], QCZ[b][:, h, :],
                                                              start=True, stop=True), [kc_t[c], qcz_t[b]], [pst[h]])
                            q_ = state["pc"]
                            state["pc"] = (q_ + 1) % 2
                            S.op("act", lambda e: e.activation(out=PTC[q_][:].rearrange("p (b c) -> p b c", b=2),
                                                               in_=ps[:, 0:1024].rearrange("p (b c) -> p b c", b=2)[:, :, 0:256],
                                                               func=AF.Exp, scale=0.125),
                                 [pst[0], pst[1]], [ptc_t[q_]])
                            if mk is not None:
                                S.op("pool", lambda e: e.tensor_tensor(
                                    out=PTC[q_][:].rearrange("p (a t) -> p a t", a=4),
                                    in0=PTC[q_][:].rearrange("p (a t) -> p a t", a=4),
                                    in1=mk[:].unsqueeze(1).to_broadcast([128, 4, 128]), op=ALU.mult),
                                    [ptc_t[q_], mk_t], [ptc_t[q_]])
                            for h in range(2):
                                S.op("pe", lambda e: e.matmul(bank(5, 0, 65, h * 256, (h + 1) * 256), VC[:, c, h, :],
                                                              PTC[q_][:, h * 256:(h + 1) * 256],
                                                              start=(ci == 0 and h == 0), stop=(ci == len(cks) - 1),
                                                              skip_group_check=True), [vc_t[c], ptc_t[q_]], [pst[5]])
                        for a4 in range(4):
                            S.op("act", lambda e: e.activation(out=RECC[0:1, a4 * 128:(a4 + 1) * 128], in_=bank(5, 0, 1, a4 * 128, (a4 + 1) * 128),
                                                               func=AF.Ln, bias=SK4[0:1, a4:a4 + 1], scale=1.0), [pst[5], sk_t], [recc_t])
                        S.op("act", lambda e: e.activation(out=RECC[0:1, :], in_=RECC[0:1, :], func=AF.Exp, scale=-1.0), [recc_t], [recc_t])
                        S.op("pe", lambda e: e.matmul(bank(4, 0, 65), onesf[0:1, 0:65], RECC[0:1, :], start=True, stop=True),
                             [recc_t, k_tok], [pst[4]])
                        S.op("dve", lambda e: e.tensor_copy(out=BCC[0:65, :], in_=bank(4, 0, 65)), [pst[4]], [bcc_t])
                        S.op("dve", lambda e: e.tensor_tensor(out=CC[0:65].rearrange("p a t -> p (a t)"), in0=bank(5, 0, 65),
                                                              in1=BCC[0:65, :], op=ALU.mult), [pst[5], bcc_t], [cc_t])
                        chk("C")

                    def emit_E1(j, b):
                        CA, CC, CB, HH = CATA2[b], CATC2[b], CATB2[b], H22[b]
                        ca_t, cc_t, cb_t, hh_t = cata2_t[b], catc2_t[b], catb2_t[b], h22_t[b]
                        if j == tiles[0] or j == 2:
                            w = 1 if j < 2 else 0
                            modrow(G1, w, 2, mr_t)
                            modrow(SH2, w, 3, mr_t)
                            modrow(SC2, w, 4, mr_t)
                        rows, rt = src_rows(j)
                        S.dma("sp", lambda e: e.dma_start(out=XT2[b][:], in_=rows), rt, [xt2_t[b]])
                        for m in range(2):
                            S.op("pe", lambda e: e.matmul(bank(6, 0, 128, m * 128, (m + 1) * 128), OBs[:, j, m * 128:(m + 1) * 128],
                                                          ident[:], start=True, stop=True), [ob_t[j], k_tok], [pst[6]])
                        S.op("dve", lambda e: e.tensor_copy(out=CB[:].rearrange("p m t -> p (m t)"), in_=bank(6, 0, 128, 0, 256)),
                             [pst[6]], [cb_t])
                        for n in range(2):
                            mms = []
                            for hd in range(8):
                                mms.append((CA[:, hd, :], WOA[:, hd, n * 512:(n + 1) * 512], ca_t))
                            for m in range(2):
                                mms.append((CB[:, m, :], WOB[:, m, n * 512:(n + 1) * 512], cb_t))
                            for hd in range(4):
                                mms.append((CC[:, hd, :], WOC[:, hd, n * 512:(n + 1) * 512], cc_t))
                            for i, (l_, r_, t_) in enumerate(mms):
                                S.op("pe", lambda e: e.matmul(bank(6 + n), l_, r_, start=(i == 0), stop=(i == len(mms) - 1)),
                                     [t_, w_t], [pst[6 + n]])
                        chk("E_PROJ")
                        O = ps[:, 6 * 512:8 * 512]
                        S.op("dve", lambda e: e.tensor_tensor(out=TMP[:], in0=O, in1=G1[:], op=ALU.mult),
                             [pst[6], pst[7], mr_t], [tmp_t])
                        S.op("dve", lambda e: e.scalar_tensor_tensor(out=RR[:], in0=XT2[b][:], scalar=ALPHA, in1=TMP[:],
                                                                     op0=ALU.mult, op1=ALU.add), [xt2_t[b], tmp_t], [rr_t])
                        ln_norm(XN2, RR, [rr_t], xn2_t)
                        S.op("dve", lambda e: e.tensor_tensor(out=XN2[:], in0=XN2[:], in1=LN1G[:], op=ALU.mult), [xn2_t, ln_t], [xn2_t])
                        S.op("pool", lambda e: e.tensor_tensor(out=RR[:], in0=XN2[:], in1=LN1B[:], op=ALU.add), [xn2_t, ln_t], [rr_t])
                        if ("x1_%d" % L) in dbg_d and j in (0, 2):
                            S.dma("sp", lambda e: e.dma_start(out=dbg_d["x1_%d" % L][(0 if j == 0 else 128):(128 if j == 0 else 256), :],
                                                              in_=RR[:]), [rr_t], [dbg_tok])
                        S.op("dve", lambda e: e.tensor_scalar(out=ACC[:], in0=RR[:], scalar1=ALPHA, scalar2=None, op0=ALU.mult),
                             [rr_t], [accb_t])
                        S.dma("sp", lambda e: e.dma_start(out=acc_d[j * 128:(j + 1) * 128, :], in_=ACC[:]), [accb_t], [acc_tok[j]])
                        ln_norm(XN2, RR, [rr_t], xn2_t)
                        S.op("dve", lambda e: e.tensor_tensor(out=XN2[:], in0=XN2[:], in1=SC2[:], op=ALU.mult), [xn2_t, mr_t], [xn2_t])
                        S.op("pool", lambda e: e.tensor_tensor(out=HH[:], in0=XN2[:], in1=SH2[:], op=ALU.add), [xn2_t, mr_t], [hh_t])
                        S.dma("sp", lambda e: e.dma_start(out=h2_d[j * 128:(j + 1) * 128, :], in_=HH[:]), [hh_t], [h2_tok[j]])
                        chk("E_LN")

                    def emit_E2(j, b):
                        HH, hh_t = H22[b], h22_t[b]
                        for k in range(8):
                            S.op("pe", lambda e: e.matmul(bank(6 + k // 4, 0, 128, (k % 4) * 128, (k % 4) * 128 + 128),
                                                          HH[:, k * 128:(k + 1) * 128], ident[:], start=True, stop=True),
                                 [hh_t, k_tok], [pst[6 + k // 4]])
                        S.op("dve", lambda e: e.tensor_copy(out=HT2[:, 0:4, :].rearrange("p a b -> p (a b)"), in_=bank(6)), [pst[6]], [ht2_t])
                        S.op("dve", lambda e: e.tensor_copy(out=HT2[:, 4:8, :].rearrange("p a b -> p (a b)"), in_=bank(7)), [pst[7]], [ht2_t])
                        for k in range(8):
                            S.op("pe", lambda e: e.matmul(bank(6, 0, 128, 0, NE), HT2[:, k, :], WR[:, k, :], start=(k == 0), stop=(k == 7)),
                                 [ht2_t, w_t], [pst[6]])
                        S.op("dve", lambda e: e.reduce_max(out=SM[:, 0:1], in_=bank(6, 0, 128, 0, NE), axis=AX.X), [pst[6]], [sm_t])
                        S.op("dve", lambda e: e.tensor_scalar(out=SM[:, 1:2], in0=SM[:, 0:1], scalar1=-1.0, scalar2=None, op0=ALU.mult),
                             [sm_t], [sm_t])
                        S.op("act", lambda e: e.activation(out=LG[:], in_=bank(6, 0, 128, 0, NE), func=AF.Exp, bias=SM[:, 1:2], scale=1.0,
                                                           accum_out=SM[:, 2:3]), [pst[6], sm_t], [lg_t, sm_t])
                        S.op("dve", lambda e: e.reciprocal(out=SM[:, 3:4], in_=SM[:, 2:3]), [sm_t], [sm_t])
                        S.op("dve", lambda e: e.tensor_scalar(out=AFF[L][:, j, :], in0=LG[:], scalar1=SM[:, 3:4], scalar2=None, op0=ALU.mult),
                             [lg_t, sm_t], [aff_t])

                    nt_ = len(tiles)
                    emit_QZ(tiles[0], 0)
                    for idx in range(nt_ + 2):
                        if idx + 1 < nt_:
                            emit_QZ(tiles[idx + 1], (idx + 1) % 2)
                        if idx < nt_:
                            emit_BC(tiles[idx], idx % 2)
                        if idx >= 2:
                            emit_E2(tiles[idx - 2], (idx - 2) % 2)
                        if 1 <= idx <= nt_:
                            emit_E1(tiles[idx - 1], (idx - 1) % 2)
                    S.barrier()
                    for nm, src in (("cata", CATA[0:65].rearrange("p a t -> p (a t)")), ("catc", CATC[0:65].rearrange("p a t -> p (a t)"))):
                        if (nm + "%d" % L) in dbg_d:
                            n_ = src.shape[1]
                            S.op("dve", lambda e: e.tensor_copy(out=TMP[0:65, 0:n_], in_=src), [cata_t, catc_t], [tmp_t])
                            S.dma("sp", lambda e: e.dma_start(out=dbg_d[nm + "%d" % L], in_=TMP[0:65, 0:n_]), [tmp_t], [dbg_tok])
                            S.barrier()
            if stop_after == ("ML", L):
                break
            with ExitStack() as pf:
                TRI = sb(pf, "TRI", [128, 128], F32)
                IOB = sb(pf, "IOB", [128, 16, 128], F32)
                IOA = sb(pf, "IOA", [128, 16, 4], F32)
                PCOL = sb(pf, "PCOL", [128, 1], F32)
                TGT = sb(pf, "TGT", [128, 32], F32)
                kf_t = Tok()
                S.dma("sp", lambda e: e.dma_start(out=TRI[:], in_=tri_d), [], [kf_t])
                S.dma("sp", lambda e: e.dma_start(out=IOB[:].rearrange("p a b -> p (a b)"), in_=iotaB_d), [], [kf_t])
                S.dma("sp", lambda e: e.dma_start(out=IOA[:].rearrange("p a b -> p (a b)"), in_=iotaA_d), [], [kf_t])
                S.dma("sp", lambda e: e.dma_start(out=PCOL[:], in_=pcol_d), [], [kf_t])
                S.dma("sp", lambda e: e.dma_start(out=TGT[:], in_=tgt_d), [], [kf_t])
                THR = sb(pf, "THR", [128, 32], F32)
                LO = sb(pf, "LO", [128, 32], F32)
                CNTP = sb(pf, "CNTP", [128, 32], F32)
                IND = sb(pf, "IND", [128, 32], F32)
                MSK = sb(pf, "MSK", [128, NT, NE], F32)
                thr_t, lo_t, cntp_t, ind_t, msk_t = (Tok() for _ in range(5))
                S.op("dve", lambda e: e.memset(LO[:], 0.0), [], [lo_t])
                S.op("dve", lambda e: e.memset(CNTP[:], 0.0), [], [cntp_t])
                S.op("dve", lambda e: e.memset(MSK[:], 0.0), [], [msk_t])
                A_ = AFF[L]
                do_ctx = not last

                def make_mask(thr):
                    S.op("dve", lambda e: e.tensor_tensor(out=MSK[:, 2:NT, :], in0=A_[:, 2:NT, :],
                                                          in1=thr[:, 0:16].unsqueeze(1).to_broadcast([128, 32, 16]), op=ALU.is_ge),
                         [aff_t, thr_t, lo_t], [msk_t])
                    if do_ctx:
                        S.op("dve", lambda e: e.tensor_tensor(out=MSK[:, 0:2, :], in0=A_[:, 0:2, :],
                                                              in1=thr[:, 16:32].unsqueeze(1).to_broadcast([128, 2, 16]), op=ALU.is_ge),
                             [aff_t, thr_t, lo_t], [msk_t])

                for it in range(28):
                    wv = 2.0 ** -(it + 1)
                    S.op("dve", lambda e: e.tensor_scalar(out=THR[:], in0=LO[:], scalar1=wv, scalar2=None, op0=ALU.add), [lo_t], [thr_t])
                    make_mask(THR)
                    S.op("dve", lambda e: e.tensor_reduce(out=CNTP[:, 0:16], in_=MSK[:, 2:NT, :].rearrange("p j e -> p e j"),
                                                          axis=AX.X, op=ALU.add), [msk_t], [cntp_t])
                    if do_ctx:
                        S.op("dve", lambda e: e.tensor_reduce(out=CNTP[:, 16:32], in_=MSK[:, 0:2, :].rearrange("p j e -> p e j"),
                                                              axis=AX.X, op=ALU.add), [msk_t], [cntp_t])
                    S.op("pe", lambda e: e.matmul(bank(0, 0, 128, 0, 32), onesf[:], CNTP[:], start=True, stop=True),
                         [cntp_t, k_tok], [pst[0]])
                    S.op("dve", lambda e: e.tensor_tensor(out=IND[:], in0=bank(0, 0, 128, 0, 32), in1=TGT[:], op=ALU.is_ge),
                         [pst[0], kf_t], [ind_t])
                    S.op("dve", lambda e: e.scalar_tensor_tensor(out=LO[:], in0=IND[:], scalar=wv, in1=LO[:], op0=ALU.mult, op1=ALU.add),
                         [ind_t, lo_t], [lo_t])
                make_mask(LO)
                POS = sb(pf, "POS", [128, NT, NE], F32)
                OFF = sb(pf, "OFF", [128, NT, NE], F32)
                POSI = sb(pf, "POSI", [128, NT, NE], I32)
                BI = sb(pf, "BI", [128, NT, NE], I32)
                AI = sb(pf, "AI", [128, NT, NE], I32)
                BFl = sb(pf, "BFl", [128, NT, NE], F32)
                AFl = sb(pf, "AFl", [128, NT, NE], F32)
                pos_t, off_t = Tok(), Tok()
                groups = [(2, NT, 1, 2)] + ([(0, 2, 3, 3)] if do_ctx else [])
                for (j0, j1, bpw, btot) in groups:
                    n_ = (j1 - j0) * NE
                    c0 = 0 if j0 == 2 else 0
                    c1 = 0 if j0 == 2 else 64
                    mview = MSK[:, j0:j1, :].rearrange("p j e -> p (j e)")
                    S.op("pe", lambda e: e.matmul(bank(bpw, 0, 128, c0, c0 + n_), TRI[:], mview, start=True, stop=True),
                         [msk_t, kf_t], [pst[bpw]])
                    S.op("pe", lambda e: e.matmul(bank(btot, 0, 128, c1, c1 + n_), onesf[:], mview, start=True, stop=True),
                         [msk_t, k_tok], [pst[btot]])
                    S.op("dve", lambda e: e.memset(OFF[:, j0, :], 0.0), [], [off_t])
                    for jj in range(j0 + 1, j1):
                        S.op("dve", lambda e: e.tensor_tensor(out=OFF[:, jj, :], in0=OFF[:, jj - 1, :],
                                                              in1=bank(btot, 0, 128, c1 + (jj - 1 - j0) * NE, c1 + (jj - j0) * NE),
                                                              op=ALU.add), [off_t, pst[btot]], [off_t])
                    pv = POS[:, j0:j1, :].rearrange("p j e -> p (j e)")
                    S.op("dve", lambda e: e.tensor_tensor(out=pv, in0=bank(bpw, 0, 128, c0, c0 + n_),
                                                          in1=OFF[:, j0:j1, :].rearrange("p j e -> p (j e)"), op=ALU.add),
                         [pst[bpw], off_t], [pos_t])
                    S.op("dve", lambda e: e.scalar_tensor_tensor(out=pv, in0=pv, scalar=1.0, in1=mview, op0=ALU.add, op1=ALU.mult),
                         [pos_t, msk_t], [pos_t])
                    S.op("dve", lambda e: e.tensor_scalar(out=pv, in0=pv, scalar1=-1.0, scalar2=None, op0=ALU.add), [pos_t], [pos_t])
                    for (o_, i_, fn) in ((POSI, POS, None), (BI, POSI, ("and", 127)), (AI, POSI, ("shr", 7)), (BFl, BI, None), (AFl, AI, None)):
                        ov = o_[:, j0:j1, :].rearrange("p j e -> p (j e)")
                        iv = i_[:, j0:j1, :].rearrange("p j e -> p (j e)")
                        if fn is None:
                            S.op("dve", lambda e: e.tensor_copy(out=ov, in_=iv), [pos_t], [pos_t])
                        else:
                            opx = ALU.bitwise_and if fn[0] == "and" else ALU.arith_shift_right
                            S.op("dve", lambda e: e.tensor_scalar(out=ov, in0=iv, scalar1=fn[1], scalar2=None, op0=opx), [pos_t], [pos_t])
                if ("pos%d" % L) in dbg_d:
                    S.dma("sp", lambda e: e.dma_start(out=dbg_d["pos%d" % L], in_=POS[:].rearrange("p j e -> p (j e)")), [pos_t], [dbg_tok])
                    S.dma("sp", lambda e: e.dma_start(out=dbg_d["aff%d" % L], in_=A_[:].rearrange("p j e -> p (j e)")), [aff_t], [dbg_tok])
                OHB = [sb(pf, "OHB%d" % i, [128, 16, 128], F32) for i in range(2)]
                ohb_t = toks(2)
                RA = sb(pf, "RA", [128, 16, 4], F32)
                R3 = [sb(pf, "R3%d" % i, [128, 16, 3, 4], F32) for i in range(2)]
                ra_t = Tok()
                r3_t = toks(2)
                for jj in range(32):
                    b = jj % 2
                    j = 2 + jj
                    S.op("dve", lambda e: e.tensor_tensor(out=OHB[b][:], in0=IOB[:], in1=BFl[:, j, :].unsqueeze(2).to_broadcast([128, 16, 128]),
                                                          op=ALU.is_equal), [kf_t, pos_t], [ohb_t[b]])
                    S.op("dve", lambda e: e.tensor_tensor(out=RA[:], in0=IOA[:], in1=AFl[:, j, :].unsqueeze(2).to_broadcast([128, 16, 4]),
                                                          op=ALU.is_equal), [kf_t, pos_t], [ra_t])
                    S.op("dve", lambda e: e.tensor_scalar(out=R3[b][:, :, 0, :], in0=RA[:], scalar1=PCOL[:, 0:1], scalar2=None, op0=ALU.mult),
                         [ra_t, kf_t], [r3_t[b]])
                    S.op("dve", lambda e: e.tensor_scalar(out=R3[b][:, :, 1, :], in0=RA[:], scalar1=float(j), scalar2=None, op0=ALU.mult),
                         [ra_t], [r3_t[b]])
                    S.op("dve", lambda e: e.tensor_tensor(out=R3[b][:, :, 2, :], in0=RA[:],
                                                          in1=A_[:, j, :].unsqueeze(2).to_broadcast([128, 16, 4]), op=ALU.mult),
                         [ra_t, aff_t], [r3_t[b]])
                    for ex in range(NE):
                        S.op("pe", lambda e: e.matmul(bank(4, 0, 128, ex * 12, ex * 12 + 12), OHB[b][:, ex, :],
                                                      R3[b][:, ex].rearrange("p c a -> p (c a)"),
                                                      start=(jj == 0 and ex == 0), stop=(jj == 31), skip_group_check=True),
                             [ohb_t[b], r3_t[b]], [pst[4]])
                CMPS = sb(pf, "CMPS", [128, 192], F32)
                cmps_t = Tok()
                S.op("dve", lambda e: e.tensor_copy(out=CMPS[:], in_=bank(4, 0, 128, 0, 192)), [pst[4]], [cmps_t])
                CMP = CMPS[:].rearrange("p (e c a) -> p e c a", e=16, c=3)
                IDXF = sb(pf, "IDXF", [128, 16, 4], F32)
                idx_t = idx_tok[L]
                S.op("dve", lambda e: e.scalar_tensor_tensor(out=IDXF[:], in0=CMP[:, :, 1, :], scalar=128.0, in1=CMP[:, :, 0, :],
                                                             op0=ALU.mult, op1=ALU.add), [cmps_t], [idx_t])
                S.op("dve", lambda e: e.tensor_copy(out=IDX[L][:], in_=IDXF[:]), [idx_t], [idx_t])
                S.op("dve", lambda e: e.tensor_copy(out=GATE[L][:], in_=CMP[:, :, 2, :]), [cmps_t], [idx_t])
                if do_ctx:
                    OHC = sb(pf, "OHC", [128, 16, 32], F32)
                    R3C = sb(pf, "R3C", [128, 16, 3], F32)
                    ohc_t, r3c_t = Tok(), Tok()
                    for j in range(2):
                        S.op("dve", lambda e: e.tensor_tensor(out=OHC[:], in0=IOB[:, :, 0:32],
                                                              in1=BFl[:, j, :].unsqueeze(2).to_broadcast([128, 16, 32]), op=ALU.is_equal),
                             [kf_t, pos_t], [ohc_t])
                        S.op("dve", lambda e: e.tensor_copy(out=R3C[:, :, 0], in_=PCOL[:, 0:1].to_broadcast([128, 16])), [kf_t], [r3c_t])
                        S.op("dve", lambda e: e.memset(R3C[:, :, 1], float(j)), [], [r3c_t])
                        S.op("dve", lambda e: e.tensor_copy(out=R3C[:, :, 2], in_=A_[:, j, :]), [aff_t], [r3c_t])
                        for ex in range(NE):
                            S.op("pe", lambda e: e.matmul(bank(5, 0, 32, ex * 3, ex * 3 + 3), OHC[:, ex, :], R3C[:, ex, :],
                                                          start=(j == 0 and ex == 0), stop=(j == 1), skip_group_check=True),
                                 [ohc_t, r3c_t], [pst[5]])
                    S.op("dve", lambda e: e.tensor_copy(out=CMPS[0:32, 0:48], in_=bank(5, 0, 32, 0, 48)), [pst[5], cmps_t, idx_t], [cmps_t])
                    CMC = CMPS[0:32, 0:48].rearrange("p (e c) -> p e c", c=3)
                    S.op("dve", lambda e: e.scalar_tensor_tensor(out=IDXF[0:32, :, 0], in0=CMC[:, :, 1], scalar=128.0, in1=CMC[:, :, 0],
                                                                 op0=ALU.mult, op1=ALU.add), [cmps_t, idx_t], [idx_t])
                    S.op("dve", lambda e: e.tensor_copy(out=IDXC[L][0:32, :], in_=IDXF[0:32, :, 0]), [idx_t], [idx_t])
                    S.op("dve", lambda e: e.tensor_copy(out=GATEC[L][0:32, :], in_=CMC[:, :, 2]), [cmps_t], [idx_t])
                if ("idx%d" % L) in dbg_d:
                    S.dma("sp", lambda e: e.dma_start(out=dbg_d["idx%d" % L], in_=IDXF[:].rearrange("p e a -> p (e a)")), [idx_t], [dbg_tok])
                    S.dma("sp", lambda e: e.dma_start(out=dbg_d["gate%d" % L], in_=GATE[L][:].rearrange("p e a -> p (e a)")), [idx_t], [dbg_tok])
                S.barrier()
            if stop_after == ("F", L):
                break
            with ExitStack() as pg:
                idx_t = idx_tok[L]
                do_ctx = not last
                NS = 544 if do_ctx else 512
                G2 = sb(pg, "G2", [128, D], F32)
                G2C = sb(pg, "G2C", [128, D], F32)
                g2_t = Tok()
                modrow(G2, 0, 5, g2_t)
                if do_ctx:
                    modrow(G2C, 1, 5, g2_t)
                XG = [[sb(pg, "XG%d_%d" % (i, a), [128, D], BF16) for a in range(5)] for i in range(2)]
                xg_t = [toks(5) for _ in range(2)]
                XGT = [sb(pg, "XGT%d" % i, [128, 8, NS], BF16) for i in range(2)]
                xgt_t = toks(2)
                HID = sb(pg, "HID", [128, 16, NS], BF16)
                hid_t = Tok()
                WG = [sb(pg, "WG%d" % i, [128, 8, 512], BF16) for i in range(4)]
                WU = [sb(pg, "WU%d" % i, [128, 8, 512], BF16) for i in range(4)]
                wg_t, wu_t = toks(4), toks(4)
                WD = [sb(pg, "WD%d" % i, [128, 16, D], BF16) for i in range(2)]
                wd_t = toks(2)
                SIL = [sb(pg, "SIL%d" % i, [128, 512], F32) for i in range(2)]
                sil_t = toks(2)
                SILC = sb(pg, "SILC", [128, 32], F32)
                silc_t = Tok()
                YS = [sb(pg, "YS%d" % i, [128, D], F32) for i in range(2)]
                ys_t = toks(2)
                ysi = 0
                fcc = 0

                def load_gather(ex):
                    xb = ex % 2
                    for a in range(4):
                        S.dma("pool", lambda e: e.indirect_dma_start(
                            out=XG[xb][a][:], out_offset=None, in_=h2_d[:, :],
                            in_offset=bass.IndirectOffsetOnAxis(ap=IDX[L][:, ex, a:a + 1], axis=0)), h2_tok + [idx_t], [xg_t[xb][a]])
                    if do_ctx:
                        S.dma("pool", lambda e: e.indirect_dma_start(
                            out=XG[xb][4][0:32, :], out_offset=None, in_=h2_d[:, :],
                            in_offset=bass.IndirectOffsetOnAxis(ap=IDXC[L][0:32, ex:ex + 1], axis=0)), h2_tok + [idx_t], [xg_t[xb][4]])

                def load_wd(ex):
                    db = ex % 2
                    wdsrc = wd_d[L, ex].rearrange("(f p) d -> p f d", p=128)
                    for q in range(4):
                        S.dma("pool", lambda e: e.dma_start(out=WD[db][:, q * 4:(q + 1) * 4, :], in_=wdsrc[:, q * 4:(q + 1) * 4, :]),
                              [], [wd_t[db]])

                def load_gu(ex, fq):
                    wgsrc = wg_d[L, ex].rearrange("(k p) f -> p k f", p=128)
                    wusrc = wu_d[L, ex].rearrange("(k p) f -> p k f", p=128)
                    S.dma("pool", lambda e: e.dma_start(out=WG[fq][:], in_=wgsrc[:, :, fq * 512:(fq + 1) * 512]), [], [wg_t[fq]])
                    S.dma("pool", lambda e: e.dma_start(out=WU[fq][:], in_=wusrc[:, :, fq * 512:(fq + 1) * 512]), [], [wu_t[fq]])

                load_gather(0)
                load_wd(0)
                for fq in range(4):
                    load_gu(0, fq)
                load_gather(1)
                load_wd(1)
                for ex in range(NE):
                    xb = ex % 2
                    db = ex % 2
                    for a in range(4):
                        for kh in range(2):
                            for kk in range(4):
                                k = kh * 4 + kk
                                S.op("pe", lambda e: e.matmul(bank(6, 0, 128, kk * 128, (kk + 1) * 128),
                                                              XG[xb][a][:, k * 128:(k + 1) * 128], ident[:], start=True, stop=True),
                                     [xg_t[xb][a], k_tok], [pst[6]])
                            eng = "act" if (a + kh) % 2 == 0 else "dve"
                            src = bank(6).rearrange("p (k t) -> p k t", k=4)
                            dst = XGT[xb][:, kh * 4:(kh + 1) * 4, a * 128:(a + 1) * 128]
                            if eng == "act":
                                S.op("act", lambda e: e.copy(out=dst, in_=src), [pst[6]], [xgt_t[xb]])
                            else:
                                S.op("dve", lambda e: e.tensor_copy(out=dst, in_=src), [pst[6]], [xgt_t[xb]])
                    if do_ctx:
                        for k in range(8):
                            S.op("pe", lambda e: e.matmul(bank(7, 0, 128, k * 32, (k + 1) * 32), XG[xb][4][0:32, k * 128:(k + 1) * 128],
                                                          ident[0:32, 0:32], start=True, stop=True), [xg_t[xb][4], k_tok], [pst[7]])
                        S.op("dve", lambda e: e.tensor_copy(out=XGT[xb][:, :, 512:544],
                                                            in_=bank(7, 0, 128, 0, 256).rearrange("p (k t) -> p k t", k=8)),
                             [pst[7]], [xgt_t[xb]])
                    if ex + 2 < NE:
                        load_gather(ex + 2)
                    for fq in range(4):
                        wb = fq
                        for fi in range(4):
                            fc = fq * 4 + fi
                            pa = fcc % 2
                            fcc += 1
                            for (Wt, wt_t, bk) in ((WG[wb], wg_t[wb], pa), (WU[wb], wu_t[wb], 2 + pa)):
                                for k in range(8):
                                    S.op("pe", lambda e: e.matmul(bank(bk), Wt[:, k, fi * 128:(fi + 1) * 128], XGT[xb][:, k, 0:512],
                                                                  start=(k == 0), stop=(k == 7)), [wt_t, xgt_t[xb]], [pst[bk]])
                            if do_ctx:
                                for (Wt, wt_t, c0) in ((WG[wb], wg_t[wb], 256), (WU[wb], wu_t[wb], 288)):
                                    for k in range(8):
                                        S.op("pe", lambda e: e.matmul(bank(7, 0, 128, c0, c0 + 32), Wt[:, k, fi * 128:(fi + 1) * 128],
                                                                      XGT[xb][:, k, 512:544], start=(k == 0), stop=(k == 7)),
                                             [wt_t, xgt_t[xb]], [pst[7]])
                            S.op("act", lambda e: e.activation(out=SIL[pa][:], in_=bank(pa), func=AF.Silu), [pst[pa]], [sil_t[pa]])
                            S.op("dve", lambda e: e.tensor_tensor(out=HID[:, fc, 0:512], in0=bank(2 + pa), in1=SIL[pa][:], op=ALU.mult),
                                 [pst[2 + pa], sil_t[pa]], [hid_t])
                            if do_ctx:
                                S.op("act", lambda e: e.activation(out=SILC[:], in_=bank(7, 0, 128, 256, 288), func=AF.Silu),
                                     [pst[7]], [silc_t])
                                S.op("dve", lambda e: e.tensor_tensor(out=HID[:, fc, 512:544], in0=bank(7, 0, 128, 288, 320), in1=SILC[:],
                                                                      op=ALU.mult), [pst[7], silc_t], [hid_t])
                        if ex + 1 < NE:
                            load_gu(ex + 1, fq)
                    slots = [(s * 128, 128, IDX[L][:, ex, s:s + 1], GATE[L][:, ex, s:s + 1], G2) for s in range(4)]
                    if do_ctx:
                        slots.append((512, 32, IDXC[L][0:32, ex:ex + 1], GATEC[L][0:32, ex:ex + 1], G2C))
                    for (s0, sn, idx_ap, gate_ap, g2row) in slots:
                        for n in range(2):
                            for fc in range(16):
                                S.op("pe", lambda e: e.matmul(bank(4 + n, 0, sn), HID[:, fc, s0:s0 + sn], WD[db][:, fc, n * 512:(n + 1) * 512],
                                                              start=(fc == 0), stop=(fc == 15)), [hid_t, wd_t[db]], [pst[4 + n]])
                        yb = ysi % 2
                        ysi += 1
                        S.op("dve", lambda e: e.scalar_tensor_tensor(out=YS[yb][0:sn, :], in0=ps[0:sn, 4 * 512:6 * 512], scalar=gate_ap,
                                                                     in1=g2row[0:sn, :], op0=ALU.mult, op1=ALU.mult),
                             [pst[4], pst[5], idx_t, g2_t], [ys_t[yb]])
                        S.dma("pool", lambda e: e.indirect_dma_start(
                            out=acc_d[:, :], out_offset=bass.IndirectOffsetOnAxis(ap=idx_ap, axis=0), in_=YS[yb][0:sn, :], in_offset=None,
                            compute_op=ALU.add),
                            [ys_t[yb], idx_t] + acc_tok, [accs_tok])
                    if ex + 2 < NE:
                        load_wd(ex + 2)
                S.barrier()
            with ExitStack() as phh:
                LN2G = sb(phh, "LN2G", [128, D], F32)
                LN2B = sb(phh, "LN2B", [128, D], F32)
                l2_t = Tok()
                rowb(LN2G, ln2g_d[L], l2_t)
                rowb(LN2B, ln2b_d[L], l2_t)
                AT = [sb(phh, "AT%d" % i, [128, D], F32) for i in range(2)]
                at_t = toks(2)
                XO = [sb(phh, "XO%d" % i, [128, D], F32) for i in range(2)]
                xo_t = toks(2)
                ST3 = sb(phh, "ST3", [128, 2, 6], F32)
                MV3 = sb(phh, "MV3", [128, 8], F32)
                st3_t, mv3_t = Tok(), Tok()
                for j in (range(2, NT) if last else range(NT)):
                    b = j % 2
                    S.dma("sp", lambda e: e.dma_start(out=AT[b][:], in_=acc_d[j * 128:(j + 1) * 128, :]), [acc_tok[j], accs_tok], [at_t[b]])
                    for hh in range(2):
                        S.op("dve", lambda e: e.bn_stats(out=ST3[:, hh, :], in_=AT[b][:, hh * 512:(hh + 1) * 512]), [at_t[b]], [st3_t])
                    S.op("dve", lambda e: e.bn_aggr(out=MV3[:, 0:2], in_=ST3[:].rearrange("p a b -> p (a b)")), [st3_t], [mv3_t])
                    S.op("act", lambda e: e.activation(out=MV3[:, 2:3], in_=MV3[:, 1:2], func=AF.Sqrt, bias=LN_EPS, scale=1.0), [mv3_t], [mv3_t])
                    S.op("dve", lambda e: e.reciprocal(out=MV3[:, 3:4], in_=MV3[:, 2:3]), [mv3_t], [mv3_t])
                    S.op("dve", lambda e: e.tensor_scalar(out=MV3[:, 4:5], in0=MV3[:, 0:1], scalar1=MV3[:, 3:4], scalar2=-1.0,
                                                          op0=ALU.mult, op1=ALU.mult), [mv3_t], [mv3_t])
                    S.op("act", lambda e: e.activation(out=XO[b][:], in_=AT[b][:], func=AF.Identity, bias=MV3[:, 4:5], scale=MV3[:, 3:4]),
                         [mv3_t, at_t[b]], [xo_t[b]])
                    S.op("dve", lambda e: e.tensor_tensor(out=XO[b][:], in0=XO[b][:], in1=LN2G[:], op=ALU.mult), [xo_t[b], l2_t], [xo_t[b]])
                    S.op("pool", lambda e: e.tensor_tensor(out=XO[b][:], in0=XO[b][:], in1=LN2B[:], op=ALU.add), [xo_t[b], l2_t], [xo_t[b]])
                    if last:
                        S.dma("sp", lambda e: e.dma_start(out=out_d[(j - 2) * 128:(j - 1) * 128, :], in_=XO[b][:]), [xo_t[b]], [out_tok[j]])
                    else:
                        S.dma("sp", lambda e: e.dma_start(out=xcur_d[j * 128:(j + 1) * 128, :], in_=XO[b][:]), [xo_t[b]], [xcur_tok[j]])
                        if ("x2_%d" % L) in dbg_d:
                            S.dma("sp", lambda e: e.dma_start(out=dbg_d["x2_%d" % L][j * 128:(j + 1) * 128, :], in_=XO[b][:]), [xo_t[b]], [dbg_tok])
                S.barrier()
            if stop_after == ("H", L):
                break
        except _Stop:
            pass
        S.enabled = True
        S.barrier()
        print("instructions:", S.nins, "waits:", S.nwait)
    return nc


_CONST = None


def _consts():
    global _CONST
    if _CONST is not None:
        return _CONST
    bf = ml_dtypes.bfloat16
    c = {}
    c["k_ident"] = np.eye(128, dtype=np.float32).astype(bf)
    c["k_ones"] = np.ones((128, 128), np.float32)
    pp = np.arange(128)
    c["k_tri"] = (pp[:, None] < pp[None, :]).astype(np.float32)
    c["k_maskL"] = (pp[:, None] >= pp[None, :]).astype(np.float32).astype(bf)
    c["k_maskU"] = (pp[:, None] <= pp[None, :]).astype(np.float32).astype(bf)
    c["k_iotaB"] = np.ascontiguousarray(np.broadcast_to(np.arange(128, dtype=np.float32)[None, None, :], (128, 16, 128))).reshape(128, 2048)
    c["k_iotaA"] = np.ascontiguousarray(np.broadcast_to(np.arange(4, dtype=np.float32)[None, None, :], (128, 16, 4))).reshape(128, 64)
    c["k_pcol"] = np.arange(128, dtype=np.float32).reshape(128, 1)
    rows = 4096 // 64
    row = np.repeat(np.arange(rows), 64).astype(np.float32)
    col = np.tile(np.arange(64), rows).astype(np.float32)
    inv_freq = (10000.0 ** (-np.arange(0, 32, 2, dtype=np.float32) / np.float32(32))).astype(np.float32)
    ang = np.stack([row[:, None] * inv_freq, col[:, None] * inv_freq], axis=1).astype(np.float32)
    cos = np.ones((NT * 128, 32), np.float32)
    sin = np.zeros((NT * 128, 32), np.float32)
    cos[256:] = np.cos(ang).astype(np.float32).reshape(4096, 32)
    sin[256:] = np.sin(ang).astype(np.float32).reshape(4096, 32)
    c["k_cos"], c["k_sin"] = cos, sin
    t = np.arange(4096, dtype=np.int64)
    ph = (t[:, None] * t[None, :]) % 4096
    a = ph.astype(np.float64) * (2 * np.pi / 4096)
    c["k_ct"] = (np.cos(a) / 64.0).astype(np.float32).astype(bf)
    c["k_sn"] = (-np.sin(a) / 64.0).astype(np.float32).astype(bf)
    t2 = np.arange(256, dtype=np.int64)
    a2 = ((t2[:, None] * t2[None, :]) % 256).astype(np.float64) * (2 * np.pi / 256)
    c["k_ctc"] = np.concatenate([np.cos(a2) / 16.0, -np.sin(a2) / 16.0], axis=1).astype(np.float32).astype(bf)
    t3 = np.arange(64, dtype=np.int64)
    a3 = ((t3[:, None] * t3[None, :]) % 64).astype(np.float64) * (2 * np.pi / 64)
    c["k_cc"] = np.concatenate([np.cos(a3) / 8.0, np.sin(a3) / 8.0], axis=1).astype(np.float32)
    tg = np.zeros((128, 32), np.float32)
    tg[:, :16] = 512.0
    tg[:, 16:] = 32.0
    c["k_tgt"] = tg
    _CONST = c
    return c


def make_in_map(inputs, b):
    m = dict(_consts())
    m["x"] = np.ascontiguousarray(inputs["x"][b])
    m["ctx"] = np.ascontiguousarray(inputs["ctx"][b])
    cT = np.zeros((128, 8, 2), np.float32)
    cT[:, :, 0] = np.asarray(inputs["c"][b]).reshape(8, 128).T
    cT[:, :, 1] = np.asarray(inputs["c_ctx"]).reshape(8, 128).T
    m["cT"] = cT.reshape(128, 16)
    for k in ("w_mod", "b_mod", "w_in", "q_norm_a", "k_norm_a", "w_fourier", "sink_c", "w_out", "ln1_g", "ln1_b",
              "w_router", "w_gate", "w_up", "w_down", "ln2_g", "ln2_b"):
        m[k] = np.ascontiguousarray(inputs[k])
    m["b_fourier"] = np.ascontiguousarray(np.asarray(inputs["b_fourier"]).reshape(2, 256))
    m["w_in_uT"] = np.ascontiguousarray(np.transpose(np.asarray(inputs["w_in"])[:, :, 768:1024], (0, 2, 1)))
    return m


_NC = None


def _get_nc():
    global _NC
    if _NC is None:
        nc = bass.Bass("TRN2", target_bir_lowering=False)
        build(nc, n_layers=2)
        _NC = nc
    return _NC


def kernel(**inputs):
    inputs = {k: np.asarray(v) for k, v in inputs.items()}
    nc = _get_nc()
    n = 8
    in_maps = [make_in_map(inputs, b) for b in range(n)]
    res = run_bass_kernel_spmd(nc, in_maps, core_ids=list(range(n)))
    out = np.stack([np.asarray(res.results[b]["out"], dtype=np.float32) for b in range(n)], axis=0)
    return out
```

```python
import numpy as np
import ml_dtypes
from contextlib import ExitStack
import concourse.bass as bass
import concourse.mybir as mybir
from concourse.bass_utils import run_bass_kernel_spmd

F32 = mybir.dt.float32
BF16 = mybir.dt.bfloat16
I32 = mybir.dt.int32
ALU = mybir.AluOpType
AF = mybir.ActivationFunctionType
AX = mybir.AxisListType

D = 1024
NT = 34
NE = 16
FF = 2048
ALPHA = 4.0 ** 0.25
LN_EPS = 1e-5
RMS_EPS = 1e-6
C_QA, C_KA, C_QC, C_KC, C_UC, C_US, C_VA, C_VC = 0, 512, 640, 896, 1024, 1280, 1536, 1664
NCOL = 1792


class _Stop(Exception):
    pass


class Tok:
    __slots__ = ("w", "r")

    def __init__(self):
        self.w = None
        self.r = {}


def toks(n):
    return [Tok() for _ in range(n)]


class Sched:
    def __init__(self, nc, es, n_dma=44):
        self.nc = nc
        self.E = {"pe": nc.tensor, "act": nc.scalar, "dve": nc.vector, "pool": nc.gpsimd, "sp": nc.sync}
        self.sem = {k: es.enter_context(nc.semaphore("c_" + k)) for k in ("pe", "act", "dve", "pool")}
        self.cnt = {k: 0 for k in self.sem}
        self.dsem = [es.enter_context(nc.semaphore("d%d" % i)) for i in range(n_dma)]
        self.dcnt = [0] * n_dma
        self.dnext = 0
        self.dnext_sw = 0
        self.known = {k: {} for k in self.E}
        self.nwait = 0
        self.nins = 0
        self.enabled = True

    def _semobj(self, key):
        return self.sem[key] if isinstance(key, str) else self.dsem[key]

    def _deps(self, e, reads, writes):
        need = {}
        for t in reads:
            if t.w is not None:
                k, v = t.w
                if not (k == e and e == "pe"):
                    if need.get(k, 0) < v:
                        need[k] = v
        for t in writes:
            if t.w is not None:
                k, v = t.w
                if k != e and need.get(k, 0) < v:
                    need[k] = v
            for k, v in t.r.items():
                if k != e and need.get(k, 0) < v:
                    need[k] = v
        return need

    def _wait(self, e, need):
        kn = self.known[e]
        for k, v in need.items():
            if kn.get(k, 0) >= v:
                continue
            self.E[e].wait_ge(self._semobj(k), v)
            kn[k] = v
            self.nwait += 1

    def _mark(self, me, reads, writes):
        k, v = me
        for t in reads:
            if t.r.get(k, 0) < v:
                t.r[k] = v
        for t in writes:
            t.w = me
            t.r = {}

    def op(self, e, fn, reads=(), writes=()):
        if not self.enabled:
            return
        self._wait(e, self._deps(e, reads, writes))
        ins = fn(self.E[e])
        self.cnt[e] += 1
        ins.then_inc(self.sem[e], 1)
        self.nins += 1
        self._mark((e, self.cnt[e]), reads, writes)

    def dma(self, e, fn, reads=(), writes=()):
        if not self.enabled:
            return
        half = len(self.dsem) // 2
        if e == "pool":
            k = half + self.dnext_sw
            self.dnext_sw = (self.dnext_sw + 1) % (len(self.dsem) - half)
        else:
            k = self.dnext
            self.dnext = (self.dnext + 1) % half
        need = self._deps(e, reads, writes)
        if self.dcnt[k] > 0:
            need[k] = max(need.get(k, 0), 16 * self.dcnt[k])
        self._wait(e, need)
        ins = fn(self.E[e])
        self.dcnt[k] += 1
        ins.then_inc(self.dsem[k], 16)
        self.nins += 1
        self._mark((k, 16 * self.dcnt[k]), reads, writes)

    def barrier(self):
        if not self.enabled:
            return
        need = {k: v for k, v in self.cnt.items() if v > 0}
        for k in range(len(self.dsem)):
            if self.dcnt[k] > 0:
                need[k] = 16 * self.dcnt[k]
        for e in self.E:
            self._wait(e, {k: v for k, v in need.items() if k != e})


def build(nc, n_layers=2, dbg=None, stop_after=None, ml_tiles=None, stop_at=None, bc_bf16=False):
    dbg = dbg or {}
    es_top = ExitStack()
    with es_top as es:
        S = Sched(nc, es)
        E = S.E

        def chk(tag):
            if stop_at == tag:
                S.enabled = False

        def dram(name, shape, dt, kind="ExternalInput"):
            return nc.dram_tensor(name, list(shape), dt, kind=kind).ap()

        x_d = dram("x", [4096, D], F32)
        ctx_d = dram("ctx", [256, D], F32)
        cT_d = dram("cT", [128, 16], F32)
        w_mod_d = dram("w_mod", [2, D, 6 * D], F32)
        b_mod_d = dram("b_mod", [2, 6 * D], F32)
        w_in_d = dram("w_in", [2, D, 1536], F32)
        w_in_uT_d = dram("w_in_uT", [2, 256, D], F32)
        qn_d = dram("q_norm_a", [2, 64], F32)
        kn_d = dram("k_norm_a", [2, 64], F32)
        wf_d = dram("w_fourier", [2, 4, 64, 64], F32)
        bf_d = dram("b_fourier", [2, 256], F32)
        sink_d = dram("sink_c", [2, 4], F32)
        w_out_d = dram("w_out", [2, D, D], F32)
        ln1g_d = dram("ln1_g", [2, D], F32)
        ln1b_d = dram("ln1_b", [2, D], F32)
        wr_d = dram("w_router", [2, D, NE], F32)
        wg_d = dram("w_gate", [2, NE, D, FF], F32)
        wu_d = dram("w_up", [2, NE, D, FF], F32)
        wd_d = dram("w_down", [2, NE, FF, D], F32)
        ln2g_d = dram("ln2_g", [2, D], F32)
        ln2b_d = dram("ln2_b", [2, D], F32)
        ident_d = dram("k_ident", [128, 128], BF16)
        onesf_d = dram("k_ones", [128, 128], F32)
        tri_d = dram("k_tri", [128, 128], F32)
        maskL_d = dram("k_maskL", [128, 128], BF16)
        maskU_d = dram("k_maskU", [128, 128], BF16)
        iotaB_d = dram("k_iotaB", [128, 16 * 128], F32)
        iotaA_d = dram("k_iotaA", [128, 64], F32)
        pcol_d = dram("k_pcol", [128, 1], F32)
        cos_d = dram("k_cos", [NT * 128, 32], F32)
        sin_d = dram("k_sin", [NT * 128, 32], F32)
        ct_d = dram("k_ct", [4096, 4096], BF16)
        sn_d = dram("k_sn", [4096, 4096], BF16)
        ctc_d = dram("k_ctc", [256, 512], BF16)
        cc_d = dram("k_cc", [64, 128], F32)
        tgt_d = dram("k_tgt", [128, 32], F32)
        out_d = dram("out", [4096, D], F32, kind="ExternalOutput")
        dbg_d = {k: dram("dbg_" + k, shp, F32, kind="ExternalOutput") for k, shp in dbg.items()}
        mod_d = dram("s_mod", [2, 2, 6 * D], F32, kind="Internal")
        xcur_d = dram("s_xcur", [NT * 128, D], F32, kind="Internal")
        h2_d = dram("s_h2", [NT * 128, D], BF16, kind="Internal")
        acc_d = dram("s_acc", [NT * 128, D], F32, kind="Internal")
        mod_tok = toks(2)
        xcur_tok = toks(NT)
        h2_tok = toks(NT)
        acc_tok = toks(NT)
        accs_tok = Tok()
        out_tok = toks(NT)
        dbg_tok = Tok()

        uid = [0]

        def sb(stack, name, shape, dt):
            uid[0] += 1
            return stack.enter_context(nc.sbuf_tensor("sb%d_%s" % (uid[0], name), list(shape), dt))

        ps = es.enter_context(nc.psum_tensor("ps", [128, 4096], F32))
        pst = toks(8)

        def bank(b, p0=0, p1=128, c0=0, c1=512):
            return ps[p0:p1, b * 512 + c0:b * 512 + c1]

        ident = sb(es, "ident", [128, 128], BF16)
        onesf = sb(es, "onesf", [128, 128], F32)
        cT = sb(es, "cT", [128, 16], F32)
        cTs = sb(es, "cTs", [128, 16], F32)
        k_tok = Tok()
        S.dma("sp", lambda e: e.dma_start(out=ident[:], in_=ident_d), [], [k_tok])
        S.dma("sp", lambda e: e.dma_start(out=onesf[:], in_=onesf_d), [], [k_tok])
        S.dma("sp", lambda e: e.dma_start(out=cT[:], in_=cT_d), [], [k_tok])
        S.op("act", lambda e: e.activation(out=cTs[:], in_=cT[:], func=AF.Silu), [k_tok], [k_tok])

        AFF = [sb(es, "AFF", [128, NT, NE], F32)] * 2
        IDX = [sb(es, "IDX", [128, 16, 4], I32)] * 2
        GATE = [sb(es, "GATE", [128, 16, 4], F32)] * 2
        IDXC = [sb(es, "IDXC", [128, 16], I32)] * 2
        GATEC = [sb(es, "GATEC", [128, 16], F32)] * 2
        idx_tok = toks(2)
        aff_t = Tok()

        def dump(name, src_ap, dst_ap, rtoks):
            if name in dbg_d:
                S.dma("sp", lambda e: e.dma_start(out=dst_ap, in_=src_ap), list(rtoks), [dbg_tok])

        try:
          for L in range(n_layers):
            last = (L == n_layers - 1)
            with ExitStack() as ph:
                wm = [sb(ph, "wm%d" % i, [128, 8, 512], F32) for i in range(6)]
                wm_t = toks(6)
                bm = sb(ph, "bm", [2, 6 * D], F32)
                md = sb(ph, "md", [2, 6 * D], F32)
                bm_t, md_t = Tok(), Tok()
                S.dma("sp", lambda e: e.dma_start(out=bm[:], in_=b_mod_d[L].partition_broadcast(2)), [], [bm_t])
                wsrc = w_mod_d[L].rearrange("(k p) n -> p k n", p=128)
                for n in range(6):
                    S.dma("sp", lambda e: e.dma_start(out=wm[n][:], in_=wsrc[:, :, n * 512:(n + 1) * 512]), [], [wm_t[n]])
                for n in range(12):
                    b = n % 2
                    wb_ = n % 6
                    for k in range(8):
                        S.op("pe", lambda e: e.matmul(bank(b, 0, 2), cTs[:, 2 * k:2 * k + 2], wm[wb_][:, k, :],
                                                      start=(k == 0), stop=(k == 7)),
                             [k_tok, wm_t[wb_]], [pst[b]])
                    if n + 6 < 12:
                        S.dma("sp", lambda e: e.dma_start(out=wm[wb_][:], in_=wsrc[:, :, (n + 6) * 512:(n + 7) * 512]), [], [wm_t[wb_]])
                    S.op("dve", lambda e: e.tensor_tensor(out=md[:, n * 512:(n + 1) * 512], in0=bank(b, 0, 2),
                                                          in1=bm[:, n * 512:(n + 1) * 512], op=ALU.add),
                         [pst[b], bm_t], [md_t])
                for c in (1, 4):
                    S.op("dve", lambda e: e.tensor_scalar(out=md[:, c * D:(c + 1) * D], in0=md[:, c * D:(c + 1) * D],
                                                          scalar1=1.0, scalar2=None, op0=ALU.add), [md_t], [md_t])
                S.dma("sp", lambda e: e.dma_start(out=mod_d[L], in_=md[:]), [md_t], [mod_tok[L]])
                dump("mod%d" % L, md[:], dbg_d.get("mod%d" % L), [md_t])
                S.barrier()
            if stop_after == ("mod", L):
                break

            def modrow(dst, which, chunk, tok):
                src = mod_d[L, which, chunk * D:(chunk + 1) * D].partition_broadcast(128)
                S.dma("sp", lambda e: e.dma_start(out=dst[:], in_=src), [mod_tok[L]], [tok])

            def rowb(dst, src_row, tok):
                S.dma("sp", lambda e: e.dma_start(out=dst[:], in_=src_row.partition_broadcast(128)), [], [tok])

            def src_rows(j):
                if L == 0:
                    return (ctx_d[j * 128:(j + 1) * 128, :] if j < 2 else x_d[(j - 2) * 128:(j - 1) * 128, :]), []
                return xcur_d[j * 128:(j + 1) * 128, :], [xcur_tok[j]]

            with ExitStack() as lay:
                QA = sb(lay, "QA", [128, NT, 4, 128], BF16)
                KA = sb(lay, "KA", [128, NT * 128], BF16)
                VA = sb(lay, "VA", [128, NT, 2, 65], BF16)
                QC = sb(lay, "QC", [128, NT, 2, 128], BF16)
                KC = sb(lay, "KC", [128, NT * 128], BF16)
                VC = sb(lay, "VC", [128, NT, 2, 65], BF16)
                OBs = sb(lay, "OBs", [128, NT, 256], BF16)
                qa_t, ka_t, va_t, qc_t, kc_t, vc_t, ob_t = (toks(NT) for _ in range(7))
                vinit = Tok()
                S.op("pool", lambda e: e.memset(VA[:], 1.0), [], [vinit])
                S.op("pool", lambda e: e.memset(VC[:], 1.0), [], [vinit])

                with ExitStack() as ph_outer:
                  U2 = sb(ph_outer, "U2", [128, NT, 512], BF16)
                  u2_t = toks(NT)
                  with ExitStack() as ph:
                    WINB = sb(ph, "WINB", [128, 8, NCOL], BF16)
                    winb_t = Tok()
                    wsrc = w_in_d[L].rearrange("(k p) n -> p k n", p=128)
                    for (dc, sc, wd) in ((C_QA, 0, 512), (C_KA, 1024, 128), (C_QC, 512, 256), (C_KC, 1280, 128),
                                         (C_VA, 1152, 128), (C_VC, 1408, 128)):
                        S.dma("pool", lambda e: e.dma_start(out=WINB[:, :, dc:dc + wd], in_=wsrc[:, :, sc:sc + wd]),
                              [], [winb_t])
                    with ExitStack() as ff:
                        CC = sb(ff, "CC", [64, 128], F32)
                        WF = sb(ff, "WF", [64, 4, 64], F32)
                        MCS = sb(ff, "MCS", [64, 2, 4, 64], F32)
                        WUT = sb(ff, "WUT", [64, 4, D], F32)
                        f_t = Tok()
                        S.dma("sp", lambda e: e.dma_start(out=CC[:], in_=cc_d), [], [f_t])
                        S.dma("sp", lambda e: e.dma_start(out=WF[:], in_=wf_d[L].rearrange("g c d -> c g d")), [], [f_t])
                        S.dma("sp", lambda e: e.dma_start(out=WUT[:], in_=w_in_uT_d[L].rearrange("(g c) d -> c g d", c=64)),
                              [], [f_t])
                        for cs in range(2):
                            S.op("pe", lambda e: e.matmul(bank(cs, 0, 64, 0, 256), CC[:, cs * 64:(cs + 1) * 64],
                                                          WF[:].rearrange("c g d -> c (g d)"), start=True, stop=True),
                                 [f_t], [pst[cs]])
                            S.op("dve", lambda e: e.tensor_copy(out=MCS[:, cs].rearrange("c g d -> c (g d)"),
                                                                in_=bank(cs, 0, 64, 0, 256)), [pst[cs]], [f_t])
                        for k in range(8):
                            b = 2 + (k % 2)
                            for cs in range(2):
                                for g in range(4):
                                    c0 = cs * 256 + g * 64
                                    S.op("pe", lambda e: e.matmul(bank(b, 0, 128, c0, c0 + 64),
                                                                  WUT[:, g, k * 128:(k + 1) * 128], MCS[:, cs, g, :],
                                                                  start=True, stop=True), [f_t], [pst[b]])
                            S.op("dve", lambda e: e.tensor_copy(out=WINB[:, k, C_UC:C_UC + 512], in_=bank(b)),
                                 [pst[b]], [winb_t])
                        S.barrier()
                    SH1 = sb(ph, "SH1", [128, D], F32)
                    SC1 = sb(ph, "SC1", [128, D], F32)
                    GQK = sb(ph, "GQK", [128, 10, 64], F32)
                    mr_t = Tok()
                    g_t = Tok()
                    for h in range(8):
                        rowb(GQK[:, h, :], qn_d[L], g_t)
                    for h in range(8, 10):
                        rowb(GQK[:, h, :], kn_d[L], g_t)
                    XT = [sb(ph, "XT%d" % i, [128, D], F32) for i in range(2)]
                    xt_t = toks(2)
                    XN = sb(ph, "XN", [128, D], F32)
                    Hb = sb(ph, "Hb", [128, D], BF16)
                    HT = sb(ph, "HT", [128, 8, 128], BF16)
                    ST = sb(ph, "ST", [128, 2, 6], F32)
                    MV = sb(ph, "MV", [128, 8], F32)
                    NEGHA = sb(ph, "NEGHA", [128, 1], F32)
                    S.op("pool", lambda e: e.memset(NEGHA[:], -0.5), [], [g_t])
                    SQ = sb(ph, "SQ", [128, 640], F32)
                    MS = sb(ph, "MS", [128, 16], F32)
                    NRM = SQ[:].rearrange("p (h d) -> p h d", d=64)
                    R = sb(ph, "R", [128, 16, 64], F32)
                    RO = sb(ph, "RO", [128, 16, 64], BF16)
                    T1 = sb(ph, "T1", [128, 16, 2, 16], F32)
                    T2 = sb(ph, "T2", [128, 16, 2, 16], F32)
                    CS = [sb(ph, "CS%d" % i, [128, 2, 32], F32) for i in range(2)]
                    cs_t = toks(2)
                    xn_t, hb_t, ht_t, st_t, mv_t, sq_t, ms_t, nrm_t, r_t, ro_t, tt_t = (Tok() for _ in range(11))
                    for j in range(NT):
                        if j == 0 or j == 2:
                            w = 1 if j == 0 else 0
                            modrow(SH1, w, 0, mr_t)
                            modrow(SC1, w, 1, mr_t)
                        b = j % 2
                        rows, rt = src_rows(j)
                        S.dma("sp", lambda e: e.dma_start(out=XT[b][:], in_=rows), rt, [xt_t[b]])
                        S.dma("sp", lambda e: e.dma_start(out=CS[b][:, 0, :], in_=cos_d[j * 128:(j + 1) * 128, :]), [], [cs_t[b]])
                        S.dma("sp", lambda e: e.dma_start(out=CS[b][:, 1, :], in_=sin_d[j * 128:(j + 1) * 128, :]), [], [cs_t[b]])
                        xt = XT[b]
                        for hh in range(2):
                            S.op("dve", lambda e: e.bn_stats(out=ST[:, hh, :], in_=xt[:, hh * 512:(hh + 1) * 512]),
                                 [xt_t[b]], [st_t])
                        S.op("dve", lambda e: e.bn_aggr(out=MV[:, 0:2], in_=ST[:].rearrange("p a b -> p (a b)")),
                             [st_t], [mv_t])
                        S.op("pool", lambda e: e.tensor_scalar(out=MV[:, 2:3], in0=MV[:, 1:2], scalar1=LN_EPS, scalar2=None,
                                                               op0=ALU.add), [mv_t], [mv_t])
                        S.op("pool", lambda e: e.tensor_tensor(out=MV[:, 3:4], in0=MV[:, 2:3], in1=NEGHA[:], op=ALU.pow),
                             [mv_t, g_t], [mv_t])
                        S.op("dve", lambda e: e.tensor_scalar(out=XN[:], in0=xt[:], scalar1=MV[:, 0:1], scalar2=MV[:, 3:4],
                                                              op0=ALU.subtract, op1=ALU.mult), [mv_t, xt_t[b]], [xn_t])
                        S.op("dve", lambda e: e.tensor_tensor(out=XN[:], in0=XN[:], in1=SC1[:], op=ALU.mult),
                             [xn_t, mr_t], [xn_t])
                        S.op("dve", lambda e: e.tensor_tensor(out=Hb[:], in0=XN[:], in1=SH1[:], op=ALU.add),
                             [xn_t, mr_t], [hb_t])
                        for k in range(8):
                            S.op("pe", lambda e: e.matmul(bank(k // 4, 0, 128, (k % 4) * 128, (k % 4) * 128 + 128),
                                                          Hb[:, k * 128:(k + 1) * 128], ident[:], start=True, stop=True),
                                 [hb_t, k_tok], [pst[k // 4]])
                        S.op("act", lambda e: e.copy(out=HT[:, 0:4, :].rearrange("p a b -> p (a b)"), in_=bank(0)),
                             [pst[0]], [ht_t])
                        S.op("dve", lambda e: e.tensor_copy(out=HT[:, 4:8, :].rearrange("p a b -> p (a b)"), in_=bank(1)),
                             [pst[1]], [ht_t])
                        for cg in range(4):
                            c0, c1 = cg * 512, min(NCOL, cg * 512 + 512)
                            for k in range(8):
                                S.op("pe", lambda e: e.matmul(bank(2 + cg, 0, 128, 0, c1 - c0), HT[:, k, :],
                                                              WINB[:, k, c0:c1], start=(k == 0), stop=(k == 7)),
                                     [ht_t, winb_t], [pst[2 + cg]])
                        P = ps[:, 2 * 512:2 * 512 + NCOL]
                        if ("p%d" % L) in dbg_d and j == 2:
                            for (a0, a1) in ((0, 1024), (1024, NCOL)):
                                S.op("dve", lambda e: e.tensor_copy(out=XN[:, 0:a1 - a0], in_=P[:, a0:a1]),
                                     [pst[2], pst[3], pst[4], pst[5]], [xn_t])
                                S.dma("sp", lambda e: e.dma_start(out=dbg_d["p%d" % L][:, a0:a1], in_=XN[:, 0:a1 - a0]),
                                      [xn_t], [dbg_tok])
                        S.op("act", lambda e: e.activation(out=SQ[:], in_=P[:, 0:640], func=AF.Square),
                             [pst[2], pst[3]], [sq_t])
                        S.op("dve", lambda e: e.tensor_reduce(out=MS[:, 0:10], in_=SQ[:].rearrange("p (h d) -> p h d", d=64),
                                                              axis=AX.X, op=ALU.add), [sq_t], [ms_t])
                        S.op("act", lambda e: e.activation(out=MS[:, 0:10], in_=MS[:, 0:10], func=AF.Sqrt, bias=RMS_EPS,
                                                           scale=1.0 / 64.0), [ms_t], [ms_t])
                        S.op("dve", lambda e: e.reciprocal(out=MS[:, 0:10], in_=MS[:, 0:10]), [ms_t], [ms_t])
                        S.op("dve", lambda e: e.tensor_tensor(out=NRM, in0=P[:, 0:640].rearrange("p (h d) -> p h d", d=64),
                                                              in1=MS[:, 0:10].unsqueeze(2).to_broadcast([128, 10, 64]),
                                                              op=ALU.mult), [ms_t, pst[2], pst[3]], [nrm_t])
                        S.op("dve", lambda e: e.tensor_tensor(
                            out=R[:, 0:8, :].rearrange("p (g kv) d -> p kv g d", kv=2),
                            in0=NRM[:, 0:8, :].rearrange("p (kv g) d -> p kv g d", kv=2),
                            in1=GQK[:, 0:8, :].rearrange("p (kv g) d -> p kv g d", kv=2), op=ALU.mult),
                            [nrm_t, g_t], [r_t])
                        S.op("dve", lambda e: e.tensor_tensor(out=R[:, 8:10, :], in0=NRM[:, 8:10, :], in1=GQK[:, 8:10, :],
                                                              op=ALU.mult), [nrm_t, g_t], [r_t])
                        S.op("act", lambda e: e.copy(
                            out=R[:, 10:14, :].rearrange("p (g kv) d -> p kv g d", kv=2),
                            in_=P[:, C_QC:C_QC + 256].rearrange("p (kv g d) -> p kv g d", kv=2, g=2)), [pst[3]], [r_t])
                        S.op("act", lambda e: e.copy(out=R[:, 14:16, :].rearrange("p h d -> p (h d)"),
                                                     in_=P[:, C_KC:C_KC + 128]), [pst[3]], [r_t])
                        Rv = R[:].rearrange("p h (a b f) -> p h a b f", a=2, b=2)
                        ROv = RO[:].rearrange("p h (a b f) -> p h a b f", a=2, b=2)
                        x1, x2 = Rv[:, :, :, 0, :], Rv[:, :, :, 1, :]
                        cosb = CS[b][:, 0, :].rearrange("p (a f) -> p a f", a=2).unsqueeze(1).to_broadcast([128, 16, 2, 16])
                        sinb = CS[b][:, 1, :].rearrange("p (a f) -> p a f", a=2).unsqueeze(1).to_broadcast([128, 16, 2, 16])
                        S.op("dve", lambda e: e.tensor_tensor(out=T1[:], in0=x1, in1=cosb, op=ALU.mult), [r_t, cs_t[b]], [tt_t])
                        S.op("dve", lambda e: e.tensor_tensor(out=T2[:], in0=x2, in1=sinb, op=ALU.mult), [r_t, cs_t[b]], [tt_t])
                        S.op("dve", lambda e: e.tensor_tensor(out=ROv[:, :, :, 0, :], in0=T1[:], in1=T2[:], op=ALU.subtract),
                             [tt_t], [ro_t])
                        S.op("dve", lambda e: e.tensor_tensor(out=T1[:], in0=x2, in1=cosb, op=ALU.mult), [r_t, cs_t[b]], [tt_t])
                        S.op("dve", lambda e: e.tensor_tensor(out=T2[:], in0=x1, in1=sinb, op=ALU.mult), [r_t, cs_t[b]], [tt_t])
                        S.op("dve", lambda e: e.tensor_tensor(out=ROv[:, :, :, 1, :], in0=T1[:], in1=T2[:], op=ALU.add),
                             [tt_t], [ro_t])
                        for blk in range(8):
                            S.op("pe", lambda e: e.matmul(bank(blk // 4, 0, 128, (blk % 4) * 128, (blk % 4) * 128 + 128),
                                                          RO[:, 2 * blk:2 * blk + 2, :].rearrange("p h d -> p (h d)"),
                                                          ident[:], start=True, stop=True), [ro_t, k_tok], [pst[blk // 4]])
                        S.op("act", lambda e: e.copy(out=QA[:, j].rearrange("p g t -> p (g t)"), in_=bank(0)),
                             [pst[0]], [qa_t[j]])
                        S.op("dve", lambda e: e.tensor_copy(out=KA[:, j * 128:(j + 1) * 128], in_=bank(1, 0, 128, 0, 128)),
                             [pst[1]], [ka_t[j]])
                        S.op("dve", lambda e: e.tensor_copy(out=QC[:, j].rearrange("p g t -> p (g t)"),
                                                            in_=bank(1, 0, 128, 128, 384)), [pst[1]], [qc_t[j]])
                        S.op("dve", lambda e: e.tensor_copy(out=KC[:, j * 128:(j + 1) * 128], in_=bank(1, 0, 128, 384, 512)),
                             [pst[1]], [kc_t[j]])
                        S.op("act", lambda e: e.copy(out=VA[:, j, :, 1:65],
                                                     in_=P[:, C_VA:C_VA + 128].rearrange("p (h d) -> p h d", d=64)),
                             [pst[5], vinit], [va_t[j]])
                        S.op("act", lambda e: e.copy(out=VC[:, j, :, 1:65],
                                                     in_=P[:, C_VC:C_VC + 128].rearrange("p (h d) -> p h d", d=64)),
                             [pst[5], vinit], [vc_t[j]])
                        S.op("dve", lambda e: e.tensor_copy(out=U2[:, j, :], in_=P[:, C_UC:C_UC + 512]), [pst[4]], [u2_t[j]])
                    S.barrier()
                  if True:
                    with ExitStack() as pd:
                        BFR = sb(pd, "BFR", [128, 256], F32)
                        bfr_t = Tok()
                        rowb(BFR, bf_d[L], bfr_t)
                        TB = [sb(pd, "TB%d" % i, [128, 2, 2048], BF16) for i in range(6)]
                        tb_t = toks(6)
                        it = 0
                        for half in range(2):
                            for tc in range(32):
                                b = it % 6
                                it += 1
                                S.dma("sp", lambda e: e.dma_start(out=TB[b][:, 0, :],
                                                                  in_=ct_d[tc * 128:(tc + 1) * 128, half * 2048:(half + 1) * 2048]),
                                      [], [tb_t[b]])
                                S.dma("sp", lambda e: e.dma_start(out=TB[b][:, 1, :],
                                                                  in_=sn_d[tc * 128:(tc + 1) * 128, half * 2048:(half + 1) * 2048]),
                                      [], [tb_t[b]])
                                for kc in range(16):
                                    for cs in range(2):
                                        S.op("pe", lambda e: e.matmul(
                                            bank(kc // 2, 0, 128, (kc % 2) * 256, (kc % 2) * 256 + 256),
                                            TB[b][:, cs, kc * 128:(kc + 1) * 128], U2[:, 2 + tc, cs * 256:(cs + 1) * 256],
                                            start=(tc == 0 and cs == 0 and kc % 2 == 0), stop=(tc == 31 and cs == 1),
                                            skip_group_check=True),
                                            [tb_t[b], u2_t[2 + tc]], [pst[kc // 2]])
                            for kc in range(16):
                                jj = 2 + half * 16 + kc
                                S.op("dve", lambda e: e.tensor_tensor(
                                    out=OBs[:, jj, :], in0=bank(kc // 2, 0, 128, (kc % 2) * 256, (kc % 2) * 256 + 256),
                                    in1=BFR[:], op=ALU.add), [pst[kc // 2], bfr_t], [ob_t[jj]])
                        if not last:
                            TBC = sb(pd, "TBC", [128, 2, 512], BF16)
                            tbc_t = Tok()
                            for tc in range(2):
                                S.dma("sp", lambda e: e.dma_start(out=TBC[:, tc, :], in_=ctc_d[tc * 128:(tc + 1) * 128, :]),
                                      [], [tbc_t])
                            for kc in range(2):
                                for tc in range(2):
                                    for cs in range(2):
                                        S.op("pe", lambda e: e.matmul(
                                            bank(0, 0, 128, kc * 256, kc * 256 + 256),
                                            TBC[:, tc, cs * 256 + kc * 128:cs * 256 + kc * 128 + 128],
                                            U2[:, tc, cs * 256:(cs + 1) * 256],
                                            start=(tc == 0 and cs == 0 and kc == 0), stop=(tc == 1 and cs == 1),
                                            skip_group_check=True),
                                            [tbc_t, u2_t[tc]], [pst[0]])
                                S.op("dve", lambda e: e.tensor_tensor(out=OBs[:, kc, :], in0=bank(0, 0, 128, kc * 256, kc * 256 + 256),
                                                                      in1=BFR[:], op=ALU.add), [pst[0], bfr_t], [ob_t[kc]])
                        S.barrier()
                if ("QA%d" % L) in dbg_d:
                    with ExitStack() as dd:
                        TMPD = sb(dd, "TMPD", [128, 4352], F32)
                        td = Tok()
                        for nm, src in (("KA", KA[:]), ("KC", KC[:])):
                            S.op("dve", lambda e: e.tensor_copy(out=TMPD[:], in_=src), ka_t + kc_t, [td])
                            dump(nm + "%d" % L, TMPD[:], dbg_d.get(nm + "%d" % L), [td])
                        for nm, src in (("QA", QA[:, 2].rearrange("p g t -> p (g t)")), ("OB", OBs[:, 2, :]),
                                        ("VA", VA[:, 2].rearrange("p h d -> p (h d)"))):
                            n = src.shape[1]
                            S.op("dve", lambda e: e.tensor_copy(out=TMPD[:, 0:n], in_=src), qa_t + ob_t + va_t, [td])
                            dump(nm + "%d" % L, TMPD[:, 0:n], dbg_d.get(nm + "%d" % L), [td])
                        S.barrier()
                if stop_after == ("A", L):
                    break
                with ExitStack() as ml:
                    WOA = sb(ml, "WOA", [128, 8, D], BF16)
                    WOB = sb(ml, "WOB", [128, 2, D], BF16)
                    WOC = sb(ml, "WOC", [128, 4, D], BF16)
                    WR = sb(ml, "WR", [128, 8, NE], BF16)
                    w_t = Tok()
                    S.op("pool", lambda e: e.memset(WOA[:], 0.0), [], [w_t])
                    S.op("pool", lambda e: e.memset(WOC[:], 0.0), [], [w_t])
                    S.dma("pool", lambda e: e.dma_start(out=WOA[1:65], in_=w_out_d[L, 0:512, :].rearrange("(h p) n -> p h n", p=64)), [w_t], [w_t])
                    S.dma("pool", lambda e: e.dma_start(out=WOB[:], in_=w_out_d[L, 512:768, :].rearrange("(h p) n -> p h n", p=128)), [], [w_t])
                    S.dma("pool", lambda e: e.dma_start(out=WOC[1:65], in_=w_out_d[L, 768:1024, :].rearrange("(h p) n -> p h n", p=64)), [w_t], [w_t])
                    S.dma("pool", lambda e: e.dma_start(out=WR[:], in_=wr_d[L].rearrange("(k p) n -> p k n", p=128)), [], [w_t])
                    G1 = sb(ml, "G1", [128, D], F32)
                    SC2 = sb(ml, "SC2", [128, D], F32)
                    SH2 = sb(ml, "SH2", [128, D], F32)
                    LN1G = sb(ml, "LN1G", [128, D], F32)
                    LN1B = sb(ml, "LN1B", [128, D], F32)
                    mr_t, ln_t = Tok(), Tok()
                    rowb(LN1G, ln1g_d[L], ln_t)
                    rowb(LN1B, ln1b_d[L], ln_t)
                    MKL = sb(ml, "MKL", [128, 128], BF16)
                    MKU = sb(ml, "MKU", [128, 128], BF16)
                    SINKE = sb(ml, "SINKE", [128, 4, 128], F32)
                    SK4 = sb(ml, "SK4", [128, 4], F32)
                    mk_t, sk_t = Tok(), Tok()
                    S.dma("sp", lambda e: e.dma_start(out=MKL[:], in_=maskL_d), [], [mk_t])
                    S.dma("sp", lambda e: e.dma_start(out=MKU[:], in_=maskU_d), [], [mk_t])
                    S.dma("sp", lambda e: e.dma_start(out=SK4[0:1, :], in_=sink_d[L:L + 1, :]), [], [sk_t])
                    S.op("act", lambda e: e.activation(out=SK4[0:1, :], in_=SK4[0:1, :], func=AF.Exp), [sk_t], [sk_t])
                    S.op("dve", lambda e: e.tensor_copy(out=SINKE[0:1, :, :], in_=SK4[0:1, :].unsqueeze(2).to_broadcast([1, 4, 128])),
                         [sk_t], [sk_t])
                    PT = [sb(ml, "PT%d" % i, [128, 512], BF16) for i in range(3)]
                    pt_t = toks(3)
                    PTC = [sb(ml, "PTC%d" % i, [128, 512], BF16) for i in range(2)]
                    ptc_t = toks(2)
                    REC = sb(ml, "REC", [128, 1024], F32)
                    BCS = sb(ml, "BCS", [128, 1024], F32)
                    RECC = REC[:, 0:512]
                    BCC = BCS[:, 0:512]
                    CATA = sb(ml, "CATA", [128, 8, 128], BF16)
                    CATC = sb(ml, "CATC", [128, 4, 128], BF16)
                    CATB = sb(ml, "CATB", [128, 2, 128], BF16)
                    rec_t, bcs_t, cata_t, catc_t, catb_t = (Tok() for _ in range(5))
                    recc_t, bcc_t = rec_t, bcs_t
                    S.op("pool", lambda e: e.memset(CATA[:], 0.0), [], [cata_t])
                    S.op("pool", lambda e: e.memset(CATC[:], 0.0), [], [catc_t])
                    QZ = [sb(ml, "QZ%d" % i, [128, 2, 512], BF16) for i in range(2)]
                    QCZ = [sb(ml, "QCZ%d" % i, [128, 2, 256], BF16) for i in range(2)]
                    qz_t, qcz_t = toks(2), toks(2)
                    for i in range(2):
                        S.op("pool", lambda e: e.memset(QZ[i][:], 0.0), [], [qz_t[i]])
                        S.op("pool", lambda e: e.memset(QCZ[i][:], 0.0), [], [qcz_t[i]])
                    XT2 = [sb(ml, "XT20", [128, D], F32)] * 2
                    xt2_t = [Tok()] * 2
                    TMP = sb(ml, "TMP", [128, D], F32)
                    RR = sb(ml, "RR", [128, D], F32)
                    XN2 = sb(ml, "XN2", [128, D], F32)
                    ACC = TMP
                    H2 = sb(ml, "H2", [128, D], BF16)
                    HT2 = sb(ml, "HT2", [128, 8, 128], BF16)
                    ST2 = sb(ml, "ST2", [128, 2, 6], F32)
                    MV2 = sb(ml, "MV2", [128, 8], F32)
                    LG = sb(ml, "LG", [128, NE], F32)
                    SM = sb(ml, "SM", [128, 4], F32)
                    tmp_t, rr_t, xn2_t, h2b_t, ht2_t, st2_t, mv2_t, lg_t, sm_t = (Tok() for _ in range(9))
                    accb_t = tmp_t

                    NEGH = sb(ml, "NEGH", [128, 1], F32)
                    ngh_t = Tok()
                    S.op("pool", lambda e: e.memset(NEGH[:], -0.5), [], [ngh_t])

                    def ln_norm(dst, src, src_toks, dst_tok):
                        for hh in range(2):
                            S.op("dve", lambda e: e.bn_stats(out=ST2[:, hh, :], in_=src[:, hh * 512:(hh + 1) * 512]),
                                 src_toks, [st2_t])
                        S.op("dve", lambda e: e.bn_aggr(out=MV2[:, 0:2], in_=ST2[:].rearrange("p a b -> p (a b)")),
                             [st2_t], [mv2_t])
                        S.op("pool", lambda e: e.tensor_scalar(out=MV2[:, 2:3], in0=MV2[:, 1:2], scalar1=LN_EPS, scalar2=None,
                                                               op0=ALU.add), [mv2_t], [mv2_t])
                        S.op("pool", lambda e: e.tensor_tensor(out=MV2[:, 3:4], in0=MV2[:, 2:3], in1=NEGH[:], op=ALU.pow),
                             [mv2_t, ngh_t], [mv2_t])
                        S.op("dve", lambda e: e.tensor_scalar(out=dst[:], in0=src[:], scalar1=MV2[:, 0:1], scalar2=MV2[:, 3:4],
                                                              op0=ALU.subtract, op1=ALU.mult), [mv2_t] + list(src_toks), [dst_tok])

                    chk("ML_SETUP")
                    tiles = list(range(2, NT)) if last else list(range(NT))
                    if ml_tiles is not None:
                        tiles = list(ml_tiles)
                    CATA2 = [CATA, sb(ml, "CATA1", [128, 8, 128], BF16)]
                    CATC2 = [CATC, sb(ml, "CATC1", [128, 4, 128], BF16)]
                    CATB2 = [CATB, sb(ml, "CATB1", [128, 2, 128], BF16)]
                    H22 = [H2, sb(ml, "H21", [128, D], BF16)]
                    cata2_t, catc2_t, catb2_t, h22_t = [cata_t, Tok()], [catc_t, Tok()], [catb_t, Tok()], [h2b_t, Tok()]
                    S.op("pool", lambda e: e.memset(CATA2[1][:], 0.0), [], [cata2_t[1]])
                    S.op("pool", lambda e: e.memset(CATC2[1][:], 0.0), [], [catc2_t[1]])
                    state = {"pi": 0, "pc": 0}

                    def emit_QZ(j, b):
                        for h in range(2):
                            S.op("pool", lambda e: e.tensor_copy(out=QZ[b][64 * h:64 * h + 64, h, :],
                                                                 in_=QA[64 * h:64 * h + 64, j].rearrange("p g t -> p (g t)")),
                                 [qa_t[j]], [qz_t[b]])
                            S.op("pool", lambda e: e.tensor_copy(out=QCZ[b][64 * h:64 * h + 64, h, :],
                                                                 in_=QC[64 * h:64 * h + 64, j].rearrange("p g t -> p (g t)")),
                                 [qc_t[j]], [qcz_t[b]])

                    def emit_BC(j, b):
                        CA, CC = CATA2[b], CATC2[b]
                        ca_t, cc_t = cata2_t[b], catc2_t[b]
                        chunks = [0, 1] if j < 2 else list(range(NT))
                        steps = [(c, h) for c in chunks for h in range(2)]
                        ns = len(steps)

                        SB_ = (0, 1, 4)

                        def emit_S(n):
                            c, h = steps[n]
                            sbk = SB_[n % 3]
                            S.op("pe", lambda e: e.matmul(bank(sbk), KA[:, c * 128:(c + 1) * 128], QZ[b][:, h, :],
                                                          start=True, stop=True), [ka_t[c], qz_t[b]], [pst[sbk]])
                        emit_S(0)
                        emit_S(1)
                        emit_S(2)
                        for n in range(ns):
                            c, h = steps[n]
                            p_ = state["pi"]
                            state["pi"] = (p_ + 1) % 3
                            sbk = SB_[n % 3]
                            S.op("act", lambda e: e.activation(out=PT[p_][:], in_=bank(sbk), func=AF.Exp, scale=0.125),
                                 [pst[sbk]], [pt_t[p_]])
                            S.op("pe", lambda e: e.matmul(bank(2 + h, 0, 65), VA[:, c, h, :], PT[p_][:],
                                                          start=(n < 2), stop=(n >= ns - 2)),
                                 [va_t[c], pt_t[p_]], [pst[2 + h]])
                            if n + 3 < ns:
                                emit_S(n + 3)
                        chk("B_LOOP")
                        for h in range(2):
                            S.op("act", lambda e: e.activation(out=REC[0:1, h * 512:(h + 1) * 512], in_=bank(2 + h, 0, 1), func=AF.Ln),
                                 [pst[2 + h]], [rec_t])
                            S.op("act", lambda e: e.activation(out=REC[0:1, h * 512:(h + 1) * 512], in_=REC[0:1, h * 512:(h + 1) * 512],
                                                               func=AF.Exp, scale=-1.0), [rec_t], [rec_t])
                            S.op("pe", lambda e: e.matmul(bank(6 + h, 0, 65), onesf[0:1, 0:65], REC[0:1, h * 512:(h + 1) * 512],
                                                          start=True, stop=True), [rec_t, k_tok], [pst[6 + h]])
                            S.op("dve", lambda e: e.tensor_copy(out=BCS[0:65, h * 512:(h + 1) * 512], in_=bank(6 + h, 0, 65)), [pst[6 + h]], [bcs_t])
                            S.op("dve", lambda e: e.tensor_tensor(out=CA[0:65, 4 * h:4 * h + 4, :].rearrange("p g t -> p (g t)"),
                                                                  in0=bank(2 + h, 0, 65), in1=BCS[0:65, h * 512:(h + 1) * 512],
                                                                  op=ALU.mult), [pst[2 + h], bcs_t], [ca_t])
                        chk("B_NORM")
                        cks = [(0, None), (1, None)]
                        if j >= 2:
                            if j - 1 >= 2:
                                cks.append((j - 1, MKL))
                            cks.append((j, None))
                            if j + 1 < NT:
                                cks.append((j + 1, MKU))
                        for ci, (c, mk) in enumerate(cks):
                            for h in range(2):
                                S.op("pe", lambda e: e.matmul(bank(h, 0, 128, 0, 256), KC[:, c * 128:(c + 1) * 128], QCZ[b][:, h, :],
                                                              start=True, stop=True), [kc_t[c], qcz_t[b]], [pst[h]])
                            q_ = state["pc"]
                            state["pc"] = (q_ + 1) % 2
                            S.op("act", lambda e: e.activation(out=PTC[q_][:].rearrange("p (b c) -> p b c", b=2),
                                                               in_=ps[:, 0:1024].rearrange("p (b c) -> p b c", b=2)[:, :, 0:256],
                                                               func=AF.Exp, scale=0.125),
                                 [pst[0], pst[1]], [ptc_t[q_]])
                            if mk is not None:
                                S.op("pool", lambda e: e.tensor_tensor(
                                    out=PTC[q_][:].rearrange("p (a t) -> p a t", a=4),
                                    in0=PTC[q_][:].rearrange("p (a t) -> p a t", a=4),
                                    in1=mk[:].unsqueeze(1).to_broadcast([128, 4, 128]), op=ALU.mult),
                                    [ptc_t[q_], mk_t], [ptc_t[q_]])
                            for h in range(2):
                                S.op("pe", lambda e: e.matmul(bank(5, 0, 65, h * 256, (h + 1) * 256), VC[:, c, h, :],
                                                              PTC[q_][:, h * 256:(h + 1) * 256],
                                                              start=(ci == 0 and h == 0), stop=(ci == len(cks) - 1),
                                                              skip_group_check=True), [vc_t[c], ptc_t[q_]], [pst[5]])
                        for a4 in range(4):
                            S.op("act", lambda e: e.activation(out=RECC[0:1, a4 * 128:(a4 + 1) * 128], in_=bank(5, 0, 1, a4 * 128, (a4 + 1) * 128),
                                                               func=AF.Ln, bias=SK4[0:1, a4:a4 + 1], scale=1.0), [pst[5], sk_t], [recc_t])
                        S.op("act", lambda e: e.activation(out=RECC[0:1, :], in_=RECC[0:1, :], func=AF.Exp, scale=-1.0), [recc_t], [recc_t])
                        S.op("pe", lambda e: e.matmul(bank(4, 0, 65), onesf[0:1, 0:65], RECC[0:1, :], start=True, stop=True),
                             [recc_t, k_tok], [pst[4]])
                        S.op("dve", lambda e: e.tensor_copy(out=BCC[0:65, :], in_=bank(4, 0, 65)), [pst[4]], [bcc_t])
                        S.op("dve", lambda e: e.tensor_tensor(out=CC[0:65].rearrange("p a t -> p (a t)"), in0=bank(5, 0, 65),
                                                              in1=BCC[0:65, :], op=ALU.mult), [pst[5], bcc_t], [cc_t])
                        chk("C")

                    def emit_E1(j, b):
                        CA, CC, CB, HH = CATA2[b], CATC2[b], CATB2[b], H22[b]
                        ca_t, cc_t, cb_t, hh_t = cata2_t[b], catc2_t[b], catb2_t[b], h22_t[b]
                        if j == tiles[0] or j == 2:
                            w = 1 if j < 2 else 0
                            modrow(G1, w, 2, mr_t)
                            modrow(SH2, w, 3, mr_t)
                            modrow(SC2, w, 4, mr_t)
                        rows, rt = src_rows(j)
                        S.dma("sp", lambda e: e.dma_start(out=XT2[b][:], in_=rows), rt, [xt2_t[b]])
                        for m in range(2):
                            S.op("pe", lambda e: e.matmul(bank(6, 0, 128, m * 128, (m + 1) * 128), OBs[:, j, m * 128:(m + 1) * 128],
                                                          ident[:], start=True, stop=True), [ob_t[j], k_tok], [pst[6]])
                        S.op("dve", lambda e: e.tensor_copy(out=CB[:].rearrange("p m t -> p (m t)"), in_=bank(6, 0, 128, 0, 256)),
                             [pst[6]], [cb_t])
                        for n in range(2):
                            mms = []
                            for hd in range(8):
                                mms.append((CA[:, hd, :], WOA[:, hd, n * 512:(n + 1) * 512], ca_t))
                            for m in range(2):
                                mms.append((CB[:, m, :], WOB[:, m, n * 512:(n + 1) * 512], cb_t))
                            for hd in range(4):
                                mms.append((CC[:, hd, :], WOC[:, hd, n * 512:(n + 1) * 512], cc_t))
                            for i, (l_, r_, t_) in enumerate(mms):
                                S.op("pe", lambda e: e.matmul(bank(6 + n), l_, r_, start=(i == 0), stop=(i == len(mms) - 1)),
                                     [t_, w_t], [pst[6 + n]])
                        chk("E_PROJ")
                        O = ps[:, 6 * 512:8 * 512]
                        S.op("dve", lambda e: e.tensor_tensor(out=TMP[:], in0=O, in1=G1[:], op=ALU.mult),
                             [pst[6], pst[7], mr_t], [tmp_t])
                        S.op("dve", lambda e: e.scalar_tensor_tensor(out=RR[:], in0=XT2[b][:], scalar=ALPHA, in1=TMP[:],
                                                                     op0=ALU.mult, op1=ALU.add), [xt2_t[b], tmp_t], [rr_t])
                        ln_norm(XN2, RR, [rr_t], xn2_t)
                        S.op("dve", lambda e: e.tensor_tensor(out=XN2[:], in0=XN2[:], in1=LN1G[:], op=ALU.mult), [xn2_t, ln_t], [xn2_t])
                        S.op("pool", lambda e: e.tensor_tensor(out=RR[:], in0=XN2[:], in1=LN1B[:], op=ALU.add), [xn2_t, ln_t], [rr_t])
                        if ("x1_%d" % L) in dbg_d and j in (0, 2):
                            S.dma("sp", lambda e: e.dma_start(out=dbg_d["x1_%d" % L][(0 if j == 0 else 128):(128 if j == 0 else 256), :],
                                                              in_=RR[:]), [rr_t], [dbg_tok])
                        S.op("dve", lambda e: e.tensor_scalar(out=ACC[:], in0=RR[:], scalar1=ALPHA, scalar2=None, op0=ALU.mult),
                             [rr_t], [accb_t])
                        S.dma("sp", lambda e: e.dma_start(out=acc_d[j * 128:(j + 1) * 128, :], in_=ACC[:]), [accb_t], [acc_tok[j]])
                        ln_norm(XN2, RR, [rr_t], xn2_t)
                        S.op("dve", lambda e: e.tensor_tensor(out=XN2[:], in0=XN2[:], in1=SC2[:], op=ALU.mult), [xn2_t, mr_t], [xn2_t])
                        S.op("pool", lambda e: e.tensor_tensor(out=HH[:], in0=XN2[:], in1=SH2[:], op=ALU.add), [xn2_t, mr_t], [hh_t])
                        S.dma("sp", lambda e: e.dma_start(out=h2_d[j * 128:(j + 1) * 128, :], in_=HH[:]), [hh_t], [h2_tok[j]])
                        chk("E_LN")

                    def emit_E2(j, b):
                        HH, hh_t = H22[b], h22_t[b]
                        for k in range(8):
                            S.op("pe", lambda e: e.matmul(bank(6 + k // 4, 0, 128, (k % 4) * 128, (k % 4) * 128 + 128),
                                                          HH[:, k * 128:(k + 1) * 128], ident[:], start=True, stop=True),
                                 [hh_t, k_tok], [pst[6 + k // 4]])
                        S.op("dve", lambda e: e.tensor_copy(out=HT2[:, 0:4, :].rearrange("p a b -> p (a b)"), in_=bank(6)), [pst[6]], [ht2_t])
                        S.op("dve", lambda e: e.tensor_copy(out=HT2[:, 4:8, :].rearrange("p a b -> p (a b)"), in_=bank(7)), [pst[7]], [ht2_t])
                        for k in range(8):
                            S.op("pe", lambda e: e.matmul(bank(6, 0, 128, 0, NE), HT2[:, k, :], WR[:, k, :], start=(k == 0), stop=(k == 7)),
                                 [ht2_t, w_t], [pst[6]])
                        S.op("dve", lambda e: e.reduce_max(out=SM[:, 0:1], in_=bank(6, 0, 128, 0, NE), axis=AX.X), [pst[6]], [sm_t])
                        S.op("dve", lambda e: e.tensor_scalar(out=SM[:, 1:2], in0=SM[:, 0:1], scalar1=-1.0, scalar2=None, op0=ALU.mult),
                             [sm_t], [sm_t])
                        S.op("act", lambda e: e.activation(out=LG[:], in_=bank(6, 0, 128, 0, NE), func=AF.Exp, bias=SM[:, 1:2], scale=1.0,
                                                           accum_out=SM[:, 2:3]), [pst[6], sm_t], [lg_t, sm_t])
                        S.op("dve", lambda e: e.reciprocal(out=SM[:, 3:4], in_=SM[:, 2:3]), [sm_t], [sm_t])
                        S.op("dve", lambda e: e.tensor_scalar(out=AFF[L][:, j, :], in0=LG[:], scalar1=SM[:, 3:4], scalar2=None, op0=ALU.mult),
                             [lg_t, sm_t], [aff_t])

                    nt_ = len(tiles)
                    emit_QZ(tiles[0], 0)
                    for idx in range(nt_ + 2):
                        if idx + 1 < nt_:
                            emit_QZ(tiles[idx + 1], (idx + 1) % 2)
                        if idx < nt_:
                            emit_BC(tiles[idx], idx % 2)
                        if idx >= 2:
                            emit_E2(tiles[idx - 2], (idx - 2) % 2)
                        if 1 <= idx <= nt_:
                            emit_E1(tiles[idx - 1], (idx - 1) % 2)
                    S.barrier()
                    for nm, src in (("cata", CATA[0:65].rearrange("p a t -> p (a t)")), ("catc", CATC[0:65].rearrange("p a t -> p (a t)"))):
                        if (nm + "%d" % L) in dbg_d:
                            n_ = src.shape[1]
                            S.op("dve", lambda e: e.tensor_copy(out=TMP[0:65, 0:n_], in_=src), [cata_t, catc_t], [tmp_t])
                            S.dma("sp", lambda e: e.dma_start(out=dbg_d[nm + "%d" % L], in_=TMP[0:65, 0:n_]), [tmp_t], [dbg_tok])
                            S.barrier()
            if stop_after == ("ML", L):
                break
            with ExitStack() as pf:
                TRI = sb(pf, "TRI", [128, 128], F32)
                IOB = sb(pf, "IOB", [128, 16, 128], F32)
                IOA = sb(pf, "IOA", [128, 16, 4], F32)
                PCOL = sb(pf, "PCOL", [128, 1], F32)
                TGT = sb(pf, "TGT", [128, 32], F32)
                kf_t = Tok()
                S.dma("sp", lambda e: e.dma_start(out=TRI[:], in_=tri_d), [], [kf_t])
                S.dma("sp", lambda e: e.dma_start(out=IOB[:].rearrange("p a b -> p (a b)"), in_=iotaB_d), [], [kf_t])
                S.dma("sp", lambda e: e.dma_start(out=IOA[:].rearrange("p a b -> p (a b)"), in_=iotaA_d), [], [kf_t])
                S.dma("sp", lambda e: e.dma_start(out=PCOL[:], in_=pcol_d), [], [kf_t])
                S.dma("sp", lambda e: e.dma_start(out=TGT[:], in_=tgt_d), [], [kf_t])
                THR = sb(pf, "THR", [128, 32], F32)
                LO = sb(pf, "LO", [128, 32], F32)
                CNTP = sb(pf, "CNTP", [128, 32], F32)
                IND = sb(pf, "IND", [128, 32], F32)
                MSK = sb(pf, "MSK", [128, NT, NE], F32)
                thr_t, lo_t, cntp_t, ind_t, msk_t = (Tok() for _ in range(5))
                S.op("dve", lambda e: e.memset(LO[:], 0.0), [], [lo_t])
                S.op("dve", lambda e: e.memset(CNTP[:], 0.0), [], [cntp_t])
                S.op("dve", lambda e: e.memset(MSK[:], 0.0), [], [msk_t])
                A_ = AFF[L]
                do_ctx = not last

                def make_mask(thr):
                    S.op("dve", lambda e: e.tensor_tensor(out=MSK[:, 2:NT, :], in0=A_[:, 2:NT, :],
                                                          in1=thr[:, 0:16].unsqueeze(1).to_broadcast([128, 32, 16]), op=ALU.is_ge),
                         [aff_t, thr_t, lo_t], [msk_t])
                    if do_ctx:
                        S.op("dve", lambda e: e.tensor_tensor(out=MSK[:, 0:2, :], in0=A_[:, 0:2, :],
                                                              in1=thr[:, 16:32].unsqueeze(1).to_broadcast([128, 2, 16]), op=ALU.is_ge),
                             [aff_t, thr_t, lo_t], [msk_t])

                for it in range(28):
                    wv = 2.0 ** -(it + 1)
                    S.op("dve", lambda e: e.tensor_scalar(out=THR[:], in0=LO[:], scalar1=wv, scalar2=None, op0=ALU.add), [lo_t], [thr_t])
                    make_mask(THR)
                    S.op("dve", lambda e: e.tensor_reduce(out=CNTP[:, 0:16], in_=MSK[:, 2:NT, :].rearrange("p j e -> p e j"),
                                                          axis=AX.X, op=ALU.add), [msk_t], [cntp_t])
                    if do_ctx:
                        S.op("dve", lambda e: e.tensor_reduce(out=CNTP[:, 16:32], in_=MSK[:, 0:2, :].rearrange("p j e -> p e j"),
                                                              axis=AX.X, op=ALU.add), [msk_t], [cntp_t])
                    S.op("pe", lambda e: e.matmul(bank(0, 0, 128, 0, 32), onesf[:], CNTP[:], start=True, stop=True),
                         [cntp_t, k_tok], [pst[0]])
                    S.op("dve", lambda e: e.tensor_tensor(out=IND[:], in0=bank(0, 0, 128, 0, 32), in1=TGT[:], op=ALU.is_ge),
                         [pst[0], kf_t], [ind_t])
                    S.op("dve", lambda e: e.scalar_tensor_tensor(out=LO[:], in0=IND[:], scalar=wv, in1=LO[:], op0=ALU.mult, op1=ALU.add),
                         [ind_t, lo_t], [lo_t])
                make_mask(LO)
                POS = sb(pf, "POS", [128, NT, NE], F32)
                OFF = sb(pf, "OFF", [128, NT, NE], F32)
                POSI = sb(pf, "POSI", [128, NT, NE], I32)
                BI = sb(pf, "BI", [128, NT, NE], I32)
                AI = sb(pf, "AI", [128, NT, NE], I32)
                BFl = sb(pf, "BFl", [128, NT, NE], F32)
                AFl = sb(pf, "AFl", [128, NT, NE], F32)
                pos_t, off_t = Tok(), Tok()
                groups = [(2, NT, 1, 2)] + ([(0, 2, 3, 3)] if do_ctx else [])
                for (j0, j1, bpw, btot) in groups:
                    n_ = (j1 - j0) * NE
                    c0 = 0 if j0 == 2 else 0
                    c1 = 0 if j0 == 2 else 64
                    mview = MSK[:, j0:j1, :].rearrange("p j e -> p (j e)")
                    S.op("pe", lambda e: e.matmul(bank(bpw, 0, 128, c0, c0 + n_), TRI[:], mview, start=True, stop=True),
                         [msk_t, kf_t], [pst[bpw]])
                    S.op("pe", lambda e: e.matmul(bank(btot, 0, 128, c1, c1 + n_), onesf[:], mview, start=True, stop=True),
                         [msk_t, k_tok], [pst[btot]])
                    S.op("dve", lambda e: e.memset(OFF[:, j0, :], 0.0), [], [off_t])
                    for jj in range(j0 + 1, j1):
                        S.op("dve", lambda e: e.tensor_tensor(out=OFF[:, jj, :], in0=OFF[:, jj - 1, :],
                                                              in1=bank(btot, 0, 128, c1 + (jj - 1 - j0) * NE, c1 + (jj - j0) * NE),
                                                              op=ALU.add), [off_t, pst[btot]], [off_t])
                    pv = POS[:, j0:j1, :].rearrange("p j e -> p (j e)")
                    S.op("dve", lambda e: e.tensor_tensor(out=pv, in0=bank(bpw, 0, 128, c0, c0 + n_),
                                                          in1=OFF[:, j0:j1, :].rearrange("p j e -> p (j e)"), op=ALU.add),
                         [pst[bpw], off_t], [pos_t])
                    S.op("dve", lambda e: e.scalar_tensor_tensor(out=pv, in0=pv, scalar=1.0, in1=mview, op0=ALU.add, op1=ALU.mult),
                         [pos_t, msk_t], [pos_t])
                    S.op("dve", lambda e: e.tensor_scalar(out=pv, in0=pv, scalar1=-1.0, scalar2=None, op0=ALU.add), [pos_t], [pos_t])
                    for (o_, i_, fn) in ((POSI, POS, None), (BI, POSI, ("and", 127)), (AI, POSI, ("shr", 7)), (BFl, BI, None), (AFl, AI, None)):
                        ov = o_[:, j0:j1, :].rearrange("p j e -> p (j e)")
                        iv = i_[:, j0:j1, :].rearrange("p j e -> p (j e)")
                        if fn is None:
                            S.op("dve", lambda e: e.tensor_copy(out=ov, in_=iv), [pos_t], [pos_t])
                        else:
                            opx = ALU.bitwise_and if fn[0] == "and" else ALU.arith_shift_right
                            S.op("dve", lambda e: e.tensor_scalar(out=ov, in0=iv, scalar1=fn[1], scalar2=None, op0=opx), [pos_t], [pos_t])
                if ("pos%d" % L) in dbg_d:
                    S.dma("sp", lambda e: e.dma_start(out=dbg_d["pos%d" % L], in_=POS[:].rearrange("p j e -> p (j e)")), [pos_t], [dbg_tok])
                    S.dma("sp", lambda e: e.dma_start(out=dbg_d["aff%d" % L], in_=A_[:].rearrange("p j e -> p (j e)")), [aff_t], [dbg_tok])
                OHB = [sb(pf, "OHB%d" % i, [128, 16, 128], F32) for i in range(2)]
                ohb_t = toks(2)
                RA = sb(pf, "RA", [128, 16, 4], F32)
                R3 = [sb(pf, "R3%d" % i, [128, 16, 3, 4], F32) for i in range(2)]
                ra_t = Tok()
                r3_t = toks(2)
                for jj in range(32):
                    b = jj % 2
                    j = 2 + jj
                    S.op("dve", lambda e: e.tensor_tensor(out=OHB[b][:], in0=IOB[:], in1=BFl[:, j, :].unsqueeze(2).to_broadcast([128, 16, 128]),
                                                          op=ALU.is_equal), [kf_t, pos_t], [ohb_t[b]])
                    S.op("dve", lambda e: e.tensor_tensor(out=RA[:], in0=IOA[:], in1=AFl[:, j, :].unsqueeze(2).to_broadcast([128, 16, 4]),
                                                          op=ALU.is_equal), [kf_t, pos_t], [ra_t])
                    S.op("dve", lambda e: e.tensor_scalar(out=R3[b][:, :, 0, :], in0=RA[:], scalar1=PCOL[:, 0:1], scalar2=None, op0=ALU.mult),
                         [ra_t, kf_t], [r3_t[b]])
                    S.op("dve", lambda e: e.tensor_scalar(out=R3[b][:, :, 1, :], in0=RA[:], scalar1=float(j), scalar2=None, op0=ALU.mult),
                         [ra_t], [r3_t[b]])
                    S.op("dve", lambda e: e.tensor_tensor(out=R3[b][:, :, 2, :], in0=RA[:],
                                                          in1=A_[:, j, :].unsqueeze(2).to_broadcast([128, 16, 4]), op=ALU.mult),
                         [ra_t, aff_t], [r3_t[b]])
                    for ex in range(NE):
                        S.op("pe", lambda e: e.matmul(bank(4, 0, 128, ex * 12, ex * 12 + 12), OHB[b][:, ex, :],
                                                      R3[b][:, ex].rearrange("p c a -> p (c a)"),
                                                      start=(jj == 0 and ex == 0), stop=(jj == 31), skip_group_check=True),
                             [ohb_t[b], r3_t[b]], [pst[4]])
                CMPS = sb(pf, "CMPS", [128, 192], F32)
                cmps_t = Tok()
                S.op("dve", lambda e: e.tensor_copy(out=CMPS[:], in_=bank(4, 0, 128, 0, 192)), [pst[4]], [cmps_t])
                CMP = CMPS[:].rearrange("p (e c a) -> p e c a", e=16, c=3)
                IDXF = sb(pf, "IDXF", [128, 16, 4], F32)
                idx_t = idx_tok[L]
                S.op("dve", lambda e: e.scalar_tensor_tensor(out=IDXF[:], in0=CMP[:, :, 1, :], scalar=128.0, in1=CMP[:, :, 0, :],
                                                             op0=ALU.mult, op1=ALU.add), [cmps_t], [idx_t])
                S.op("dve", lambda e: e.tensor_copy(out=IDX[L][:], in_=IDXF[:]), [idx_t], [idx_t])
                S.op("dve", lambda e: e.tensor_copy(out=GATE[L][:], in_=CMP[:, :, 2, :]), [cmps_t], [idx_t])
                if do_ctx:
                    OHC = sb(pf, "OHC", [128, 16, 32], F32)
                    R3C = sb(pf, "R3C", [128, 16, 3], F32)
                    ohc_t, r3c_t = Tok(), Tok()
                    for j in range(2):
                        S.op("dve", lambda e: e.tensor_tensor(out=OHC[:], in0=IOB[:, :, 0:32],
                                                              in1=BFl[:, j, :].unsqueeze(2).to_broadcast([128, 16, 32]), op=ALU.is_equal),
                             [kf_t, pos_t], [ohc_t])
                        S.op("dve", lambda e: e.tensor_copy(out=R3C[:, :, 0], in_=PCOL[:, 0:1].to_broadcast([128, 16])), [kf_t], [r3c_t])
                        S.op("dve", lambda e: e.memset(R3C[:, :, 1], float(j)), [], [r3c_t])
                        S.op("dve", lambda e: e.tensor_copy(out=R3C[:, :, 2], in_=A_[:, j, :]), [aff_t], [r3c_t])
                        for ex in range(NE):
                            S.op("pe", lambda e: e.matmul(bank(5, 0, 32, ex * 3, ex * 3 + 3), OHC[:, ex, :], R3C[:, ex, :],
                                                          start=(j == 0 and ex == 0), stop=(j == 1), skip_group_check=True),
                                 [ohc_t, r3c_t], [pst[5]])
                    S.op("dve", lambda e: e.tensor_copy(out=CMPS[0:32, 0:48], in_=bank(5, 0, 32, 0, 48)), [pst[5], cmps_t, idx_t], [cmps_t])
                    CMC = CMPS[0:32, 0:48].rearrange("p (e c) -> p e c", c=3)
                    S.op("dve", lambda e: e.scalar_tensor_tensor(out=IDXF[0:32, :, 0], in0=CMC[:, :, 1], scalar=128.0, in1=CMC[:, :, 0],
                                                                 op0=ALU.mult, op1=ALU.add), [cmps_t, idx_t], [idx_t])
                    S.op("dve", lambda e: e.tensor_copy(out=IDXC[L][0:32, :], in_=IDXF[0:32, :, 0]), [idx_t], [idx_t])
                    S.op("dve", lambda e: e.tensor_copy(out=GATEC[L][0:32, :], in_=CMC[:, :, 2]), [cmps_t], [idx_t])
                if ("idx%d" % L) in dbg_d:
                    S.dma("sp", lambda e: e.dma_start(out=dbg_d["idx%d" % L], in_=IDXF[:].rearrange("p e a -> p (e a)")), [idx_t], [dbg_tok])
                    S.dma("sp", lambda e: e.dma_start(out=dbg_d["gate%d" % L], in_=GATE[L][:].rearrange("p e a -> p (e a)")), [idx_t], [dbg_tok])
                S.barrier()
            if stop_after == ("F", L):
                break
            with ExitStack() as pg:
                idx_t = idx_tok[L]
                do_ctx = not last
                NS = 544 if do_ctx else 512
                G2 = sb(pg, "G2", [128, D], F32)
                G2C = sb(pg, "G2C", [128, D], F32)
                g2_t = Tok()
                modrow(G2, 0, 5, g2_t)
                if do_ctx:
                    modrow(G2C, 1, 5, g2_t)
                XG = [[sb(pg, "XG%d_%d" % (i, a), [128, D], BF16) for a in range(5)] for i in range(2)]
                xg_t = [toks(5) for _ in range(2)]
                XGT = [sb(pg, "XGT%d" % i, [128, 8, NS], BF16) for i in range(2)]
                xgt_t = toks(2)
                HID = sb(pg, "HID", [128, 16, NS], BF16)
                hid_t = Tok()
                WG = [sb(pg, "WG%d" % i, [128, 8, 512], BF16) for i in range(4)]
                WU = [sb(pg, "WU%d" % i, [128, 8, 512], BF16) for i in range(4)]
                wg_t, wu_t = toks(4), toks(4)
                WD = [sb(pg, "WD%d" % i, [128, 16, D], BF16) for i in range(2)]
                wd_t = toks(2)
                SIL = [sb(pg, "SIL%d" % i, [128, 512], F32) for i in range(2)]
                sil_t = toks(2)
                SILC = sb(pg, "SILC", [128, 32], F32)
                silc_t = Tok()
                YS = [sb(pg, "YS%d" % i, [128, D], F32) for i in range(2)]
                ys_t = toks(2)
                ysi = 0
                fcc = 0

                def load_gather(ex):
                    xb = ex % 2
                    for a in range(4):
                        S.dma("pool", lambda e: e.indirect_dma_start(
                            out=XG[xb][a][:], out_offset=None, in_=h2_d[:, :],
                            in_offset=bass.IndirectOffsetOnAxis(ap=IDX[L][:, ex, a:a + 1], axis=0)), h2_tok + [idx_t], [xg_t[xb][a]])
                    if do_ctx:
                        S.dma("pool", lambda e: e.indirect_dma_start(
                            out=XG[xb][4][0:32, :], out_offset=None, in_=h2_d[:, :],
                            in_offset=bass.IndirectOffsetOnAxis(ap=IDXC[L][0:32, ex:ex + 1], axis=0)), h2_tok + [idx_t], [xg_t[xb][4]])

                def load_wd(ex):
                    db = ex % 2
                    wdsrc = wd_d[L, ex].rearrange("(f p) d -> p f d", p=128)
                    for q in range(4):
                        S.dma("pool", lambda e: e.dma_start(out=WD[db][:, q * 4:(q + 1) * 4, :], in_=wdsrc[:, q * 4:(q + 1) * 4, :]),
                              [], [wd_t[db]])

                def load_gu(ex, fq):
                    wgsrc = wg_d[L, ex].rearrange("(k p) f -> p k f", p=128)
                    wusrc = wu_d[L, ex].rearrange("(k p) f -> p k f", p=128)
                    S.dma("pool", lambda e: e.dma_start(out=WG[fq][:], in_=wgsrc[:, :, fq * 512:(fq + 1) * 512]), [], [wg_t[fq]])
                    S.dma("pool", lambda e: e.dma_start(out=WU[fq][:], in_=wusrc[:, :, fq * 512:(fq + 1) * 512]), [], [wu_t[fq]])

                load_gather(0)
                load_wd(0)
                for fq in range(4):
                    load_gu(0, fq)
                load_gather(1)
                load_wd(1)
                for ex in range(NE):
                    xb = ex % 2
                    db = ex % 2
                    for a in range(4):
                        for kh in range(2):
                            for kk in range(4):
                                k = kh * 4 + kk
                                S.op("pe", lambda e: e.matmul(bank(6, 0, 128, kk * 128, (kk + 1) * 128),
                                                              XG[xb][a][:, k * 128:(k + 1) * 128], ident[:], start=True, stop=True),
                                     [xg_t[xb][a], k_tok], [pst[6]])
                            eng = "act" if (a + kh) % 2 == 0 else "dve"
                            src = bank(6).rearrange("p (k t) -> p k t", k=4)
                            dst = XGT[xb][:, kh * 4:(kh + 1) * 4, a * 128:(a + 1) * 128]
                            if eng == "act":
                                S.op("act", lambda e: e.copy(out=dst, in_=src), [pst[6]], [xgt_t[xb]])
                            else:
                                S.op("dve", lambda e: e.tensor_copy(out=dst, in_=src), [pst[6]], [xgt_t[xb]])
                    if do_ctx:
                        for k in range(8):
                            S.op("pe", lambda e: e.matmul(bank(7, 0, 128, k * 32, (k + 1) * 32), XG[xb][4][0:32, k * 128:(k + 1) * 128],
                                                          ident[0:32, 0:32], start=True, stop=True), [xg_t[xb][4], k_tok], [pst[7]])
                        S.op("dve", lambda e: e.tensor_copy(out=XGT[xb][:, :, 512:544],
                                                            in_=bank(7, 0, 128, 0, 256).rearrange("p (k t) -> p k t", k=8)),
                             [pst[7]], [xgt_t[xb]])
                    if ex + 2 < NE:
                        load_gather(ex + 2)
                    for fq in range(4):
                        wb = fq
                        for fi in range(4):
                            fc = fq * 4 + fi
                            pa = fcc % 2
                            fcc += 1
                            for (Wt, wt_t, bk) in ((WG[wb], wg_t[wb], pa), (WU[wb], wu_t[wb], 2 + pa)):
                                for k in range(8):
                                    S.op("pe", lambda e: e.matmul(bank(bk), Wt[:, k, fi * 128:(fi + 1) * 128], XGT[xb][:, k, 0:512],
                                                                  start=(k == 0), stop=(k == 7)), [wt_t, xgt_t[xb]], [pst[bk]])
                            if do_ctx:
                                for (Wt, wt_t, c0) in ((WG[wb], wg_t[wb], 256), (WU[wb], wu_t[wb], 288)):
                                    for k in range(8):
                                        S.op("pe", lambda e: e.matmul(bank(7, 0, 128, c0, c0 + 32), Wt[:, k, fi * 128:(fi + 1) * 128],
                                                                      XGT[xb][:, k, 512:544], start=(k == 0), stop=(k == 7)),
                                             [wt_t, xgt_t[xb]], [pst[7]])
                            S.op("act", lambda e: e.activation(out=SIL[pa][:], in_=bank(pa), func=AF.Silu), [pst[pa]], [sil_t[pa]])
                            S.op("dve", lambda e: e.tensor_tensor(out=HID[:, fc, 0:512], in0=bank(2 + pa), in1=SIL[pa][:], op=ALU.mult),
                                 [pst[2 + pa], sil_t[pa]], [hid_t])
                            if do_ctx:
                                S.op("act", lambda e: e.activation(out=SILC[:], in_=bank(7, 0, 128, 256, 288), func=AF.Silu),
                                     [pst[7]], [silc_t])
                                S.op("dve", lambda e: e.tensor_tensor(out=HID[:, fc, 512:544], in0=bank(7, 0, 128, 288, 320), in1=SILC[:],
                                                                      op=ALU.mult), [pst[7], silc_t], [hid_t])
                        if ex + 1 < NE:
                            load_gu(ex + 1, fq)
                    slots = [(s * 128, 128, IDX[L][:, ex, s:s + 1], GATE[L][:, ex, s:s + 1], G2) for s in range(4)]
                    if do_ctx:
                        slots.append((512, 32, IDXC[L][0:32, ex:ex + 1], GATEC[L][0:32, ex:ex + 1], G2C))
                    for (s0, sn, idx_ap, gate_ap, g2row) in slots:
                        for n in range(2):
                            for fc in range(16):
                                S.op("pe", lambda e: e.matmul(bank(4 + n, 0, sn), HID[:, fc, s0:s0 + sn], WD[db][:, fc, n * 512:(n + 1) * 512],
                                                              start=(fc == 0), stop=(fc == 15)), [hid_t, wd_t[db]], [pst[4 + n]])
                        yb = ysi % 2
                        ysi += 1
                        S.op("dve", lambda e: e.scalar_tensor_tensor(out=YS[yb][0:sn, :], in0=ps[0:sn, 4 * 512:6 * 512], scalar=gate_ap,
                                                                     in1=g2row[0:sn, :], op0=ALU.mult, op1=ALU.mult),
                             [pst[4], pst[5], idx_t, g2_t], [ys_t[yb]])
                        S.dma("pool", lambda e: e.indirect_dma_start(
                            out=acc_d[:, :], out_offset=bass.IndirectOffsetOnAxis(ap=idx_ap, axis=0), in_=YS[yb][0:sn, :], in_offset=None,
                            compute_op=ALU.add),
                            [ys_t[yb], idx_t] + acc_tok, [accs_tok])
                    if ex + 2 < NE:
                        load_wd(ex + 2)
                S.barrier()
            with ExitStack() as phh:
                LN2G = sb(phh, "LN2G", [128, D], F32)
                LN2B = sb(phh, "LN2B", [128, D], F32)
                l2_t = Tok()
                rowb(LN2G, ln2g_d[L], l2_t)
                rowb(LN2B, ln2b_d[L], l2_t)
                AT = [sb(phh, "AT%d" % i, [128, D], F32) for i in range(3)]
                at_t = toks(3)
                XO = [sb(phh, "XO%d" % i, [128, D], F32) for i in range(3)]
                xo_t = toks(3)
                NEGH3 = sb(phh, "NEGH3", [128, 1], F32)
                S.op("pool", lambda e: e.memset(NEGH3[:], -0.5), [], [l2_t])
                ST3 = sb(phh, "ST3", [128, 2, 6], F32)
                MV3 = sb(phh, "MV3", [128, 8], F32)
                st3_t, mv3_t = Tok(), Tok()
                for j in (range(2, NT) if last else range(NT)):
                    b = j % 3
                    S.dma("sp", lambda e: e.dma_start(out=AT[b][:], in_=acc_d[j * 128:(j + 1) * 128, :]), [acc_tok[j], accs_tok], [at_t[b]])
                    for hh in range(2):
                        S.op("dve", lambda e: e.bn_stats(out=ST3[:, hh, :], in_=AT[b][:, hh * 512:(hh + 1) * 512]), [at_t[b]], [st3_t])
                    S.op("dve", lambda e: e.bn_aggr(out=MV3[:, 0:2], in_=ST3[:].rearrange("p a b -> p (a b)")), [st3_t], [mv3_t])
                    S.op("pool", lambda e: e.tensor_scalar(out=MV3[:, 2:3], in0=MV3[:, 1:2], scalar1=LN_EPS, scalar2=None, op0=ALU.add),
                         [mv3_t], [mv3_t])
                    S.op("pool", lambda e: e.tensor_tensor(out=MV3[:, 3:4], in0=MV3[:, 2:3], in1=NEGH3[:], op=ALU.pow), [mv3_t, l2_t], [mv3_t])
                    S.op("dve", lambda e: e.tensor_scalar(out=XO[b][:], in0=AT[b][:], scalar1=MV3[:, 0:1], scalar2=MV3[:, 3:4],
                                                          op0=ALU.subtract, op1=ALU.mult), [mv3_t, at_t[b]], [xo_t[b]])
                    S.op("dve", lambda e: e.tensor_tensor(out=XO[b][:], in0=XO[b][:], in1=LN2G[:], op=ALU.mult), [xo_t[b], l2_t], [xo_t[b]])
                    S.op("pool", lambda e: e.tensor_tensor(out=XO[b][:], in0=XO[b][:], in1=LN2B[:], op=ALU.add), [xo_t[b], l2_t], [xo_t[b]])
                    if last:
                        S.dma("sp", lambda e: e.dma_start(out=out_d[(j - 2) * 128:(j - 1) * 128, :], in_=XO[b][:]), [xo_t[b]], [out_tok[j]])
                    else:
                        S.dma("sp", lambda e: e.dma_start(out=xcur_d[j * 128:(j + 1) * 128, :], in_=XO[b][:]), [xo_t[b]], [xcur_tok[j]])
                        if ("x2_%d" % L) in dbg_d:
                            S.dma("sp", lambda e: e.dma_start(out=dbg_d["x2_%d" % L][j * 128:(j + 1) * 128, :], in_=XO[b][:]), [xo_t[b]], [dbg_tok])
                S.barrier()
            if stop_after == ("H", L):
                break
        except _Stop:
            pass
        S.enabled = True
        S.barrier()
        print("instructions:", S.nins, "waits:", S.nwait)
    return nc


_CONST = None


def _consts():
    global _CONST
    if _CONST is not None:
        return _CONST
    bf = ml_dtypes.bfloat16
    c = {}
    c["k_ident"] = np.eye(128, dtype=np.float32).astype(bf)
    c["k_ones"] = np.ones((128, 128), np.float32)
    pp = np.arange(128)
    c["k_tri"] = (pp[:, None] < pp[None, :]).astype(np.float32)
    c["k_maskL"] = (pp[:, None] >= pp[None, :]).astype(np.float32).astype(bf)
    c["k_maskU"] = (pp[:, None] <= pp[None, :]).astype(np.float32).astype(bf)
    c["k_iotaB"] = np.ascontiguousarray(np.broadcast_to(np.arange(128, dtype=np.float32)[None, None, :], (128, 16, 128))).reshape(128, 2048)
    c["k_iotaA"] = np.ascontiguousarray(np.broadcast_to(np.arange(4, dtype=np.float32)[None, None, :], (128, 16, 4))).reshape(128, 64)
    c["k_pcol"] = np.arange(128, dtype=np.float32).reshape(128, 1)
    rows = 4096 // 64
    row = np.repeat(np.arange(rows), 64).astype(np.float32)
    col = np.tile(np.arange(64), rows).astype(np.float32)
    inv_freq = (10000.0 ** (-np.arange(0, 32, 2, dtype=np.float32) / np.float32(32))).astype(np.float32)
    ang = np.stack([row[:, None] * inv_freq, col[:, None] * inv_freq], axis=1).astype(np.float32)
    cos = np.ones((NT * 128, 32), np.float32)
    sin = np.zeros((NT * 128, 32), np.float32)
    cos[256:] = np.cos(ang).astype(np.float32).reshape(4096, 32)
    sin[256:] = np.sin(ang).astype(np.float32).reshape(4096, 32)
    c["k_cos"], c["k_sin"] = cos, sin
    t = np.arange(4096, dtype=np.int64)
    ph = (t[:, None] * t[None, :]) % 4096
    a = ph.astype(np.float64) * (2 * np.pi / 4096)
    c["k_ct"] = (np.cos(a) / 64.0).astype(np.float32).astype(bf)
    c["k_sn"] = (-np.sin(a) / 64.0).astype(np.float32).astype(bf)
    t2 = np.arange(256, dtype=np.int64)
    a2 = ((t2[:, None] * t2[None, :]) % 256).astype(np.float64) * (2 * np.pi / 256)
    c["k_ctc"] = np.concatenate([np.cos(a2) / 16.0, -np.sin(a2) / 16.0], axis=1).astype(np.float32).astype(bf)
    t3 = np.arange(64, dtype=np.int64)
    a3 = ((t3[:, None] * t3[None, :]) % 64).astype(np.float64) * (2 * np.pi / 64)
    c["k_cc"] = np.concatenate([np.cos(a3) / 8.0, np.sin(a3) / 8.0], axis=1).astype(np.float32)
    tg = np.zeros((128, 32), np.float32)
    tg[:, :16] = 512.0
    tg[:, 16:] = 32.0
    c["k_tgt"] = tg
    _CONST = c
    return c


def make_in_map(inputs, b):
    m = dict(_consts())
    m["x"] = np.ascontiguousarray(inputs["x"][b])
    m["ctx"] = np.ascontiguousarray(inputs["ctx"][b])
    cT = np.zeros((128, 8, 2), np.float32)
    cT[:, :, 0] = np.asarray(inputs["c"][b]).reshape(8, 128).T
    cT[:, :, 1] = np.asarray(inputs["c_ctx"]).reshape(8, 128).T
    m["cT"] = cT.reshape(128, 16)
    for k in ("w_mod", "b_mod", "w_in", "q_norm_a", "k_norm_a", "w_fourier", "sink_c", "w_out", "ln1_g", "ln1_b",
              "w_router", "w_gate", "w_up", "w_down", "ln2_g", "ln2_b"):
        m[k] = np.ascontiguousarray(inputs[k])
    m["b_fourier"] = np.ascontiguousarray(np.asarray(inputs["b_fourier"]).reshape(2, 256))
    m["w_in_uT"] = np.ascontiguousarray(np.transpose(np.asarray(inputs["w_in"])[:, :, 768:1024], (0, 2, 1)))
    return m


_NC = None


def _get_nc():
    global _NC
    if _NC is None:
        nc = bass.Bass("TRN2", target_bir_lowering=False)
        build(nc, n_layers=2)
        _NC = nc
    return _NC


def kernel(**inputs):
    inputs = {k: np.asarray(v) for k, v in inputs.items()}
    nc = _get_nc()
    n = 8
    in_maps = [make_in_map(inputs, b) for b in range(n)]
    res = run_bass_kernel_spmd(nc, in_maps, core_ids=list(range(n)))
    out = np.stack([np.asarray(res.results[b]["out"], dtype=np.float32) for b in range(n)], axis=0)
    return out
```
